# Optimizing a Trainium2 kernel written in Bass

```python
import numpy as np
import jax
import jax.numpy as jnp
from jax import lax

D_MODEL = 1024
BATCH = 8
SEQ = 4096
DEPTH = 4

N_HEADS = 16
HEAD_DIM = D_MODEL // N_HEADS
ROPE_THETA = 500000.0
ROPE_DIM = HEAD_DIM // 4
D_FF = 4 * D_MODEL
NORM_EPS = 1e-6
N_MIXERS = 3
N_LAYERS_A = (DEPTH + 2) // N_MIXERS
N_LAYERS_B = (DEPTH + 1) // N_MIXERS
N_LAYERS_C = DEPTH // N_MIXERS
SB_BLOCK = 128
NSA_KV_GROUPS = 4
NSA_Q_PER_GROUP = N_HEADS // NSA_KV_GROUPS
KV_WIDTH = NSA_KV_GROUPS * HEAD_DIM
NSA_IN = D_MODEL + 6 * KV_WIDTH + 3 * N_HEADS
CMP_LEN = 32
CMP_STRIDE = 16
CMP_HIDDEN = 4 * HEAD_DIM
SLC_LEN = 64
N_SELECT = 16
WINDOW = 512
NSA_CHUNK = 32
FORCE_BONUS = 1e4
NEG_INF = -1e30
CONV_WIDTH = 31

kernel_name = 'hybrid_stickbreak_nsa_conformer_trunk'


def rms_norm(x, g):
    xf = x.astype(jnp.float32)
    y = xf * lax.rsqrt(jnp.mean(xf * xf, axis=-1, keepdims=True) + NORM_EPS)
    return (y * g.astype(jnp.float32)).astype(x.dtype)


def layer_norm(x, g, b):
    xf = x.astype(jnp.float32)
    mu = jnp.mean(xf, axis=-1, keepdims=True)
    var = jnp.mean(jnp.square(xf - mu), axis=-1, keepdims=True)
    y = (xf - mu) * lax.rsqrt(var + NORM_EPS)
    return (y * g.astype(jnp.float32) + b.astype(jnp.float32)).astype(x.dtype)


def partial_rope(x, pos):
    half = ROPE_DIM // 2
    inv_freq = ROPE_THETA ** (-jnp.arange(half, dtype=jnp.float32) / half)
    ang = pos.astype(jnp.float32)[..., None] * inv_freq
    cos = jnp.cos(ang)[:, :, None, :]
    sin = jnp.sin(ang)[:, :, None, :]
    x1 = x[..., :half].astype(jnp.float32)
    x2 = x[..., half:ROPE_DIM].astype(jnp.float32)
    rot = jnp.concatenate([x1 * cos - x2 * sin, x2 * cos + x1 * sin], axis=-1).astype(x.dtype)
    return jnp.concatenate([rot, x[..., ROPE_DIM:]], axis=-1)


def stick_breaking_attention(u, w_in, w_out):
    B, S, _ = u.shape
    qkv = (u @ w_in).reshape(B, S, 3, N_HEADS, HEAD_DIM)
    q, k, v = qkv[:, :, 0], qkv[:, :, 1], qkv[:, :, 2]
    scale = HEAD_DIM ** -0.5
    outs = []
    for blk in range(S // SB_BLOCK):
        t0, t1 = blk * SB_BLOCK, (blk + 1) * SB_BLOCK
        z = jnp.einsum('bthd,bshd->bhts', q[:, t0:t1], k[:, :t1]).astype(jnp.float32) * scale
        t_idx = t0 + jnp.arange(SB_BLOCK)[:, None]
        s_idx = jnp.arange(t1)[None, :]
        strict = s_idx < t_idx
        log_keep = jnp.where(strict, jax.nn.log_sigmoid(-z), 0.0)
        tail = lax.cumsum(log_keep, axis=3, reverse=True) - log_keep
        a = jnp.where(strict, jnp.exp(jax.nn.log_sigmoid(z) + tail), 0.0)
        outs.append(jnp.einsum('bhts,bshd->bthd', a.astype(v.dtype), v[:, :t1]))
    o = jnp.concatenate(outs, axis=1).reshape(B, S, D_MODEL)
    return o @ w_out


def _cmp_to_slc_overlap(n_cmp, n_slc):
    c0 = np.arange(n_cmp)[:, None] * CMP_STRIDE
    s0 = np.arange(n_slc)[None, :] * SLC_LEN
    ov = np.minimum(c0 + CMP_LEN, s0 + SLC_LEN) - np.maximum(c0, s0)
    return jnp.asarray(np.maximum(ov, 0) / CMP_LEN, dtype=jnp.float32)


def _compress(x, pe, w1, w2, cmp_idx):
    B = x.shape[0]
    n_cmp = cmp_idx.shape[0]
    blocks = x[:, cmp_idx] + pe[None, None, :, None, :]
    flat = blocks.transpose(0, 1, 3, 2, 4).reshape(B, n_cmp, NSA_KV_GROUPS, CMP_LEN * HEAD_DIM)
    return jax.nn.gelu(flat @ w1) @ w2


def native_sparse_attention(u, positions, w_in, w_out, pe_k, w1_k, w2_k, pe_v, w1_v, w2_v):
    B, S, _ = u.shape
    G, R, dh = NSA_KV_GROUPS, NSA_Q_PER_GROUP, HEAD_DIM
    scale = dh ** -0.5
    proj = u @ w_in
    q = partial_rope(proj[..., :D_MODEL].reshape(B, S, N_HEADS, dh), positions)
    kv = proj[..., D_MODEL:D_MODEL + 6 * KV_WIDTH].reshape(B, S, 6, G, dh)
    gates = jax.nn.sigmoid(proj[..., D_MODEL + 6 * KV_WIDTH:].astype(jnp.float32))
    gates = gates.reshape(B, S, N_HEADS, 3).astype(u.dtype)
    k_cmp_raw, v_cmp_raw = kv[:, :, 0], kv[:, :, 1]
    k_slc = partial_rope(kv[:, :, 2], positions)
    v_slc = kv[:, :, 3]
    k_win = partial_rope(kv[:, :, 4], positions)
    v_win = kv[:, :, 5]

    n_cmp = (S - CMP_LEN) // CMP_STRIDE + 1
    cmp_idx = jnp.arange(n_cmp)[:, None] * CMP_STRIDE + jnp.arange(CMP_LEN)[None, :]
    cmp_end = jnp.arange(n_cmp) * CMP_STRIDE + CMP_LEN - 1
    k_cmp = partial_rope(_compress(k_cmp_raw, pe_k, w1_k, w2_k, cmp_idx), positions[:, cmp_end])
    v_cmp = _compress(v_cmp_raw, pe_v, w1_v, w2_v, cmp_idx)

    n_slc = S // SLC_LEN
    n_sel = min(N_SELECT, n_slc)
    overlap = _cmp_to_slc_overlap(n_cmp, n_slc)
    k_blocks = k_slc.reshape(B, n_slc, SLC_LEN, G, dh).transpose(0, 3, 1, 2, 4)
    v_blocks = v_slc.reshape(B, n_slc, SLC_LEN, G, dh).transpose(0, 3, 1, 2, 4)
    blk_ids = jnp.arange(n_slc)
    b_ix = jnp.arange(B)[:, None, None, None]
    g_ix = jnp.arange(G)[None, :, None, None]

    pad = ((0, 0), (WINDOW, 0), (0, 0), (0, 0))
    k_win_pad = jnp.pad(k_win, pad)
    v_win_pad = jnp.pad(v_win, pad)

    def chunk(ci):
        t0 = ci * NSA_CHUNK
        t_pos = t0 + jnp.arange(NSA_CHUNK)
        qc = lax.dynamic_slice_in_dim(q, t0, NSA_CHUNK, axis=1).reshape(B, NSA_CHUNK, G, R, dh)
        gc = lax.dynamic_slice_in_dim(gates, t0, NSA_CHUNK, axis=1).reshape(B, NSA_CHUNK, G, R, 3)

        sc = jnp.einsum('btgrd,bngd->bgrtn', qc, k_cmp).astype(jnp.float32) * scale
        cmp_ok = cmp_end[None, :] <= t_pos[:, None]
        any_ok = jnp.any(cmp_ok, axis=-1).astype(jnp.float32)
        p_cmp = jax.nn.softmax(jnp.where(cmp_ok, sc, NEG_INF), axis=-1) * any_ok[:, None]
        o_cmp = jnp.einsum('bgrtn,bngd->btgrd', p_cmp.astype(v_cmp.dtype), v_cmp)

        imp = jnp.einsum('bgrtn,nj->bgtj', p_cmp, overlap)
        cur = t_pos[:, None] // SLC_LEN
        forced = (blk_ids[None] == 0) | (blk_ids[None] == cur) | (blk_ids[None] == cur - 1)
        causal_blk = blk_ids[None] * SLC_LEN <= t_pos[:, None]
        score = jnp.where(causal_blk, jnp.where(forced, FORCE_BONUS, imp), NEG_INF)
        _, sel = lax.top_k(score, n_sel)
        ks = k_blocks[b_ix, g_ix, sel]
        vs = v_blocks[b_ix, g_ix, sel]
        ss = jnp.einsum('btgrd,bgtnkd->bgrtnk', qc, ks).astype(jnp.float32) * scale
        tok = sel[..., None] * SLC_LEN + jnp.arange(SLC_LEN)
        tok_ok = tok <= t_pos[None, None, :, None, None]
        ss = jnp.where(tok_ok[:, :, None], ss, NEG_INF).reshape(B, G, R, NSA_CHUNK, n_sel * SLC_LEN)
        p_slc = jax.nn.softmax(ss, axis=-1).reshape(B, G, R, NSA_CHUNK, n_sel, SLC_LEN)
        o_slc = jnp.einsum('bgrtnk,bgtnkd->btgrd', p_slc.astype(vs.dtype), vs)

        band = NSA_CHUNK + WINDOW
        kw = lax.dynamic_slice_in_dim(k_win_pad, t0, band, axis=1)
        vw = lax.dynamic_slice_in_dim(v_win_pad, t0, band, axis=1)
        key_pos = t0 - WINDOW + jnp.arange(band)
        win_ok = ((key_pos[None] <= t_pos[:, None]) & (key_pos[None] > t_pos[:, None] - WINDOW)
                  & (key_pos[None] >= 0))
        sw = jnp.einsum('btgrd,bsgd->bgrts', qc, kw).astype(jnp.float32) * scale
        p_win = jax.nn.softmax(jnp.where(win_ok, sw, NEG_INF), axis=-1)
        o_win = jnp.einsum('bgrts,bsgd->btgrd', p_win.astype(vw.dtype), vw)

        o = gc[..., 0:1] * o_cmp + gc[..., 1:2] * o_slc + gc[..., 2:3] * o_win
        return o.reshape(B, NSA_CHUNK, D_MODEL)

    o = lax.map(chunk, jnp.arange(S // NSA_CHUNK))
    o = o.transpose(1, 0, 2, 3).reshape(B, S, D_MODEL)
    return o @ w_out


def conformer_conv(u, w_in, b_in, dw, dw_b, ln_g, ln_b, w_out, b_out):
    a, g = jnp.split(u @ w_in + b_in, 2, axis=-1)
    h = a * jax.nn.sigmoid(g)
    h = lax.conv_general_dilated(h, dw, window_strides=(1,), padding=[(CONV_WIDTH - 1, 0)],
                                 dimension_numbers=('NWC', 'WIO', 'NWC'),
                                 feature_group_count=D_MODEL) + dw_b
    h = jax.nn.silu(layer_norm(h, ln_g, ln_b))
    return h @ w_out + b_out


def squared_relu_mlp(u, w1, w2):
    return jnp.square(jax.nn.relu(u @ w1)) @ w2


def setup_inputs(seed: int = 0) -> dict:
    key = jax.random.key(seed)
    keys = iter(jax.random.split(key, 40))

    def nrm(shape, scale):
        return jax.random.normal(next(keys), shape, jnp.float32) * scale

    def gain(shape):
        return 1.0 + nrm(shape, 0.05)

    D = D_MODEL
    return {
        'x': nrm((BATCH, SEQ, D), 1.0),
        'c': nrm((BATCH, D), 1.0),
        'positions': jnp.broadcast_to(jnp.arange(SEQ, dtype=jnp.int32), (BATCH, SEQ)),
        'ada_w': nrm((DEPTH, D, 6 * D), 0.5 * D ** -0.5),
        'ada_b': nrm((DEPTH, 6 * D), 0.01),
        'mix_pre_g': gain((DEPTH, D)),
        'mix_post_g': gain((DEPTH, D)),
        'ffn_pre_g': gain((DEPTH, D)),
        'ffn_post_g': gain((DEPTH, D)),
        'ffn_w1': nrm((DEPTH, D, D_FF), D ** -0.5),
        'ffn_w2': nrm((DEPTH, D_FF, D), D_FF ** -0.5),
        'sb_w_in': nrm((N_LAYERS_A, D, 3 * D), D ** -0.5),
        'sb_w_out': nrm((N_LAYERS_A, D, D), D ** -0.5),
        'nsa_w_in': nrm((N_LAYERS_B, D, NSA_IN), D ** -0.5),
        'nsa_w_out': nrm((N_LAYERS_B, D, D), D ** -0.5),
        'nsa_pe_k': nrm((N_LAYERS_B, CMP_LEN, HEAD_DIM), 0.1),
        'nsa_w1_k': nrm((N_LAYERS_B, CMP_LEN * HEAD_DIM, CMP_HIDDEN), (CMP_LEN * HEAD_DIM) ** -0.5),
        'nsa_w2_k': nrm((N_LAYERS_B, CMP_HIDDEN, HEAD_DIM), CMP_HIDDEN ** -0.5),
        'nsa_pe_v': nrm((N_LAYERS_B, CMP_LEN, HEAD_DIM), 0.1),
        'nsa_w1_v': nrm((N_LAYERS_B, CMP_LEN * HEAD_DIM, CMP_HIDDEN), (CMP_LEN * HEAD_DIM) ** -0.5),
        'nsa_w2_v': nrm((N_LAYERS_B, CMP_HIDDEN, HEAD_DIM), CMP_HIDDEN ** -0.5),
        'cv_w_in': nrm((N_LAYERS_C, D, 2 * D), D ** -0.5),
        'cv_b_in': nrm((N_LAYERS_C, 2 * D), 0.01),
        'cv_dw': nrm((N_LAYERS_C, CONV_WIDTH, 1, D), CONV_WIDTH ** -0.5),
        'cv_dw_b': nrm((N_LAYERS_C, D), 0.01),
        'cv_ln_g': gain((N_LAYERS_C, D)),
        'cv_ln_b': nrm((N_LAYERS_C, D), 0.01),
        'cv_w_out': nrm((N_LAYERS_C, D, D), D ** -0.5),
        'cv_b_out': nrm((N_LAYERS_C, D), 0.01),
    }


def reference(x, c, positions, ada_w, ada_b, mix_pre_g, mix_post_g, ffn_pre_g, ffn_post_g,
              ffn_w1, ffn_w2, sb_w_in, sb_w_out, nsa_w_in, nsa_w_out,
              nsa_pe_k, nsa_w1_k, nsa_w2_k, nsa_pe_v, nsa_w1_v, nsa_w2_v,
              cv_w_in, cv_b_in, cv_dw, cv_dw_b, cv_ln_g, cv_ln_b, cv_w_out, cv_b_out):
    cond = jax.nn.silu(c)
    h = x
    for i in range(DEPTH):
        mod = cond @ ada_w[i] + ada_b[i]
        sh1, sc1, g1, sh2, sc2, g2 = [m[:, None, :] for m in jnp.split(mod, 6, axis=-1)]

        u = rms_norm(h, mix_pre_g[i]) * (1.0 + sc1) + sh1
        kind, j = i % N_MIXERS, i // N_MIXERS
        if kind == 0:
            y = stick_breaking_attention(u, sb_w_in[j], sb_w_out[j])
        elif kind == 1:
            y = native_sparse_attention(u, positions, nsa_w_in[j], nsa_w_out[j],
                                        nsa_pe_k[j], nsa_w1_k[j], nsa_w2_k[j],
                                        nsa_pe_v[j], nsa_w1_v[j], nsa_w2_v[j])
        else:
            y = conformer_conv(u, cv_w_in[j], cv_b_in[j], cv_dw[j], cv_dw_b[j],
                               cv_ln_g[j], cv_ln_b[j], cv_w_out[j], cv_b_out[j])
        h = h + g1 * rms_norm(y, mix_post_g[i])

        u = rms_norm(h, ffn_pre_g[i]) * (1.0 + sc2) + sh2
        y = squared_relu_mlp(u, ffn_w1[i], ffn_w2[i])
        h = h + g2 * rms_norm(y, ffn_post_g[i])
    return h
```

```python
import contextlib
import os
import numpy as np
import ml_dtypes
import concourse.bass as bass
import concourse.mybir as mybir
from concourse.bass_utils import run_bass_kernel_spmd

F32 = mybir.dt.float32
BF16 = mybir.dt.bfloat16
I32 = mybir.dt.int32
AF = mybir.ActivationFunctionType
ALU = mybir.AluOpType
AX = mybir.AxisListType

D = 1024
S = 4096
DEPTH = 4
NH = 16
DH = 64
DFF = 4096
EPS = 1e-6
NT = S // 128
NG = S // 512
NSA_IN = 2608
NCMP = 255
BIG = 30000.0
NSA_STOP = os.environ.get("NSA_STOP", "")
N1_PARTS = os.environ.get("N1_PARTS", "urvg")
ROPE_MODE = os.environ.get("ROPE_MODE", "")


class Src:
    __slots__ = ("sem", "count", "name")

    def __init__(self, sem, name):
        self.sem = sem
        self.count = 0
        self.name = name


class Buf:
    __slots__ = ("name", "w", "r", "const", "excl")

    def __init__(self, name, const=False):
        self.name = name
        self.excl = False
        self.w = []
        self.r = []
        self.const = const


class T:
    __slots__ = ("t", "b")

    def __init__(self, t, name):
        self.t = t
        self.b = Buf(name)

    def __getitem__(self, idx):
        return self.t[idx]


class KB:
    def __init__(self):
        self.nc = bass.Bass("TRN2", target_bir_lowering=False)
        nc = self.nc
        self.st = contextlib.ExitStack()
        self.eng = {"pe": nc.tensor, "act": nc.scalar, "dve": nc.vector, "pool": nc.gpsimd, "sp": nc.sync}
        self.src = {}
        for n in self.eng:
            self.src[n] = Src(self.st.enter_context(nc.semaphore("sem_" + n)), n)
        self.dslots = {}
        self.dnext = {}
        for q, k in (("sp", 12), ("pool", 6), ("act", 4)):
            self.dslots[q] = [Src(self.st.enter_context(nc.semaphore(f"dq_{q}{i}")), f"dq_{q}{i}") for i in range(k)]
            self.dnext[q] = 0
        self.seen = {n: {} for n in self.eng}
        self.ninstr = 0

    def _wait(self, en, tok):
        src, val = tok
        if val <= 0:
            return
        if en == "pe" and src is self.src["pe"]:
            return
        seen = self.seen[en]
        if seen.get(src, 0) >= val:
            return
        self.eng[en].wait_ge(src.sem, val)
        seen[src] = val
        self.ninstr += 1

    def _deps(self, en, r, w, wa=()):
        for x in r:
            b = x.b if isinstance(x, T) else x
            for tok in b.w:
                self._wait(en, tok)
            if b.excl:
                own = self.src.get(en)
                for tok in b.r:
                    if tok[0] is not own:
                        self._wait(en, tok)
        for x in w:
            b = x.b if isinstance(x, T) else x
            for tok in b.w:
                self._wait(en, tok)
            for tok in b.r:
                self._wait(en, tok)
        for x in wa:
            b = x.b if isinstance(x, T) else x
            for tok in b.r:
                self._wait(en, tok)

    def _mark(self, tok, r, w, wa=()):
        for x in r:
            b = x.b if isinstance(x, T) else x
            if not b.const:
                b.r.append(tok)
        for x in w:
            b = x.b if isinstance(x, T) else x
            b.w = [tok]
            b.r = []
        for x in wa:
            b = x.b if isinstance(x, T) else x
            if b.r:
                b.w = []
                b.r = []
            b.w.append(tok)

    def op(self, en, fn, r=(), w=()):
        self._deps(en, r, w)
        ins = fn()
        s = self.src[en]
        s.count += 1
        ins.then_inc(s.sem, 1)
        self._mark((s, s.count), r, w)
        self.ninstr += 1
        return ins

    def pe(self, fn, r=(), w=()):
        return self.op("pe", fn, r, w)

    def act(self, fn, r=(), w=()):
        return self.op("act", fn, r, w)

    def dve(self, fn, r=(), w=()):
        return self.op("dve", fn, r, w)

    def pool(self, fn, r=(), w=()):
        return self.op("pool", fn, r, w)

    def dma(self, q, out, in_, r=(), w=(), wa=()):
        self._deps(q, r, w, wa)
        slots = self.dslots[q]
        sl = slots[self.dnext[q] % len(slots)]
        self.dnext[q] += 1
        self._wait(q, (sl, sl.count))
        ins = self.eng[q].dma_start(out=out, in_=in_)
        sl.count += 16
        ins.then_inc(sl.sem, 16)
        self._mark((sl, sl.count), r, w, wa)
        self.ninstr += 1
        return ins

    def barrier(self):
        toks = [(s, s.count) for s in self.src.values()]
        for q in self.dslots:
            toks += [(s, s.count) for s in self.dslots[q]]
        for en in self.eng:
            for tok in toks:
                if tok[0] is self.src[en]:
                    continue
                self._wait(en, tok)

    def _uniq(self, name):
        self.nalloc = getattr(self, "nalloc", 0) + 1
        return f"{name}_{self.nalloc}"

    def sb(self, ctx, name, shape, dtype):
        name = self._uniq("s_" + name)
        return T(ctx.enter_context(self.nc.sbuf_tensor(name, list(shape), dtype)), name)

    def ps(self, ctx, name, shape=(128, 512), dtype=F32):
        name = self._uniq("p_" + name)
        t = T(ctx.enter_context(self.nc.psum_tensor(name, list(shape), dtype)), name)
        t.b.excl = True
        return t

    def dram(self, name, shape, dtype, kind="Internal"):
        return self.nc.dram_tensor(name, list(shape), dtype, kind=kind).ap()


class Prog:
    def __init__(self, layers=(0, 1, 2, 3), debug=None):
        self.K = KB()
        self.nc = self.K.nc
        self.layers = tuple(layers)
        self.debug = debug or {}
        self.in_names = []
        self._declare_io()

    def _in(self, name, shape, dtype=F32):
        self.in_names.append(name)
        return self.K.dram(name, shape, dtype, kind="ExternalInput")

    def _declare_io(self):
        K = self.K
        self.x = self._in("x", [S, D])
        self.cT = self._in("cT", [128, 8])
        self.pos = self._in("pos", [1, S], I32)
        self.ada_w = self._in("ada_w", [DEPTH, D, 6 * D])
        self.ada_b = self._in("ada_b", [DEPTH, 6 * D])
        self.gains = {n: self._in(n, [DEPTH, D]) for n in ("mix_pre_g", "mix_post_g", "ffn_pre_g", "ffn_post_g")}
        self.ffn_w1 = self._in("ffn_w1", [DEPTH, D, DFF])
        self.ffn_w2 = self._in("ffn_w2", [DEPTH, DFF, D])
        self.sb_w_in = self._in("sb_w_in", [2, D, 3 * D])
        self.sb_w_out = self._in("sb_w_out", [2, D, D])
        self.nsa_w_in = self._in("nsa_w_in", [1, D, NSA_IN])
        self.nsa_w_out = self._in("nsa_w_out", [1, D, D])
        self.nsa_peT = {"k": self._in("nsa_pe_kT", [64, 32]), "v": self._in("nsa_pe_vT", [64, 32])}
        self.nsa_w1 = {"k": self._in("nsa_w1_k", [1, 2048, 256]), "v": self._in("nsa_w1_v", [1, 2048, 256])}
        self.nsa_w2 = {"k": self._in("nsa_w2_k", [1, 256, 64]), "v": self._in("nsa_w2_v", [1, 256, 64])}
        self.cv_w_in = self._in("cv_w_in", [1, D, 2 * D])
        self.cv_b_inT = self._in("cv_b_inT", [128, 16])
        self.cv_dwT = self._in("cv_dwT", [128, 8, 31])
        self.cv_vecT = self._in("cv_vecT", [128, 3, 8])
        self.cv_w_out = self._in("cv_w_out", [1, D, D])
        self.cv_b_out = self._in("cv_b_out", [1, D])
        self.c_ident = self._in("c_ident", [128, 128], BF16)
        self.c_tri = self._in("c_tri", [128, 128], BF16)
        self.c_ones = self._in("c_ones", [128, 128], BF16)
        self.c_sbmask = self._in("c_sbmask", [128, 512], BF16)
        self.c_sbbias = self._in("c_sbbias", [128, 512], BF16)
        self.c_onesf = self._in("c_onesf", [128, 128], F32)
        self.c_perm = self._in("c_perm", [64, 64], BF16)
        self.c_invf = self._in("c_invf", [64, 1], F32)
        self.c_esel = self._in("c_esel", [64, 32 * 128], BF16)
        self.c_winb = self._in("c_winb", [128, 8 * 512], BF16)
        self.c_causb = self._in("c_causb", [128, 512], BF16)
        self.c_band = self._in("c_band", [128, 512], BF16)
        self.c_wcm = self._in("c_wcm", [128, 128], F32)
        self.c_wfb = self._in("c_wfb", [128, 128], F32)
        self.c_anyok = self._in("c_anyok", [128, 1], F32)
        self.out = K.dram("out", [S, D], F32, kind="ExternalOutput")
        self.modv = K.dram("modv", [DEPTH, 6 * D], F32)
        self.UT = K.dram("UT", [D, S], BF16)
        self.QT = K.dram("QT", [D, S], BF16)
        self.KT = K.dram("KT", [D, S], BF16)
        self.Vd = K.dram("Vd", [S, D], BF16)
        self.HG = K.dram("HG", [D, S], F32)
        self.NQ = K.dram("NQ", [16, 64, S], BF16)
        self.NK = K.dram("NK", [8, 64, S], BF16)
        self.NC = K.dram("NC", [8, 64, S], BF16)
        self.NV = K.dram("NV", [S, 2 * 4 * 65], BF16)
        self.NGt = K.dram("NGt", [S, 48], F32)
        self.b_nq = Buf("nq"); self.b_nk = Buf("nk"); self.b_ncr = Buf("ncr"); self.b_nv = Buf("nv"); self.b_ngt = Buf("ngt")
        self.b_h = [Buf(f"h{t}") for t in range(NT)]
        self.b_ut = [Buf(f"ut{g}") for g in range(NG)]
        self.b_modv = [Buf(f"modv{i}") for i in range(DEPTH)]
        self.b_qt = [Buf(f"qt{g}") for g in range(NG)]
        self.b_kt = [Buf(f"kt{g}") for g in range(NG)]
        self.b_vd = [Buf(f"vd{t}") for t in range(NT)]
        self.b_hg = [Buf(f"hg{g}") for g in range(NG)]
        self.cbuf = Buf("const", const=True)
        for dbg_name, shape in self.debug.items():
            setattr(self, "dbg_" + dbg_name, K.dram("dbg_" + dbg_name, shape, F32, kind="ExternalOutput"))

    def load_consts(self):
        K, nc = self.K, self.nc
        st = K.st
        self.ident = K.sb(st, "ident", [128, 128], BF16)
        self.tri = K.sb(st, "tri", [128, 128], BF16)
        self.ones = K.sb(st, "ones", [128, 128], BF16)
        for t, src in ((self.ident, self.c_ident), (self.tri, self.c_tri), (self.ones, self.c_ones)):
            K.dma("sp", t[:], src, r=[self.cbuf], w=[t])
            t.b.const = True

    def phase_mod(self):
        K, nc = self.K, self.nc
        with contextlib.ExitStack() as ph:
            cT = K.sb(ph, "cT", [128, 8], F32)
            sig = K.sb(ph, "sig", [128, 8], F32)
            cond = K.sb(ph, "cond", [128, 8], F32)
            wsl = [K.sb(ph, f"adaw{i}", [128, 8, 512], F32) for i in range(2)]
            brow = K.sb(ph, "brow", [1, 6 * D], F32)
            mrow = K.sb(ph, "mrow", [1, 6 * D], F32)
            pss = [K.ps(ph, f"modps{i}") for i in range(2)]
            K.dma("sp", cT[:], self.cT, r=[self.cbuf], w=[cT])
            K.act(lambda: nc.scalar.activation(out=sig[:], in_=cT[:], func=AF.Sigmoid), r=[cT], w=[sig])
            K.dve(lambda: nc.vector.tensor_tensor(out=cond[:], in0=cT[:], in1=sig[:], op=ALU.mult), r=[cT, sig], w=[cond])
            it = 0
            for i in self.layers:
                K.dma("sp", brow[:], self.ada_b[i:i + 1, :], r=[self.cbuf], w=[brow])
                wv = self.ada_w[i].rearrange("(kc p) n -> p kc n", p=128)
                for n in range(12):
                    wt = wsl[it % 2]
                    ps = pss[it % 2]
                    it += 1
                    K.dma("sp", wt[:], wv[:, :, n * 512:(n + 1) * 512], r=[self.cbuf], w=[wt])
                    for kc in range(8):
                        K.pe(lambda kc=kc: nc.tensor.matmul(ps[0:1, :], lhsT=cond[:, kc:kc + 1], rhs=wt[:, kc, :],
                                                            start=(kc == 0), stop=(kc == 7)), r=[cond, wt], w=[ps])
                    K.dve(lambda n=n: nc.vector.tensor_tensor(out=mrow[0:1, n * 512:(n + 1) * 512], in0=ps[0:1, :],
                                                              in1=brow[0:1, n * 512:(n + 1) * 512], op=ALU.add),
                          r=[ps, brow], w=[mrow])
                K.dma("sp", self.modv[i:i + 1, :], mrow[:], r=[mrow], w=[self.b_modv[i]])
            K.barrier()

    def load_vec(self, ph, name, src_row):
        t = self.K.sb(ph, name, [128, D], F32)
        return t

    def mod_slice(self, i, k):
        return self.modv[i:i + 1, k * D:(k + 1) * D]

    def bcast_load(self, t, src_row, rbufs):
        self.K.dma("sp", t[:], src_row.to_broadcast([128, D]), r=rbufs, w=[t])

    def phase_norm(self, i, which, src_is_x):
        K, nc = self.K, self.nc
        gname = "mix_pre_g" if which == 0 else "ffn_pre_g"
        ksh, ksc = (0, 1) if which == 0 else (3, 4)
        src = self.x if src_is_x else self.out
        with contextlib.ExitStack() as ph:
            A = K.sb(ph, "nA", [128, D], F32)
            B = K.sb(ph, "nB", [128, D], F32)
            Gn = K.sb(ph, "nG", [128, D], F32)
            self.bcast_load(A, self.mod_slice(i, ksc), [self.b_modv[i]])
            self.bcast_load(B, self.mod_slice(i, ksh), [self.b_modv[i]])
            self.bcast_load(Gn, self.gains[gname][i:i + 1, :], [self.cbuf])
            K.dve(lambda: nc.vector.scalar_tensor_tensor(out=A[:], in0=A[:], scalar=1.0, in1=Gn[:], op0=ALU.add, op1=ALU.mult),
                  r=[A, Gn], w=[A])
            hs = [K.sb(ph, f"nh{j}", [128, D], F32) for j in range(3)]
            junk = K.sb(ph, "njunk", [128, D], BF16)
            tmp = [K.sb(ph, f"ntmp{j}", [128, D], F32) for j in range(2)]
            ub = [K.sb(ph, f"nub{j}", [128, D], BF16) for j in range(2)]
            st = [K.sb(ph, f"nst{j}", [128, 4], F32) for j in range(2)]
            utg = [K.sb(ph, f"nutg{j}", [128, 8, 512], BF16) for j in range(2)]
            pst = [K.ps(ph, f"npst{j}", (128, 1024), BF16) for j in range(2)]
            UTv = self.UT.rearrange("(kc p) t -> p kc t", p=128)

            def load(t):
                K.dma("sp", hs[t % 3][:], src[t * 128:(t + 1) * 128, :], r=[self.b_h[t]], w=[hs[t % 3]])

            load(0)
            load(1)
            for t in range(NT):
                if t + 2 < NT:
                    load(t + 2)
                h = hs[t % 3]
                s_ = st[t % 2]
                tm = tmp[t % 2]
                u = ub[t % 2]
                pt = pst[t % 2]
                g = t // 4
                ug = utg[g % 2]
                K.act(lambda: nc.scalar.activation(out=junk[:], in_=h[:], func=AF.Square, accum_out=s_[:, 0:1]),
                      r=[h], w=[junk, s_])
                K.act(lambda: nc.scalar.activation(out=s_[:, 1:2], in_=s_[:, 0:1], func=AF.Sqrt, scale=1.0 / D, bias=EPS),
                      r=[s_], w=[s_])
                K.dve(lambda: nc.vector.reciprocal(out=s_[:, 2:3], in_=s_[:, 1:2]), r=[s_], w=[s_])
                K.dve(lambda: nc.vector.scalar_tensor_tensor(out=tm[:], in0=h[:], scalar=s_[:, 2:3], in1=A[:], op0=ALU.mult, op1=ALU.mult),
                      r=[h, s_, A], w=[tm])
                K.pool(lambda: nc.gpsimd.tensor_tensor(out=u[:], in0=tm[:], in1=B[:], op=ALU.add), r=[tm, B], w=[u])
                for kc in range(8):
                    K.pe(lambda kc=kc: nc.tensor.transpose(out=pt[:, kc * 128:(kc + 1) * 128], in_=u[:, kc * 128:(kc + 1) * 128],
                                                          identity=self.ident[:]), r=[u, self.ident], w=[pt])
                tt = t % 4
                K.act(lambda: nc.scalar.copy(out=ug[:, :, tt * 128:(tt + 1) * 128],
                                             in_=pt[:].rearrange("p (kc t) -> p kc t", kc=8)), r=[pt], w=[ug])
                if tt == 3:
                    K.dma("sp", UTv[:, :, g * 512:(g + 1) * 512], ug[:], r=[ug], w=[self.b_ut[g]])
            K.barrier()

    def epilogue_setup(self, ph, i, which):
        K, nc = self.K, self.nc
        gname = "mix_post_g" if which == 0 else "ffn_post_g"
        kg = 2 if which == 0 else 5
        G = K.sb(ph, "eG", [128, D], F32)
        Gn = K.sb(ph, "eGn", [128, D], F32)
        self.bcast_load(G, self.mod_slice(i, kg), [self.b_modv[i]])
        self.bcast_load(Gn, self.gains[gname][i:i + 1, :], [self.cbuf])
        K.dve(lambda: nc.vector.tensor_tensor(out=G[:], in0=G[:], in1=Gn[:], op=ALU.mult), r=[G, Gn], w=[G])
        e = {
            "G": G,
            "h": [K.sb(ph, f"eh{j}", [128, D], F32) for j in range(2)],
            "tmp": [K.sb(ph, f"etmp{j}", [128, 512], F32) for j in range(2)],
            "junk": K.sb(ph, "ejunk", [128, 512], BF16),
            "st": [K.sb(ph, f"est{j}", [128, 8], F32) for j in range(2)],
            "n": 0,
            "src_is_x": False,
        }
        return e

    def epi_load(self, e, t):
        src = self.x if self.resid_from_x else self.out
        h = e["h"][t % 2]
        self.K.dma("sp", h[:], src[t * 128:(t + 1) * 128, :], r=[self.b_h[t]], w=[h])

    def epilogue(self, e, t, ybanks):
        K, nc = self.K, self.nc
        h = e["h"][t % 2]
        s_ = e["st"][t % 2]
        junk = e["junk"]
        G = e["G"]
        for hf in range(2):
            K.act(lambda hf=hf: nc.scalar.activation(out=junk[:], in_=ybanks[hf][:], func=AF.Square, accum_out=s_[:, hf:hf + 1]),
                  r=[ybanks[hf]], w=[junk, s_])
        K.dve(lambda: nc.vector.tensor_tensor(out=s_[:, 2:3], in0=s_[:, 0:1], in1=s_[:, 1:2], op=ALU.add), r=[s_], w=[s_])
        K.act(lambda: nc.scalar.activation(out=s_[:, 3:4], in_=s_[:, 2:3], func=AF.Sqrt, scale=1.0 / D, bias=EPS),
              r=[s_], w=[s_])
        K.dve(lambda: nc.vector.reciprocal(out=s_[:, 4:5], in_=s_[:, 3:4]), r=[s_], w=[s_])
        for hf in range(2):
            tm = e["tmp"][hf]
            K.dve(lambda hf=hf, tm=tm: nc.vector.scalar_tensor_tensor(out=tm[:], in0=ybanks[hf][:], scalar=s_[:, 4:5],
                                                                      in1=G[:, hf * 512:(hf + 1) * 512], op0=ALU.mult, op1=ALU.mult),
                  r=[ybanks[hf], s_, G], w=[tm])
            K.pool(lambda hf=hf, tm=tm: nc.gpsimd.tensor_tensor(out=h[:, hf * 512:(hf + 1) * 512], in0=h[:, hf * 512:(hf + 1) * 512],
                                                                 in1=tm[:], op=ALU.add), r=[tm, h], w=[h])
        K.dma("sp", self.out[t * 128:(t + 1) * 128, :], h[:], r=[h], w=[self.b_h[t]])

    def load_w(self, wt, src, kchunks, ncols, col0=0, split=2048):
        K = self.K
        sv = src.rearrange("(kc p) n -> p kc n", p=128)
        for kc in range(kchunks):
            for c0 in range(0, ncols, split):
                c1 = min(ncols, c0 + split)
                K.dma("pool", wt[:, kc, c0:c1], sv[:, kc, col0 + c0:col0 + c1], r=[self.cbuf], wa=[wt])

    def phase_ffn(self, i):
        K, nc = self.K, self.nc
        with contextlib.ExitStack() as ph:
            w1 = K.sb(ph, "fw1", [128, 8, DFF], BF16)
            w2 = K.sb(ph, "fw2", [128, 32, D], BF16)
            self.load_w(w1, self.ffn_w1[i], 8, DFF)
            self.load_w(w2, self.ffn_w2[i], 32, D, split=1024)
            e = self.epilogue_setup(ph, i, 1)
            utg = [K.sb(ph, f"futg{j}", [128, 8, 512], BF16) for j in range(2)]
            hid = K.sb(ph, "fhid", [128, 32, 512], BF16)
            rl = [K.sb(ph, f"frl{j}", [128, 512], F32) for j in range(2)]
            psA = [K.ps(ph, f"fpsA{j}") for j in range(2)]
            psY = [[K.ps(ph, f"fpsY{j}{hf}") for hf in range(2)] for j in range(2)]
            UTv = self.UT.rearrange("(kc p) t -> p kc t", p=128)

            def load_u(g):
                K.dma("sp", utg[g % 2][:], UTv[:, :, g * 512:(g + 1) * 512], r=[self.b_ut[g]], w=[utg[g % 2]])

            load_u(0)
            for g in range(NG):
                if g + 1 < NG:
                    load_u(g + 1)
                ug = utg[g % 2]
                for fc in range(32):
                    ps = psA[fc % 2]
                    r_ = rl[fc % 2]
                    for kc in range(8):
                        K.pe(lambda kc=kc, fc=fc, ps=ps: nc.tensor.matmul(ps[:], lhsT=w1[:, kc, fc * 128:(fc + 1) * 128], rhs=ug[:, kc, :],
                                                                          start=(kc == 0), stop=(kc == 7)), r=[w1, ug], w=[ps])
                    K.act(lambda ps=ps, r_=r_: nc.scalar.activation(out=r_[:], in_=ps[:], func=AF.Relu), r=[ps], w=[r_])
                    K.pool(lambda fc=fc, r_=r_: nc.gpsimd.tensor_tensor(out=hid[:, fc, :], in0=r_[:], in1=r_[:], op=ALU.mult),
                           r=[r_], w=[hid])
                for tt in range(4):
                    t = g * 4 + tt
                    self.epi_load(e, t)
                    yb = psY[tt % 2]
                    for hf in range(2):
                        for fc in range(32):
                            K.pe(lambda fc=fc, hf=hf, tt=tt: nc.tensor.matmul(yb[hf][:], lhsT=hid[:, fc, tt * 128:(tt + 1) * 128],
                                                                             rhs=w2[:, fc, hf * 512:(hf + 1) * 512],
                                                                             start=(fc == 0), stop=(fc == 31)), r=[hid, w2], w=[yb[hf]])
                    self.epilogue(e, t, yb)
            K.barrier()

    def phase_sb_proj(self, j):
        K, nc = self.K, self.nc
        with contextlib.ExitStack() as ph:
            w = K.sb(ph, "sw", [128, 8, 3 * D], BF16)
            self.load_w(w, self.sb_w_in[j], 8, 3 * D, split=3072)
            utg = [K.sb(ph, f"sutg{k}", [128, 8, 512], BF16) for k in range(2)]
            stg = [K.sb(ph, f"sstg{k}", [128, 512], BF16) for k in range(4)]
            vst = [K.sb(ph, f"svst{k}", [128, D], BF16) for k in range(2)]
            pss = [K.ps(ph, f"sps{k}") for k in range(4)]
            UTv = self.UT.rearrange("(kc p) t -> p kc t", p=128)

            def load_u(g):
                K.dma("sp", utg[g % 2][:], UTv[:, :, g * 512:(g + 1) * 512], r=[self.b_ut[g]], w=[utg[g % 2]])

            load_u(0)
            n = 0
            for g in range(NG):
                if g + 1 < NG:
                    load_u(g + 1)
                ug = utg[g % 2]
                for fcn in range(16):
                    ps = pss[n % 4]
                    sg = stg[n % 4]
                    for kc in range(8):
                        K.pe(lambda kc=kc, fcn=fcn, ps=ps: nc.tensor.matmul(ps[:], lhsT=w[:, kc, fcn * 128:(fcn + 1) * 128], rhs=ug[:, kc, :],
                                                                           start=(kc == 0), stop=(kc == 7)), r=[w, ug], w=[ps])
                    if n % 2 == 0:
                        K.act(lambda ps=ps, sg=sg: nc.scalar.copy(out=sg[:], in_=ps[:]), r=[ps], w=[sg])
                    else:
                        K.dve(lambda ps=ps, sg=sg: nc.vector.tensor_copy(out=sg[:], in_=ps[:]), r=[ps], w=[sg])
                    dst = self.QT if fcn < 8 else self.KT
                    fo = (fcn % 8) * 128
                    bb = self.b_qt[g] if fcn < 8 else self.b_kt[g]
                    K.dma("sp", dst[fo:fo + 128, g * 512:(g + 1) * 512], sg[:], r=[sg], wa=[bb])
                    n += 1
                for tt in range(4):
                    t = g * 4 + tt
                    vs = vst[t % 2]
                    for hf in range(2):
                        ps = pss[n % 4]
                        n += 1
                        for kc in range(8):
                            K.pe(lambda kc=kc, hf=hf, tt=tt, ps=ps: nc.tensor.matmul(ps[:], lhsT=ug[:, kc, tt * 128:(tt + 1) * 128],
                                                                                    rhs=w[:, kc, 2 * D + hf * 512:2 * D + (hf + 1) * 512],
                                                                                    start=(kc == 0), stop=(kc == 7)), r=[w, ug], w=[ps])
                        if hf == 0:
                            K.act(lambda ps=ps, vs=vs: nc.scalar.copy(out=vs[:, 0:512], in_=ps[:]), r=[ps], w=[vs])
                        else:
                            K.dve(lambda ps=ps, vs=vs: nc.vector.tensor_copy(out=vs[:, 512:1024], in_=ps[:]), r=[ps], w=[vs])
                    K.dma("sp", self.Vd[t * 128:(t + 1) * 128, :], vs[:], r=[vs], w=[self.b_vd[t]])
            K.barrier()

    def phase_sb_core(self, i, j):
        K, nc = self.K, self.nc
        with contextlib.ExitStack() as ph:
            O = K.sb(ph, "aO", [128, NT, D], BF16)
            with contextlib.ExitStack() as ph2:
                V = K.sb(ph2, "aV", [128, NT, D], BF16)
                Vv = self.Vd.rearrange("(t p) f -> p t f", p=128)
                for t0 in range(0, NT, 8):
                    K.dma("sp", V[:, t0:t0 + 8, :], Vv[:, t0:t0 + 8, :], r=self.b_vd[t0:t0 + 8], wa=[V])
                msk = K.sb(ph2, "amsk", [128, 512], BF16)
                mbias = K.sb(ph2, "ambias", [128, 512], BF16)
                K.dma("sp", msk[:], self.c_sbmask, r=[self.cbuf], w=[msk])
                K.dma("sp", mbias[:], self.c_sbbias, r=[self.cbuf], w=[mbias])
                qh = [K.sb(ph2, f"aq{k}", [64, S], BF16) for k in range(2)]
                kh = [K.sb(ph2, f"ak{k}", [64, S], BF16) for k in range(2)]
                nk = [K.sb(ph2, f"ank{k}", [64, S], BF16) for k in range(2)]
                Et = [K.sb(ph2, f"aE{k}", [128, 512], F32) for k in range(2)]
                Lt = [K.sb(ph2, f"aL{k}", [128, 512], BF16) for k in range(3)]
                Lr = [K.sb(ph2, f"aLr{k}", [128, 512], BF16) for k in range(2)]
                At = [K.sb(ph2, f"aA{k}", [128, 512], BF16) for k in range(3)]
                Rt = [K.sb(ph2, f"aR{k}", [128, 512], BF16) for k in range(2)]
                zps = [K.ps(ph2, f"azps{k}") for k in range(2)]
                cps = [K.ps(ph2, f"acps{k}") for k in range(2)]
                avs = [K.ps(ph2, f"aav{k}") for k in range(2)]

                def load_head(h):
                    s = h % 2
                    K.dma("sp", qh[s][:], self.QT[h * 64:(h + 1) * 64, :], r=self.b_qt, w=[qh[s]])
                    K.dma("sp", kh[s][:], self.KT[h * 64:(h + 1) * 64, :], r=self.b_kt, w=[kh[s]])
                    K.pool(lambda: nc.gpsimd.tensor_scalar(out=nk[s][:], in0=kh[s][:], scalar1=-0.125, scalar2=None, op0=ALU.mult),
                           r=[kh[s]], w=[nk[s]])

                load_head(0)
                u = 0
                ci = 0
                for h in range(NH):
                    if h + 1 < NH:
                        load_head(h + 1)
                    q_, k_, nk_ = qh[h % 2], kh[h % 2], nk[h % 2]
                    for qc in range(NG):
                        av = avs[ci % 2]
                        ci += 1
                        K.pool(lambda: nc.gpsimd.memset(Rt[0][:], 0.0), w=[Rt[0]])
                        K.pool(lambda: nc.gpsimd.memset(Rt[1][:], 0.0), w=[Rt[1]])
                        rcur = 0
                        first_av = True
                        nkt = 4 * qc + 4
                        for i_ in range(nkt - 1, -1, -1):
                            a = i_ - 4 * qc
                            diag = a >= 0
                            c0 = 128 * a if diag else 0
                            W = 512 - c0
                            zp, cp = zps[u % 2], cps[u % 2]
                            E, L, A = Et[u % 2], Lt[u % 3], At[u % 3]
                            u += 1
                            qs = q_[:, qc * 512 + c0:(qc + 1) * 512]
                            ks = k_[:, i_ * 128:(i_ + 1) * 128]
                            nks = nk_[:, i_ * 128:(i_ + 1) * 128]
                            K.pe(lambda: nc.tensor.matmul(zp[:, 0:W], lhsT=ks, rhs=qs, start=True, stop=True), r=[q_, k_], w=[zp])
                            K.act(lambda: nc.scalar.activation(out=E[:, 0:W], in_=zp[:, 0:W], func=AF.Exp, scale=0.125), r=[zp], w=[E])
                            if diag:
                                Lraw = Lr[u % 2]
                                K.act(lambda: nc.scalar.activation(out=Lraw[:, 0:W], in_=E[:, 0:W], func=AF.Ln, bias=1.0), r=[E], w=[Lraw])
                                K.dve(lambda: nc.vector.tensor_tensor(out=L[:, 0:W], in0=Lraw[:, 0:W], in1=msk[:, 0:W], op=ALU.mult),
                                      r=[Lraw, msk], w=[L])
                            else:
                                K.act(lambda: nc.scalar.activation(out=L[:, 0:W], in_=E[:, 0:W], func=AF.Ln, bias=1.0), r=[E], w=[L])
                            Rc = Rt[rcur]
                            Rn = Rt[1 - rcur]
                            lastmm = "nk" if not diag else "bias"
                            K.pe(lambda: nc.tensor.matmul(cp[:, 0:W], lhsT=self.tri[:], rhs=L[:, 0:W], start=True, stop=False),
                                 r=[self.tri, L], w=[cp])
                            if i_ != nkt - 1:
                                K.pe(lambda: nc.tensor.matmul(cp[:, 0:W], lhsT=self.ones[:], rhs=Rc[:, c0:512], start=False, stop=False),
                                     r=[self.ones, Rc], w=[cp])
                            K.pe(lambda: nc.tensor.matmul(cp[:, 0:W], lhsT=nks, rhs=qs, start=False, stop=(not diag)),
                                 r=[nk_, q_], w=[cp])
                            if diag:
                                K.pe(lambda: nc.tensor.matmul(cp[:, 0:W], lhsT=self.ident[:], rhs=mbias[:, 0:W], start=False, stop=True),
                                     r=[self.ident, mbias], w=[cp])
                            K.act(lambda: nc.scalar.activation(out=A[:, 0:W], in_=cp[:, 0:W], func=AF.Exp, scale=-1.0), r=[cp], w=[A])
                            if i_ != 0:
                                K.dve(lambda: nc.vector.tensor_tensor(out=Rn[:, c0:512], in0=Rc[:, c0:512], in1=L[:, 0:W], op=ALU.add),
                                      r=[Rc, L], w=[Rn])
                                rcur = 1 - rcur
                            for c in range(a if diag else 0, 4):
                                K.pe(lambda c=c: nc.tensor.matmul(av[:, c * 64:(c + 1) * 64], lhsT=A[:, c * 128 - c0:(c + 1) * 128 - c0],
                                                                  rhs=V[:, i_, h * 64:(h + 1) * 64], start=first_av, stop=False,
                                                                  skip_group_check=True), r=[A, V], w=[av])
                                first_av = False
                        K.dve(lambda: nc.vector.tensor_copy(out=O[:, qc * 4:(qc + 1) * 4, h * 64:(h + 1) * 64],
                                                            in_=av[:, 0:256].rearrange("p (c d) -> p c d", c=4)), r=[av], w=[O])
                K.barrier()
            self.out_proj(ph, i, self.sb_w_out[j], O, None)
            K.barrier()

    def out_proj(self, ph, i, w_dram, O, bias_row):
        K, nc = self.K, self.nc
        with contextlib.ExitStack() as ph3:
            w = K.sb(ph3, "ow", [128, 8, D], BF16)
            self.load_w(w, w_dram, 8, D, split=1024)
            e = self.epilogue_setup(ph3, i, 0)
            oT = [K.sb(ph3, f"ooT{k}", [128, 8, 128], BF16) for k in range(2)]
            ptr = [K.ps(ph3, f"optr{k}", (128, 1024), BF16) for k in range(2)]
            psY = [[K.ps(ph3, f"opsY{k}{hf}") for hf in range(2)] for k in range(2)]
            brow = None
            if bias_row is not None:
                brow = K.sb(ph3, "obrow", [1, D], BF16)
                K.dma("pool", brow[:], bias_row, r=[self.cbuf], w=[brow])
            for t in range(NT):
                self.epi_load(e, t)
                pt = ptr[t % 2]
                ot = oT[t % 2]
                for kc in range(8):
                    K.pe(lambda kc=kc: nc.tensor.transpose(out=pt[:, kc * 128:(kc + 1) * 128], in_=O[:, t, kc * 128:(kc + 1) * 128],
                                                          identity=self.ident[:]), r=[O, self.ident], w=[pt])
                K.act(lambda: nc.scalar.copy(out=ot[:], in_=pt[:].rearrange("p (kc t) -> p kc t", kc=8)), r=[pt], w=[ot])
                yb = psY[t % 2]
                for hf in range(2):
                    for kc in range(8):
                        K.pe(lambda kc=kc, hf=hf: nc.tensor.matmul(yb[hf][:], lhsT=ot[:, kc, :], rhs=w[:, kc, hf * 512:(hf + 1) * 512],
                                                                   start=(kc == 0), stop=(kc == 7 and brow is None)), r=[ot, w], w=[yb[hf]])
                    if brow is not None:
                        K.pe(lambda hf=hf: nc.tensor.matmul(yb[hf][:], lhsT=self.ones[0:1, :], rhs=brow[0:1, hf * 512:(hf + 1) * 512],
                                                            start=False, stop=True), r=[self.ones, brow], w=[yb[hf]])
                self.epilogue(e, t, yb)

    def build(self):
        K = self.K
        self.load_consts()
        self.phase_mod()
        first = True
        for i in self.layers:
            kind, j = i % 3, i // 3
            self.phase_norm(i, 0, src_is_x=first)
            self.resid_from_x = first
            first = False
            if kind == 0:
                self.phase_sb_proj(j)
                self.phase_sb_core(i, j)
            elif kind == 1:
                self.phase_nsa(i, j)
            else:
                self.phase_conv(i, j)
            self.resid_from_x = False
            self.phase_norm(i, 1, src_is_x=False)
            self.phase_ffn(i)
        K.barrier()
        K.st.close()
        return self.nc

    def copy_x_to_out(self):
        K = self.K
        with contextlib.ExitStack() as ph:
            bufs = [K.sb(ph, f"cx{k}", [128, 4, D], F32) for k in range(2)]
            xv = self.x.rearrange("(g c p) f -> g p c f", p=128, c=4)
            ov = self.out.rearrange("(g c p) f -> g p c f", p=128, c=4)
            for g in range(NG):
                b = bufs[g % 2]
                K.dma("sp", b[:], xv[g], r=[], w=[b])
                K.dma("sp", ov[g], b[:], r=[b], w=[self.b_h[4 * g + c] for c in range(4)])
            K.barrier()


    def phase_nsa(self, i, j):
        K, nc = self.K, self.nc
        UTv = self.UT.rearrange("(kc p) t -> p kc t", p=128)
        TWO_PI = 2.0 * np.pi
        with contextlib.ExitStack() as nsa:
            kcT = K.sb(nsa, "n_kcT", [64, 4, 256], BF16)
            vcmp = K.sb(nsa, "n_vcmp", [128, 2, 4, 64], BF16)
            cscmp = K.sb(nsa, "n_cscmp", [64, 2, 256], F32)
            perm = K.sb(nsa, "n_perm", [64, 64], BF16)
            K.dma("sp", perm[:], self.c_perm, r=[self.cbuf], w=[perm])
            with contextlib.ExitStack() as ph:
                w = K.sb(ph, "nw", [128, 8, NSA_IN], BF16)
                self.load_w(w, self.nsa_w_in[j], 8, NSA_IN, split=NSA_IN)
                cosT = K.sb(ph, "ncos", [64, S], F32)
                sinT = K.sb(ph, "nsin", [64, S], F32)
                with contextlib.ExitStack() as ph0:
                    posi = K.sb(ph0, "nposi", [64, S], I32)
                    ang = K.sb(ph0, "nang", [64, S], F32)
                    t1 = K.sb(ph0, "nt1", [64, S], F32)
                    t2 = K.sb(ph0, "nt2", [64, S], F32)
                    ki = K.sb(ph0, "nki", [64, S], I32)
                    invf = K.sb(ph0, "ninvf", [64, 1], F32)
                    K.dma("sp", posi[:], self.pos.to_broadcast([64, S]), r=[self.cbuf], w=[posi])
                    K.dma("sp", invf[:], self.c_invf, r=[self.cbuf], w=[invf])
                    K.dve(lambda: nc.vector.tensor_copy(out=ang[:], in_=posi[:]), r=[posi], w=[ang])
                    K.dve(lambda: nc.vector.tensor_scalar(out=ang[:], in0=ang[:], scalar1=invf[:, 0:1], scalar2=None, op0=ALU.mult),
                          r=[ang, invf], w=[ang])
                    for tab, shift in ((sinT, 0.0), (cosT, 0.5 * np.pi)):
                        K.dve(lambda: nc.vector.tensor_scalar(out=t1[:], in0=ang[:], scalar1=shift, scalar2=1.0 / TWO_PI,
                                                              op0=ALU.add, op1=ALU.mult), r=[ang], w=[t1])
                        K.dve(lambda: nc.vector.tensor_copy(out=ki[:], in_=t1[:]), r=[t1], w=[ki])
                        K.dve(lambda: nc.vector.tensor_copy(out=t1[:], in_=ki[:]), r=[ki], w=[t1])
                        K.dve(lambda: nc.vector.scalar_tensor_tensor(out=t2[:], in0=t1[:], scalar=-TWO_PI, in1=ang[:], op0=ALU.mult, op1=ALU.add),
                              r=[t1, ang], w=[t2])
                        K.dve(lambda: nc.vector.tensor_scalar(out=t2[:], in0=t2[:], scalar1=shift, scalar2=None, op0=ALU.add), r=[t2], w=[t2])
                        K.dve(lambda: nc.vector.tensor_scalar(out=t1[:], in0=t2[:], scalar1=np.pi, scalar2=-TWO_PI, op0=ALU.is_gt, op1=ALU.mult),
                              r=[t2], w=[t1])
                        K.dve(lambda: nc.vector.tensor_tensor(out=t2[:], in0=t2[:], in1=t1[:], op=ALU.add), r=[t2, t1], w=[t2])
                        K.dve(lambda: nc.vector.tensor_scalar(out=t1[:], in0=t2[:], scalar1=-np.pi, scalar2=TWO_PI, op0=ALU.is_lt, op1=ALU.mult),
                              r=[t2], w=[t1])
                        K.dve(lambda: nc.vector.tensor_tensor(out=t2[:], in0=t2[:], in1=t1[:], op=ALU.add), r=[t2, t1], w=[t2])
                        K.dve(lambda: nc.vector.tensor_scalar(out=t2[:], in0=t2[:], scalar1=-3.1415925, scalar2=3.1415925, op0=ALU.max, op1=ALU.min),
                              r=[t2], w=[t2])
                        K.act(lambda: nc.scalar.activation(out=tab[:], in_=t2[:], func=AF.Sin), r=[t2], w=[tab])
                    K.dve(lambda: nc.vector.tensor_copy(out=cscmp[:, 0, 0:255], in_=cosT[:, 31:S:16]), r=[cosT], w=[cscmp])
                    K.dve(lambda: nc.vector.tensor_copy(out=cscmp[:, 1, 0:255], in_=sinT[:, 31:S:16]), r=[sinT, cscmp], w=[cscmp])
                    K.barrier()
                if NSA_STOP == "n0":
                    return
                utg = [K.sb(ph, f"nutg{k}", [128, 8, 512], BF16) for k in range(2)]
                xb = [K.sb(ph, f"nxb{k}", [64, 512], BF16) for k in range(2)]
                r1 = [K.sb(ph, f"nr1{k}", [64, 512], F32) for k in range(2)]
                r2 = [K.sb(ph, f"nr2{k}", [64, 512], F32) for k in range(2)]
                ob = [K.sb(ph, f"nob{k}", [64, 512], BF16) for k in range(4)]
                va = [K.sb(ph, f"nva{k}", [128, 8, 65], BF16) for k in range(2)]
                gt = [K.sb(ph, f"ngt{k}", [128, 48], F32) for k in range(2)]
                for v_ in va:
                    K.pool(lambda: nc.gpsimd.memset(v_[:], 1.0), w=[v_])
                pp = [K.ps(ph, f"npp{k}") for k in range(3)]
                pr = [K.ps(ph, f"npr{k}") for k in range(2)]
                pv = [K.ps(ph, f"npv{k}") for k in range(2)]
                pg = K.ps(ph, "npg")

                def load_u(g):
                    K.dma("sp", utg[g % 2][:], UTv[:, :, g * 512:(g + 1) * 512], r=[self.b_ut[g]], w=[utg[g % 2]])

                units = []
                for h in range(16):
                    units.append((h * 64, self.NQ, h, True, self.b_nq))
                for g4 in range(4):
                    units.append((D + 2 * 256 + g4 * 64, self.NK, g4, True, self.b_nk))
                for g4 in range(4):
                    units.append((D + 4 * 256 + g4 * 64, self.NK, 4 + g4, True, self.b_nk))
                for g4 in range(4):
                    units.append((D + 0 * 256 + g4 * 64, self.NC, g4, False, self.b_ncr))
                for g4 in range(4):
                    units.append((D + 1 * 256 + g4 * 64, self.NC, 4 + g4, False, self.b_ncr))
                load_u(0)
                n = 0
                nr = 0
                for g in range(NG):
                    if g + 1 < NG:
                        load_u(g + 1)
                    ug = utg[g % 2]
                    tsl = slice(g * 512, (g + 1) * 512)
                    for (col, dst, ui, rope, bb) in units:
                        if (rope and "r" not in N1_PARTS) or ((not rope) and "u" not in N1_PARTS):
                            continue
                        ps = pp[n % 3]
                        o_ = ob[n % 4]
                        n += 1
                        for kc in range(8):
                            K.pe(lambda: nc.tensor.matmul(ps[0:64, :], lhsT=w[:, kc, col:col + 64], rhs=ug[:, kc, :],
                                                          start=(kc == 0), stop=(kc == 7)), r=[w, ug], w=[ps])
                        if rope and "asu" not in ROPE_MODE:
                            x_, a_, b_, p2 = xb[nr % 2], r1[nr % 2], r2[nr % 2], pr[nr % 2]
                            nr += 1
                            K.act(lambda: nc.scalar.copy(out=x_[:], in_=ps[0:64, :]), r=[ps], w=[x_])
                            if "noperm" not in ROPE_MODE:
                                K.pe(lambda: nc.tensor.matmul(p2[0:64, :], lhsT=perm[:], rhs=x_[:], start=True, stop=True), r=[perm, x_], w=[p2])
                            K.dve(lambda: nc.vector.tensor_tensor(out=a_[:], in0=ps[0:64, :], in1=cosT[:, tsl], op=ALU.mult), r=[ps, cosT], w=[a_])
                            if "noperm" not in ROPE_MODE:
                                K.dve(lambda: nc.vector.tensor_tensor(out=b_[:], in0=p2[0:64, :], in1=sinT[:, tsl], op=ALU.mult), r=[p2, sinT], w=[b_])
                            else:
                                K.dve(lambda: nc.vector.tensor_tensor(out=b_[:], in0=ps[0:64, :], in1=sinT[:, tsl], op=ALU.mult), r=[ps, sinT], w=[b_])
                            if "dveadd" in ROPE_MODE:
                                K.dve(lambda: nc.vector.tensor_tensor(out=o_[:], in0=a_[:], in1=b_[:], op=ALU.add), r=[a_, b_], w=[o_])
                            else:
                                K.pool(lambda: nc.gpsimd.tensor_tensor(out=o_[:], in0=a_[:], in1=b_[:], op=ALU.add), r=[a_, b_], w=[o_])
                        else:
                            K.act(lambda: nc.scalar.copy(out=o_[:], in_=ps[0:64, :]), r=[ps], w=[o_])
                        K.dma("sp", dst[ui, :, tsl], o_[:], r=[o_], wa=[bb])
                    for tt in range(4):
                        t = g * 4 + tt
                        v_ = va[t % 2]
                        pv_ = pv[t % 2]
                        for m, c0 in ((0, D + 3 * 256), (1, D + 5 * 256)) if "v" in N1_PARTS else ():
                            for kc in range(8):
                                K.pe(lambda: nc.tensor.matmul(pv_[:, m * 256:(m + 1) * 256], lhsT=ug[:, kc, tt * 128:(tt + 1) * 128],
                                                              rhs=w[:, kc, c0:c0 + 256], start=(kc == 0), stop=(kc == 7)), r=[w, ug], w=[pv_])
                        if "v" in N1_PARTS:
                            K.dve(lambda: nc.vector.tensor_copy(out=v_[:, :, 0:64], in_=pv_[:].rearrange("p (u d) -> p u d", d=64)), r=[pv_], w=[v_])
                            K.dma("sp", self.NV[t * 128:(t + 1) * 128, :], v_[:].rearrange("p u d -> p (u d)"), r=[v_], wa=[self.b_nv])
                        g_ = gt[t % 2]
                        if "g" not in N1_PARTS:
                            continue
                        for kc in range(8):
                            K.pe(lambda: nc.tensor.matmul(pg[:, 0:48], lhsT=ug[:, kc, tt * 128:(tt + 1) * 128], rhs=w[:, kc, 2560:2608],
                                                          start=(kc == 0), stop=(kc == 7)), r=[w, ug], w=[pg])
                        K.act(lambda: nc.scalar.activation(out=g_[:], in_=pg[:, 0:48], func=AF.Sigmoid), r=[pg], w=[g_])
                        K.dma("sp", self.NGt[t * 128:(t + 1) * 128, :], g_[:], r=[g_], wa=[self.b_ngt])
                K.barrier()
            if NSA_STOP == "n1":
                return
            with contextlib.ExitStack() as ph:
                raw = K.sb(ph, "craw", [64, 8, S], BF16)
                for u_ in range(8):
                    K.dma("sp", raw[:, u_, :], self.NC[u_], r=[self.b_ncr], wa=[raw])
                K.pool(lambda: nc.gpsimd.memset(vcmp[:], 0.0), w=[vcmp])
                K.pool(lambda: nc.gpsimd.memset(kcT[:], 0.0), w=[kcT])
                hps = [K.ps(ph, f"chps{k}") for k in range(2)]
                bps = K.ps(ph, "cbps")
                ops_ = [K.ps(ph, f"cops{k}") for k in range(2)]
                p2 = K.ps(ph, "cp2")
                for kv in ("k", "v"):
                    w1 = K.sb(ph, "cw1" + kv, [64, 32, 256], BF16)
                    w1v = self.nsa_w1[kv][j].rearrange("(l d) h -> d l h", d=64)
                    for l0 in range(0, 32, 8):
                        K.dma("pool", w1[:, l0:l0 + 8, :], w1v[:, l0:l0 + 8, :], r=[self.cbuf], wa=[w1])
                    w2 = K.sb(ph, "cw2" + kv, [128, 2, 64], BF16)
                    K.dma("pool", w2[:], self.nsa_w2[kv][j].rearrange("(hc p) d -> p hc d", p=128), r=[self.cbuf], w=[w2])
                    peT = K.sb(ph, "cpeT" + kv, [64, 32], F32)
                    peTb = K.sb(ph, "cpeTb" + kv, [64, 32], BF16)
                    K.dma("sp", peT[:], self.nsa_peT[kv], r=[self.cbuf], w=[peT])
                    K.dve(lambda: nc.vector.tensor_copy(out=peTb[:], in_=peT[:]), r=[peT], w=[peTb])
                    bias = K.sb(ph, "cbias" + kv, [128, 2], F32)
                    for hc in range(2):
                        for l in range(32):
                            K.pe(lambda: nc.tensor.matmul(bps[:, hc:hc + 1], lhsT=w1[:, l, hc * 128:(hc + 1) * 128], rhs=peTb[:, l:l + 1],
                                                          start=(l == 0), stop=(l == 31)), r=[w1, peTb], w=[bps])
                        K.dve(lambda: nc.vector.tensor_copy(out=bias[:, hc:hc + 1], in_=bps[:, hc:hc + 1]), r=[bps], w=[bias])
                    xb_ = K.sb(ph, "cxb" + kv, [128, 256], F32)
                    x2_ = K.sb(ph, "cx2" + kv, [128, 256], F32)
                    x3_ = K.sb(ph, "cx3" + kv, [128, 256], F32)
                    hidT = K.sb(ph, "chid" + kv, [128, 2, 256], BF16)
                    kx = K.sb(ph, "ckx" + kv, [64, 256], BF16)
                    ka = K.sb(ph, "cka" + kv, [64, 256], F32)
                    kb_ = K.sb(ph, "ckb" + kv, [64, 256], F32)
                    for g4 in range(4):
                        ui = g4 if kv == "k" else 4 + g4
                        for hc in range(2):
                            hp = hps[hc]
                            for l in range(32):
                                K.pe(lambda: nc.tensor.matmul(hp[:, 0:255], lhsT=w1[:, l, hc * 128:(hc + 1) * 128],
                                                              rhs=raw[:, ui, l:l + 16 * 254 + 1:16], start=(l == 0), stop=(l == 31)),
                                     r=[w1, raw], w=[hp])
                            K.dve(lambda: nc.vector.tensor_scalar(out=xb_[:, 0:255], in0=hp[:, 0:255], scalar1=bias[:, hc:hc + 1], scalar2=None,
                                                                  op0=ALU.add), r=[hp, bias], w=[xb_])
                            K.pool(lambda: nc.gpsimd.tensor_tensor(out=x2_[:, 0:255], in0=xb_[:, 0:255], in1=xb_[:, 0:255], op=ALU.mult), r=[xb_], w=[x2_])
                            K.dve(lambda: nc.vector.tensor_scalar(out=x2_[:, 0:255], in0=x2_[:, 0:255], scalar1=0.044715, scalar2=1.0,
                                                                  op0=ALU.mult, op1=ALU.add), r=[x2_], w=[x2_])
                            K.dve(lambda: nc.vector.tensor_tensor(out=x3_[:, 0:255], in0=x2_[:, 0:255], in1=xb_[:, 0:255], op=ALU.mult), r=[x2_, xb_], w=[x3_])
                            K.act(lambda: nc.scalar.activation(out=x3_[:, 0:255], in_=x3_[:, 0:255], func=AF.Tanh, scale=0.7978845608028654),
                                  r=[x3_], w=[x3_])
                            K.dve(lambda: nc.vector.scalar_tensor_tensor(out=x2_[:, 0:255], in0=x3_[:, 0:255], scalar=1.0, in1=xb_[:, 0:255],
                                                                         op0=ALU.add, op1=ALU.mult), r=[x3_, xb_], w=[x2_])
                            K.pool(lambda: nc.gpsimd.tensor_scalar(out=hidT[:, hc, 0:255], in0=x2_[:, 0:255], scalar1=0.5, scalar2=None, op0=ALU.mult),
                                   r=[x2_], w=[hidT])
                        if kv == "k":
                            op_ = ops_[0]
                            for hc in range(2):
                                K.pe(lambda: nc.tensor.matmul(op_[0:64, 0:255], lhsT=w2[:, hc, :], rhs=hidT[:, hc, 0:255],
                                                              start=(hc == 0), stop=(hc == 1)), r=[w2, hidT], w=[op_])
                            K.act(lambda: nc.scalar.copy(out=kx[:, 0:255], in_=op_[0:64, 0:255]), r=[op_], w=[kx])
                            K.pe(lambda: nc.tensor.matmul(p2[0:64, 0:255], lhsT=perm[:], rhs=kx[:, 0:255], start=True, stop=True), r=[perm, kx], w=[p2])
                            K.dve(lambda: nc.vector.tensor_tensor(out=ka[:, 0:255], in0=op_[0:64, 0:255], in1=cscmp[:, 0, 0:255], op=ALU.mult),
                                  r=[op_, cscmp], w=[ka])
                            K.dve(lambda: nc.vector.tensor_tensor(out=kb_[:, 0:255], in0=p2[0:64, 0:255], in1=cscmp[:, 1, 0:255], op=ALU.mult),
                                  r=[p2, cscmp], w=[kb_])
                            K.pool(lambda: nc.gpsimd.tensor_tensor(out=kcT[:, g4, 0:255], in0=ka[:, 0:255], in1=kb_[:, 0:255], op=ALU.add),
                                   r=[ka, kb_], w=[kcT])
                        else:
                            for nch, m in ((0, 128), (1, 127)):
                                op_ = ops_[nch]
                                for hc in range(2):
                                    K.pe(lambda: nc.tensor.matmul(op_[0:m, 0:64], lhsT=hidT[:, hc, nch * 128:nch * 128 + m], rhs=w2[:, hc, :],
                                                                  start=(hc == 0), stop=(hc == 1)), r=[w2, hidT], w=[op_])
                                K.act(lambda: nc.scalar.copy(out=vcmp[0:m, nch, g4, :], in_=op_[0:m, 0:64]), r=[op_], w=[vcmp])
                K.barrier()
            if NSA_STOP == "n2":
                return
            with contextlib.ExitStack() as ph:
                O = K.sb(ph, "nO", [128, NT, D], BF16)
                with contextlib.ExitStack() as ph2:
                    self._nsa_attn(ph2, O, kcT, vcmp)
                    K.barrier()
                self.out_proj(ph, i, self.nsa_w_out[j], O, None)
                K.barrier()

    def _nsa_attn(self, ph, O, kcT, vcmp):
        K, nc = self.K, self.nc
        V = K.sb(ph, "tV", [128, NT, 8 * 65], BF16)
        NVv = self.NV.rearrange("(t p) f -> p t f", p=128)
        for t0 in range(0, NT, 8):
            K.dma("sp", V[:, t0:t0 + 8, :], NVv[:, t0:t0 + 8, :], r=[self.b_nv], wa=[V])
        GT = K.sb(ph, "tGT", [128, NT, 48], F32)
        NGv = self.NGt.rearrange("(t p) f -> p t f", p=128)
        for t0 in range(0, NT, 8):
            K.dma("sp", GT[:, t0:t0 + 8, :], NGv[:, t0:t0 + 8, :], r=[self.b_ngt], wa=[GT])
        esel = K.sb(ph, "tesel", [64, 32 * 128], BF16)
        winb = K.sb(ph, "twinb", [128, 8 * 512], BF16)
        causb = K.sb(ph, "tcausb", [128, 512], BF16)
        band = K.sb(ph, "tband", [128, 512], BF16)
        wcm = K.sb(ph, "twcm", [128, 128], F32)
        wfb = K.sb(ph, "twfb", [128, 128], F32)
        anyok = K.sb(ph, "tanyok", [128, 1], F32)
        for t_, src in ((esel, self.c_esel), (winb, self.c_winb), (causb, self.c_causb), (band, self.c_band),
                        (wcm, self.c_wcm), (wfb, self.c_wfb), (anyok, self.c_anyok)):
            K.dma("sp", t_[:], src, r=[self.cbuf], w=[t_])
        ks = K.sb(ph, "tks", [64, S], BF16)
        kw = K.sb(ph, "tkw", [64, S], BF16)
        qh = [K.sb(ph, f"tq{k}", [64, S], BF16) for k in range(4)]
        psg = K.sb(ph, "tpsg", [128, 4, 256], F32)
        pun = [K.sb(ph, f"tpun{k}", [128, 256], F32) for k in range(2)]
        pb = [K.sb(ph, f"tpb{k}", [128, 256], BF16) for k in range(2)]
        pTs = [K.sb(ph, f"tpT{k}", [128, 2, 128], BF16) for k in range(2)]
        st = [K.sb(ph, f"tst{k}", [128, 8], F32) for k in range(2)]
        s4 = K.sb(ph, "ts4", [128, 64], F32)
        imp = K.sb(ph, "timp", [128, 64], F32)
        sc = K.sb(ph, "tsc", [128, 64], F32)
        wk = K.sb(ph, "twk", [128, 64], F32)
        m8a = K.sb(ph, "tm8a", [128, 8], F32)
        m8b = K.sb(ph, "tm8b", [128, 8], F32)
        selt = K.sb(ph, "tsel", [128, 64], F32)
        negm = K.sb(ph, "tnegm", [128, 64], BF16)
        nmT = K.sb(ph, "tnmT", [64, 512], BF16)
        Pt = [K.sb(ph, f"tP{k}", [128, 512], BF16) for k in range(3)]
        Oq = K.sb(ph, "tOq", [128, 4, 256], F32)
        cf = [K.sb(ph, f"tcf{k}", [128, 8], F32) for k in range(2)]
        sps = [K.ps(ph, f"tsps{k}") for k in range(2)]
        accs = K.ps(ph, "taccs")
        accw = K.ps(ph, "taccw")
        cps = K.ps(ph, "tcps")
        pTp = K.ps(ph, "tpTp", (128, 1024), BF16)
        ocp = K.ps(ph, "tocp")
        nmp = K.ps(ph, "tnmp", (128, 1024), BF16)
        u = 0
        nc_ = 0
        for g in range(4):
            K.dma("sp", ks[:], self.NK[g], r=[self.b_nk], w=[ks])
            K.dma("sp", kw[:], self.NK[4 + g], r=[self.b_nk], w=[kw])
            for r in range(4):
                K.dma("sp", qh[r][:], self.NQ[4 * g + r], r=[self.b_nq], w=[qh[r]])
            for qc in range(NG):
                K.pool(lambda: nc.gpsimd.memset(psg[:], 0.0), w=[psg])
                for c in range(4):
                    T_ = 4 * qc + c
                    ncols = min(8 * T_ + 7, NCMP)
                    b0 = 256 - 8 * T_
                    for r in range(4):
                        h = 4 * g + r
                        q_ = qh[r]
                        s_ = st[nc_ % 2]
                        pu, pb_, pT = pun[nc_ % 2], pb[nc_ % 2], pTs[nc_ % 2]
                        nc_ += 1
                        K.pe(lambda: nc.tensor.matmul(cps[:, 0:ncols], lhsT=q_[:, T_ * 128:(T_ + 1) * 128], rhs=kcT[:, g, 0:ncols],
                                                      start=True, stop=False), r=[q_, kcT], w=[cps])
                        K.pe(lambda: nc.tensor.matmul(cps[:, 0:ncols], lhsT=self.ident[:], rhs=band[:, b0:b0 + ncols],
                                                      start=False, stop=True), r=[self.ident, band], w=[cps])
                        K.dve(lambda: nc.vector.reduce_max(out=s_[:, 0:1], in_=cps[:, 0:ncols], axis=AX.X), r=[cps], w=[s_])
                        K.dve(lambda: nc.vector.tensor_scalar(out=s_[:, 1:2], in0=s_[:, 0:1], scalar1=-0.125, scalar2=None, op0=ALU.mult),
                              r=[s_], w=[s_])
                        K.act(lambda: nc.scalar.activation(out=pu[:, 0:ncols], in_=cps[:, 0:ncols], func=AF.Exp, scale=0.125,
                                                           bias=s_[:, 1:2], accum_out=s_[:, 2:3]), r=[cps, s_], w=[pu, s_])
                        K.dve(lambda: nc.vector.reciprocal(out=s_[:, 3:4], in_=s_[:, 2:3]), r=[s_], w=[s_])
                        if T_ == 0:
                            K.dve(lambda: nc.vector.tensor_tensor(out=s_[:, 3:4], in0=s_[:, 3:4], in1=anyok[:], op=ALU.mult), r=[s_, anyok], w=[s_])
                        if r == 0:
                            K.dve(lambda: nc.vector.tensor_scalar(out=psg[:, c, 0:ncols], in0=pu[:, 0:ncols], scalar1=s_[:, 3:4], scalar2=None,
                                                                  op0=ALU.mult), r=[pu, s_], w=[psg])
                        else:
                            K.dve(lambda: nc.vector.scalar_tensor_tensor(out=psg[:, c, 0:ncols], in0=pu[:, 0:ncols], scalar=s_[:, 3:4],
                                                                         in1=psg[:, c, 0:ncols], op0=ALU.mult, op1=ALU.add), r=[pu, s_, psg], w=[psg])
                        K.pool(lambda: nc.gpsimd.tensor_scalar(out=pb_[:, 0:ncols], in0=pu[:, 0:ncols], scalar1=s_[:, 3:4], scalar2=None,
                                                               op0=ALU.mult), r=[pu, s_], w=[pb_])
                        chunks = [(0, min(128, ncols))] + ([(1, ncols - 128)] if ncols > 128 else [])
                        for ch, wd in chunks:
                            K.pe(lambda: nc.tensor.transpose(out=pTp[0:wd, ch * 128:(ch + 1) * 128], in_=pb_[:, ch * 128:ch * 128 + wd],
                                                             identity=self.ident[:]), r=[pb_, self.ident], w=[pTp])
                        for ch, wd in chunks:
                            K.act(lambda: nc.scalar.copy(out=pT[0:wd, ch, :], in_=pTp[0:wd, ch * 128:(ch + 1) * 128]), r=[pTp], w=[pT])
                        for k_, (ch, wd) in enumerate(chunks):
                            K.pe(lambda: nc.tensor.matmul(ocp[:, 0:64], lhsT=pT[0:wd, ch, :], rhs=vcmp[0:wd, ch, g, :],
                                                          start=(k_ == 0), stop=(k_ == len(chunks) - 1)), r=[pT, vcmp], w=[ocp])
                        K.dve(lambda: nc.vector.tensor_scalar(out=Oq[:, c, r * 64:(r + 1) * 64], in0=ocp[:, 0:64],
                                                              scalar1=GT[:, T_, 3 * h:3 * h + 1], scalar2=None, op0=ALU.mult),
                              r=[ocp, GT], w=[Oq])
                    pv4 = psg[:, c, :].rearrange("p (j f) -> p j f", f=4)
                    K.dve(lambda: nc.vector.tensor_reduce(out=s4[:], in_=pv4, axis=AX.X, op=ALU.add), r=[psg], w=[s4])
                    K.dve(lambda: nc.vector.scalar_tensor_tensor(out=imp[:], in0=pv4[:, :, 3], scalar=-0.5, in1=s4[:], op0=ALU.mult, op1=ALU.add),
                          r=[psg, s4], w=[imp])
                    K.dve(lambda: nc.vector.scalar_tensor_tensor(out=imp[:, 1:64], in0=pv4[:, 0:63, 3], scalar=0.5, in1=imp[:, 1:64],
                                                                 op0=ALU.mult, op1=ALU.add), r=[psg, imp], w=[imp])
                    w0 = 64 - 2 * T_
                    K.dve(lambda: nc.vector.tensor_tensor(out=sc[:], in0=imp[:], in1=wcm[:, w0:w0 + 64], op=ALU.mult), r=[imp, wcm], w=[sc])
                    K.dve(lambda: nc.vector.tensor_tensor(out=sc[:], in0=sc[:], in1=wfb[:, w0:w0 + 64], op=ALU.add), r=[sc, wfb], w=[sc])
                    K.dve(lambda: nc.vector.memset(sc[:, 0:1], 1.0e4), r=[sc], w=[sc])
                    K.dve(lambda: nc.vector.max(out=m8a[:], in_=sc[:]), r=[sc], w=[m8a])
                    K.dve(lambda: nc.vector.match_replace(out=wk[:], in_to_replace=m8a[:], in_values=sc[:], imm_value=-3.0e38), r=[sc, m8a], w=[wk])
                    K.dve(lambda: nc.vector.max(out=m8b[:], in_=wk[:]), r=[wk], w=[m8b])
                    K.dve(lambda: nc.vector.tensor_scalar(out=selt[:], in0=sc[:], scalar1=m8b[:, 7:8], scalar2=None, op0=ALU.is_ge),
                          r=[sc, m8b], w=[selt])
                    K.dve(lambda: nc.vector.tensor_scalar(out=negm[:], in0=selt[:], scalar1=-1.0, scalar2=BIG, op0=ALU.add, op1=ALU.mult),
                          r=[selt], w=[negm])
                    K.pe(lambda: nc.tensor.transpose(out=nmp[0:64, c * 128:(c + 1) * 128], in_=negm[:], identity=self.ident[:]),
                         r=[negm, self.ident], w=[nmp])
                K.act(lambda: nc.scalar.copy(out=nmT[:], in_=nmp[0:64, 0:512]), r=[nmp], w=[nmT])
                for r in range(4):
                    h = 4 * g + r
                    q_ = qh[r]
                    first = True
                    for i_ in range(0, 4 * qc + 4):
                        a = i_ - 4 * qc
                        diag = a >= 0
                        c0 = 128 * a if diag else 0
                        W = 512 - c0
                        sp_, P = sps[u % 2], Pt[u % 3]
                        u += 1
                        qs = q_[:, qc * 512 + c0:(qc + 1) * 512]
                        K.pe(lambda: nc.tensor.matmul(sp_[:, 0:W], lhsT=ks[:, i_ * 128:(i_ + 1) * 128], rhs=qs, start=True, stop=False),
                             r=[ks, q_], w=[sp_])
                        K.pe(lambda: nc.tensor.matmul(sp_[:, 0:W], lhsT=esel[:, i_ * 128:(i_ + 1) * 128], rhs=nmT[:, c0:512],
                                                      start=False, stop=(not diag)), r=[esel, nmT], w=[sp_])
                        if diag:
                            K.pe(lambda: nc.tensor.matmul(sp_[:, 0:W], lhsT=self.ident[:], rhs=causb[:, 0:W], start=False, stop=True),
                                 r=[self.ident, causb], w=[sp_])
                        K.act(lambda: nc.scalar.activation(out=P[:, 0:W], in_=sp_[:, 0:W], func=AF.Exp, scale=0.125), r=[sp_], w=[P])
                        for c in range(a if diag else 0, 4):
                            K.pe(lambda: nc.tensor.matmul(accs[:, c * 65:(c + 1) * 65], lhsT=P[:, c * 128 - c0:(c + 1) * 128 - c0],
                                                          rhs=V[:, i_, g * 65:(g + 1) * 65], start=first, stop=False, skip_group_check=True),
                                 r=[P, V], w=[accs])
                            first = False
                    first = True
                    for i_ in range(max(0, 4 * qc - 4), 4 * qc + 4):
                        e_ = i_ - (4 * qc - 4)
                        if e_ < 4:
                            c0, c1 = 0, 128 * (e_ + 1)
                        else:
                            c0, c1 = 128 * (e_ - 4), 512
                        W = c1 - c0
                        sp_, P = sps[u % 2], Pt[u % 3]
                        u += 1
                        qs = q_[:, qc * 512 + c0:qc * 512 + c1]
                        K.pe(lambda: nc.tensor.matmul(sp_[:, 0:W], lhsT=kw[:, i_ * 128:(i_ + 1) * 128], rhs=qs, start=True, stop=False),
                             r=[kw, q_], w=[sp_])
                        K.pe(lambda: nc.tensor.matmul(sp_[:, 0:W], lhsT=self.ident[:], rhs=winb[:, e_ * 512 + c0:e_ * 512 + c1],
                                                      start=False, stop=True), r=[self.ident, winb], w=[sp_])
                        K.act(lambda: nc.scalar.activation(out=P[:, 0:W], in_=sp_[:, 0:W], func=AF.Exp, scale=0.125), r=[sp_], w=[P])
                        for c in range(c0 // 128, c1 // 128):
                            K.pe(lambda: nc.tensor.matmul(accw[:, c * 65:(c + 1) * 65], lhsT=P[:, c * 128 - c0:(c + 1) * 128 - c0],
                                                          rhs=V[:, i_, (4 + g) * 65:(5 + g) * 65], start=first, stop=False, skip_group_check=True),
                                 r=[P, V], w=[accw])
                            first = False
                    for bi, acc in ((1, accs), (2, accw)):
                        cf_ = cf[bi - 1]
                        av = acc[:, 0:260].rearrange("p (c d) -> p c d", d=65)
                        K.dve(lambda: nc.vector.reciprocal(out=cf_[:, 0:4], in_=av[:, :, 64]), r=[acc], w=[cf_])
                        K.dve(lambda: nc.vector.tensor_tensor(out=cf_[:, 4:8], in0=cf_[:, 0:4], in1=GT[:, 4 * qc:4 * qc + 4, 3 * h + bi], op=ALU.mult),
                              r=[cf_, GT], w=[cf_])
                        for c in range(4):
                            K.dve(lambda: nc.vector.scalar_tensor_tensor(out=Oq[:, c, r * 64:(r + 1) * 64], in0=av[:, c, 0:64], scalar=cf_[:, 4 + c:5 + c],
                                                                         in1=Oq[:, c, r * 64:(r + 1) * 64], op0=ALU.mult, op1=ALU.add),
                                  r=[acc, cf_, Oq], w=[Oq])
                K.pool(lambda: nc.gpsimd.tensor_copy(out=O[:, 4 * qc:4 * qc + 4, g * 256:(g + 1) * 256], in_=Oq[:]), r=[Oq], w=[O])


    def phase_conv(self, i, j):
        K, nc = self.K, self.nc
        UTv = self.UT.rearrange("(kc p) t -> p kc t", p=128)
        with contextlib.ExitStack() as ph:
            w = K.sb(ph, "cw", [128, 8, 2 * D], BF16)
            self.load_w(w, self.cv_w_in[j], 8, 2 * D)
            bcol = K.sb(ph, "cbcol", [128, 16], F32)
            K.dma("sp", bcol[:], self.cv_b_inT, r=[self.cbuf], w=[bcol])
            utg = [K.sb(ph, f"cutg{k}", [128, 8, 512], BF16) for k in range(2)]
            sgt = [K.sb(ph, f"csg{k}", [128, 512], F32) for k in range(2)]
            hgt = [K.sb(ph, f"chg{k}", [128, 512], F32) for k in range(3)]
            psa = [K.ps(ph, f"cpsa{k}") for k in range(2)]
            psg = [K.ps(ph, f"cpsg{k}") for k in range(2)]

            def load_u(g):
                K.dma("sp", utg[g % 2][:], UTv[:, :, g * 512:(g + 1) * 512], r=[self.b_ut[g]], w=[utg[g % 2]])

            load_u(0)
            n = 0
            for g in range(NG):
                if g + 1 < NG:
                    load_u(g + 1)
                ug = utg[g % 2]
                for cc in range(8):
                    pa, pg = psa[n % 2], psg[n % 2]
                    sg, hg = sgt[n % 2], hgt[n % 3]
                    n += 1
                    for kc in range(8):
                        K.pe(lambda: nc.tensor.matmul(pa[:], lhsT=w[:, kc, cc * 128:(cc + 1) * 128], rhs=ug[:, kc, :],
                                                      start=(kc == 0), stop=(kc == 7)), r=[w, ug], w=[pa])
                    for kc in range(8):
                        K.pe(lambda: nc.tensor.matmul(pg[:], lhsT=w[:, kc, D + cc * 128:D + (cc + 1) * 128], rhs=ug[:, kc, :],
                                                      start=(kc == 0), stop=(kc == 7)), r=[w, ug], w=[pg])
                    K.act(lambda: nc.scalar.activation(out=sg[:], in_=pg[:], func=AF.Sigmoid, bias=bcol[:, 8 + cc:9 + cc]),
                          r=[pg, bcol], w=[sg])
                    K.dve(lambda: nc.vector.scalar_tensor_tensor(out=hg[:], in0=pa[:], scalar=bcol[:, cc:cc + 1], in1=sg[:],
                                                                 op0=ALU.add, op1=ALU.mult), r=[pa, bcol, sg], w=[hg])
                    K.dma("sp", self.HG[cc * 128:(cc + 1) * 128, g * 512:(g + 1) * 512], hg[:], r=[hg], wa=[self.b_hg[g]])
            K.barrier()
        with contextlib.ExitStack() as ph:
            w = K.sb(ph, "cow", [128, 8, D], BF16)
            self.load_w(w, self.cv_w_out[j], 8, D, split=1024)
            brow = K.sb(ph, "cobrow", [1, D], BF16)
            K.dma("pool", brow[:], self.cv_b_out[j:j + 1, :], r=[self.cbuf], w=[brow])
            dw = K.sb(ph, "cdw", [128, 8, 31], F32)
            vec = K.sb(ph, "cvec", [128, 3, 8], F32)
            onesf = K.sb(ph, "conesf", [128, 128], F32)
            K.dma("sp", dw[:], self.cv_dwT, r=[self.cbuf], w=[dw])
            K.dma("sp", vec[:], self.cv_vecT, r=[self.cbuf], w=[vec])
            K.dma("sp", onesf[:], self.c_onesf, r=[self.cbuf], w=[onesf])
            e = self.epilogue_setup(ph, i, 0)
            xin = [K.sb(ph, f"cxin{k}", [128, 8, 542], F32) for k in range(2)]
            acc = K.sb(ph, "cacc", [128, 8, 512], F32)
            sq = [K.sb(ph, f"csq{k}", [128, 512], F32) for k in range(2)]
            mt = K.sb(ph, "cm", [128, 512], F32)
            msq = K.sb(ph, "cmsq", [128, 512], F32)
            var = K.sb(ph, "cvar", [128, 512], F32)
            rstd = K.sb(ph, "crstd", [128, 512], F32)
            dt_ = [K.sb(ph, f"cd{k}", [128, 512], F32) for k in range(2)]
            xh = [K.sb(ph, f"cxh{k}", [128, 512], F32) for k in range(2)]
            hT = K.sb(ph, "chT", [128, 8, 512], BF16)
            s1 = K.ps(ph, "cs1")
            s2 = K.ps(ph, "cs2")
            psY = [[K.ps(ph, f"cpsY{k}{hf}") for hf in range(2)] for k in range(2)]
            HGv = self.HG.rearrange("(cc p) t -> p cc t", p=128)

            def load_x(g):
                xt = xin[g % 2]
                if g == 0:
                    K.pool(lambda: nc.gpsimd.memset(xt[:, :, 0:30], 0.0), w=[xt])
                    K.dma("sp", xt[:, :, 30:542], HGv[:, :, 0:512], r=[self.b_hg[0]], wa=[xt])
                else:
                    K.dma("sp", xt[:], HGv[:, :, g * 512 - 30:g * 512 + 512], r=[self.b_hg[g - 1], self.b_hg[g]], w=[xt])

            load_x(0)
            for g in range(NG):
                if g + 1 < NG:
                    load_x(g + 1)
                xt = xin[g % 2]
                for cc in range(8):
                    K.dve(lambda: nc.vector.tensor_scalar(out=acc[:, cc, :], in0=xt[:, cc, 0:512], scalar1=dw[:, cc, 0:1],
                                                          scalar2=vec[:, 0, cc:cc + 1], op0=ALU.mult, op1=ALU.add),
                          r=[xt, dw, vec], w=[acc])
                    for k in range(1, 31):
                        K.dve(lambda: nc.vector.scalar_tensor_tensor(out=acc[:, cc, :], in0=xt[:, cc, k:k + 512], scalar=dw[:, cc, k:k + 1],
                                                                     in1=acc[:, cc, :], op0=ALU.mult, op1=ALU.add),
                              r=[xt, dw, acc], w=[acc])
                    sq_ = sq[cc % 2]
                    K.act(lambda: nc.scalar.activation(out=sq_[:], in_=acc[:, cc, :], func=AF.Square), r=[acc], w=[sq_])
                    K.pe(lambda: nc.tensor.matmul(s1[:], lhsT=onesf[:], rhs=acc[:, cc, :], start=(cc == 0), stop=(cc == 7)),
                         r=[onesf, acc], w=[s1])
                    K.pe(lambda: nc.tensor.matmul(s2[:], lhsT=onesf[:], rhs=sq_[:], start=(cc == 0), stop=(cc == 7)),
                         r=[onesf, sq_], w=[s2])
                K.dve(lambda: nc.vector.tensor_scalar(out=mt[:], in0=s1[:], scalar1=1.0 / D, scalar2=None, op0=ALU.mult), r=[s1], w=[mt])
                K.pool(lambda: nc.gpsimd.tensor_tensor(out=msq[:], in0=mt[:], in1=mt[:], op=ALU.mult), r=[mt], w=[msq])
                K.dve(lambda: nc.vector.scalar_tensor_tensor(out=var[:], in0=s2[:], scalar=1.0 / D, in1=msq[:], op0=ALU.mult, op1=ALU.subtract),
                      r=[s2, msq], w=[var])
                K.act(lambda: nc.scalar.activation(out=var[:], in_=var[:], func=AF.Sqrt, bias=EPS), r=[var], w=[var])
                K.dve(lambda: nc.vector.reciprocal(out=rstd[:], in_=var[:]), r=[var], w=[rstd])
                for cc in range(8):
                    d_, x_ = dt_[cc % 2], xh[cc % 2]
                    K.pool(lambda: nc.gpsimd.tensor_tensor(out=d_[:], in0=acc[:, cc, :], in1=mt[:], op=ALU.subtract), r=[acc, mt], w=[d_])
                    K.dve(lambda: nc.vector.tensor_tensor(out=x_[:], in0=d_[:], in1=rstd[:], op=ALU.mult), r=[d_, rstd], w=[x_])
                    K.act(lambda: nc.scalar.activation(out=hT[:, cc, :], in_=x_[:], func=AF.Silu, scale=vec[:, 1, cc:cc + 1],
                                                       bias=vec[:, 2, cc:cc + 1]), r=[x_, vec], w=[hT])
                for tt in range(4):
                    t = g * 4 + tt
                    self.epi_load(e, t)
                    yb = psY[tt % 2]
                    for hf in range(2):
                        for cc in range(8):
                            K.pe(lambda: nc.tensor.matmul(yb[hf][:], lhsT=hT[:, cc, tt * 128:(tt + 1) * 128], rhs=w[:, cc, hf * 512:(hf + 1) * 512],
                                                          start=(cc == 0), stop=False), r=[hT, w], w=[yb[hf]])
                        K.pe(lambda: nc.tensor.matmul(yb[hf][:], lhsT=self.ones[0:1, :], rhs=brow[0:1, hf * 512:(hf + 1) * 512],
                                                      start=False, stop=True), r=[self.ones, brow], w=[yb[hf]])
                    self.epilogue(e, t, yb)
            K.barrier()


def host_consts():
    bf = ml_dtypes.bfloat16
    p = np.arange(128)[:, None]
    y = np.arange(512)[None, :]
    c = {}
    c["c_ident"] = np.eye(128, dtype=np.float32).astype(bf)
    jj = np.arange(128)[:, None]
    ss = np.arange(128)[None, :]
    c["c_tri"] = (jj >= ss).astype(np.float32).astype(bf)
    c["c_ones"] = np.ones((128, 128), np.float32).astype(bf)
    c["c_sbmask"] = (y > p).astype(np.float32).astype(bf)
    c["c_sbbias"] = np.where(y > p, 0.0, BIG).astype(np.float32).astype(bf)
    c["c_onesf"] = np.ones((128, 128), np.float32)
    perm = np.zeros((64, 64), np.float32)
    for i in range(8):
        perm[i + 8, i] = -1.0
        perm[i, i + 8] = 1.0
    c["c_perm"] = perm.astype(bf)
    invf = np.zeros((64, 1), np.float32)
    fr = (500000.0 ** (-np.arange(8, dtype=np.float32) / 8.0)).astype(np.float32)
    invf[0:8, 0] = fr
    invf[8:16, 0] = fr
    c["c_invf"] = invf
    jj = np.arange(64)[:, None, None]
    ii = np.arange(32)[None, :, None]
    sk = np.arange(128)[None, None, :]
    c["c_esel"] = (jj == 2 * ii + (sk >= 64)).astype(np.float32).reshape(64, 32 * 128).astype(bf)
    e = np.arange(8)[None, :, None]
    f = np.arange(512)[None, None, :]
    pp = np.arange(128)[:, None, None]
    dlt = f - pp + 512 - 128 * e
    c["c_winb"] = np.where((dlt >= 0) & (dlt < 512), 0.0, -BIG).astype(np.float32).reshape(128, 8 * 512).astype(bf)
    c["c_causb"] = np.where(y >= p, 0.0, -BIG).astype(np.float32).astype(bf)
    m = np.arange(512)[None, :] - 256
    c["c_band"] = np.where(p >= 16 * m + 31, 0.0, -BIG).astype(np.float32).astype(bf)
    col = np.arange(128)[None, :]
    dl = (col - 64) - (p >= 64)
    c["c_wcm"] = (dl < -1).astype(np.float32)
    c["c_wfb"] = np.where(dl > 0, -1.0e9, np.where(dl >= -1, 1.0e4, 0.0)).astype(np.float32)
    c["c_anyok"] = (np.arange(128)[:, None] >= 31).astype(np.float32)
    return c


def make_in_map(inputs, b, prog):
    m = {}
    m["x"] = np.ascontiguousarray(inputs["x"][b])
    m["cT"] = np.ascontiguousarray(np.asarray(inputs["c"][b]).reshape(8, 128).T)
    m["pos"] = np.ascontiguousarray(np.asarray(inputs["positions"][b]).reshape(1, S).astype(np.int32))
    for n in ("ada_w", "ada_b", "mix_pre_g", "mix_post_g", "ffn_pre_g", "ffn_post_g", "ffn_w1", "ffn_w2", "sb_w_in", "sb_w_out",
              "nsa_w_in", "nsa_w_out", "nsa_w1_k", "nsa_w2_k", "nsa_w1_v", "nsa_w2_v",
              "cv_w_in", "cv_w_out", "cv_b_out"):
        m[n] = np.asarray(inputs[n])
    m["nsa_pe_kT"] = np.ascontiguousarray(np.asarray(inputs["nsa_pe_k"])[0].T)
    m["nsa_pe_vT"] = np.ascontiguousarray(np.asarray(inputs["nsa_pe_v"])[0].T)
    m["cv_b_inT"] = np.ascontiguousarray(np.asarray(inputs["cv_b_in"]).reshape(16, 128).T)
    m["cv_dwT"] = np.ascontiguousarray(np.asarray(inputs["cv_dw"]).reshape(31, 8, 128).transpose(2, 1, 0))
    m["cv_vecT"] = np.ascontiguousarray(np.stack([np.asarray(inputs[k]).reshape(8, 128).T for k in ("cv_dw_b", "cv_ln_g", "cv_ln_b")], axis=1))
    m.update(host_consts())
    return {k: v for k, v in m.items() if k in prog.in_names}


_PROG_CACHE = {}


def kernel(**inputs):
    prog = Prog()
    nc = prog.build()
    B = inputs["x"].shape[0]
    in_maps = [make_in_map(inputs, b, prog) for b in range(B)]
    res = run_bass_kernel_spmd(nc, in_maps, core_ids=list(range(B)))
    return np.stack([np.asarray(r["out"]) for r in res.results], axis=0).astype(np.float32)
```

```python
import contextlib
import os
import numpy as np
import ml_dtypes
import concourse.bass as bass
import concourse.mybir as mybir
from concourse.bass_utils import run_bass_kernel_spmd

F32 = mybir.dt.float32
BF16 = mybir.dt.bfloat16
I32 = mybir.dt.int32
AF = mybir.ActivationFunctionType
ALU = mybir.AluOpType
AX = mybir.AxisListType

D = 1024
S = 4096
DEPTH = 4
NH = 16
DH = 64
DFF = 4096
EPS = 1e-6
NT = S // 128
NG = S // 512
NSA_IN = 2608
NCMP = 255
BIG = 30000.0
NSA_STOP = os.environ.get("NSA_STOP", "")
N1_PARTS = os.environ.get("N1_PARTS", "urvg")
ROPE_MODE = os.environ.get("ROPE_MODE", "")


class Src:
    __slots__ = ("sem", "count", "name")

    def __init__(self, sem, name):
        self.sem = sem
        self.count = 0
        self.name = name


class Buf:
    __slots__ = ("name", "w", "r", "const", "excl")

    def __init__(self, name, const=False):
        self.name = name
        self.excl = False
        self.w = []
        self.r = []
        self.const = const


class T:
    __slots__ = ("t", "b")

    def __init__(self, t, name):
        self.t = t
        self.b = Buf(name)

    def __getitem__(self, idx):
        return self.t[idx]


class KB:
    def __init__(self):
        self.nc = bass.Bass("TRN2", target_bir_lowering=False)
        nc = self.nc
        self.st = contextlib.ExitStack()
        self.eng = {"pe": nc.tensor, "act": nc.scalar, "dve": nc.vector, "pool": nc.gpsimd, "sp": nc.sync}
        self.src = {}
        for n in self.eng:
            self.src[n] = Src(self.st.enter_context(nc.semaphore("sem_" + n)), n)
        self.dslots = {}
        self.dnext = {}
        for q, k in (("sp", 12), ("pool", 6), ("act", 4)):
            self.dslots[q] = [Src(self.st.enter_context(nc.semaphore(f"dq_{q}{i}")), f"dq_{q}{i}") for i in range(k)]
            self.dnext[q] = 0
        self.seen = {n: {} for n in self.eng}
        self.ninstr = 0

    def _wait(self, en, tok):
        src, val = tok
        if val <= 0:
            return
        if en == "pe" and src is self.src["pe"]:
            return
        seen = self.seen[en]
        if seen.get(src, 0) >= val:
            return
        self.eng[en].wait_ge(src.sem, val)
        seen[src] = val
        self.ninstr += 1

    def _deps(self, en, r, w, wa=()):
        for x in r:
            b = x.b if isinstance(x, T) else x
            for tok in b.w:
                self._wait(en, tok)
            if b.excl:
                own = self.src.get(en)
                for tok in b.r:
                    if tok[0] is not own:
                        self._wait(en, tok)
        for x in w:
            b = x.b if isinstance(x, T) else x
            for tok in b.w:
                self._wait(en, tok)
            for tok in b.r:
                self._wait(en, tok)
        for x in wa:
            b = x.b if isinstance(x, T) else x
            for tok in b.r:
                self._wait(en, tok)

    def _mark(self, tok, r, w, wa=()):
        for x in r:
            b = x.b if isinstance(x, T) else x
            if not b.const:
                b.r.append(tok)
        for x in w:
            b = x.b if isinstance(x, T) else x
            b.w = [tok]
            b.r = []
        for x in wa:
            b = x.b if isinstance(x, T) else x
            if b.r:
                b.w = []
                b.r = []
            b.w.append(tok)

    def op(self, en, fn, r=(), w=()):
        self._deps(en, r, w)
        ins = fn()
        s = self.src[en]
        s.count += 1
        ins.then_inc(s.sem, 1)
        self._mark((s, s.count), r, w)
        self.ninstr += 1
        return ins

    def pe(self, fn, r=(), w=()):
        return self.op("pe", fn, r, w)

    def act(self, fn, r=(), w=()):
        return self.op("act", fn, r, w)

    def dve(self, fn, r=(), w=()):
        return self.op("dve", fn, r, w)

    def pool(self, fn, r=(), w=()):
        return self.op("pool", fn, r, w)

    def dma(self, q, out, in_, r=(), w=(), wa=()):
        self._deps(q, r, w, wa)
        slots = self.dslots[q]
        sl = slots[self.dnext[q] % len(slots)]
        self.dnext[q] += 1
        self._wait(q, (sl, sl.count))
        ins = self.eng[q].dma_start(out=out, in_=in_)
        sl.count += 16
        ins.then_inc(sl.sem, 16)
        self._mark((sl, sl.count), r, w, wa)
        self.ninstr += 1
        return ins

    def barrier(self):
        toks = [(s, s.count) for s in self.src.values()]
        for q in self.dslots:
            toks += [(s, s.count) for s in self.dslots[q]]
        for en in self.eng:
            for tok in toks:
                if tok[0] is self.src[en]:
                    continue
                self._wait(en, tok)

    def _uniq(self, name):
        self.nalloc = getattr(self, "nalloc", 0) + 1
        return f"{name}_{self.nalloc}"

    def sb(self, ctx, name, shape, dtype):
        name = self._uniq("s_" + name)
        return T(ctx.enter_context(self.nc.sbuf_tensor(name, list(shape), dtype)), name)

    def ps(self, ctx, name, shape=(128, 512), dtype=F32):
        name = self._uniq("p_" + name)
        t = T(ctx.enter_context(self.nc.psum_tensor(name, list(shape), dtype)), name)
        t.b.excl = True
        return t

    def dram(self, name, shape, dtype, kind="Internal"):
        return self.nc.dram_tensor(name, list(shape), dtype, kind=kind).ap()


class Prog:
    def __init__(self, layers=(0, 1, 2, 3), debug=None):
        self.K = KB()
        self.nc = self.K.nc
        self.layers = tuple(layers)
        self.debug = debug or {}
        self.in_names = []
        self._declare_io()

    def _in(self, name, shape, dtype=F32):
        self.in_names.append(name)
        return self.K.dram(name, shape, dtype, kind="ExternalInput")

    def _declare_io(self):
        K = self.K
        self.x = self._in("x", [S, D])
        self.cT = self._in("cT", [128, 8])
        self.pos = self._in("pos", [1, S], I32)
        self.ada_w = self._in("ada_w", [DEPTH, D, 6 * D])
        self.ada_b = self._in("ada_b", [DEPTH, 6 * D])
        self.gains = {n: self._in(n, [DEPTH, D]) for n in ("mix_pre_g", "mix_post_g", "ffn_pre_g", "ffn_post_g")}
        self.ffn_w1 = self._in("ffn_w1", [DEPTH, D, DFF])
        self.ffn_w2 = self._in("ffn_w2", [DEPTH, DFF, D])
        self.sb_w_in = self._in("sb_w_in", [2, D, 3 * D])
        self.sb_w_out = self._in("sb_w_out", [2, D, D])
        self.nsa_w_in = self._in("nsa_w_in", [1, D, NSA_IN])
        self.nsa_w_out = self._in("nsa_w_out", [1, D, D])
        self.nsa_peT = {"k": self._in("nsa_pe_kT", [64, 32]), "v": self._in("nsa_pe_vT", [64, 32])}
        self.nsa_w1 = {"k": self._in("nsa_w1_k", [1, 2048, 256]), "v": self._in("nsa_w1_v", [1, 2048, 256])}
        self.nsa_w2 = {"k": self._in("nsa_w2_k", [1, 256, 64]), "v": self._in("nsa_w2_v", [1, 256, 64])}
        self.cv_w_in = self._in("cv_w_in", [1, D, 2 * D])
        self.cv_b_inT = self._in("cv_b_inT", [128, 16])
        self.cv_dwT = self._in("cv_dwT", [128, 8, 31])
        self.cv_vecT = self._in("cv_vecT", [128, 3, 8])
        self.cv_w_out = self._in("cv_w_out", [1, D, D])
        self.cv_b_out = self._in("cv_b_out", [1, D])
        self.c_ident = self._in("c_ident", [128, 128], BF16)
        self.c_tri = self._in("c_tri", [128, 128], BF16)
        self.c_ones = self._in("c_ones", [128, 128], BF16)
        self.c_sbmask = self._in("c_sbmask", [128, 512], BF16)
        self.c_sbbias = self._in("c_sbbias", [128, 512], BF16)
        self.c_onesf = self._in("c_onesf", [128, 128], F32)
        self.c_perm = self._in("c_perm", [64, 64], BF16)
        self.c_invf = self._in("c_invf", [64, 1], F32)
        self.c_esel = self._in("c_esel", [64, 32 * 128], BF16)
        self.c_winb = self._in("c_winb", [128, 8 * 512], BF16)
        self.c_causb = self._in("c_causb", [128, 512], BF16)
        self.c_band = self._in("c_band", [128, 512], BF16)
        self.c_wcm = self._in("c_wcm", [128, 128], F32)
        self.c_wfb = self._in("c_wfb", [128, 128], F32)
        self.c_anyok = self._in("c_anyok", [128, 1], F32)
        self.out = K.dram("out", [S, D], F32, kind="ExternalOutput")
        self.modv = K.dram("modv", [DEPTH, 6 * D], F32)
        self.UT = K.dram("UT", [D, S], BF16)
        self.QT = K.dram("QT", [D, S], BF16)
        self.KT = K.dram("KT", [D, S], BF16)
        self.Vd = K.dram("Vd", [S, D], BF16)
        self.HG = K.dram("HG", [D, S], BF16)
        self.NQ = K.dram("NQ", [16, 64, S], BF16)
        self.NK = K.dram("NK", [8, 64, S], BF16)
        self.NC = K.dram("NC", [8, 64, S], BF16)
        self.NV = K.dram("NV", [S, 2 * 4 * 65], BF16)
        self.NGt = K.dram("NGt", [S, 48], F32)
        self.b_nq = Buf("nq"); self.b_nk = Buf("nk"); self.b_ncr = Buf("ncr"); self.b_nv = Buf("nv"); self.b_ngt = Buf("ngt")
        self.b_h = [Buf(f"h{t}") for t in range(NT)]
        self.b_ut = [Buf(f"ut{g}") for g in range(NG)]
        self.b_modv = [Buf(f"modv{i}") for i in range(DEPTH)]
        self.b_qt = [Buf(f"qt{g}") for g in range(NG)]
        self.b_kt = [Buf(f"kt{g}") for g in range(NG)]
        self.b_vd = [Buf(f"vd{t}") for t in range(NT)]
        self.b_hg = [Buf(f"hg{g}") for g in range(NG)]
        self.cbuf = Buf("const", const=True)
        for dbg_name, shape in self.debug.items():
            setattr(self, "dbg_" + dbg_name, K.dram("dbg_" + dbg_name, shape, F32, kind="ExternalOutput"))

    def load_consts(self):
        K, nc = self.K, self.nc
        st = K.st
        self.ident = K.sb(st, "ident", [128, 128], BF16)
        self.tri = K.sb(st, "tri", [128, 128], BF16)
        self.ones = K.sb(st, "ones", [128, 128], BF16)
        for t, src in ((self.ident, self.c_ident), (self.tri, self.c_tri), (self.ones, self.c_ones)):
            K.dma("sp", t[:], src, r=[self.cbuf], w=[t])
            t.b.const = True

    def phase_mod(self):
        K, nc = self.K, self.nc
        with contextlib.ExitStack() as ph:
            cT = K.sb(ph, "cT", [128, 8], F32)
            sig = K.sb(ph, "sig", [128, 8], F32)
            cond = K.sb(ph, "cond", [128, 8], F32)
            wsl = [K.sb(ph, f"adaw{i}", [128, 8, 512], F32) for i in range(2)]
            brow = K.sb(ph, "brow", [1, 6 * D], F32)
            mrow = K.sb(ph, "mrow", [1, 6 * D], F32)
            pss = [K.ps(ph, f"modps{i}") for i in range(2)]
            K.dma("sp", cT[:], self.cT, r=[self.cbuf], w=[cT])
            K.act(lambda: nc.scalar.activation(out=sig[:], in_=cT[:], func=AF.Sigmoid), r=[cT], w=[sig])
            K.dve(lambda: nc.vector.tensor_tensor(out=cond[:], in0=cT[:], in1=sig[:], op=ALU.mult), r=[cT, sig], w=[cond])
            it = 0
            for i in self.layers:
                K.dma("sp", brow[:], self.ada_b[i:i + 1, :], r=[self.cbuf], w=[brow])
                wv = self.ada_w[i].rearrange("(kc p) n -> p kc n", p=128)
                for n in range(12):
                    wt = wsl[it % 2]
                    ps = pss[it % 2]
                    it += 1
                    K.dma("sp", wt[:], wv[:, :, n * 512:(n + 1) * 512], r=[self.cbuf], w=[wt])
                    for kc in range(8):
                        K.pe(lambda kc=kc: nc.tensor.matmul(ps[0:1, :], lhsT=cond[:, kc:kc + 1], rhs=wt[:, kc, :],
                                                            start=(kc == 0), stop=(kc == 7)), r=[cond, wt], w=[ps])
                    K.dve(lambda n=n: nc.vector.tensor_tensor(out=mrow[0:1, n * 512:(n + 1) * 512], in0=ps[0:1, :],
                                                              in1=brow[0:1, n * 512:(n + 1) * 512], op=ALU.add),
                          r=[ps, brow], w=[mrow])
                K.dma("sp", self.modv[i:i + 1, :], mrow[:], r=[mrow], w=[self.b_modv[i]])
            K.barrier()

    def load_vec(self, ph, name, src_row):
        t = self.K.sb(ph, name, [128, D], F32)
        return t

    def mod_slice(self, i, k):
        return self.modv[i:i + 1, k * D:(k + 1) * D]

    def bcast_load(self, t, src_row, rbufs):
        self.K.dma("sp", t[:], src_row.to_broadcast([128, D]), r=rbufs, w=[t])

    def phase_norm(self, i, which, src_is_x):
        K, nc = self.K, self.nc
        gname = "mix_pre_g" if which == 0 else "ffn_pre_g"
        ksh, ksc = (0, 1) if which == 0 else (3, 4)
        src = self.x if src_is_x else self.out
        with contextlib.ExitStack() as ph:
            A = K.sb(ph, "nA", [128, D], F32)
            B = K.sb(ph, "nB", [128, D], F32)
            Gn = K.sb(ph, "nG", [128, D], F32)
            self.bcast_load(A, self.mod_slice(i, ksc), [self.b_modv[i]])
            self.bcast_load(B, self.mod_slice(i, ksh), [self.b_modv[i]])
            self.bcast_load(Gn, self.gains[gname][i:i + 1, :], [self.cbuf])
            K.dve(lambda: nc.vector.scalar_tensor_tensor(out=A[:], in0=A[:], scalar=1.0, in1=Gn[:], op0=ALU.add, op1=ALU.mult),
                  r=[A, Gn], w=[A])
            hs = [K.sb(ph, f"nh{j}", [128, D], F32) for j in range(3)]
            junk = K.sb(ph, "njunk", [128, D], BF16)
            tmp = [K.sb(ph, f"ntmp{j}", [128, D], F32) for j in range(2)]
            ub = [K.sb(ph, f"nub{j}", [128, D], BF16) for j in range(3)]
            st = [K.sb(ph, f"nst{j}", [128, 4], F32) for j in range(2)]
            utg = [K.sb(ph, f"nutg{j}", [128, 8, 512], BF16) for j in range(2)]
            pst = [K.ps(ph, f"npst{j}", (128, 1024), BF16) for j in range(2)]
            UTv = self.UT.rearrange("(kc p) t -> p kc t", p=128)

            def load(t):
                K.dma("sp", hs[t % 3][:], src[t * 128:(t + 1) * 128, :], r=[self.b_h[t]], w=[hs[t % 3]])

            load(0)
            load(1)
            items = []
            for t in range(NT):
                def s0(t=t):
                    if t + 2 < NT:
                        load(t + 2)
                    h, s_, tm, u = hs[t % 3], st[t % 2], tmp[t % 2], ub[t % 3]
                    K.act(lambda: nc.scalar.activation(out=junk[:], in_=h[:], func=AF.Square, accum_out=s_[:, 0:1]),
                          r=[h], w=[junk, s_])
                    K.act(lambda: nc.scalar.activation(out=s_[:, 1:2], in_=s_[:, 0:1], func=AF.Sqrt, scale=1.0 / D, bias=EPS),
                          r=[s_], w=[s_])
                    K.dve(lambda: nc.vector.reciprocal(out=s_[:, 2:3], in_=s_[:, 1:2]), r=[s_], w=[s_])
                    K.dve(lambda: nc.vector.scalar_tensor_tensor(out=tm[:], in0=h[:], scalar=s_[:, 2:3], in1=A[:], op0=ALU.mult, op1=ALU.mult),
                          r=[h, s_, A], w=[tm])
                    K.pool(lambda: nc.gpsimd.tensor_tensor(out=u[:], in0=tm[:], in1=B[:], op=ALU.add), r=[tm, B], w=[u])

                def s1(t=t):
                    u, pt = ub[t % 3], pst[t % 2]
                    g = t // 4
                    ug = utg[g % 2]
                    for kc in range(8):
                        K.pe(lambda: nc.tensor.transpose(out=pt[:, kc * 128:(kc + 1) * 128], in_=u[:, kc * 128:(kc + 1) * 128],
                                                         identity=self.ident[:]), r=[u, self.ident], w=[pt])
                    tt = t % 4
                    K.act(lambda: nc.scalar.copy(out=ug[:, :, tt * 128:(tt + 1) * 128],
                                                 in_=pt[:].rearrange("p (kc t) -> p kc t", kc=8)), r=[pt], w=[ug])
                    if tt == 3:
                        K.dma("sp", UTv[:, :, g * 512:(g + 1) * 512], ug[:], r=[ug], w=[self.b_ut[g]])
                items.append([s0, s1])
            run_pipeline(items)
            K.barrier()

    def epilogue_setup(self, ph, i, which):
        K, nc = self.K, self.nc
        gname = "mix_post_g" if which == 0 else "ffn_post_g"
        kg = 2 if which == 0 else 5
        G = K.sb(ph, "eG", [128, D], F32)
        Gn = K.sb(ph, "eGn", [128, D], F32)
        self.bcast_load(G, self.mod_slice(i, kg), [self.b_modv[i]])
        self.bcast_load(Gn, self.gains[gname][i:i + 1, :], [self.cbuf])
        K.dve(lambda: nc.vector.tensor_tensor(out=G[:], in0=G[:], in1=Gn[:], op=ALU.mult), r=[G, Gn], w=[G])
        e = {
            "G": G,
            "h": [K.sb(ph, f"eh{j}", [128, D], F32) for j in range(2)],
            "tmp": [K.sb(ph, f"etmp{j}", [128, 512], F32) for j in range(2)],
            "junk": K.sb(ph, "ejunk", [128, 512], BF16),
            "st": [K.sb(ph, f"est{j}", [128, 8], F32) for j in range(2)],
            "n": 0,
            "src_is_x": False,
        }
        return e

    def epi_load(self, e, t):
        src = self.x if self.resid_from_x else self.out
        h = e["h"][t % 2]
        self.K.dma("sp", h[:], src[t * 128:(t + 1) * 128, :], r=[self.b_h[t]], w=[h])

    def epilogue(self, e, t, ybanks):
        K, nc = self.K, self.nc
        h = e["h"][t % 2]
        s_ = e["st"][t % 2]
        junk = e["junk"]
        G = e["G"]
        for hf in range(2):
            K.act(lambda hf=hf: nc.scalar.activation(out=junk[:], in_=ybanks[hf][:], func=AF.Square, accum_out=s_[:, hf:hf + 1]),
                  r=[ybanks[hf]], w=[junk, s_])
        K.dve(lambda: nc.vector.tensor_tensor(out=s_[:, 2:3], in0=s_[:, 0:1], in1=s_[:, 1:2], op=ALU.add), r=[s_], w=[s_])
        K.act(lambda: nc.scalar.activation(out=s_[:, 3:4], in_=s_[:, 2:3], func=AF.Sqrt, scale=1.0 / D, bias=EPS),
              r=[s_], w=[s_])
        K.dve(lambda: nc.vector.reciprocal(out=s_[:, 4:5], in_=s_[:, 3:4]), r=[s_], w=[s_])
        for hf in range(2):
            tm = e["tmp"][hf]
            K.dve(lambda hf=hf, tm=tm: nc.vector.scalar_tensor_tensor(out=tm[:], in0=ybanks[hf][:], scalar=s_[:, 4:5],
                                                                      in1=G[:, hf * 512:(hf + 1) * 512], op0=ALU.mult, op1=ALU.mult),
                  r=[ybanks[hf], s_, G], w=[tm])
            K.pool(lambda hf=hf, tm=tm: nc.gpsimd.tensor_tensor(out=h[:, hf * 512:(hf + 1) * 512], in0=h[:, hf * 512:(hf + 1) * 512],
                                                                 in1=tm[:], op=ALU.add), r=[tm, h], w=[h])
        K.dma("sp", self.out[t * 128:(t + 1) * 128, :], h[:], r=[h], w=[self.b_h[t]])

    def load_w(self, wt, src, kchunks, ncols, col0=0, split=2048):
        K = self.K
        sv = src.rearrange("(kc p) n -> p kc n", p=128)
        for kc in range(kchunks):
            for c0 in range(0, ncols, split):
                c1 = min(ncols, c0 + split)
                K.dma("pool", wt[:, kc, c0:c1], sv[:, kc, col0 + c0:col0 + c1], r=[self.cbuf], wa=[wt])

    def phase_ffn(self, i):
        K, nc = self.K, self.nc
        with contextlib.ExitStack() as ph:
            w1 = K.sb(ph, "fw1", [128, 8, DFF], BF16)
            w2 = K.sb(ph, "fw2", [128, 32, D], BF16)
            self.load_w(w1, self.ffn_w1[i], 8, DFF)
            self.load_w(w2, self.ffn_w2[i], 32, D, split=1024)
            e = self.epilogue_setup(ph, i, 1)
            utg = [K.sb(ph, f"futg{j}", [128, 8, 512], BF16) for j in range(2)]
            hid = K.sb(ph, "fhid", [128, 32, 512], BF16)
            rl = [K.sb(ph, f"frl{j}", [128, 512], F32) for j in range(2)]
            psA = [K.ps(ph, f"fpsA{j}") for j in range(2)]
            psY = [[K.ps(ph, f"fpsY{j}{hf}") for hf in range(2)] for j in range(2)]
            UTv = self.UT.rearrange("(kc p) t -> p kc t", p=128)

            def load_u(g):
                K.dma("sp", utg[g % 2][:], UTv[:, :, g * 512:(g + 1) * 512], r=[self.b_ut[g]], w=[utg[g % 2]])

            load_u(0)
            for g in range(NG):
                if g + 1 < NG:
                    load_u(g + 1)
                ug = utg[g % 2]
                for fc in range(32):
                    ps = psA[fc % 2]
                    r_ = rl[fc % 2]
                    for kc in range(8):
                        K.pe(lambda kc=kc, fc=fc, ps=ps: nc.tensor.matmul(ps[:], lhsT=w1[:, kc, fc * 128:(fc + 1) * 128], rhs=ug[:, kc, :],
                                                                          start=(kc == 0), stop=(kc == 7)), r=[w1, ug], w=[ps])
                    K.act(lambda ps=ps, r_=r_: nc.scalar.activation(out=r_[:], in_=ps[:], func=AF.Relu), r=[ps], w=[r_])
                    K.pool(lambda fc=fc, r_=r_: nc.gpsimd.tensor_tensor(out=hid[:, fc, :], in0=r_[:], in1=r_[:], op=ALU.mult),
                           r=[r_], w=[hid])
                for tt in range(4):
                    t = g * 4 + tt
                    self.epi_load(e, t)
                    yb = psY[tt % 2]
                    for hf in range(2):
                        for fc in range(32):
                            K.pe(lambda fc=fc, hf=hf, tt=tt: nc.tensor.matmul(yb[hf][:], lhsT=hid[:, fc, tt * 128:(tt + 1) * 128],
                                                                             rhs=w2[:, fc, hf * 512:(hf + 1) * 512],
                                                                             start=(fc == 0), stop=(fc == 31)), r=[hid, w2], w=[yb[hf]])
                    self.epilogue(e, t, yb)
            K.barrier()

    def phase_sb_proj(self, j):
        K, nc = self.K, self.nc
        with contextlib.ExitStack() as ph:
            w = K.sb(ph, "sw", [128, 8, 3 * D], BF16)
            self.load_w(w, self.sb_w_in[j], 8, 3 * D, split=3072)
            utg = [K.sb(ph, f"sutg{k}", [128, 8, 512], BF16) for k in range(2)]
            stg = [K.sb(ph, f"sstg{k}", [128, 512], BF16) for k in range(4)]
            vst = [K.sb(ph, f"svst{k}", [128, D], BF16) for k in range(2)]
            pss = [K.ps(ph, f"sps{k}") for k in range(4)]
            UTv = self.UT.rearrange("(kc p) t -> p kc t", p=128)

            def load_u(g):
                K.dma("sp", utg[g % 2][:], UTv[:, :, g * 512:(g + 1) * 512], r=[self.b_ut[g]], w=[utg[g % 2]])

            load_u(0)
            n = 0
            for g in range(NG):
                if g + 1 < NG:
                    load_u(g + 1)
                ug = utg[g % 2]
                for fcn in range(16):
                    ps = pss[n % 4]
                    sg = stg[n % 4]
                    for kc in range(8):
                        K.pe(lambda kc=kc, fcn=fcn, ps=ps: nc.tensor.matmul(ps[:], lhsT=w[:, kc, fcn * 128:(fcn + 1) * 128], rhs=ug[:, kc, :],
                                                                           start=(kc == 0), stop=(kc == 7)), r=[w, ug], w=[ps])
                    if n % 2 == 0:
                        K.act(lambda ps=ps, sg=sg: nc.scalar.copy(out=sg[:], in_=ps[:]), r=[ps], w=[sg])
                    else:
                        K.dve(lambda ps=ps, sg=sg: nc.vector.tensor_copy(out=sg[:], in_=ps[:]), r=[ps], w=[sg])
                    dst = self.QT if fcn < 8 else self.KT
                    fo = (fcn % 8) * 128
                    bb = self.b_qt[g] if fcn < 8 else self.b_kt[g]
                    K.dma("sp", dst[fo:fo + 128, g * 512:(g + 1) * 512], sg[:], r=[sg], wa=[bb])
                    n += 1
                for tt in range(4):
                    t = g * 4 + tt
                    vs = vst[t % 2]
                    for hf in range(2):
                        ps = pss[n % 4]
                        n += 1
                        for kc in range(8):
                            K.pe(lambda kc=kc, hf=hf, tt=tt, ps=ps: nc.tensor.matmul(ps[:], lhsT=ug[:, kc, tt * 128:(tt + 1) * 128],
                                                                                    rhs=w[:, kc, 2 * D + hf * 512:2 * D + (hf + 1) * 512],
                                                                                    start=(kc == 0), stop=(kc == 7)), r=[w, ug], w=[ps])
                        if hf == 0:
                            K.act(lambda ps=ps, vs=vs: nc.scalar.copy(out=vs[:, 0:512], in_=ps[:]), r=[ps], w=[vs])
                        else:
                            K.dve(lambda ps=ps, vs=vs: nc.vector.tensor_copy(out=vs[:, 512:1024], in_=ps[:]), r=[ps], w=[vs])
                    K.dma("sp", self.Vd[t * 128:(t + 1) * 128, :], vs[:], r=[vs], w=[self.b_vd[t]])
            K.barrier()

    def phase_sb_core(self, i, j):
        K, nc = self.K, self.nc
        with contextlib.ExitStack() as ph:
            O = K.sb(ph, "aO", [128, NT, D], BF16)
            with contextlib.ExitStack() as ph2:
                V = K.sb(ph2, "aV", [128, NT, D], BF16)
                Vv = self.Vd.rearrange("(t p) f -> p t f", p=128)
                for t0 in range(0, NT, 8):
                    K.dma("sp", V[:, t0:t0 + 8, :], Vv[:, t0:t0 + 8, :], r=self.b_vd[t0:t0 + 8], wa=[V])
                msk = K.sb(ph2, "amsk", [128, 512], BF16)
                mbias = K.sb(ph2, "ambias", [128, 512], BF16)
                K.dma("sp", msk[:], self.c_sbmask, r=[self.cbuf], w=[msk])
                K.dma("sp", mbias[:], self.c_sbbias, r=[self.cbuf], w=[mbias])
                qh = [K.sb(ph2, f"aq{k}", [64, S], BF16) for k in range(2)]
                kh = [K.sb(ph2, f"ak{k}", [64, S], BF16) for k in range(2)]
                Et = [K.sb(ph2, f"aE{k}", [128, 512], F32) for k in range(3)]
                Xt = [K.sb(ph2, f"aX{k}", [128, 512], F32) for k in range(2)]
                Lt = [K.sb(ph2, f"aL{k}", [128, 512], BF16) for k in range(3)]
                Lr = [K.sb(ph2, f"aLr{k}", [128, 512], BF16) for k in range(2)]
                At = [K.sb(ph2, f"aA{k}", [128, 512], BF16) for k in range(4)]
                Rt = [K.sb(ph2, f"aR{k}", [128, 512], BF16) for k in range(2)]
                zps = [K.ps(ph2, f"azps{k}") for k in range(2)]
                cps = [K.ps(ph2, f"acps{k}") for k in range(2)]
                avs = [K.ps(ph2, f"aav{k}") for k in range(2)]

                Rq = [K.sb(ph2, f"aRq{k}", [128, 512], BF16) for k in range(2)]

                def load_head(h):
                    s = h % 2
                    K.dma("sp", qh[s][:], self.QT[h * 64:(h + 1) * 64, :], r=self.b_qt, w=[qh[s]])
                    K.dma("sp", kh[s][:], self.KT[h * 64:(h + 1) * 64, :], r=self.b_kt, w=[kh[s]])

                units = []
                ci = 0
                for h in range(NH):
                    for qc in range(NG):
                        nkt = 4 * qc + 4
                        for k_, i_ in enumerate(range(nkt - 1, -1, -1)):
                            units.append(dict(h=h, qc=qc, i=i_, k=k_, nkt=nkt, ci=ci, idx=len(units),
                                              lasth=(qc == NG - 1 and i_ == 0)))
                        ci += 1
                Rall = [[Rt[0], Rt[1]], [Rq[0], Rq[1]]]

                def geom(U):
                    a = U["i"] - 4 * U["qc"]
                    diag = a >= 0
                    c0 = 128 * a if diag else 0
                    return a, diag, c0, 512 - c0

                def stageA(U):
                    u, h, qc, i_ = U["idx"], U["h"], U["qc"], U["i"]
                    a, diag, c0, W = geom(U)
                    q_, k_ = qh[h % 2], kh[h % 2]
                    zp, E, L = zps[u % 2], Et[u % 3], Lt[u % 3]
                    qs = q_[:, qc * 512 + c0:(qc + 1) * 512]
                    K.pe(lambda: nc.tensor.matmul(zp[:, 0:W], lhsT=k_[:, i_ * 128:(i_ + 1) * 128], rhs=qs, start=True, stop=True),
                         r=[q_, k_], w=[zp])
                    K.act(lambda: nc.scalar.activation(out=E[:, 0:W], in_=zp[:, 0:W], func=AF.Exp, scale=0.125), r=[zp], w=[E])
                    if diag:
                        Lraw = Lr[u % 2]
                        K.act(lambda: nc.scalar.activation(out=Lraw[:, 0:W], in_=E[:, 0:W], func=AF.Ln, bias=1.0), r=[E], w=[Lraw])
                        K.dve(lambda: nc.vector.tensor_tensor(out=L[:, 0:W], in0=Lraw[:, 0:W], in1=msk[:, 0:W], op=ALU.mult),
                              r=[Lraw, msk], w=[L])
                    else:
                        K.act(lambda: nc.scalar.activation(out=L[:, 0:W], in_=E[:, 0:W], func=AF.Ln, bias=1.0), r=[E], w=[L])

                def stageB(U):
                    u, h, qc, i_, k = U["idx"], U["h"], U["qc"], U["i"], U["k"]
                    a, diag, c0, W = geom(U)
                    cp, L, A, E, X = cps[u % 2], Lt[u % 3], At[u % 4], Et[u % 3], Xt[u % 2]
                    Rp = Rall[U["ci"] % 2]
                    Rc, Rn = Rp[k % 2], Rp[(k + 1) % 2]
                    if k == 0:
                        K.pool(lambda: nc.gpsimd.memset(Rp[0][:], 0.0), w=[Rp[0]])
                        K.pool(lambda: nc.gpsimd.memset(Rp[1][:], 0.0), w=[Rp[1]])
                    K.pe(lambda: nc.tensor.matmul(cp[:, 0:W], lhsT=self.tri[:], rhs=L[:, 0:W], start=True, stop=(k == 0 and not diag)),
                         r=[self.tri, L], w=[cp])
                    if k != 0:
                        K.pe(lambda: nc.tensor.matmul(cp[:, 0:W], lhsT=self.ones[:], rhs=Rc[:, c0:512], start=False, stop=(not diag)),
                             r=[self.ones, Rc], w=[cp])
                    if diag:
                        K.pe(lambda: nc.tensor.matmul(cp[:, 0:W], lhsT=self.ident[:], rhs=mbias[:, 0:W], start=False, stop=True),
                             r=[self.ident, mbias], w=[cp])
                    if i_ != 0:
                        K.dve(lambda: nc.vector.tensor_tensor(out=Rn[:, c0:512], in0=Rc[:, c0:512], in1=L[:, 0:W], op=ALU.add),
                              r=[Rc, L], w=[Rn])
                    K.act(lambda: nc.scalar.activation(out=X[:, 0:W], in_=cp[:, 0:W], func=AF.Exp, scale=-1.0), r=[cp], w=[X])
                    K.dve(lambda: nc.vector.tensor_tensor(out=A[:, 0:W], in0=E[:, 0:W], in1=X[:, 0:W], op=ALU.mult), r=[E, X], w=[A])

                def stageC(U):
                    u, h, qc, i_, k = U["idx"], U["h"], U["qc"], U["i"], U["k"]
                    a, diag, c0, W = geom(U)
                    A = At[u % 4]
                    av = avs[U["ci"] % 2]
                    for n_, c in enumerate(range(a if diag else 0, 4)):
                        K.pe(lambda: nc.tensor.matmul(av[:, c * 64:(c + 1) * 64], lhsT=A[:, c * 128 - c0:(c + 1) * 128 - c0],
                                                      rhs=V[:, i_, h * 64:(h + 1) * 64], start=(k == 0 and n_ == 0), stop=False,
                                                      skip_group_check=True), r=[A, V], w=[av])
                    if i_ == 0:
                        K.dve(lambda: nc.vector.tensor_copy(out=O[:, qc * 4:(qc + 1) * 4, h * 64:(h + 1) * 64],
                                                            in_=av[:, 0:256].rearrange("p (c d) -> p c d", c=4)), r=[av], w=[O])
                    if U["lasth"] and h + 2 < NH:
                        load_head(h + 2)

                load_head(0)
                load_head(1)
                n = len(units)
                for kk in range(n + 3):
                    if kk < n:
                        stageA(units[kk])
                    if 0 <= kk - 1 < n:
                        stageB(units[kk - 1])
                    if 0 <= kk - 3 < n:
                        stageC(units[kk - 3])
                K.barrier()
            self.out_proj(ph, i, self.sb_w_out[j], O, None)
            K.barrier()

    def out_proj(self, ph, i, w_dram, O, bias_row):
        K, nc = self.K, self.nc
        with contextlib.ExitStack() as ph3:
            w = K.sb(ph3, "ow", [128, 8, D], BF16)
            self.load_w(w, w_dram, 8, D, split=1024)
            e = self.epilogue_setup(ph3, i, 0)
            oT = [K.sb(ph3, f"ooT{k}", [128, 8, 128], BF16) for k in range(2)]
            ptr = [K.ps(ph3, f"optr{k}", (128, 1024), BF16) for k in range(2)]
            psY = [[K.ps(ph3, f"opsY{k}{hf}") for hf in range(2)] for k in range(2)]
            brow = None
            if bias_row is not None:
                brow = K.sb(ph3, "obrow", [1, D], BF16)
                K.dma("pool", brow[:], bias_row, r=[self.cbuf], w=[brow])
            for t in range(NT):
                self.epi_load(e, t)
                pt = ptr[t % 2]
                ot = oT[t % 2]
                for kc in range(8):
                    K.pe(lambda kc=kc: nc.tensor.transpose(out=pt[:, kc * 128:(kc + 1) * 128], in_=O[:, t, kc * 128:(kc + 1) * 128],
                                                          identity=self.ident[:]), r=[O, self.ident], w=[pt])
                K.act(lambda: nc.scalar.copy(out=ot[:], in_=pt[:].rearrange("p (kc t) -> p kc t", kc=8)), r=[pt], w=[ot])
                yb = psY[t % 2]
                for hf in range(2):
                    for kc in range(8):
                        K.pe(lambda kc=kc, hf=hf: nc.tensor.matmul(yb[hf][:], lhsT=ot[:, kc, :], rhs=w[:, kc, hf * 512:(hf + 1) * 512],
                                                                   start=(kc == 0), stop=(kc == 7 and brow is None)), r=[ot, w], w=[yb[hf]])
                    if brow is not None:
                        K.pe(lambda hf=hf: nc.tensor.matmul(yb[hf][:], lhsT=self.ones[0:1, :], rhs=brow[0:1, hf * 512:(hf + 1) * 512],
                                                            start=False, stop=True), r=[self.ones, brow], w=[yb[hf]])
                self.epilogue(e, t, yb)

    def build(self):
        K = self.K
        self.load_consts()
        self.phase_mod()
        first = True
        for i in self.layers:
            kind, j = i % 3, i // 3
            self.phase_norm(i, 0, src_is_x=first)
            self.resid_from_x = first
            first = False
            if kind == 0:
                self.phase_sb_proj(j)
                self.phase_sb_core(i, j)
            elif kind == 1:
                self.phase_nsa(i, j)
            else:
                self.phase_conv(i, j)
            self.resid_from_x = False
            self.phase_norm(i, 1, src_is_x=False)
            self.phase_ffn(i)
        K.barrier()
        K.st.close()
        return self.nc

    def copy_x_to_out(self):
        K = self.K
        with contextlib.ExitStack() as ph:
            bufs = [K.sb(ph, f"cx{k}", [128, 4, D], F32) for k in range(2)]
            xv = self.x.rearrange("(g c p) f -> g p c f", p=128, c=4)
            ov = self.out.rearrange("(g c p) f -> g p c f", p=128, c=4)
            for g in range(NG):
                b = bufs[g % 2]
                K.dma("sp", b[:], xv[g], r=[], w=[b])
                K.dma("sp", ov[g], b[:], r=[b], w=[self.b_h[4 * g + c] for c in range(4)])
            K.barrier()


    def phase_nsa(self, i, j):
        K, nc = self.K, self.nc
        UTv = self.UT.rearrange("(kc p) t -> p kc t", p=128)
        TWO_PI = 2.0 * np.pi
        with contextlib.ExitStack() as nsa:
            kcT = K.sb(nsa, "n_kcT", [64, 4, 256], BF16)
            vcmp = K.sb(nsa, "n_vcmp", [128, 2, 4, 64], BF16)
            cscmp = K.sb(nsa, "n_cscmp", [64, 2, 256], F32)
            perm = K.sb(nsa, "n_perm", [64, 64], BF16)
            K.dma("sp", perm[:], self.c_perm, r=[self.cbuf], w=[perm])
            with contextlib.ExitStack() as ph:
                w = K.sb(ph, "nw", [128, 8, NSA_IN], BF16)
                self.load_w(w, self.nsa_w_in[j], 8, NSA_IN, split=NSA_IN)
                cosT = K.sb(ph, "ncos", [64, S], F32)
                sinT = K.sb(ph, "nsin", [64, S], F32)
                with contextlib.ExitStack() as ph0:
                    posi = K.sb(ph0, "nposi", [64, S], I32)
                    ang = K.sb(ph0, "nang", [64, S], F32)
                    t1 = K.sb(ph0, "nt1", [64, S], F32)
                    t2 = K.sb(ph0, "nt2", [64, S], F32)
                    ki = K.sb(ph0, "nki", [64, S], I32)
                    invf = K.sb(ph0, "ninvf", [64, 1], F32)
                    K.dma("sp", posi[:], self.pos.to_broadcast([64, S]), r=[self.cbuf], w=[posi])
                    K.dma("sp", invf[:], self.c_invf, r=[self.cbuf], w=[invf])
                    K.dve(lambda: nc.vector.tensor_copy(out=ang[:], in_=posi[:]), r=[posi], w=[ang])
                    K.dve(lambda: nc.vector.tensor_scalar(out=ang[:], in0=ang[:], scalar1=invf[:, 0:1], scalar2=None, op0=ALU.mult),
                          r=[ang, invf], w=[ang])
                    for tab, shift in ((sinT, 0.0), (cosT, 0.5 * np.pi)):
                        K.dve(lambda: nc.vector.tensor_scalar(out=t1[:], in0=ang[:], scalar1=shift, scalar2=1.0 / TWO_PI,
                                                              op0=ALU.add, op1=ALU.mult), r=[ang], w=[t1])
                        K.dve(lambda: nc.vector.tensor_copy(out=ki[:], in_=t1[:]), r=[t1], w=[ki])
                        K.dve(lambda: nc.vector.tensor_copy(out=t1[:], in_=ki[:]), r=[ki], w=[t1])
                        K.dve(lambda: nc.vector.scalar_tensor_tensor(out=t2[:], in0=t1[:], scalar=-TWO_PI, in1=ang[:], op0=ALU.mult, op1=ALU.add),
                              r=[t1, ang], w=[t2])
                        K.dve(lambda: nc.vector.tensor_scalar(out=t2[:], in0=t2[:], scalar1=shift, scalar2=None, op0=ALU.add), r=[t2], w=[t2])
                        K.dve(lambda: nc.vector.tensor_scalar(out=t1[:], in0=t2[:], scalar1=np.pi, scalar2=-TWO_PI, op0=ALU.is_gt, op1=ALU.mult),
                              r=[t2], w=[t1])
                        K.dve(lambda: nc.vector.tensor_tensor(out=t2[:], in0=t2[:], in1=t1[:], op=ALU.add), r=[t2, t1], w=[t2])
                        K.dve(lambda: nc.vector.tensor_scalar(out=t1[:], in0=t2[:], scalar1=-np.pi, scalar2=TWO_PI, op0=ALU.is_lt, op1=ALU.mult),
                              r=[t2], w=[t1])
                        K.dve(lambda: nc.vector.tensor_tensor(out=t2[:], in0=t2[:], in1=t1[:], op=ALU.add), r=[t2, t1], w=[t2])
                        K.dve(lambda: nc.vector.tensor_scalar(out=t2[:], in0=t2[:], scalar1=-3.1415925, scalar2=3.1415925, op0=ALU.max, op1=ALU.min),
                              r=[t2], w=[t2])
                        K.act(lambda: nc.scalar.activation(out=tab[:], in_=t2[:], func=AF.Sin), r=[t2], w=[tab])
                    K.dve(lambda: nc.vector.tensor_copy(out=cscmp[:, 0, 0:255], in_=cosT[:, 31:S:16]), r=[cosT], w=[cscmp])
                    K.dve(lambda: nc.vector.tensor_copy(out=cscmp[:, 1, 0:255], in_=sinT[:, 31:S:16]), r=[sinT, cscmp], w=[cscmp])
                    K.barrier()
                if NSA_STOP == "n0":
                    return
                utg = [K.sb(ph, f"nutg{k}", [128, 8, 512], BF16) for k in range(2)]
                xb = [K.sb(ph, f"nxb{k}", [64, 512], BF16) for k in range(2)]
                r1 = [K.sb(ph, f"nr1{k}", [64, 512], F32) for k in range(2)]
                r2 = [K.sb(ph, f"nr2{k}", [64, 512], F32) for k in range(2)]
                ob = [K.sb(ph, f"nob{k}", [64, 512], BF16) for k in range(4)]
                va = [K.sb(ph, f"nva{k}", [128, 8, 65], BF16) for k in range(2)]
                gt = [K.sb(ph, f"ngt{k}", [128, 48], F32) for k in range(2)]
                for v_ in va:
                    K.pool(lambda: nc.gpsimd.memset(v_[:], 1.0), w=[v_])
                pp = [K.ps(ph, f"npp{k}") for k in range(3)]
                pr = [K.ps(ph, f"npr{k}") for k in range(2)]
                pv = [K.ps(ph, f"npv{k}") for k in range(2)]
                pg = K.ps(ph, "npg")

                def load_u(g):
                    K.dma("sp", utg[g % 2][:], UTv[:, :, g * 512:(g + 1) * 512], r=[self.b_ut[g]], w=[utg[g % 2]])

                units = []
                for h in range(16):
                    units.append((h * 64, self.NQ, h, True, self.b_nq))
                for g4 in range(4):
                    units.append((D + 2 * 256 + g4 * 64, self.NK, g4, True, self.b_nk))
                for g4 in range(4):
                    units.append((D + 4 * 256 + g4 * 64, self.NK, 4 + g4, True, self.b_nk))
                for g4 in range(4):
                    units.append((D + 0 * 256 + g4 * 64, self.NC, g4, False, self.b_ncr))
                for g4 in range(4):
                    units.append((D + 1 * 256 + g4 * 64, self.NC, 4 + g4, False, self.b_ncr))
                load_u(0)
                n = 0
                nr = 0
                for g in range(NG):
                    if g + 1 < NG:
                        load_u(g + 1)
                    ug = utg[g % 2]
                    tsl = slice(g * 512, (g + 1) * 512)
                    for (col, dst, ui, rope, bb) in units:
                        if (rope and "r" not in N1_PARTS) or ((not rope) and "u" not in N1_PARTS):
                            continue
                        ps = pp[n % 3]
                        o_ = ob[n % 4]
                        n += 1
                        for kc in range(8):
                            K.pe(lambda: nc.tensor.matmul(ps[0:64, :], lhsT=w[:, kc, col:col + 64], rhs=ug[:, kc, :],
                                                          start=(kc == 0), stop=(kc == 7)), r=[w, ug], w=[ps])
                        if rope and "asu" not in ROPE_MODE:
                            x_, a_, b_, p2 = xb[nr % 2], r1[nr % 2], r2[nr % 2], pr[nr % 2]
                            nr += 1
                            K.act(lambda: nc.scalar.copy(out=x_[:], in_=ps[0:64, :]), r=[ps], w=[x_])
                            if "noperm" not in ROPE_MODE:
                                K.pe(lambda: nc.tensor.matmul(p2[0:64, :], lhsT=perm[:], rhs=x_[:], start=True, stop=True), r=[perm, x_], w=[p2])
                            K.dve(lambda: nc.vector.tensor_tensor(out=a_[:], in0=ps[0:64, :], in1=cosT[:, tsl], op=ALU.mult), r=[ps, cosT], w=[a_])
                            if "noperm" not in ROPE_MODE:
                                K.dve(lambda: nc.vector.tensor_tensor(out=b_[:], in0=p2[0:64, :], in1=sinT[:, tsl], op=ALU.mult), r=[p2, sinT], w=[b_])
                            else:
                                K.dve(lambda: nc.vector.tensor_tensor(out=b_[:], in0=ps[0:64, :], in1=sinT[:, tsl], op=ALU.mult), r=[ps, sinT], w=[b_])
                            if "dveadd" in ROPE_MODE:
                                K.dve(lambda: nc.vector.tensor_tensor(out=o_[:], in0=a_[:], in1=b_[:], op=ALU.add), r=[a_, b_], w=[o_])
                            else:
                                K.pool(lambda: nc.gpsimd.tensor_tensor(out=o_[:], in0=a_[:], in1=b_[:], op=ALU.add), r=[a_, b_], w=[o_])
                        else:
                            K.act(lambda: nc.scalar.copy(out=o_[:], in_=ps[0:64, :]), r=[ps], w=[o_])
                        K.dma("sp", dst[ui, :, tsl], o_[:], r=[o_], wa=[bb])
                    for tt in range(4):
                        t = g * 4 + tt
                        v_ = va[t % 2]
                        pv_ = pv[t % 2]
                        for m, c0 in ((0, D + 3 * 256), (1, D + 5 * 256)) if "v" in N1_PARTS else ():
                            for kc in range(8):
                                K.pe(lambda: nc.tensor.matmul(pv_[:, m * 256:(m + 1) * 256], lhsT=ug[:, kc, tt * 128:(tt + 1) * 128],
                                                              rhs=w[:, kc, c0:c0 + 256], start=(kc == 0), stop=(kc == 7)), r=[w, ug], w=[pv_])
                        if "v" in N1_PARTS:
                            K.dve(lambda: nc.vector.tensor_copy(out=v_[:, :, 0:64], in_=pv_[:].rearrange("p (u d) -> p u d", d=64)), r=[pv_], w=[v_])
                            K.dma("sp", self.NV[t * 128:(t + 1) * 128, :], v_[:].rearrange("p u d -> p (u d)"), r=[v_], wa=[self.b_nv])
                        g_ = gt[t % 2]
                        if "g" not in N1_PARTS:
                            continue
                        for kc in range(8):
                            K.pe(lambda: nc.tensor.matmul(pg[:, 0:48], lhsT=ug[:, kc, tt * 128:(tt + 1) * 128], rhs=w[:, kc, 2560:2608],
                                                          start=(kc == 0), stop=(kc == 7)), r=[w, ug], w=[pg])
                        K.act(lambda: nc.scalar.activation(out=g_[:], in_=pg[:, 0:48], func=AF.Sigmoid), r=[pg], w=[g_])
                        K.dma("sp", self.NGt[t * 128:(t + 1) * 128, :], g_[:], r=[g_], wa=[self.b_ngt])
                K.barrier()
            if NSA_STOP == "n1":
                return
            with contextlib.ExitStack() as ph:
                raw = K.sb(ph, "craw", [64, 8, S], BF16)
                for u_ in range(8):
                    K.dma("sp", raw[:, u_, :], self.NC[u_], r=[self.b_ncr], wa=[raw])
                K.pool(lambda: nc.gpsimd.memset(vcmp[:], 0.0), w=[vcmp])
                K.pool(lambda: nc.gpsimd.memset(kcT[:], 0.0), w=[kcT])
                hps = [K.ps(ph, f"chps{k}") for k in range(2)]
                bps = K.ps(ph, "cbps")
                ops_ = [K.ps(ph, f"cops{k}") for k in range(2)]
                p2 = K.ps(ph, "cp2")
                for kv in ("k", "v"):
                    w1 = K.sb(ph, "cw1" + kv, [64, 32, 256], BF16)
                    w1v = self.nsa_w1[kv][j].rearrange("(l d) h -> d l h", d=64)
                    for l0 in range(0, 32, 8):
                        K.dma("pool", w1[:, l0:l0 + 8, :], w1v[:, l0:l0 + 8, :], r=[self.cbuf], wa=[w1])
                    w2 = K.sb(ph, "cw2" + kv, [128, 2, 64], BF16)
                    K.dma("pool", w2[:], self.nsa_w2[kv][j].rearrange("(hc p) d -> p hc d", p=128), r=[self.cbuf], w=[w2])
                    peT = K.sb(ph, "cpeT" + kv, [64, 32], F32)
                    peTb = K.sb(ph, "cpeTb" + kv, [64, 32], BF16)
                    K.dma("sp", peT[:], self.nsa_peT[kv], r=[self.cbuf], w=[peT])
                    K.dve(lambda: nc.vector.tensor_copy(out=peTb[:], in_=peT[:]), r=[peT], w=[peTb])
                    bias = K.sb(ph, "cbias" + kv, [128, 2], F32)
                    for hc in range(2):
                        for l in range(32):
                            K.pe(lambda: nc.tensor.matmul(bps[:, hc:hc + 1], lhsT=w1[:, l, hc * 128:(hc + 1) * 128], rhs=peTb[:, l:l + 1],
                                                          start=(l == 0), stop=(l == 31)), r=[w1, peTb], w=[bps])
                        K.dve(lambda: nc.vector.tensor_copy(out=bias[:, hc:hc + 1], in_=bps[:, hc:hc + 1]), r=[bps], w=[bias])
                    xb_ = K.sb(ph, "cxb" + kv, [128, 256], F32)
                    x2_ = K.sb(ph, "cx2" + kv, [128, 256], F32)
                    x3_ = K.sb(ph, "cx3" + kv, [128, 256], F32)
                    hidT = K.sb(ph, "chid" + kv, [128, 2, 256], BF16)
                    kx = K.sb(ph, "ckx" + kv, [64, 256], BF16)
                    ka = K.sb(ph, "cka" + kv, [64, 256], F32)
                    kb_ = K.sb(ph, "ckb" + kv, [64, 256], F32)
                    for g4 in range(4):
                        ui = g4 if kv == "k" else 4 + g4
                        for hc in range(2):
                            hp = hps[hc]
                            for l in range(32):
                                K.pe(lambda: nc.tensor.matmul(hp[:, 0:255], lhsT=w1[:, l, hc * 128:(hc + 1) * 128],
                                                              rhs=raw[:, ui, l:l + 16 * 254 + 1:16], start=(l == 0), stop=(l == 31)),
                                     r=[w1, raw], w=[hp])
                            K.dve(lambda: nc.vector.tensor_scalar(out=xb_[:, 0:255], in0=hp[:, 0:255], scalar1=bias[:, hc:hc + 1], scalar2=None,
                                                                  op0=ALU.add), r=[hp, bias], w=[xb_])
                            K.pool(lambda: nc.gpsimd.tensor_tensor(out=x2_[:, 0:255], in0=xb_[:, 0:255], in1=xb_[:, 0:255], op=ALU.mult), r=[xb_], w=[x2_])
                            K.dve(lambda: nc.vector.tensor_scalar(out=x2_[:, 0:255], in0=x2_[:, 0:255], scalar1=0.044715, scalar2=1.0,
                                                                  op0=ALU.mult, op1=ALU.add), r=[x2_], w=[x2_])
                            K.dve(lambda: nc.vector.tensor_tensor(out=x3_[:, 0:255], in0=x2_[:, 0:255], in1=xb_[:, 0:255], op=ALU.mult), r=[x2_, xb_], w=[x3_])
                            K.act(lambda: nc.scalar.activation(out=x3_[:, 0:255], in_=x3_[:, 0:255], func=AF.Tanh, scale=0.7978845608028654),
                                  r=[x3_], w=[x3_])
                            K.dve(lambda: nc.vector.scalar_tensor_tensor(out=x2_[:, 0:255], in0=x3_[:, 0:255], scalar=1.0, in1=xb_[:, 0:255],
                                                                         op0=ALU.add, op1=ALU.mult), r=[x3_, xb_], w=[x2_])
                            K.pool(lambda: nc.gpsimd.tensor_scalar(out=hidT[:, hc, 0:255], in0=x2_[:, 0:255], scalar1=0.5, scalar2=None, op0=ALU.mult),
                                   r=[x2_], w=[hidT])
                        if kv == "k":
                            op_ = ops_[0]
                            for hc in range(2):
                                K.pe(lambda: nc.tensor.matmul(op_[0:64, 0:255], lhsT=w2[:, hc, :], rhs=hidT[:, hc, 0:255],
                                                              start=(hc == 0), stop=(hc == 1)), r=[w2, hidT], w=[op_])
                            K.act(lambda: nc.scalar.copy(out=kx[:, 0:255], in_=op_[0:64, 0:255]), r=[op_], w=[kx])
                            K.pe(lambda: nc.tensor.matmul(p2[0:64, 0:255], lhsT=perm[:], rhs=kx[:, 0:255], start=True, stop=True), r=[perm, kx], w=[p2])
                            K.dve(lambda: nc.vector.tensor_tensor(out=ka[:, 0:255], in0=op_[0:64, 0:255], in1=cscmp[:, 0, 0:255], op=ALU.mult),
                                  r=[op_, cscmp], w=[ka])
                            K.dve(lambda: nc.vector.tensor_tensor(out=kb_[:, 0:255], in0=p2[0:64, 0:255], in1=cscmp[:, 1, 0:255], op=ALU.mult),
                                  r=[p2, cscmp], w=[kb_])
                            K.pool(lambda: nc.gpsimd.tensor_tensor(out=kcT[:, g4, 0:255], in0=ka[:, 0:255], in1=kb_[:, 0:255], op=ALU.add),
                                   r=[ka, kb_], w=[kcT])
                        else:
                            for nch, m in ((0, 128), (1, 127)):
                                op_ = ops_[nch]
                                for hc in range(2):
                                    K.pe(lambda: nc.tensor.matmul(op_[0:m, 0:64], lhsT=hidT[:, hc, nch * 128:nch * 128 + m], rhs=w2[:, hc, :],
                                                                  start=(hc == 0), stop=(hc == 1)), r=[w2, hidT], w=[op_])
                                K.act(lambda: nc.scalar.copy(out=vcmp[0:m, nch, g4, :], in_=op_[0:m, 0:64]), r=[op_], w=[vcmp])
                K.barrier()
            if NSA_STOP == "n2":
                return
            with contextlib.ExitStack() as ph:
                O = K.sb(ph, "nO", [128, NT, D], BF16)
                with contextlib.ExitStack() as ph2:
                    self._nsa_attn(ph2, O, kcT, vcmp)
                    K.barrier()
                self.out_proj(ph, i, self.nsa_w_out[j], O, None)
                K.barrier()

    def _nsa_attn(self, ph, O, kcT, vcmp):
        K, nc = self.K, self.nc
        V = K.sb(ph, "tV", [128, NT, 8 * 65], BF16)
        NVv = self.NV.rearrange("(t p) f -> p t f", p=128)
        for t0 in range(0, NT, 8):
            K.dma("sp", V[:, t0:t0 + 8, :], NVv[:, t0:t0 + 8, :], r=[self.b_nv], wa=[V])
        GT = K.sb(ph, "tGT", [128, NT, 48], F32)
        NGv = self.NGt.rearrange("(t p) f -> p t f", p=128)
        for t0 in range(0, NT, 8):
            K.dma("sp", GT[:, t0:t0 + 8, :], NGv[:, t0:t0 + 8, :], r=[self.b_ngt], wa=[GT])
        esel = K.sb(ph, "tesel", [64, 32 * 128], BF16)
        winb = K.sb(ph, "twinb", [128, 8 * 512], BF16)
        causb = K.sb(ph, "tcausb", [128, 512], BF16)
        band = K.sb(ph, "tband", [128, 512], BF16)
        wcm = K.sb(ph, "twcm", [128, 128], F32)
        wfb = K.sb(ph, "twfb", [128, 128], F32)
        anyok = K.sb(ph, "tanyok", [128, 1], F32)
        for t_, src in ((esel, self.c_esel), (winb, self.c_winb), (causb, self.c_causb), (band, self.c_band),
                        (wcm, self.c_wcm), (wfb, self.c_wfb), (anyok, self.c_anyok)):
            K.dma("sp", t_[:], src, r=[self.cbuf], w=[t_])
        ks = K.sb(ph, "tks", [64, S], BF16)
        kw = K.sb(ph, "tkw", [64, S], BF16)
        qh = [K.sb(ph, f"tq{k}", [64, S], BF16) for k in range(4)]
        psg = K.sb(ph, "tpsg", [128, 4, 256], F32)
        NS = 4
        pun = [K.sb(ph, f"tpun{k}", [128, 256], F32) for k in range(NS)]
        pb = [K.sb(ph, f"tpb{k}", [128, 256], BF16) for k in range(NS)]
        pTs = [K.sb(ph, f"tpT{k}", [128, 2, 128], BF16) for k in range(NS)]
        st = [K.sb(ph, f"tst{k}", [128, 8], F32) for k in range(NS)]
        s4 = K.sb(ph, "ts4", [128, 64], F32)
        imp = K.sb(ph, "timp", [128, 64], F32)
        sc = K.sb(ph, "tsc", [128, 64], F32)
        wk = K.sb(ph, "twk", [128, 64], F32)
        m8a = K.sb(ph, "tm8a", [128, 8], F32)
        m8b = K.sb(ph, "tm8b", [128, 8], F32)
        selt = K.sb(ph, "tsel", [128, 64], F32)
        negm = [K.sb(ph, f"tnegm{k}", [128, 64], BF16) for k in range(2)]
        nmT = [K.sb(ph, f"tnmT{k}", [64, 512], BF16) for k in range(2)]
        Pt = [K.sb(ph, f"tP{k}", [128, 512], BF16) for k in range(3)]
        Oq = [K.sb(ph, f"tOq{k}", [128, 4, 256], F32) for k in range(2)]
        cf = [K.sb(ph, f"tcf{k}", [128, 8], F32) for k in range(2)]
        sps = [K.ps(ph, f"tsps{k}") for k in range(2)]
        accs = K.ps(ph, "taccs")
        accw = K.ps(ph, "taccw")
        cpsb = [K.ps(ph, f"tcps{k}") for k in range(2)]
        misc = K.ps(ph, "tmisc", (128, 1024), BF16)
        ocp = K.ps(ph, "tocp")
        items = []
        cnt = {"cmp": 0, "u": 0, "gq": 0}

        def add_loads(g):
            def f():
                K.dma("sp", ks[:], self.NK[g], r=[self.b_nk], w=[ks])
                K.dma("sp", kw[:], self.NK[4 + g], r=[self.b_nk], w=[kw])
                for r in range(4):
                    K.dma("sp", qh[r][:], self.NQ[4 * g + r], r=[self.b_nq], w=[qh[r]])
            items.append([f])

        def add_cmp(g, qc, c, r, gq):
            T_ = 4 * qc + c
            ncols = min(8 * T_ + 7, NCMP)
            b0 = 256 - 8 * T_
            h = 4 * g + r
            q_ = qh[r]
            n_ = cnt["cmp"]
            cnt["cmp"] += 1
            s_, pu, pb_, pT, cps = st[n_ % NS], pun[n_ % NS], pb[n_ % NS], pTs[n_ % NS], cpsb[n_ % 2]
            Oq_ = Oq[gq % 2]
            chunks = [(0, min(128, ncols))] + ([(1, ncols - 128)] if ncols > 128 else [])

            def s0():
                K.pe(lambda: nc.tensor.matmul(cps[:, 0:ncols], lhsT=q_[:, T_ * 128:(T_ + 1) * 128], rhs=kcT[:, g, 0:ncols],
                                              start=True, stop=False), r=[q_, kcT], w=[cps])
                K.pe(lambda: nc.tensor.matmul(cps[:, 0:ncols], lhsT=self.ident[:], rhs=band[:, b0:b0 + ncols],
                                              start=False, stop=True), r=[self.ident, band], w=[cps])
                K.dve(lambda: nc.vector.reduce_max(out=s_[:, 0:1], in_=cps[:, 0:ncols], axis=AX.X), r=[cps], w=[s_])
                K.dve(lambda: nc.vector.tensor_scalar(out=s_[:, 1:2], in0=s_[:, 0:1], scalar1=-0.125, scalar2=None, op0=ALU.mult),
                      r=[s_], w=[s_])
                K.act(lambda: nc.scalar.activation(out=pu[:, 0:ncols], in_=cps[:, 0:ncols], func=AF.Exp, scale=0.125,
                                                   bias=s_[:, 1:2], accum_out=s_[:, 2:3]), r=[cps, s_], w=[pu, s_])
                K.dve(lambda: nc.vector.reciprocal(out=s_[:, 3:4], in_=s_[:, 2:3]), r=[s_], w=[s_])
                if T_ == 0:
                    K.dve(lambda: nc.vector.tensor_tensor(out=s_[:, 3:4], in0=s_[:, 3:4], in1=anyok[:], op=ALU.mult), r=[s_, anyok], w=[s_])
                if r == 0:
                    K.dve(lambda: nc.vector.tensor_scalar(out=psg[:, c, 0:ncols], in0=pu[:, 0:ncols], scalar1=s_[:, 3:4], scalar2=None,
                                                          op0=ALU.mult), r=[pu, s_], w=[psg])
                else:
                    K.dve(lambda: nc.vector.scalar_tensor_tensor(out=psg[:, c, 0:ncols], in0=pu[:, 0:ncols], scalar=s_[:, 3:4],
                                                                 in1=psg[:, c, 0:ncols], op0=ALU.mult, op1=ALU.add), r=[pu, s_, psg], w=[psg])
                K.pool(lambda: nc.gpsimd.tensor_scalar(out=pb_[:, 0:ncols], in0=pu[:, 0:ncols], scalar1=s_[:, 3:4], scalar2=None,
                                                       op0=ALU.mult), r=[pu, s_], w=[pb_])

            def s1():
                for ch, wd in chunks:
                    K.pe(lambda: nc.tensor.transpose(out=misc[0:wd, ch * 128:(ch + 1) * 128], in_=pb_[:, ch * 128:ch * 128 + wd],
                                                     identity=self.ident[:]), r=[pb_, self.ident], w=[misc])
                for ch, wd in chunks:
                    K.act(lambda: nc.scalar.copy(out=pT[0:wd, ch, :], in_=misc[0:wd, ch * 128:(ch + 1) * 128]), r=[misc], w=[pT])

            def s2():
                for k_, (ch, wd) in enumerate(chunks):
                    K.pe(lambda: nc.tensor.matmul(ocp[:, 0:64], lhsT=pT[0:wd, ch, :], rhs=vcmp[0:wd, ch, g, :],
                                                  start=(k_ == 0), stop=(k_ == len(chunks) - 1)), r=[pT, vcmp], w=[ocp])
                K.dve(lambda: nc.vector.tensor_scalar(out=Oq_[:, c, r * 64:(r + 1) * 64], in0=ocp[:, 0:64],
                                                      scalar1=GT[:, T_, 3 * h:3 * h + 1], scalar2=None, op0=ALU.mult),
                      r=[ocp, GT], w=[Oq_])
            items.append([s0, s1, s2])

        def add_select(g, qc, c, gq):
            T_ = 4 * qc + c
            w0 = 64 - 2 * T_
            nm_ = negm[c % 2]

            def f():
                pv4 = psg[:, c, :].rearrange("p (j f) -> p j f", f=4)
                K.dve(lambda: nc.vector.tensor_reduce(out=s4[:], in_=pv4, axis=AX.X, op=ALU.add), r=[psg], w=[s4])
                K.dve(lambda: nc.vector.scalar_tensor_tensor(out=imp[:], in0=pv4[:, :, 3], scalar=-0.5, in1=s4[:], op0=ALU.mult, op1=ALU.add),
                      r=[psg, s4], w=[imp])
                K.dve(lambda: nc.vector.scalar_tensor_tensor(out=imp[:, 1:64], in0=pv4[:, 0:63, 3], scalar=0.5, in1=imp[:, 1:64],
                                                             op0=ALU.mult, op1=ALU.add), r=[psg, imp], w=[imp])
                K.dve(lambda: nc.vector.tensor_tensor(out=sc[:], in0=imp[:], in1=wcm[:, w0:w0 + 64], op=ALU.mult), r=[imp, wcm], w=[sc])
                K.dve(lambda: nc.vector.tensor_tensor(out=sc[:], in0=sc[:], in1=wfb[:, w0:w0 + 64], op=ALU.add), r=[sc, wfb], w=[sc])
                K.dve(lambda: nc.vector.memset(sc[:, 0:1], 1.0e4), r=[sc], w=[sc])
                K.dve(lambda: nc.vector.max(out=m8a[:], in_=sc[:]), r=[sc], w=[m8a])
                K.dve(lambda: nc.vector.match_replace(out=wk[:], in_to_replace=m8a[:], in_values=sc[:], imm_value=-3.0e38), r=[sc, m8a], w=[wk])
                K.dve(lambda: nc.vector.max(out=m8b[:], in_=wk[:]), r=[wk], w=[m8b])
                K.dve(lambda: nc.vector.tensor_scalar(out=selt[:], in0=sc[:], scalar1=m8b[:, 7:8], scalar2=None, op0=ALU.is_ge),
                      r=[sc, m8b], w=[selt])
                K.dve(lambda: nc.vector.tensor_scalar(out=nm_[:], in0=selt[:], scalar1=-1.0, scalar2=BIG, op0=ALU.add, op1=ALU.mult),
                      r=[selt], w=[nm_])
                K.pe(lambda: nc.tensor.transpose(out=misc[0:64, 512 + c * 128:512 + (c + 1) * 128], in_=nm_[:], identity=self.ident[:]),
                     r=[nm_, self.ident], w=[misc])
                if c == 3:
                    K.act(lambda: nc.scalar.copy(out=nmT[gq % 2][:], in_=misc[0:64, 512:1024]), r=[misc], w=[nmT[gq % 2]])
            items.append([f])

        def add_unit(g, qc, r, i_, kind, first, gq):
            a = i_ - 4 * qc
            if kind == "s":
                diag = a >= 0
                c0, c1 = (128 * a if diag else 0), 512
                kt, acc, voff = ks, accs, g * 65
            else:
                e_ = i_ - (4 * qc - 4)
                c0, c1 = (0, 128 * (e_ + 1)) if e_ < 4 else (128 * (e_ - 4), 512)
                kt, acc, voff = kw, accw, (4 + g) * 65
            W = c1 - c0
            q_ = qh[r]
            u = cnt["u"]
            cnt["u"] += 1
            sp_, P = sps[u % 2], Pt[u % 3]
            nm = nmT[gq % 2]

            def s0():
                qs = q_[:, qc * 512 + c0:qc * 512 + c1]
                K.pe(lambda: nc.tensor.matmul(sp_[:, 0:W], lhsT=kt[:, i_ * 128:(i_ + 1) * 128], rhs=qs, start=True, stop=False),
                     r=[kt, q_], w=[sp_])
                if kind == "s":
                    K.pe(lambda: nc.tensor.matmul(sp_[:, 0:W], lhsT=esel[:, i_ * 128:(i_ + 1) * 128], rhs=nm[:, c0:512],
                                                  start=False, stop=(a < 0)), r=[esel, nm], w=[sp_])
                    if a >= 0:
                        K.pe(lambda: nc.tensor.matmul(sp_[:, 0:W], lhsT=self.ident[:], rhs=causb[:, 0:W], start=False, stop=True),
                             r=[self.ident, causb], w=[sp_])
                else:
                    K.pe(lambda: nc.tensor.matmul(sp_[:, 0:W], lhsT=self.ident[:], rhs=winb[:, e_ * 512 + c0:e_ * 512 + c1],
                                                  start=False, stop=True), r=[self.ident, winb], w=[sp_])
                K.act(lambda: nc.scalar.activation(out=P[:, 0:W], in_=sp_[:, 0:W], func=AF.Exp, scale=0.125), r=[sp_], w=[P])

            def s1():
                for n_, c in enumerate(range(c0 // 128, c1 // 128)):
                    K.pe(lambda: nc.tensor.matmul(acc[:, c * 65:(c + 1) * 65], lhsT=P[:, c * 128 - c0:(c + 1) * 128 - c0],
                                                  rhs=V[:, i_, voff:voff + 65], start=(first and n_ == 0), stop=False, skip_group_check=True),
                         r=[P, V], w=[acc])
            items.append([s0, s1])

        def add_combine(g, qc, r, gq):
            h = 4 * g + r
            Oq_ = Oq[gq % 2]

            def f():
                for bi, acc in ((1, accs), (2, accw)):
                    cf_ = cf[bi - 1]
                    av = acc[:, 0:260].rearrange("p (c d) -> p c d", d=65)
                    K.dve(lambda: nc.vector.reciprocal(out=cf_[:, 0:4], in_=av[:, :, 64]), r=[acc], w=[cf_])
                    K.dve(lambda: nc.vector.tensor_tensor(out=cf_[:, 4:8], in0=cf_[:, 0:4], in1=GT[:, 4 * qc:4 * qc + 4, 3 * h + bi], op=ALU.mult),
                          r=[cf_, GT], w=[cf_])
                    for c in range(4):
                        K.dve(lambda: nc.vector.scalar_tensor_tensor(out=Oq_[:, c, r * 64:(r + 1) * 64], in0=av[:, c, 0:64], scalar=cf_[:, 4 + c:5 + c],
                                                                     in1=Oq_[:, c, r * 64:(r + 1) * 64], op0=ALU.mult, op1=ALU.add),
                              r=[acc, cf_, Oq_], w=[Oq_])
                if r == 3:
                    K.pool(lambda: nc.gpsimd.tensor_copy(out=O[:, 4 * qc:4 * qc + 4, g * 256:(g + 1) * 256], in_=Oq_[:]), r=[Oq_], w=[O])
            items.append([None, f])

        for g in range(4):
            add_loads(g)
            for qc in range(NG):
                gq = cnt["gq"]
                cnt["gq"] += 1
                items.append([lambda: K.pool(lambda: nc.gpsimd.memset(psg[:], 0.0), w=[psg])])
                for c in range(4):
                    for r in range(4):
                        add_cmp(g, qc, c, r, gq)
                    add_select(g, qc, c, gq)
                for r in range(4):
                    first = True
                    for i_ in range(0, 4 * qc + 4):
                        add_unit(g, qc, r, i_, "s", first, gq)
                        first = False
                    first = True
                    for i_ in range(max(0, 4 * qc - 4), 4 * qc + 4):
                        add_unit(g, qc, r, i_, "w", first, gq)
                        first = False
                    add_combine(g, qc, r, gq)
        run_pipeline(items)


    def phase_conv(self, i, j):
        K, nc = self.K, self.nc
        UTv = self.UT.rearrange("(kc p) t -> p kc t", p=128)
        with contextlib.ExitStack() as ph:
            w = K.sb(ph, "cw", [128, 8, 2 * D], BF16)
            self.load_w(w, self.cv_w_in[j], 8, 2 * D)
            bcol = K.sb(ph, "cbcol", [128, 16], F32)
            K.dma("sp", bcol[:], self.cv_b_inT, r=[self.cbuf], w=[bcol])
            utg = [K.sb(ph, f"cutg{k}", [128, 8, 512], BF16) for k in range(2)]
            sgt = [K.sb(ph, f"csg{k}", [128, 512], F32) for k in range(2)]
            hgt = [K.sb(ph, f"chg{k}", [128, 512], BF16) for k in range(3)]
            psa = [K.ps(ph, f"cpsa{k}") for k in range(2)]
            psg = [K.ps(ph, f"cpsg{k}") for k in range(2)]

            def load_u(g):
                K.dma("sp", utg[g % 2][:], UTv[:, :, g * 512:(g + 1) * 512], r=[self.b_ut[g]], w=[utg[g % 2]])

            load_u(0)
            n = 0
            for g in range(NG):
                if g + 1 < NG:
                    load_u(g + 1)
                ug = utg[g % 2]
                for cc in range(8):
                    pa, pg = psa[n % 2], psg[n % 2]
                    sg, hg = sgt[n % 2], hgt[n % 3]
                    n += 1
                    for kc in range(8):
                        K.pe(lambda: nc.tensor.matmul(pa[:], lhsT=w[:, kc, cc * 128:(cc + 1) * 128], rhs=ug[:, kc, :],
                                                      start=(kc == 0), stop=(kc == 7)), r=[w, ug], w=[pa])
                    for kc in range(8):
                        K.pe(lambda: nc.tensor.matmul(pg[:], lhsT=w[:, kc, D + cc * 128:D + (cc + 1) * 128], rhs=ug[:, kc, :],
                                                      start=(kc == 0), stop=(kc == 7)), r=[w, ug], w=[pg])
                    K.act(lambda: nc.scalar.activation(out=sg[:], in_=pg[:], func=AF.Sigmoid, bias=bcol[:, 8 + cc:9 + cc]),
                          r=[pg, bcol], w=[sg])
                    K.dve(lambda: nc.vector.scalar_tensor_tensor(out=hg[:], in0=pa[:], scalar=bcol[:, cc:cc + 1], in1=sg[:],
                                                                 op0=ALU.add, op1=ALU.mult), r=[pa, bcol, sg], w=[hg])
                    K.dma("sp", self.HG[cc * 128:(cc + 1) * 128, g * 512:(g + 1) * 512], hg[:], r=[hg], wa=[self.b_hg[g]])
            K.barrier()
        with contextlib.ExitStack() as ph:
            w = K.sb(ph, "cow", [128, 8, D], BF16)
            self.load_w(w, self.cv_w_out[j], 8, D, split=1024)
            brow = K.sb(ph, "cobrow", [1, D], BF16)
            K.dma("pool", brow[:], self.cv_b_out[j:j + 1, :], r=[self.cbuf], w=[brow])
            dw = K.sb(ph, "cdw", [128, 8, 31], F32)
            vec = K.sb(ph, "cvec", [128, 3, 8], F32)
            onesf = K.sb(ph, "conesf", [128, 128], F32)
            K.dma("sp", dw[:], self.cv_dwT, r=[self.cbuf], w=[dw])
            K.dma("sp", vec[:], self.cv_vecT, r=[self.cbuf], w=[vec])
            K.dma("sp", onesf[:], self.c_onesf, r=[self.cbuf], w=[onesf])
            e = self.epilogue_setup(ph, i, 0)
            xin = [K.sb(ph, f"cxin{k}", [128, 8, 542], BF16) for k in range(2)]
            Dg = K.sb(ph, "cDg", [128, 8, 31, 128], BF16)
            for cc in range(8):
                for k in range(31):
                    if (cc * 31 + k) % 3 == 2:
                        K.pool(lambda: nc.gpsimd.tensor_scalar(out=Dg[:, cc, k, :], in0=self.ident[:], scalar1=dw[:, cc, k:k + 1], scalar2=None,
                                                               op0=ALU.mult), r=[self.ident, dw], w=[Dg])
                    else:
                        K.dve(lambda: nc.vector.tensor_scalar(out=Dg[:, cc, k, :], in0=self.ident[:], scalar1=dw[:, cc, k:k + 1], scalar2=None,
                                                              op0=ALU.mult), r=[self.ident, dw], w=[Dg])
            cvp = [K.ps(ph, f"ccvp{k}") for k in range(2)]
            acc = K.sb(ph, "cacc", [128, 8, 512], F32)
            sq = [K.sb(ph, f"csq{k}", [128, 512], F32) for k in range(2)]
            mt = K.sb(ph, "cm", [128, 512], F32)
            msq = K.sb(ph, "cmsq", [128, 512], F32)
            var = K.sb(ph, "cvar", [128, 512], F32)
            rstd = K.sb(ph, "crstd", [128, 512], F32)
            dt_ = [K.sb(ph, f"cd{k}", [128, 512], F32) for k in range(2)]
            xh = [K.sb(ph, f"cxh{k}", [128, 512], F32) for k in range(2)]
            hT = K.sb(ph, "chT", [128, 8, 512], BF16)
            s1 = K.ps(ph, "cs1")
            s2 = K.ps(ph, "cs2")
            psY = [[K.ps(ph, f"cpsY{k}{hf}") for hf in range(2)] for k in range(2)]
            HGv = self.HG.rearrange("(cc p) t -> p cc t", p=128)

            def load_x(g):
                xt = xin[g % 2]
                if g == 0:
                    K.pool(lambda: nc.gpsimd.memset(xt[:, :, 0:30], 0.0), w=[xt])
                    K.dma("sp", xt[:, :, 30:542], HGv[:, :, 0:512], r=[self.b_hg[0]], wa=[xt])
                else:
                    K.dma("sp", xt[:], HGv[:, :, g * 512 - 30:g * 512 + 512], r=[self.b_hg[g - 1], self.b_hg[g]], w=[xt])

            load_x(0)
            for g in range(NG):
                if g + 1 < NG:
                    load_x(g + 1)
                xt = xin[g % 2]
                for cc in range(8):
                    cv = cvp[cc % 2]
                    for k in range(31):
                        K.pe(lambda: nc.tensor.matmul(cv[:], lhsT=Dg[:, cc, k, :], rhs=xt[:, cc, k:k + 512], start=(k == 0), stop=(k == 30)),
                             r=[Dg, xt], w=[cv])
                    sq_ = sq[cc % 2]
                    K.act(lambda: nc.scalar.activation(out=sq_[:], in_=cv[:], func=AF.Square, bias=vec[:, 0, cc:cc + 1]), r=[cv, vec], w=[sq_])
                    K.dve(lambda: nc.vector.tensor_scalar(out=acc[:, cc, :], in0=cv[:], scalar1=vec[:, 0, cc:cc + 1], scalar2=None, op0=ALU.add),
                          r=[cv, vec], w=[acc])
                    K.pe(lambda: nc.tensor.matmul(s1[:], lhsT=onesf[:], rhs=acc[:, cc, :], start=(cc == 0), stop=(cc == 7)),
                         r=[onesf, acc], w=[s1])
                    K.pe(lambda: nc.tensor.matmul(s2[:], lhsT=onesf[:], rhs=sq_[:], start=(cc == 0), stop=(cc == 7)),
                         r=[onesf, sq_], w=[s2])
                K.dve(lambda: nc.vector.tensor_scalar(out=mt[:], in0=s1[:], scalar1=1.0 / D, scalar2=None, op0=ALU.mult), r=[s1], w=[mt])
                K.pool(lambda: nc.gpsimd.tensor_tensor(out=msq[:], in0=mt[:], in1=mt[:], op=ALU.mult), r=[mt], w=[msq])
                K.dve(lambda: nc.vector.scalar_tensor_tensor(out=var[:], in0=s2[:], scalar=1.0 / D, in1=msq[:], op0=ALU.mult, op1=ALU.subtract),
                      r=[s2, msq], w=[var])
                K.act(lambda: nc.scalar.activation(out=var[:], in_=var[:], func=AF.Sqrt, bias=EPS), r=[var], w=[var])
                K.dve(lambda: nc.vector.reciprocal(out=rstd[:], in_=var[:]), r=[var], w=[rstd])
                for cc in range(8):
                    d_, x_ = dt_[cc % 2], xh[cc % 2]
                    K.pool(lambda: nc.gpsimd.tensor_tensor(out=d_[:], in0=acc[:, cc, :], in1=mt[:], op=ALU.subtract), r=[acc, mt], w=[d_])
                    K.dve(lambda: nc.vector.tensor_tensor(out=x_[:], in0=d_[:], in1=rstd[:], op=ALU.mult), r=[d_, rstd], w=[x_])
                    K.act(lambda: nc.scalar.activation(out=hT[:, cc, :], in_=x_[:], func=AF.Silu, scale=vec[:, 1, cc:cc + 1],
                                                       bias=vec[:, 2, cc:cc + 1]), r=[x_, vec], w=[hT])
                for tt in range(4):
                    t = g * 4 + tt
                    self.epi_load(e, t)
                    yb = psY[tt % 2]
                    for hf in range(2):
                        for cc in range(8):
                            K.pe(lambda: nc.tensor.matmul(yb[hf][:], lhsT=hT[:, cc, tt * 128:(tt + 1) * 128], rhs=w[:, cc, hf * 512:(hf + 1) * 512],
                                                          start=(cc == 0), stop=False), r=[hT, w], w=[yb[hf]])
                        K.pe(lambda: nc.tensor.matmul(yb[hf][:], lhsT=self.ones[0:1, :], rhs=brow[0:1, hf * 512:(hf + 1) * 512],
                                                      start=False, stop=True), r=[self.ones, brow], w=[yb[hf]])
                    self.epilogue(e, t, yb)
            K.barrier()


def run_pipeline(items):
    n = len(items)
    depth = max(len(it) for it in items)
    for t in range(n + depth - 1):
        for j in range(depth):
            k = t - j
            if 0 <= k < n and j < len(items[k]) and items[k][j] is not None:
                items[k][j]()


def host_consts():
    bf = ml_dtypes.bfloat16
    p = np.arange(128)[:, None]
    y = np.arange(512)[None, :]
    c = {}
    c["c_ident"] = np.eye(128, dtype=np.float32).astype(bf)
    jj = np.arange(128)[:, None]
    ss = np.arange(128)[None, :]
    c["c_tri"] = (jj >= ss).astype(np.float32).astype(bf)
    c["c_ones"] = np.ones((128, 128), np.float32).astype(bf)
    c["c_sbmask"] = (y > p).astype(np.float32).astype(bf)
    c["c_sbbias"] = np.where(y > p, 0.0, BIG).astype(np.float32).astype(bf)
    c["c_onesf"] = np.ones((128, 128), np.float32)
    perm = np.zeros((64, 64), np.float32)
    for i in range(8):
        perm[i + 8, i] = -1.0
        perm[i, i + 8] = 1.0
    c["c_perm"] = perm.astype(bf)
    invf = np.zeros((64, 1), np.float32)
    fr = (500000.0 ** (-np.arange(8, dtype=np.float32) / 8.0)).astype(np.float32)
    invf[0:8, 0] = fr
    invf[8:16, 0] = fr
    c["c_invf"] = invf
    jj = np.arange(64)[:, None, None]
    ii = np.arange(32)[None, :, None]
    sk = np.arange(128)[None, None, :]
    c["c_esel"] = (jj == 2 * ii + (sk >= 64)).astype(np.float32).reshape(64, 32 * 128).astype(bf)
    e = np.arange(8)[None, :, None]
    f = np.arange(512)[None, None, :]
    pp = np.arange(128)[:, None, None]
    dlt = f - pp + 512 - 128 * e
    c["c_winb"] = np.where((dlt >= 0) & (dlt < 512), 0.0, -BIG).astype(np.float32).reshape(128, 8 * 512).astype(bf)
    c["c_causb"] = np.where(y >= p, 0.0, -BIG).astype(np.float32).astype(bf)
    m = np.arange(512)[None, :] - 256
    c["c_band"] = np.where(p >= 16 * m + 31, 0.0, -BIG).astype(np.float32).astype(bf)
    col = np.arange(128)[None, :]
    dl = (col - 64) - (p >= 64)
    c["c_wcm"] = (dl < -1).astype(np.float32)
    c["c_wfb"] = np.where(dl > 0, -1.0e9, np.where(dl >= -1, 1.0e4, 0.0)).astype(np.float32)
    c["c_anyok"] = (np.arange(128)[:, None] >= 31).astype(np.float32)
    return c


def make_in_map(inputs, b, prog):
    m = {}
    m["x"] = np.ascontiguousarray(inputs["x"][b])
    m["cT"] = np.ascontiguousarray(np.asarray(inputs["c"][b]).reshape(8, 128).T)
    m["pos"] = np.ascontiguousarray(np.asarray(inputs["positions"][b]).reshape(1, S).astype(np.int32))
    for n in ("ada_w", "ada_b", "mix_pre_g", "mix_post_g", "ffn_pre_g", "ffn_post_g", "ffn_w1", "ffn_w2", "sb_w_in", "sb_w_out",
              "nsa_w_in", "nsa_w_out", "nsa_w1_k", "nsa_w2_k", "nsa_w1_v", "nsa_w2_v",
              "cv_w_in", "cv_w_out", "cv_b_out"):
        m[n] = np.asarray(inputs[n])
    m["nsa_pe_kT"] = np.ascontiguousarray(np.asarray(inputs["nsa_pe_k"])[0].T)
    m["nsa_pe_vT"] = np.ascontiguousarray(np.asarray(inputs["nsa_pe_v"])[0].T)
    m["cv_b_inT"] = np.ascontiguousarray(np.asarray(inputs["cv_b_in"]).reshape(16, 128).T)
    m["cv_dwT"] = np.ascontiguousarray(np.asarray(inputs["cv_dw"]).reshape(31, 8, 128).transpose(2, 1, 0))
    m["cv_vecT"] = np.ascontiguousarray(np.stack([np.asarray(inputs[k]).reshape(8, 128).T for k in ("cv_dw_b", "cv_ln_g", "cv_ln_b")], axis=1))
    m.update(host_consts())
    return {k: v for k, v in m.items() if k in prog.in_names}


_PROG_CACHE = {}


def kernel(**inputs):
    prog = Prog()
    nc = prog.build()
    B = inputs["x"].shape[0]
    in_maps = [make_in_map(inputs, b, prog) for b in range(B)]
    res = run_bass_kernel_spmd(nc, in_maps, core_ids=list(range(B)))
    return np.stack([np.asarray(r["out"]) for r in res.results], axis=0).astype(np.float32)
```

```python
import contextlib
import os
import numpy as np
import ml_dtypes
import concourse.bass as bass
import concourse.mybir as mybir
from concourse.bass_utils import run_bass_kernel_spmd

F32 = mybir.dt.float32
BF16 = mybir.dt.bfloat16
I32 = mybir.dt.int32
AF = mybir.ActivationFunctionType
ALU = mybir.AluOpType
AX = mybir.AxisListType

D = 1024
S = 4096
DEPTH = 4
NH = 16
DH = 64
DFF = 4096
EPS = 1e-6
NT = S // 128
NG = S // 512
NSA_IN = 2608
NCMP = 255
BIG = 30000.0
NSA_STOP = os.environ.get("NSA_STOP", "")
NSA_INTERLEAVE = bool(int(os.environ.get("NSA_INTERLEAVE", "0")))
N1_PARTS = os.environ.get("N1_PARTS", "urvg")
ROPE_MODE = os.environ.get("ROPE_MODE", "")


class Src:
    __slots__ = ("sem", "count", "name")

    def __init__(self, sem, name):
        self.sem = sem
        self.count = 0
        self.name = name


class Buf:
    __slots__ = ("name", "w", "r", "const", "excl")

    def __init__(self, name, const=False):
        self.name = name
        self.excl = False
        self.w = []
        self.r = []
        self.const = const


class T:
    __slots__ = ("t", "b")

    def __init__(self, t, name):
        self.t = t
        self.b = Buf(name)

    def __getitem__(self, idx):
        return self.t[idx]


class KB:
    def __init__(self):
        self.nc = bass.Bass("TRN2", target_bir_lowering=False)
        nc = self.nc
        self.st = contextlib.ExitStack()
        self.eng = {"pe": nc.tensor, "act": nc.scalar, "dve": nc.vector, "pool": nc.gpsimd, "sp": nc.sync}
        self.src = {}
        for n in self.eng:
            self.src[n] = Src(self.st.enter_context(nc.semaphore("sem_" + n)), n)
        self.dslots = {}
        self.dnext = {}
        for q, k in (("sp", 12), ("pool", 6), ("act", 4)):
            self.dslots[q] = [Src(self.st.enter_context(nc.semaphore(f"dq_{q}{i}")), f"dq_{q}{i}") for i in range(k)]
            self.dnext[q] = 0
        self.seen = {n: {} for n in self.eng}
        self.ninstr = 0

    def _wait(self, en, tok):
        src, val = tok
        if val <= 0:
            return
        if en == "pe" and src is self.src["pe"]:
            return
        seen = self.seen[en]
        if seen.get(src, 0) >= val:
            return
        self.eng[en].wait_ge(src.sem, val)
        seen[src] = val
        self.ninstr += 1

    def _deps(self, en, r, w, wa=()):
        for x in r:
            b = x.b if isinstance(x, T) else x
            for tok in b.w:
                self._wait(en, tok)
            if b.excl:
                own = self.src.get(en)
                for tok in b.r:
                    if tok[0] is not own:
                        self._wait(en, tok)
        for x in w:
            b = x.b if isinstance(x, T) else x
            for tok in b.w:
                self._wait(en, tok)
            for tok in b.r:
                self._wait(en, tok)
        for x in wa:
            b = x.b if isinstance(x, T) else x
            for tok in b.r:
                self._wait(en, tok)

    def _mark(self, tok, r, w, wa=()):
        for x in r:
            b = x.b if isinstance(x, T) else x
            if not b.const:
                b.r.append(tok)
        for x in w:
            b = x.b if isinstance(x, T) else x
            b.w = [tok]
            b.r = []
        for x in wa:
            b = x.b if isinstance(x, T) else x
            if b.r:
                b.w = []
                b.r = []
            b.w.append(tok)

    def op(self, en, fn, r=(), w=()):
        self._deps(en, r, w)
        ins = fn()
        s = self.src[en]
        s.count += 1
        ins.then_inc(s.sem, 1)
        self._mark((s, s.count), r, w)
        self.ninstr += 1
        return ins

    def pe(self, fn, r=(), w=()):
        return self.op("pe", fn, r, w)

    def act(self, fn, r=(), w=()):
        return self.op("act", fn, r, w)

    def dve(self, fn, r=(), w=()):
        return self.op("dve", fn, r, w)

    def pool(self, fn, r=(), w=()):
        return self.op("pool", fn, r, w)

    def dma(self, q, out, in_, r=(), w=(), wa=()):
        self._deps(q, r, w, wa)
        slots = self.dslots[q]
        sl = slots[self.dnext[q] % len(slots)]
        self.dnext[q] += 1
        self._wait(q, (sl, sl.count))
        ins = self.eng[q].dma_start(out=out, in_=in_)
        sl.count += 16
        ins.then_inc(sl.sem, 16)
        self._mark((sl, sl.count), r, w, wa)
        self.ninstr += 1
        return ins

    def barrier(self):
        toks = [(s, s.count) for s in self.src.values()]
        for q in self.dslots:
            toks += [(s, s.count) for s in self.dslots[q]]
        for en in self.eng:
            for tok in toks:
                if tok[0] is self.src[en]:
                    continue
                self._wait(en, tok)

    def _uniq(self, name):
        self.nalloc = getattr(self, "nalloc", 0) + 1
        return f"{name}_{self.nalloc}"

    def sb(self, ctx, name, shape, dtype):
        name = self._uniq("s_" + name)
        return T(ctx.enter_context(self.nc.sbuf_tensor(name, list(shape), dtype)), name)

    def ps(self, ctx, name, shape=(128, 512), dtype=F32):
        name = self._uniq("p_" + name)
        t = T(ctx.enter_context(self.nc.psum_tensor(name, list(shape), dtype)), name)
        t.b.excl = True
        return t

    def dram(self, name, shape, dtype, kind="Internal"):
        return self.nc.dram_tensor(name, list(shape), dtype, kind=kind).ap()


class Prog:
    def __init__(self, layers=(0, 1, 2, 3), debug=None):
        self.K = KB()
        self.nc = self.K.nc
        self.layers = tuple(layers)
        self.debug = debug or {}
        self.in_names = []
        self._declare_io()

    def _in(self, name, shape, dtype=F32):
        self.in_names.append(name)
        return self.K.dram(name, shape, dtype, kind="ExternalInput")

    def _declare_io(self):
        K = self.K
        self.x = self._in("x", [S, D])
        self.cT = self._in("cT", [128, 8])
        self.pos = self._in("pos", [1, S], I32)
        self.ada_w = self._in("ada_w", [DEPTH, D, 6 * D])
        self.ada_b = self._in("ada_b", [DEPTH, 6 * D])
        self.gains = {n: self._in(n, [DEPTH, D]) for n in ("mix_pre_g", "mix_post_g", "ffn_pre_g", "ffn_post_g")}
        self.ffn_w1 = self._in("ffn_w1", [DEPTH, D, DFF])
        self.ffn_w2 = self._in("ffn_w2", [DEPTH, DFF, D])
        self.sb_w_in = self._in("sb_w_in", [2, D, 3 * D])
        self.sb_w_out = self._in("sb_w_out", [2, D, D])
        self.nsa_w_in = self._in("nsa_w_in", [1, D, NSA_IN])
        self.nsa_w_out = self._in("nsa_w_out", [1, D, D])
        self.nsa_peT = {"k": self._in("nsa_pe_kT", [64, 32]), "v": self._in("nsa_pe_vT", [64, 32])}
        self.nsa_w1 = {"k": self._in("nsa_w1_k", [1, 2048, 256]), "v": self._in("nsa_w1_v", [1, 2048, 256])}
        self.nsa_w2 = {"k": self._in("nsa_w2_k", [1, 256, 64]), "v": self._in("nsa_w2_v", [1, 256, 64])}
        self.cv_w_in = self._in("cv_w_in", [1, D, 2 * D])
        self.cv_b_inT = self._in("cv_b_inT", [128, 16])
        self.cv_dwT = self._in("cv_dwT", [128, 8, 31])
        self.cv_vecT = self._in("cv_vecT", [128, 3, 8])
        self.cv_w_out = self._in("cv_w_out", [1, D, D])
        self.cv_b_out = self._in("cv_b_out", [1, D])
        self.c_ident = self._in("c_ident", [128, 128], BF16)
        self.c_tri = self._in("c_tri", [128, 128], BF16)
        self.c_ones = self._in("c_ones", [128, 128], BF16)
        self.c_sbmask = self._in("c_sbmask", [128, 512], BF16)
        self.c_sbbias = self._in("c_sbbias", [128, 512], BF16)
        self.c_onesf = self._in("c_onesf", [128, 128], F32)
        self.c_perm = self._in("c_perm", [64, 64], BF16)
        self.c_invf = self._in("c_invf", [64, 1], F32)
        self.c_esel = self._in("c_esel", [64, 32 * 128], BF16)
        self.c_win01 = self._in("c_win01", [128, 8 * 512], BF16)
        self.c_caus01 = self._in("c_caus01", [128, 512], BF16)
        self.c_band = self._in("c_band", [128, 512], BF16)
        self.c_wcm = self._in("c_wcm", [128, 128], F32)
        self.c_wfb = self._in("c_wfb", [128, 128], F32)
        self.c_anyok = self._in("c_anyok", [128, 1], F32)
        self.out = K.dram("out", [S, D], F32, kind="ExternalOutput")
        self.modv = K.dram("modv", [DEPTH, 6 * D], F32)
        self.UT = K.dram("UT", [D, S], BF16)
        self.QT = K.dram("QT", [D, S], BF16)
        self.KT = K.dram("KT", [D, S], BF16)
        self.Vd = K.dram("Vd", [S, D], BF16)
        self.HG = K.dram("HG", [D, S], BF16)
        self.NQ = K.dram("NQ", [16, 64, S], BF16)
        self.NK = K.dram("NK", [8, 64, S], BF16)
        self.NC = K.dram("NC", [8, 64, S], BF16)
        self.NV = K.dram("NV", [S, 2 * 4 * 65], BF16)
        self.NGt = K.dram("NGt", [S, 48], F32)
        self.Od = self.Vd
        self.b_od = Buf("od")
        self.b_nq = Buf("nq"); self.b_nk = Buf("nk"); self.b_ncr = Buf("ncr"); self.b_nv = Buf("nv"); self.b_ngt = Buf("ngt")
        self.b_h = [Buf(f"h{t}") for t in range(NT)]
        self.b_ut = [Buf(f"ut{g}") for g in range(NG)]
        self.b_modv = [Buf(f"modv{i}") for i in range(DEPTH)]
        self.b_qt = [Buf(f"qt{g}") for g in range(NG)]
        self.b_kt = [Buf(f"kt{g}") for g in range(NG)]
        self.b_vd = [Buf(f"vd{t}") for t in range(NT)]
        self.b_hg = [Buf(f"hg{g}") for g in range(NG)]
        self.cbuf = Buf("const", const=True)
        for dbg_name, shape in self.debug.items():
            setattr(self, "dbg_" + dbg_name, K.dram("dbg_" + dbg_name, shape, F32, kind="ExternalOutput"))

    def load_consts(self):
        K, nc = self.K, self.nc
        st = K.st
        self.ident = K.sb(st, "ident", [128, 128], BF16)
        self.tri = K.sb(st, "tri", [128, 128], BF16)
        self.ones = K.sb(st, "ones", [128, 128], BF16)
        for t, src in ((self.ident, self.c_ident), (self.tri, self.c_tri), (self.ones, self.c_ones)):
            K.dma("sp", t[:], src, r=[self.cbuf], w=[t])
            t.b.const = True

    def phase_mod(self):
        K, nc = self.K, self.nc
        with contextlib.ExitStack() as ph:
            cT = K.sb(ph, "cT", [128, 8], F32)
            sig = K.sb(ph, "sig", [128, 8], F32)
            cond = K.sb(ph, "cond", [128, 8], F32)
            wsl = [K.sb(ph, f"adaw{i}", [128, 8, 512], F32) for i in range(2)]
            brow = K.sb(ph, "brow", [1, 6 * D], F32)
            mrow = K.sb(ph, "mrow", [1, 6 * D], F32)
            pss = [K.ps(ph, f"modps{i}") for i in range(2)]
            K.dma("sp", cT[:], self.cT, r=[self.cbuf], w=[cT])
            K.act(lambda: nc.scalar.activation(out=sig[:], in_=cT[:], func=AF.Sigmoid), r=[cT], w=[sig])
            K.dve(lambda: nc.vector.tensor_tensor(out=cond[:], in0=cT[:], in1=sig[:], op=ALU.mult), r=[cT, sig], w=[cond])
            it = 0
            for i in self.layers:
                K.dma("sp", brow[:], self.ada_b[i:i + 1, :], r=[self.cbuf], w=[brow])
                wv = self.ada_w[i].rearrange("(kc p) n -> p kc n", p=128)
                for n in range(12):
                    wt = wsl[it % 2]
                    ps = pss[it % 2]
                    it += 1
                    K.dma("sp", wt[:], wv[:, :, n * 512:(n + 1) * 512], r=[self.cbuf], w=[wt])
                    for kc in range(8):
                        K.pe(lambda kc=kc: nc.tensor.matmul(ps[0:1, :], lhsT=cond[:, kc:kc + 1], rhs=wt[:, kc, :],
                                                            start=(kc == 0), stop=(kc == 7)), r=[cond, wt], w=[ps])
                    K.dve(lambda n=n: nc.vector.tensor_tensor(out=mrow[0:1, n * 512:(n + 1) * 512], in0=ps[0:1, :],
                                                              in1=brow[0:1, n * 512:(n + 1) * 512], op=ALU.add),
                          r=[ps, brow], w=[mrow])
                K.dma("sp", self.modv[i:i + 1, :], mrow[:], r=[mrow], w=[self.b_modv[i]])
            K.barrier()

    def load_vec(self, ph, name, src_row):
        t = self.K.sb(ph, name, [128, D], F32)
        return t

    def mod_slice(self, i, k):
        return self.modv[i:i + 1, k * D:(k + 1) * D]

    def bcast_load(self, t, src_row, rbufs):
        self.K.dma("sp", t[:], src_row.to_broadcast([128, D]), r=rbufs, w=[t])

    def phase_norm(self, i, which, src_is_x):
        K, nc = self.K, self.nc
        gname = "mix_pre_g" if which == 0 else "ffn_pre_g"
        ksh, ksc = (0, 1) if which == 0 else (3, 4)
        src = self.x if src_is_x else self.out
        with contextlib.ExitStack() as ph:
            A = K.sb(ph, "nA", [128, D], F32)
            B = K.sb(ph, "nB", [128, D], F32)
            Gn = K.sb(ph, "nG", [128, D], F32)
            self.bcast_load(A, self.mod_slice(i, ksc), [self.b_modv[i]])
            self.bcast_load(B, self.mod_slice(i, ksh), [self.b_modv[i]])
            self.bcast_load(Gn, self.gains[gname][i:i + 1, :], [self.cbuf])
            K.dve(lambda: nc.vector.scalar_tensor_tensor(out=A[:], in0=A[:], scalar=1.0, in1=Gn[:], op0=ALU.add, op1=ALU.mult),
                  r=[A, Gn], w=[A])
            hs = [K.sb(ph, f"nh{j}", [128, D], F32) for j in range(3)]
            junk = K.sb(ph, "njunk", [128, D], BF16)
            tmp = [K.sb(ph, f"ntmp{j}", [128, D], F32) for j in range(2)]
            ub = [K.sb(ph, f"nub{j}", [128, D], BF16) for j in range(3)]
            st = [K.sb(ph, f"nst{j}", [128, 4], F32) for j in range(2)]
            utg = [K.sb(ph, f"nutg{j}", [128, 8, 512], BF16) for j in range(2)]
            pst = [K.ps(ph, f"npst{j}", (128, 1024), BF16) for j in range(2)]
            UTv = self.UT.rearrange("(kc p) t -> p kc t", p=128)

            def load(t):
                K.dma("sp", hs[t % 3][:], src[t * 128:(t + 1) * 128, :], r=[self.b_h[t]], w=[hs[t % 3]])

            load(0)
            load(1)
            items = []
            for t in range(NT):
                def s0(t=t):
                    if t + 2 < NT:
                        load(t + 2)
                    h, s_, tm, u = hs[t % 3], st[t % 2], tmp[t % 2], ub[t % 3]
                    K.act(lambda: nc.scalar.activation(out=junk[:], in_=h[:], func=AF.Square, accum_out=s_[:, 0:1]),
                          r=[h], w=[junk, s_])
                    K.act(lambda: nc.scalar.activation(out=s_[:, 1:2], in_=s_[:, 0:1], func=AF.Sqrt, scale=1.0 / D, bias=EPS),
                          r=[s_], w=[s_])
                    K.dve(lambda: nc.vector.reciprocal(out=s_[:, 2:3], in_=s_[:, 1:2]), r=[s_], w=[s_])
                    K.dve(lambda: nc.vector.scalar_tensor_tensor(out=tm[:], in0=h[:], scalar=s_[:, 2:3], in1=A[:], op0=ALU.mult, op1=ALU.mult),
                          r=[h, s_, A], w=[tm])
                    K.pool(lambda: nc.gpsimd.tensor_tensor(out=u[:], in0=tm[:], in1=B[:], op=ALU.add), r=[tm, B], w=[u])

                def s1(t=t):
                    u, pt = ub[t % 3], pst[t % 2]
                    g = t // 4
                    ug = utg[g % 2]
                    for kc in range(8):
                        K.pe(lambda: nc.tensor.transpose(out=pt[:, kc * 128:(kc + 1) * 128], in_=u[:, kc * 128:(kc + 1) * 128],
                                                         identity=self.ident[:]), r=[u, self.ident], w=[pt])
                    tt = t % 4
                    K.act(lambda: nc.scalar.copy(out=ug[:, :, tt * 128:(tt + 1) * 128],
                                                 in_=pt[:].rearrange("p (kc t) -> p kc t", kc=8)), r=[pt], w=[ug])
                    if tt == 3:
                        K.dma("sp", UTv[:, :, g * 512:(g + 1) * 512], ug[:], r=[ug], w=[self.b_ut[g]])
                items.append([s0, s1])
            run_pipeline(items)
            K.barrier()

    def epilogue_setup(self, ph, i, which):
        K, nc = self.K, self.nc
        gname = "mix_post_g" if which == 0 else "ffn_post_g"
        kg = 2 if which == 0 else 5
        G = K.sb(ph, "eG", [128, D], F32)
        Gn = K.sb(ph, "eGn", [128, D], F32)
        self.bcast_load(G, self.mod_slice(i, kg), [self.b_modv[i]])
        self.bcast_load(Gn, self.gains[gname][i:i + 1, :], [self.cbuf])
        K.dve(lambda: nc.vector.tensor_tensor(out=G[:], in0=G[:], in1=Gn[:], op=ALU.mult), r=[G, Gn], w=[G])
        e = {
            "G": G,
            "h": [K.sb(ph, f"eh{j}", [128, D], F32) for j in range(2)],
            "tmp": [K.sb(ph, f"etmp{j}", [128, 512], F32) for j in range(2)],
            "junk": K.sb(ph, "ejunk", [128, 512], BF16),
            "st": [K.sb(ph, f"est{j}", [128, 8], F32) for j in range(2)],
            "n": 0,
            "src_is_x": False,
        }
        return e

    def epi_load(self, e, t):
        src = self.x if self.resid_from_x else self.out
        h = e["h"][t % 2]
        self.K.dma("sp", h[:], src[t * 128:(t + 1) * 128, :], r=[self.b_h[t]], w=[h])

    def epilogue(self, e, t, ybanks):
        K, nc = self.K, self.nc
        h = e["h"][t % 2]
        s_ = e["st"][t % 2]
        junk = e["junk"]
        G = e["G"]
        for hf in range(2):
            K.act(lambda hf=hf: nc.scalar.activation(out=junk[:], in_=ybanks[hf][:], func=AF.Square, accum_out=s_[:, hf:hf + 1]),
                  r=[ybanks[hf]], w=[junk, s_])
        K.dve(lambda: nc.vector.tensor_tensor(out=s_[:, 2:3], in0=s_[:, 0:1], in1=s_[:, 1:2], op=ALU.add), r=[s_], w=[s_])
        K.act(lambda: nc.scalar.activation(out=s_[:, 3:4], in_=s_[:, 2:3], func=AF.Sqrt, scale=1.0 / D, bias=EPS),
              r=[s_], w=[s_])
        K.dve(lambda: nc.vector.reciprocal(out=s_[:, 4:5], in_=s_[:, 3:4]), r=[s_], w=[s_])
        for hf in range(2):
            tm = e["tmp"][hf]
            K.dve(lambda hf=hf, tm=tm: nc.vector.scalar_tensor_tensor(out=tm[:], in0=ybanks[hf][:], scalar=s_[:, 4:5],
                                                                      in1=G[:, hf * 512:(hf + 1) * 512], op0=ALU.mult, op1=ALU.mult),
                  r=[ybanks[hf], s_, G], w=[tm])
            K.pool(lambda hf=hf, tm=tm: nc.gpsimd.tensor_tensor(out=h[:, hf * 512:(hf + 1) * 512], in0=h[:, hf * 512:(hf + 1) * 512],
                                                                 in1=tm[:], op=ALU.add), r=[tm, h], w=[h])
        K.dma("sp", self.out[t * 128:(t + 1) * 128, :], h[:], r=[h], w=[self.b_h[t]])

    def load_w(self, wt, src, kchunks, ncols, col0=0, split=2048):
        K = self.K
        sv = src.rearrange("(kc p) n -> p kc n", p=128)
        for kc in range(kchunks):
            for c0 in range(0, ncols, split):
                c1 = min(ncols, c0 + split)
                K.dma("pool", wt[:, kc, c0:c1], sv[:, kc, col0 + c0:col0 + c1], r=[self.cbuf], wa=[wt])

    def phase_ffn(self, i):
        K, nc = self.K, self.nc
        with contextlib.ExitStack() as ph:
            w1 = K.sb(ph, "fw1", [128, 8, DFF], BF16)
            w2 = K.sb(ph, "fw2", [128, 32, D], BF16)
            self.load_w(w1, self.ffn_w1[i], 8, DFF)
            self.load_w(w2, self.ffn_w2[i], 32, D, split=1024)
            e = self.epilogue_setup(ph, i, 1)
            utg = [K.sb(ph, f"futg{j}", [128, 8, 512], BF16) for j in range(2)]
            hid = K.sb(ph, "fhid", [128, 32, 512], BF16)
            rl = [K.sb(ph, f"frl{j}", [128, 512], F32) for j in range(2)]
            psA = [K.ps(ph, f"fpsA{j}") for j in range(2)]
            psY = [[K.ps(ph, f"fpsY{j}{hf}") for hf in range(2)] for j in range(2)]
            UTv = self.UT.rearrange("(kc p) t -> p kc t", p=128)

            def load_u(g):
                K.dma("sp", utg[g % 2][:], UTv[:, :, g * 512:(g + 1) * 512], r=[self.b_ut[g]], w=[utg[g % 2]])

            load_u(0)
            for g in range(NG):
                if g + 1 < NG:
                    load_u(g + 1)
                ug = utg[g % 2]
                for fc in range(32):
                    ps = psA[fc % 2]
                    r_ = rl[fc % 2]
                    for kc in range(8):
                        K.pe(lambda kc=kc, fc=fc, ps=ps: nc.tensor.matmul(ps[:], lhsT=w1[:, kc, fc * 128:(fc + 1) * 128], rhs=ug[:, kc, :],
                                                                          start=(kc == 0), stop=(kc == 7)), r=[w1, ug], w=[ps])
                    K.act(lambda ps=ps, r_=r_: nc.scalar.activation(out=r_[:], in_=ps[:], func=AF.Relu), r=[ps], w=[r_])
                    K.pool(lambda fc=fc, r_=r_: nc.gpsimd.tensor_tensor(out=hid[:, fc, :], in0=r_[:], in1=r_[:], op=ALU.mult),
                           r=[r_], w=[hid])
                for tt in range(4):
                    t = g * 4 + tt
                    self.epi_load(e, t)
                    yb = psY[tt % 2]
                    for hf in range(2):
                        for fc in range(32):
                            K.pe(lambda fc=fc, hf=hf, tt=tt: nc.tensor.matmul(yb[hf][:], lhsT=hid[:, fc, tt * 128:(tt + 1) * 128],
                                                                             rhs=w2[:, fc, hf * 512:(hf + 1) * 512],
                                                                             start=(fc == 0), stop=(fc == 31)), r=[hid, w2], w=[yb[hf]])
                    self.epilogue(e, t, yb)
            K.barrier()

    def phase_sb_proj(self, j):
        K, nc = self.K, self.nc
        with contextlib.ExitStack() as ph:
            w = K.sb(ph, "sw", [128, 8, 3 * D], BF16)
            self.load_w(w, self.sb_w_in[j], 8, 3 * D, split=3072)
            utg = [K.sb(ph, f"sutg{k}", [128, 8, 512], BF16) for k in range(2)]
            stg = [K.sb(ph, f"sstg{k}", [128, 512], BF16) for k in range(4)]
            vst = [K.sb(ph, f"svst{k}", [128, D], BF16) for k in range(2)]
            pss = [K.ps(ph, f"sps{k}") for k in range(4)]
            UTv = self.UT.rearrange("(kc p) t -> p kc t", p=128)

            def load_u(g):
                K.dma("sp", utg[g % 2][:], UTv[:, :, g * 512:(g + 1) * 512], r=[self.b_ut[g]], w=[utg[g % 2]])

            load_u(0)
            n = 0
            for g in range(NG):
                if g + 1 < NG:
                    load_u(g + 1)
                ug = utg[g % 2]
                for fcn in range(16):
                    ps = pss[n % 4]
                    sg = stg[n % 4]
                    for kc in range(8):
                        K.pe(lambda kc=kc, fcn=fcn, ps=ps: nc.tensor.matmul(ps[:], lhsT=w[:, kc, fcn * 128:(fcn + 1) * 128], rhs=ug[:, kc, :],
                                                                           start=(kc == 0), stop=(kc == 7)), r=[w, ug], w=[ps])
                    if n % 2 == 0:
                        K.act(lambda ps=ps, sg=sg: nc.scalar.copy(out=sg[:], in_=ps[:]), r=[ps], w=[sg])
                    else:
                        K.dve(lambda ps=ps, sg=sg: nc.vector.tensor_copy(out=sg[:], in_=ps[:]), r=[ps], w=[sg])
                    dst = self.QT if fcn < 8 else self.KT
                    fo = (fcn % 8) * 128
                    bb = self.b_qt[g] if fcn < 8 else self.b_kt[g]
                    K.dma("sp", dst[fo:fo + 128, g * 512:(g + 1) * 512], sg[:], r=[sg], wa=[bb])
                    n += 1
                for tt in range(4):
                    t = g * 4 + tt
                    vs = vst[t % 2]
                    for hf in range(2):
                        ps = pss[n % 4]
                        n += 1
                        for kc in range(8):
                            K.pe(lambda kc=kc, hf=hf, tt=tt, ps=ps: nc.tensor.matmul(ps[:], lhsT=ug[:, kc, tt * 128:(tt + 1) * 128],
                                                                                    rhs=w[:, kc, 2 * D + hf * 512:2 * D + (hf + 1) * 512],
                                                                                    start=(kc == 0), stop=(kc == 7)), r=[w, ug], w=[ps])
                        if hf == 0:
                            K.act(lambda ps=ps, vs=vs: nc.scalar.copy(out=vs[:, 0:512], in_=ps[:]), r=[ps], w=[vs])
                        else:
                            K.dve(lambda ps=ps, vs=vs: nc.vector.tensor_copy(out=vs[:, 512:1024], in_=ps[:]), r=[ps], w=[vs])
                    K.dma("sp", self.Vd[t * 128:(t + 1) * 128, :], vs[:], r=[vs], w=[self.b_vd[t]])
            K.barrier()

    def phase_sb_core(self, i, j):
        K, nc = self.K, self.nc
        with contextlib.ExitStack() as ph:
            O = K.sb(ph, "aO", [128, NT, D], BF16)
            with contextlib.ExitStack() as ph2:
                V = K.sb(ph2, "aV", [128, NT, D], BF16)
                Vv = self.Vd.rearrange("(t p) f -> p t f", p=128)
                for t0 in range(0, NT, 8):
                    K.dma("sp", V[:, t0:t0 + 8, :], Vv[:, t0:t0 + 8, :], r=self.b_vd[t0:t0 + 8], wa=[V])
                msk = K.sb(ph2, "amsk", [128, 512], BF16)
                mbias = K.sb(ph2, "ambias", [128, 512], BF16)
                K.dma("sp", msk[:], self.c_sbmask, r=[self.cbuf], w=[msk])
                K.dma("sp", mbias[:], self.c_sbbias, r=[self.cbuf], w=[mbias])
                qh = [K.sb(ph2, f"aq{k}", [64, S], BF16) for k in range(2)]
                kh = [K.sb(ph2, f"ak{k}", [64, S], BF16) for k in range(2)]
                Et = [K.sb(ph2, f"aE{k}", [128, 512], F32) for k in range(3)]
                Xt = [K.sb(ph2, f"aX{k}", [128, 512], F32) for k in range(2)]
                Lt = [K.sb(ph2, f"aL{k}", [128, 512], BF16) for k in range(3)]
                Lr = [K.sb(ph2, f"aLr{k}", [128, 512], BF16) for k in range(2)]
                At = [K.sb(ph2, f"aA{k}", [128, 512], BF16) for k in range(4)]
                Rt = [K.sb(ph2, f"aR{k}", [128, 512], BF16) for k in range(2)]
                zps = [K.ps(ph2, f"azps{k}") for k in range(2)]
                cps = [K.ps(ph2, f"acps{k}") for k in range(2)]
                avs = [K.ps(ph2, f"aav{k}") for k in range(2)]

                Rq = [K.sb(ph2, f"aRq{k}", [128, 512], BF16) for k in range(2)]

                def load_head(h):
                    s = h % 2
                    K.dma("sp", qh[s][:], self.QT[h * 64:(h + 1) * 64, :], r=self.b_qt, w=[qh[s]])
                    K.dma("sp", kh[s][:], self.KT[h * 64:(h + 1) * 64, :], r=self.b_kt, w=[kh[s]])

                units = []
                ci = 0
                for h in range(NH):
                    for qc in range(NG):
                        nkt = 4 * qc + 4
                        for k_, i_ in enumerate(range(nkt - 1, -1, -1)):
                            units.append(dict(h=h, qc=qc, i=i_, k=k_, nkt=nkt, ci=ci, idx=len(units),
                                              lasth=(qc == NG - 1 and i_ == 0)))
                        ci += 1
                Rall = [[Rt[0], Rt[1]], [Rq[0], Rq[1]]]

                def geom(U):
                    a = U["i"] - 4 * U["qc"]
                    diag = a >= 0
                    c0 = 128 * a if diag else 0
                    return a, diag, c0, 512 - c0

                def stageA(U):
                    u, h, qc, i_ = U["idx"], U["h"], U["qc"], U["i"]
                    a, diag, c0, W = geom(U)
                    q_, k_ = qh[h % 2], kh[h % 2]
                    zp, E, L = zps[u % 2], Et[u % 3], Lt[u % 3]
                    qs = q_[:, qc * 512 + c0:(qc + 1) * 512]
                    K.pe(lambda: nc.tensor.matmul(zp[:, 0:W], lhsT=k_[:, i_ * 128:(i_ + 1) * 128], rhs=qs, start=True, stop=True),
                         r=[q_, k_], w=[zp])
                    K.act(lambda: nc.scalar.activation(out=E[:, 0:W], in_=zp[:, 0:W], func=AF.Exp, scale=0.125), r=[zp], w=[E])
                    if diag:
                        Lraw = Lr[u % 2]
                        K.act(lambda: nc.scalar.activation(out=Lraw[:, 0:W], in_=E[:, 0:W], func=AF.Ln, bias=1.0), r=[E], w=[Lraw])
                        K.dve(lambda: nc.vector.tensor_tensor(out=L[:, 0:W], in0=Lraw[:, 0:W], in1=msk[:, 0:W], op=ALU.mult),
                              r=[Lraw, msk], w=[L])
                    else:
                        K.act(lambda: nc.scalar.activation(out=L[:, 0:W], in_=E[:, 0:W], func=AF.Ln, bias=1.0), r=[E], w=[L])

                def stageB(U):
                    u, h, qc, i_, k = U["idx"], U["h"], U["qc"], U["i"], U["k"]
                    a, diag, c0, W = geom(U)
                    cp, L, A, E, X = cps[u % 2], Lt[u % 3], At[u % 4], Et[u % 3], Xt[u % 2]
                    Rp = Rall[U["ci"] % 2]
                    Rc, Rn = Rp[k % 2], Rp[(k + 1) % 2]
                    if k == 0:
                        K.pool(lambda: nc.gpsimd.memset(Rp[0][:], 0.0), w=[Rp[0]])
                        K.pool(lambda: nc.gpsimd.memset(Rp[1][:], 0.0), w=[Rp[1]])
                    K.pe(lambda: nc.tensor.matmul(cp[:, 0:W], lhsT=self.tri[:], rhs=L[:, 0:W], start=True, stop=(k == 0 and not diag)),
                         r=[self.tri, L], w=[cp])
                    if k != 0:
                        K.pe(lambda: nc.tensor.matmul(cp[:, 0:W], lhsT=self.ones[:], rhs=Rc[:, c0:512], start=False, stop=(not diag)),
                             r=[self.ones, Rc], w=[cp])
                    if diag:
                        K.pe(lambda: nc.tensor.matmul(cp[:, 0:W], lhsT=self.ident[:], rhs=mbias[:, 0:W], start=False, stop=True),
                             r=[self.ident, mbias], w=[cp])
                    if i_ != 0:
                        K.dve(lambda: nc.vector.tensor_tensor(out=Rn[:, c0:512], in0=Rc[:, c0:512], in1=L[:, 0:W], op=ALU.add),
                              r=[Rc, L], w=[Rn])
                    K.act(lambda: nc.scalar.activation(out=X[:, 0:W], in_=cp[:, 0:W], func=AF.Exp, scale=-1.0), r=[cp], w=[X])
                    K.dve(lambda: nc.vector.tensor_tensor(out=A[:, 0:W], in0=E[:, 0:W], in1=X[:, 0:W], op=ALU.mult), r=[E, X], w=[A])

                def stageC(U):
                    u, h, qc, i_, k = U["idx"], U["h"], U["qc"], U["i"], U["k"]
                    a, diag, c0, W = geom(U)
                    A = At[u % 4]
                    av = avs[U["ci"] % 2]
                    for n_, c in enumerate(range(a if diag else 0, 4)):
                        K.pe(lambda: nc.tensor.matmul(av[:, c * 64:(c + 1) * 64], lhsT=A[:, c * 128 - c0:(c + 1) * 128 - c0],
                                                      rhs=V[:, i_, h * 64:(h + 1) * 64], start=(k == 0 and n_ == 0), stop=False,
                                                      skip_group_check=True), r=[A, V], w=[av])
                    if i_ == 0:
                        K.dve(lambda: nc.vector.tensor_copy(out=O[:, qc * 4:(qc + 1) * 4, h * 64:(h + 1) * 64],
                                                            in_=av[:, 0:256].rearrange("p (c d) -> p c d", c=4)), r=[av], w=[O])
                    if U["lasth"] and h + 2 < NH:
                        load_head(h + 2)

                load_head(0)
                load_head(1)
                n = len(units)
                for kk in range(n + 3):
                    if kk < n:
                        stageA(units[kk])
                    if 0 <= kk - 1 < n:
                        stageB(units[kk - 1])
                    if 0 <= kk - 3 < n:
                        stageC(units[kk - 3])
                K.barrier()
            self.out_proj(ph, i, self.sb_w_out[j], O, None)
            K.barrier()

    def out_proj(self, ph, i, w_dram, O, bias_row):
        K, nc = self.K, self.nc
        with contextlib.ExitStack() as ph3:
            w = K.sb(ph3, "ow", [128, 8, D], BF16)
            self.load_w(w, w_dram, 8, D, split=1024)
            e = self.epilogue_setup(ph3, i, 0)
            oT = [K.sb(ph3, f"ooT{k}", [128, 8, 128], BF16) for k in range(2)]
            ptr = [K.ps(ph3, f"optr{k}", (128, 1024), BF16) for k in range(2)]
            psY = [[K.ps(ph3, f"opsY{k}{hf}") for hf in range(2)] for k in range(2)]
            brow = None
            if bias_row is not None:
                brow = K.sb(ph3, "obrow", [1, D], BF16)
                K.dma("pool", brow[:], bias_row, r=[self.cbuf], w=[brow])
            oin = None
            if O is None:
                oin = [K.sb(ph3, f"ooin{k}", [128, D], BF16) for k in range(3)]
                for t in range(2):
                    K.dma("sp", oin[t % 3][:], self.Od[t * 128:(t + 1) * 128, :], r=[self.b_od], w=[oin[t % 3]])
            for t in range(NT):
                self.epi_load(e, t)
                pt = ptr[t % 2]
                ot = oT[t % 2]
                if oin is not None and t + 2 < NT:
                    K.dma("sp", oin[(t + 2) % 3][:], self.Od[(t + 2) * 128:(t + 3) * 128, :], r=[self.b_od], w=[oin[(t + 2) % 3]])
                for kc in range(8):
                    src_ = O[:, t, kc * 128:(kc + 1) * 128] if oin is None else oin[t % 3][:, kc * 128:(kc + 1) * 128]
                    srcT = O if oin is None else oin[t % 3]
                    K.pe(lambda kc=kc: nc.tensor.transpose(out=pt[:, kc * 128:(kc + 1) * 128], in_=src_,
                                                          identity=self.ident[:]), r=[srcT, self.ident], w=[pt])
                K.act(lambda: nc.scalar.copy(out=ot[:], in_=pt[:].rearrange("p (kc t) -> p kc t", kc=8)), r=[pt], w=[ot])
                yb = psY[t % 2]
                for hf in range(2):
                    for kc in range(8):
                        K.pe(lambda kc=kc, hf=hf: nc.tensor.matmul(yb[hf][:], lhsT=ot[:, kc, :], rhs=w[:, kc, hf * 512:(hf + 1) * 512],
                                                                   start=(kc == 0), stop=(kc == 7 and brow is None)), r=[ot, w], w=[yb[hf]])
                    if brow is not None:
                        K.pe(lambda hf=hf: nc.tensor.matmul(yb[hf][:], lhsT=self.ones[0:1, :], rhs=brow[0:1, hf * 512:(hf + 1) * 512],
                                                            start=False, stop=True), r=[self.ones, brow], w=[yb[hf]])
                self.epilogue(e, t, yb)

    def build(self):
        K = self.K
        self.load_consts()
        self.phase_mod()
        first = True
        for i in self.layers:
            kind, j = i % 3, i // 3
            self.phase_norm(i, 0, src_is_x=first)
            self.resid_from_x = first
            first = False
            if kind == 0:
                self.phase_sb_proj(j)
                self.phase_sb_core(i, j)
            elif kind == 1:
                self.phase_nsa(i, j)
            else:
                self.phase_conv(i, j)
            self.resid_from_x = False
            self.phase_norm(i, 1, src_is_x=False)
            self.phase_ffn(i)
        K.barrier()
        K.st.close()
        return self.nc

    def copy_x_to_out(self):
        K = self.K
        with contextlib.ExitStack() as ph:
            bufs = [K.sb(ph, f"cx{k}", [128, 4, D], F32) for k in range(2)]
            xv = self.x.rearrange("(g c p) f -> g p c f", p=128, c=4)
            ov = self.out.rearrange("(g c p) f -> g p c f", p=128, c=4)
            for g in range(NG):
                b = bufs[g % 2]
                K.dma("sp", b[:], xv[g], r=[], w=[b])
                K.dma("sp", ov[g], b[:], r=[b], w=[self.b_h[4 * g + c] for c in range(4)])
            K.barrier()


    def phase_nsa(self, i, j):
        K, nc = self.K, self.nc
        UTv = self.UT.rearrange("(kc p) t -> p kc t", p=128)
        TWO_PI = 2.0 * np.pi
        with contextlib.ExitStack() as nsa:
            kcT = K.sb(nsa, "n_kcT", [64, 4, 256], BF16)
            vcmp = K.sb(nsa, "n_vcmp", [128, 2, 4, 64], BF16)
            cscmp = K.sb(nsa, "n_cscmp", [64, 2, 256], F32)
            perm = K.sb(nsa, "n_perm", [64, 64], BF16)
            K.dma("sp", perm[:], self.c_perm, r=[self.cbuf], w=[perm])
            with contextlib.ExitStack() as ph:
                w = K.sb(ph, "nw", [128, 8, NSA_IN], BF16)
                self.load_w(w, self.nsa_w_in[j], 8, NSA_IN, split=NSA_IN)
                cosT = K.sb(ph, "ncos", [64, S], F32)
                sinT = K.sb(ph, "nsin", [64, S], F32)
                with contextlib.ExitStack() as ph0:
                    posi = K.sb(ph0, "nposi", [64, S], I32)
                    ang = K.sb(ph0, "nang", [64, S], F32)
                    t1 = K.sb(ph0, "nt1", [64, S], F32)
                    t2 = K.sb(ph0, "nt2", [64, S], F32)
                    ki = K.sb(ph0, "nki", [64, S], I32)
                    invf = K.sb(ph0, "ninvf", [64, 1], F32)
                    K.dma("sp", posi[:], self.pos.to_broadcast([64, S]), r=[self.cbuf], w=[posi])
                    K.dma("sp", invf[:], self.c_invf, r=[self.cbuf], w=[invf])
                    K.dve(lambda: nc.vector.tensor_copy(out=ang[:], in_=posi[:]), r=[posi], w=[ang])
                    K.dve(lambda: nc.vector.tensor_scalar(out=ang[:], in0=ang[:], scalar1=invf[:, 0:1], scalar2=None, op0=ALU.mult),
                          r=[ang, invf], w=[ang])
                    for tab, shift in ((sinT, 0.0), (cosT, 0.5 * np.pi)):
                        K.dve(lambda: nc.vector.tensor_scalar(out=t1[:], in0=ang[:], scalar1=shift, scalar2=1.0 / TWO_PI,
                                                              op0=ALU.add, op1=ALU.mult), r=[ang], w=[t1])
                        K.dve(lambda: nc.vector.tensor_copy(out=ki[:], in_=t1[:]), r=[t1], w=[ki])
                        K.dve(lambda: nc.vector.tensor_copy(out=t1[:], in_=ki[:]), r=[ki], w=[t1])
                        K.dve(lambda: nc.vector.scalar_tensor_tensor(out=t2[:], in0=t1[:], scalar=-TWO_PI, in1=ang[:], op0=ALU.mult, op1=ALU.add),
                              r=[t1, ang], w=[t2])
                        K.dve(lambda: nc.vector.tensor_scalar(out=t2[:], in0=t2[:], scalar1=shift, scalar2=None, op0=ALU.add), r=[t2], w=[t2])
                        K.dve(lambda: nc.vector.tensor_scalar(out=t1[:], in0=t2[:], scalar1=np.pi, scalar2=-TWO_PI, op0=ALU.is_gt, op1=ALU.mult),
                              r=[t2], w=[t1])
                        K.dve(lambda: nc.vector.tensor_tensor(out=t2[:], in0=t2[:], in1=t1[:], op=ALU.add), r=[t2, t1], w=[t2])
                        K.dve(lambda: nc.vector.tensor_scalar(out=t1[:], in0=t2[:], scalar1=-np.pi, scalar2=TWO_PI, op0=ALU.is_lt, op1=ALU.mult),
                              r=[t2], w=[t1])
                        K.dve(lambda: nc.vector.tensor_tensor(out=t2[:], in0=t2[:], in1=t1[:], op=ALU.add), r=[t2, t1], w=[t2])
                        K.dve(lambda: nc.vector.tensor_scalar(out=t2[:], in0=t2[:], scalar1=-3.1415925, scalar2=3.1415925, op0=ALU.max, op1=ALU.min),
                              r=[t2], w=[t2])
                        K.act(lambda: nc.scalar.activation(out=tab[:], in_=t2[:], func=AF.Sin), r=[t2], w=[tab])
                    K.dve(lambda: nc.vector.tensor_copy(out=cscmp[:, 0, 0:255], in_=cosT[:, 31:S:16]), r=[cosT], w=[cscmp])
                    K.dve(lambda: nc.vector.tensor_copy(out=cscmp[:, 1, 0:255], in_=sinT[:, 31:S:16]), r=[sinT, cscmp], w=[cscmp])
                    K.barrier()
                if NSA_STOP == "n0":
                    return
                utg = [K.sb(ph, f"nutg{k}", [128, 8, 512], BF16) for k in range(2)]
                xb = [K.sb(ph, f"nxb{k}", [64, 512], BF16) for k in range(2)]
                r1 = [K.sb(ph, f"nr1{k}", [64, 512], F32) for k in range(2)]
                r2 = [K.sb(ph, f"nr2{k}", [64, 512], F32) for k in range(2)]
                ob = [K.sb(ph, f"nob{k}", [64, 512], BF16) for k in range(4)]
                va = [K.sb(ph, f"nva{k}", [128, 8, 65], BF16) for k in range(2)]
                gt = [K.sb(ph, f"ngt{k}", [128, 48], F32) for k in range(2)]
                for v_ in va:
                    K.pool(lambda: nc.gpsimd.memset(v_[:], 1.0), w=[v_])
                pp = [K.ps(ph, f"npp{k}") for k in range(3)]
                pr = [K.ps(ph, f"npr{k}") for k in range(2)]
                pv = [K.ps(ph, f"npv{k}") for k in range(2)]
                pg = K.ps(ph, "npg")

                def load_u(g):
                    K.dma("sp", utg[g % 2][:], UTv[:, :, g * 512:(g + 1) * 512], r=[self.b_ut[g]], w=[utg[g % 2]])

                units = []
                for h in range(16):
                    units.append((h * 64, self.NQ, h, True, self.b_nq))
                for g4 in range(4):
                    units.append((D + 2 * 256 + g4 * 64, self.NK, g4, True, self.b_nk))
                for g4 in range(4):
                    units.append((D + 4 * 256 + g4 * 64, self.NK, 4 + g4, True, self.b_nk))
                for g4 in range(4):
                    units.append((D + 0 * 256 + g4 * 64, self.NC, g4, False, self.b_ncr))
                for g4 in range(4):
                    units.append((D + 1 * 256 + g4 * 64, self.NC, 4 + g4, False, self.b_ncr))
                load_u(0)
                n = 0
                nr = 0
                for g in range(NG):
                    if g + 1 < NG:
                        load_u(g + 1)
                    ug = utg[g % 2]
                    tsl = slice(g * 512, (g + 1) * 512)
                    for (col, dst, ui, rope, bb) in units:
                        if (rope and "r" not in N1_PARTS) or ((not rope) and "u" not in N1_PARTS):
                            continue
                        ps = pp[n % 3]
                        o_ = ob[n % 4]
                        n += 1
                        for kc in range(8):
                            K.pe(lambda: nc.tensor.matmul(ps[0:64, :], lhsT=w[:, kc, col:col + 64], rhs=ug[:, kc, :],
                                                          start=(kc == 0), stop=(kc == 7)), r=[w, ug], w=[ps])
                        if rope and "asu" not in ROPE_MODE:
                            x_, a_, b_, p2 = xb[nr % 2], r1[nr % 2], r2[nr % 2], pr[nr % 2]
                            nr += 1
                            K.act(lambda: nc.scalar.copy(out=x_[:], in_=ps[0:64, :]), r=[ps], w=[x_])
                            if "noperm" not in ROPE_MODE:
                                K.pe(lambda: nc.tensor.matmul(p2[0:64, :], lhsT=perm[:], rhs=x_[:], start=True, stop=True), r=[perm, x_], w=[p2])
                            K.dve(lambda: nc.vector.tensor_tensor(out=a_[:], in0=ps[0:64, :], in1=cosT[:, tsl], op=ALU.mult), r=[ps, cosT], w=[a_])
                            if "noperm" not in ROPE_MODE:
                                K.dve(lambda: nc.vector.tensor_tensor(out=b_[:], in0=p2[0:64, :], in1=sinT[:, tsl], op=ALU.mult), r=[p2, sinT], w=[b_])
                            else:
                                K.dve(lambda: nc.vector.tensor_tensor(out=b_[:], in0=ps[0:64, :], in1=sinT[:, tsl], op=ALU.mult), r=[ps, sinT], w=[b_])
                            if "dveadd" in ROPE_MODE:
                                K.dve(lambda: nc.vector.tensor_tensor(out=o_[:], in0=a_[:], in1=b_[:], op=ALU.add), r=[a_, b_], w=[o_])
                            else:
                                K.pool(lambda: nc.gpsimd.tensor_tensor(out=o_[:], in0=a_[:], in1=b_[:], op=ALU.add), r=[a_, b_], w=[o_])
                        else:
                            K.act(lambda: nc.scalar.copy(out=o_[:], in_=ps[0:64, :]), r=[ps], w=[o_])
                        K.dma("sp", dst[ui, :, tsl], o_[:], r=[o_], wa=[bb])
                    for tt in range(4):
                        t = g * 4 + tt
                        v_ = va[t % 2]
                        pv_ = pv[t % 2]
                        for m, c0 in ((0, D + 3 * 256), (1, D + 5 * 256)) if "v" in N1_PARTS else ():
                            for kc in range(8):
                                K.pe(lambda: nc.tensor.matmul(pv_[:, m * 256:(m + 1) * 256], lhsT=ug[:, kc, tt * 128:(tt + 1) * 128],
                                                              rhs=w[:, kc, c0:c0 + 256], start=(kc == 0), stop=(kc == 7)), r=[w, ug], w=[pv_])
                        if "v" in N1_PARTS:
                            K.dve(lambda: nc.vector.tensor_copy(out=v_[:, :, 0:64], in_=pv_[:].rearrange("p (u d) -> p u d", d=64)), r=[pv_], w=[v_])
                            K.dma("sp", self.NV[t * 128:(t + 1) * 128, :], v_[:].rearrange("p u d -> p (u d)"), r=[v_], wa=[self.b_nv])
                        g_ = gt[t % 2]
                        if "g" not in N1_PARTS:
                            continue
                        for kc in range(8):
                            K.pe(lambda: nc.tensor.matmul(pg[:, 0:48], lhsT=ug[:, kc, tt * 128:(tt + 1) * 128], rhs=w[:, kc, 2560:2608],
                                                          start=(kc == 0), stop=(kc == 7)), r=[w, ug], w=[pg])
                        K.act(lambda: nc.scalar.activation(out=g_[:], in_=pg[:, 0:48], func=AF.Sigmoid), r=[pg], w=[g_])
                        K.dma("sp", self.NGt[t * 128:(t + 1) * 128, :], g_[:], r=[g_], wa=[self.b_ngt])
                K.barrier()
            if NSA_STOP == "n1":
                return
            with contextlib.ExitStack() as ph:
                raw = K.sb(ph, "craw", [64, 8, S], BF16)
                for u_ in range(8):
                    K.dma("sp", raw[:, u_, :], self.NC[u_], r=[self.b_ncr], wa=[raw])
                K.pool(lambda: nc.gpsimd.memset(vcmp[:], 0.0), w=[vcmp])
                K.pool(lambda: nc.gpsimd.memset(kcT[:], 0.0), w=[kcT])
                hps = [K.ps(ph, f"chps{k}") for k in range(2)]
                bps = K.ps(ph, "cbps")
                ops_ = [K.ps(ph, f"cops{k}") for k in range(2)]
                p2 = K.ps(ph, "cp2")
                for kv in ("k", "v"):
                    w1 = K.sb(ph, "cw1" + kv, [64, 32, 256], BF16)
                    w1v = self.nsa_w1[kv][j].rearrange("(l d) h -> d l h", d=64)
                    for l0 in range(0, 32, 8):
                        K.dma("pool", w1[:, l0:l0 + 8, :], w1v[:, l0:l0 + 8, :], r=[self.cbuf], wa=[w1])
                    w2 = K.sb(ph, "cw2" + kv, [128, 2, 64], BF16)
                    K.dma("pool", w2[:], self.nsa_w2[kv][j].rearrange("(hc p) d -> p hc d", p=128), r=[self.cbuf], w=[w2])
                    peT = K.sb(ph, "cpeT" + kv, [64, 32], F32)
                    peTb = K.sb(ph, "cpeTb" + kv, [64, 32], BF16)
                    K.dma("sp", peT[:], self.nsa_peT[kv], r=[self.cbuf], w=[peT])
                    K.dve(lambda: nc.vector.tensor_copy(out=peTb[:], in_=peT[:]), r=[peT], w=[peTb])
                    bias = K.sb(ph, "cbias" + kv, [128, 2], F32)
                    for hc in range(2):
                        for l in range(32):
                            K.pe(lambda: nc.tensor.matmul(bps[:, hc:hc + 1], lhsT=w1[:, l, hc * 128:(hc + 1) * 128], rhs=peTb[:, l:l + 1],
                                                          start=(l == 0), stop=(l == 31)), r=[w1, peTb], w=[bps])
                        K.dve(lambda: nc.vector.tensor_copy(out=bias[:, hc:hc + 1], in_=bps[:, hc:hc + 1]), r=[bps], w=[bias])
                    xb_ = K.sb(ph, "cxb" + kv, [128, 256], F32)
                    x2_ = K.sb(ph, "cx2" + kv, [128, 256], F32)
                    x3_ = K.sb(ph, "cx3" + kv, [128, 256], F32)
                    hidT = K.sb(ph, "chid" + kv, [128, 2, 256], BF16)
                    kx = K.sb(ph, "ckx" + kv, [64, 256], BF16)
                    ka = K.sb(ph, "cka" + kv, [64, 256], F32)
                    kb_ = K.sb(ph, "ckb" + kv, [64, 256], F32)
                    for g4 in range(4):
                        ui = g4 if kv == "k" else 4 + g4
                        for hc in range(2):
                            hp = hps[hc]
                            for l in range(32):
                                K.pe(lambda: nc.tensor.matmul(hp[:, 0:255], lhsT=w1[:, l, hc * 128:(hc + 1) * 128],
                                                              rhs=raw[:, ui, l:l + 16 * 254 + 1:16], start=(l == 0), stop=(l == 31)),
                                     r=[w1, raw], w=[hp])
                            K.dve(lambda: nc.vector.tensor_scalar(out=xb_[:, 0:255], in0=hp[:, 0:255], scalar1=bias[:, hc:hc + 1], scalar2=None,
                                                                  op0=ALU.add), r=[hp, bias], w=[xb_])
                            K.pool(lambda: nc.gpsimd.tensor_tensor(out=x2_[:, 0:255], in0=xb_[:, 0:255], in1=xb_[:, 0:255], op=ALU.mult), r=[xb_], w=[x2_])
                            K.dve(lambda: nc.vector.tensor_scalar(out=x2_[:, 0:255], in0=x2_[:, 0:255], scalar1=0.044715, scalar2=1.0,
                                                                  op0=ALU.mult, op1=ALU.add), r=[x2_], w=[x2_])
                            K.dve(lambda: nc.vector.tensor_tensor(out=x3_[:, 0:255], in0=x2_[:, 0:255], in1=xb_[:, 0:255], op=ALU.mult), r=[x2_, xb_], w=[x3_])
                            K.act(lambda: nc.scalar.activation(out=x3_[:, 0:255], in_=x3_[:, 0:255], func=AF.Tanh, scale=0.7978845608028654),
                                  r=[x3_], w=[x3_])
                            K.dve(lambda: nc.vector.scalar_tensor_tensor(out=x2_[:, 0:255], in0=x3_[:, 0:255], scalar=1.0, in1=xb_[:, 0:255],
                                                                         op0=ALU.add, op1=ALU.mult), r=[x3_, xb_], w=[x2_])
                            K.pool(lambda: nc.gpsimd.tensor_scalar(out=hidT[:, hc, 0:255], in0=x2_[:, 0:255], scalar1=0.5, scalar2=None, op0=ALU.mult),
                                   r=[x2_], w=[hidT])
                        if kv == "k":
                            op_ = ops_[0]
                            for hc in range(2):
                                K.pe(lambda: nc.tensor.matmul(op_[0:64, 0:255], lhsT=w2[:, hc, :], rhs=hidT[:, hc, 0:255],
                                                              start=(hc == 0), stop=(hc == 1)), r=[w2, hidT], w=[op_])
                            K.act(lambda: nc.scalar.copy(out=kx[:, 0:255], in_=op_[0:64, 0:255]), r=[op_], w=[kx])
                            K.pe(lambda: nc.tensor.matmul(p2[0:64, 0:255], lhsT=perm[:], rhs=kx[:, 0:255], start=True, stop=True), r=[perm, kx], w=[p2])
                            K.dve(lambda: nc.vector.tensor_tensor(out=ka[:, 0:255], in0=op_[0:64, 0:255], in1=cscmp[:, 0, 0:255], op=ALU.mult),
                                  r=[op_, cscmp], w=[ka])
                            K.dve(lambda: nc.vector.tensor_tensor(out=kb_[:, 0:255], in0=p2[0:64, 0:255], in1=cscmp[:, 1, 0:255], op=ALU.mult),
                                  r=[p2, cscmp], w=[kb_])
                            K.pool(lambda: nc.gpsimd.tensor_tensor(out=kcT[:, g4, 0:255], in0=ka[:, 0:255], in1=kb_[:, 0:255], op=ALU.add),
                                   r=[ka, kb_], w=[kcT])
                        else:
                            for nch, m in ((0, 128), (1, 127)):
                                op_ = ops_[nch]
                                for hc in range(2):
                                    K.pe(lambda: nc.tensor.matmul(op_[0:m, 0:64], lhsT=hidT[:, hc, nch * 128:nch * 128 + m], rhs=w2[:, hc, :],
                                                                  start=(hc == 0), stop=(hc == 1)), r=[w2, hidT], w=[op_])
                                K.act(lambda: nc.scalar.copy(out=vcmp[0:m, nch, g4, :], in_=op_[0:m, 0:64]), r=[op_], w=[vcmp])
                K.barrier()
            if NSA_STOP == "n2":
                return
            with contextlib.ExitStack() as ph:
                with contextlib.ExitStack() as ph2:
                    self._nsa_attn(ph2, None, kcT, vcmp)
                    K.barrier()
                self.out_proj(ph, i, self.nsa_w_out[j], None, None)
                K.barrier()

    def _nsa_attn(self, ph, O, kcT, vcmp):
        K, nc = self.K, self.nc
        V = K.sb(ph, "tV", [128, NT, 8 * 65], BF16)
        NVv = self.NV.rearrange("(t p) f -> p t f", p=128)
        for t0 in range(0, NT, 8):
            K.dma("sp", V[:, t0:t0 + 8, :], NVv[:, t0:t0 + 8, :], r=[self.b_nv], wa=[V])
        GT = K.sb(ph, "tGT", [128, NT, 48], F32)
        NGv = self.NGt.rearrange("(t p) f -> p t f", p=128)
        for t0 in range(0, NT, 8):
            K.dma("sp", GT[:, t0:t0 + 8, :], NGv[:, t0:t0 + 8, :], r=[self.b_ngt], wa=[GT])
        esel = K.sb(ph, "tesel", [64, 32 * 128], BF16)
        winb = K.sb(ph, "twin01", [128, 8 * 512], BF16)
        causb = K.sb(ph, "tcaus01", [128, 512], BF16)
        Mks = [K.sb(ph, f"tMk{k}", [128, 28 + 4 * k, 512], BF16) for k in range(2)]
        Ob = [K.sb(ph, f"tOb{k}", [128, 4, 256], BF16) for k in range(2)]
        Odv = self.Od.rearrange("(t p) f -> p t f", p=128)
        band = K.sb(ph, "tband", [128, 512], BF16)
        wcm = K.sb(ph, "twcm", [128, 128], F32)
        wfb = K.sb(ph, "twfb", [128, 128], F32)
        anyok = K.sb(ph, "tanyok", [128, 1], F32)
        for t_, src in ((esel, self.c_esel), (winb, self.c_win01), (causb, self.c_caus01), (band, self.c_band),
                        (wcm, self.c_wcm), (wfb, self.c_wfb), (anyok, self.c_anyok)):
            K.dma("sp", t_[:], src, r=[self.cbuf], w=[t_])
        ks = K.sb(ph, "tks", [64, S], BF16)
        kw = K.sb(ph, "tkw", [64, S], BF16)
        qh = [K.sb(ph, f"tq{k}", [64, S], BF16) for k in range(4)]
        psg = K.sb(ph, "tpsg", [128, 4, 256], F32)
        NS = 3
        pun = [K.sb(ph, f"tpun{k}", [128, 256], F32) for k in range(NS)]
        pb = [K.sb(ph, f"tpb{k}", [128, 256], BF16) for k in range(NS)]
        pTs = [K.sb(ph, f"tpT{k}", [128, 2, 128], BF16) for k in range(NS)]
        st = [K.sb(ph, f"tst{k}", [128, 8], F32) for k in range(NS)]
        s4 = K.sb(ph, "ts4", [128, 64], F32)
        imp = K.sb(ph, "timp", [128, 64], F32)
        sc = K.sb(ph, "tsc", [128, 64], F32)
        wk = K.sb(ph, "twk", [128, 64], F32)
        m8a = K.sb(ph, "tm8a", [128, 8], F32)
        m8b = K.sb(ph, "tm8b", [128, 8], F32)
        selt = K.sb(ph, "tsel", [128, 64], F32)
        negm = [K.sb(ph, f"tnegm{k}", [128, 64], BF16) for k in range(2)]
        nmT = [K.sb(ph, f"tnmT{k}", [64, 512], BF16) for k in range(2)]
        Pt = [K.sb(ph, f"tP{k}", [128, 512], BF16) for k in range(5)]
        Oq = [K.sb(ph, f"tOq{k}", [128, 4, 256], F32) for k in range(2)]
        cf = [K.sb(ph, f"tcf{k}", [128, 8], F32) for k in range(2)]
        sps = [K.ps(ph, f"tsps{k}") for k in range(2)]
        accs = K.ps(ph, "taccs")
        accw = K.ps(ph, "taccw")
        cpsb = [K.ps(ph, f"tcps{k}") for k in range(2)]
        misc = K.ps(ph, "tmisc", (128, 1024), BF16)
        ocp = K.ps(ph, "tocp")
        items = []
        cnt = {"cmp": 0, "u": 0, "gq": 0}

        def add_loads(g):
            def f():
                K.dma("sp", ks[:], self.NK[g], r=[self.b_nk], w=[ks])
                K.dma("sp", kw[:], self.NK[4 + g], r=[self.b_nk], w=[kw])
                for r in range(4):
                    K.dma("sp", qh[r][:], self.NQ[4 * g + r], r=[self.b_nq], w=[qh[r]])
            items.append([f])

        def add_cmp(g, qc, c, r, gq):
            T_ = 4 * qc + c
            ncols = min(8 * T_ + 7, NCMP)
            b0 = 256 - 8 * T_
            h = 4 * g + r
            q_ = qh[r]
            Oq_ = Oq[gq % 2]
            chunks = [(0, min(128, ncols))] + ([(1, ncols - 128)] if ncols > 128 else [])
            stt = {}

            def s0():
                n_ = cnt["cmp"]
                cnt["cmp"] += 1
                stt["n"] = n_
                s_, pu, pb_, cps = st[n_ % NS], pun[n_ % NS], pb[n_ % NS], cpsb[n_ % 2]
                K.pe(lambda: nc.tensor.matmul(cps[:, 0:ncols], lhsT=q_[:, T_ * 128:(T_ + 1) * 128], rhs=kcT[:, g, 0:ncols],
                                              start=True, stop=False), r=[q_, kcT], w=[cps])
                K.pe(lambda: nc.tensor.matmul(cps[:, 0:ncols], lhsT=self.ident[:], rhs=band[:, b0:b0 + ncols],
                                              start=False, stop=True), r=[self.ident, band], w=[cps])
                K.dve(lambda: nc.vector.reduce_max(out=s_[:, 0:1], in_=cps[:, 0:ncols], axis=AX.X), r=[cps], w=[s_])
                K.dve(lambda: nc.vector.tensor_scalar(out=s_[:, 1:2], in0=s_[:, 0:1], scalar1=-0.125, scalar2=None, op0=ALU.mult),
                      r=[s_], w=[s_])
                K.act(lambda: nc.scalar.activation(out=pu[:, 0:ncols], in_=cps[:, 0:ncols], func=AF.Exp, scale=0.125,
                                                   bias=s_[:, 1:2], accum_out=s_[:, 2:3]), r=[cps, s_], w=[pu, s_])
                K.dve(lambda: nc.vector.reciprocal(out=s_[:, 3:4], in_=s_[:, 2:3]), r=[s_], w=[s_])
                if T_ == 0:
                    K.dve(lambda: nc.vector.tensor_tensor(out=s_[:, 3:4], in0=s_[:, 3:4], in1=anyok[:], op=ALU.mult), r=[s_, anyok], w=[s_])
                if r == 0:
                    K.dve(lambda: nc.vector.tensor_scalar(out=psg[:, c, 0:ncols], in0=pu[:, 0:ncols], scalar1=s_[:, 3:4], scalar2=None,
                                                          op0=ALU.mult), r=[pu, s_], w=[psg])
                else:
                    K.dve(lambda: nc.vector.scalar_tensor_tensor(out=psg[:, c, 0:ncols], in0=pu[:, 0:ncols], scalar=s_[:, 3:4],
                                                                 in1=psg[:, c, 0:ncols], op0=ALU.mult, op1=ALU.add), r=[pu, s_, psg], w=[psg])
                K.pool(lambda: nc.gpsimd.tensor_scalar(out=pb_[:, 0:ncols], in0=pu[:, 0:ncols], scalar1=s_[:, 3:4], scalar2=None,
                                                       op0=ALU.mult), r=[pu, s_], w=[pb_])

            def s1():
                n_ = stt["n"]
                pb_, pT = pb[n_ % NS], pTs[n_ % NS]
                for ch, wd in chunks:
                    K.pe(lambda: nc.tensor.transpose(out=misc[0:wd, ch * 128:(ch + 1) * 128], in_=pb_[:, ch * 128:ch * 128 + wd],
                                                     identity=self.ident[:]), r=[pb_, self.ident], w=[misc])
                for ch, wd in chunks:
                    K.act(lambda: nc.scalar.copy(out=pT[0:wd, ch, :], in_=misc[0:wd, ch * 128:(ch + 1) * 128]), r=[misc], w=[pT])

            def s2():
                pT = pTs[stt["n"] % NS]
                for k_, (ch, wd) in enumerate(chunks):
                    K.pe(lambda: nc.tensor.matmul(ocp[:, 0:64], lhsT=pT[0:wd, ch, :], rhs=vcmp[0:wd, ch, g, :],
                                                  start=(k_ == 0), stop=(k_ == len(chunks) - 1)), r=[pT, vcmp], w=[ocp])
                K.dve(lambda: nc.vector.tensor_scalar(out=Oq_[:, c, r * 64:(r + 1) * 64], in0=ocp[:, 0:64],
                                                      scalar1=GT[:, T_, 3 * h:3 * h + 1], scalar2=None, op0=ALU.mult),
                      r=[ocp, GT], w=[Oq_])
            items.append([s0, s1, s2])

        def add_select(g, qc, c, gq):
            T_ = 4 * qc + c
            w0 = 64 - 2 * T_
            nm_ = negm[c % 2]

            def f():
                pv4 = psg[:, c, :].rearrange("p (j f) -> p j f", f=4)
                K.dve(lambda: nc.vector.tensor_reduce(out=s4[:], in_=pv4, axis=AX.X, op=ALU.add), r=[psg], w=[s4])
                K.dve(lambda: nc.vector.scalar_tensor_tensor(out=imp[:], in0=pv4[:, :, 3], scalar=-0.5, in1=s4[:], op0=ALU.mult, op1=ALU.add),
                      r=[psg, s4], w=[imp])
                K.dve(lambda: nc.vector.scalar_tensor_tensor(out=imp[:, 1:64], in0=pv4[:, 0:63, 3], scalar=0.5, in1=imp[:, 1:64],
                                                             op0=ALU.mult, op1=ALU.add), r=[psg, imp], w=[imp])
                K.dve(lambda: nc.vector.tensor_tensor(out=sc[:], in0=imp[:], in1=wcm[:, w0:w0 + 64], op=ALU.mult), r=[imp, wcm], w=[sc])
                K.dve(lambda: nc.vector.tensor_tensor(out=sc[:], in0=sc[:], in1=wfb[:, w0:w0 + 64], op=ALU.add), r=[sc, wfb], w=[sc])
                K.dve(lambda: nc.vector.memset(sc[:, 0:1], 1.0e4), r=[sc], w=[sc])
                K.dve(lambda: nc.vector.max(out=m8a[:], in_=sc[:]), r=[sc], w=[m8a])
                K.dve(lambda: nc.vector.match_replace(out=wk[:], in_to_replace=m8a[:], in_values=sc[:], imm_value=-3.0e38), r=[sc, m8a], w=[wk])
                K.dve(lambda: nc.vector.max(out=m8b[:], in_=wk[:]), r=[wk], w=[m8b])
                K.dve(lambda: nc.vector.tensor_scalar(out=nm_[:], in0=sc[:], scalar1=m8b[:, 7:8], scalar2=None, op0=ALU.is_ge),
                      r=[sc, m8b], w=[nm_])
                K.pe(lambda: nc.tensor.transpose(out=misc[0:64, 512 + c * 128:512 + (c + 1) * 128], in_=nm_[:], identity=self.ident[:]),
                     r=[nm_, self.ident], w=[misc])
                if c == 3:
                    K.act(lambda: nc.scalar.copy(out=nmT[gq % 2][:], in_=misc[0:64, 512:1024]), r=[misc], w=[nmT[gq % 2]])
            items.append([f])

        def add_unit(g, qc, r, i_, kind, first, gq):
            a = i_ - 4 * qc
            if kind == "s":
                diag = a >= 0
                c0, c1 = (128 * a if diag else 0), 512
                kt, acc, voff = ks, accs, g * 65
            else:
                e_ = i_ - (4 * qc - 4)
                c0, c1 = (0, 128 * (e_ + 1)) if e_ < 4 else (128 * (e_ - 4), 512)
                kt, acc, voff = kw, accw, (4 + g) * 65
            W = c1 - c0
            q_ = qh[r]
            Mk = Mks[gq % 2]
            stt = {}

            def s0():
                u = cnt["u"]
                cnt["u"] += 1
                stt["u"] = u
                sp_, P = sps[u % 2], Pt[u % 5]
                qs = q_[:, qc * 512 + c0:qc * 512 + c1]
                K.pe(lambda: nc.tensor.matmul(sp_[:, 0:W], lhsT=kt[:, i_ * 128:(i_ + 1) * 128], rhs=qs, start=True, stop=True),
                     r=[kt, q_], w=[sp_])
                K.act(lambda: nc.scalar.activation(out=P[:, 0:W], in_=sp_[:, 0:W], func=AF.Exp, scale=0.125), r=[sp_], w=[P])
                if kind == "s":
                    K.dve(lambda: nc.vector.tensor_tensor(out=P[:, 0:W], in0=P[:, 0:W], in1=Mk[:, i_, 0:W], op=ALU.mult), r=[P, Mk], w=[P])
                else:
                    K.dve(lambda: nc.vector.tensor_tensor(out=P[:, 0:W], in0=P[:, 0:W], in1=winb[:, e_ * 512 + c0:e_ * 512 + c1], op=ALU.mult),
                          r=[P, winb], w=[P])

            def s1():
                P = Pt[stt["u"] % 5]
                for n_, c in enumerate(range(c0 // 128, c1 // 128)):
                    K.pe(lambda: nc.tensor.matmul(acc[:, c * 65:(c + 1) * 65], lhsT=P[:, c * 128 - c0:(c + 1) * 128 - c0],
                                                  rhs=V[:, i_, voff:voff + 65], start=(first and n_ == 0), stop=False, skip_group_check=True),
                         r=[P, V], w=[acc])
            items.append([s0, None, None, s1])

        def add_mask(g, qc, i_, gq):
            a = i_ - 4 * qc
            c0 = 128 * a if a >= 0 else 0
            W = 512 - c0
            nm = nmT[gq % 2]
            Mk = Mks[gq % 2]
            stt = {}

            def s0():
                u = cnt["u"]
                cnt["u"] += 1
                stt["u"] = u
                sp_ = sps[u % 2]
                K.pe(lambda: nc.tensor.matmul(sp_[:, 0:W], lhsT=esel[:, i_ * 128:(i_ + 1) * 128], rhs=nm[:, c0:512], start=True, stop=True),
                     r=[esel, nm], w=[sp_])

            def s1():
                sp_ = sps[stt["u"] % 2]
                if a >= 0:
                    K.dve(lambda: nc.vector.tensor_tensor(out=Mk[:, i_, 0:W], in0=sp_[:, 0:W], in1=causb[:, 0:W], op=ALU.mult),
                          r=[sp_, causb], w=[Mk])
                else:
                    K.act(lambda: nc.scalar.copy(out=Mk[:, i_, 0:W], in_=sp_[:, 0:W]), r=[sp_], w=[Mk])
            items.append([s0, s1])

        def add_combine(g, qc, r, gq):
            h = 4 * g + r
            Oq_ = Oq[gq % 2]

            def f():
                for bi, acc in ((1, accs), (2, accw)):
                    cf_ = cf[bi - 1]
                    av = acc[:, 0:260].rearrange("p (c d) -> p c d", d=65)
                    K.dve(lambda: nc.vector.reciprocal(out=cf_[:, 0:4], in_=av[:, :, 64]), r=[acc], w=[cf_])
                    K.dve(lambda: nc.vector.tensor_tensor(out=cf_[:, 4:8], in0=cf_[:, 0:4], in1=GT[:, 4 * qc:4 * qc + 4, 3 * h + bi], op=ALU.mult),
                          r=[cf_, GT], w=[cf_])
                    for c in range(4):
                        K.dve(lambda: nc.vector.scalar_tensor_tensor(out=Oq_[:, c, r * 64:(r + 1) * 64], in0=av[:, c, 0:64], scalar=cf_[:, 4 + c:5 + c],
                                                                     in1=Oq_[:, c, r * 64:(r + 1) * 64], op0=ALU.mult, op1=ALU.add),
                              r=[acc, cf_, Oq_], w=[Oq_])
                if r == 3:
                    ob_ = Ob[gq % 2]
                    K.pool(lambda: nc.gpsimd.tensor_copy(out=ob_[:], in_=Oq_[:]), r=[Oq_], w=[ob_])
                    K.dma("sp", Odv[:, 4 * qc:4 * qc + 4, g * 256:(g + 1) * 256], ob_[:], r=[ob_], wa=[self.b_od])
            items.append([None, None, None, f])

        pre, un = [], []
        for g in range(4):
            for qc in range(NG):
                gq = cnt["gq"]
                cnt["gq"] += 1
                del items[:]
                if qc == 0:
                    add_loads(g)
                items.append([lambda: K.pool(lambda: nc.gpsimd.memset(psg[:], 0.0), w=[psg])])
                for c in range(4):
                    for r in range(4):
                        add_cmp(g, qc, c, r, gq)
                    add_select(g, qc, c, gq)
                for i_ in range(0, 4 * qc + 4):
                    add_mask(g, qc, i_, gq)
                items.append([lambda: None])
                pre.append(list(items))
                del items[:]
                for r in range(4):
                    first = True
                    for i_ in range(0, 4 * qc + 4):
                        add_unit(g, qc, r, i_, "s", first, gq)
                        first = False
                    first = True
                    for i_ in range(max(0, 4 * qc - 4), 4 * qc + 4):
                        add_unit(g, qc, r, i_, "w", first, gq)
                        first = False
                    add_combine(g, qc, r, gq)
                un.append(list(items))
        final = list(pre[0])
        ngq = len(un)
        for gq in range(ngq):
            nxt = pre[gq + 1] if gq + 1 < ngq else []
            if not nxt or (gq + 1) % NG == 0 or not NSA_INTERLEAVE:
                final += un[gq]
                final += [[lambda: None]] * 4
                final += nxt
            else:
                a_, b_ = un[gq], nxt
                bi = 0
                for ai, it in enumerate(a_):
                    final.append(it)
                    tgt = ((ai + 1) * len(b_)) // len(a_)
                    while bi < tgt:
                        final.append(b_[bi])
                        bi += 1
                final += b_[bi:]
        del items[:]
        items.extend(final)
        run_pipeline(items)


    def phase_conv(self, i, j):
        K, nc = self.K, self.nc
        UTv = self.UT.rearrange("(kc p) t -> p kc t", p=128)
        with contextlib.ExitStack() as ph:
            w = K.sb(ph, "cw", [128, 8, 2 * D], BF16)
            self.load_w(w, self.cv_w_in[j], 8, 2 * D)
            bcol = K.sb(ph, "cbcol", [128, 16], F32)
            K.dma("sp", bcol[:], self.cv_b_inT, r=[self.cbuf], w=[bcol])
            utg = [K.sb(ph, f"cutg{k}", [128, 8, 512], BF16) for k in range(2)]
            sgt = [K.sb(ph, f"csg{k}", [128, 512], F32) for k in range(2)]
            hgt = [K.sb(ph, f"chg{k}", [128, 512], BF16) for k in range(3)]
            psa = [K.ps(ph, f"cpsa{k}") for k in range(2)]
            psg = [K.ps(ph, f"cpsg{k}") for k in range(2)]

            def load_u(g):
                K.dma("sp", utg[g % 2][:], UTv[:, :, g * 512:(g + 1) * 512], r=[self.b_ut[g]], w=[utg[g % 2]])

            load_u(0)
            n = 0
            for g in range(NG):
                if g + 1 < NG:
                    load_u(g + 1)
                ug = utg[g % 2]
                for cc in range(8):
                    pa, pg = psa[n % 2], psg[n % 2]
                    sg, hg = sgt[n % 2], hgt[n % 3]
                    n += 1
                    for kc in range(8):
                        K.pe(lambda: nc.tensor.matmul(pa[:], lhsT=w[:, kc, cc * 128:(cc + 1) * 128], rhs=ug[:, kc, :],
                                                      start=(kc == 0), stop=(kc == 7)), r=[w, ug], w=[pa])
                    for kc in range(8):
                        K.pe(lambda: nc.tensor.matmul(pg[:], lhsT=w[:, kc, D + cc * 128:D + (cc + 1) * 128], rhs=ug[:, kc, :],
                                                      start=(kc == 0), stop=(kc == 7)), r=[w, ug], w=[pg])
                    K.act(lambda: nc.scalar.activation(out=sg[:], in_=pg[:], func=AF.Sigmoid, bias=bcol[:, 8 + cc:9 + cc]),
                          r=[pg, bcol], w=[sg])
                    K.dve(lambda: nc.vector.scalar_tensor_tensor(out=hg[:], in0=pa[:], scalar=bcol[:, cc:cc + 1], in1=sg[:],
                                                                 op0=ALU.add, op1=ALU.mult), r=[pa, bcol, sg], w=[hg])
                    K.dma("sp", self.HG[cc * 128:(cc + 1) * 128, g * 512:(g + 1) * 512], hg[:], r=[hg], wa=[self.b_hg[g]])
            K.barrier()
        with contextlib.ExitStack() as ph:
            w = K.sb(ph, "cow", [128, 8, D], BF16)
            self.load_w(w, self.cv_w_out[j], 8, D, split=1024)
            brow = K.sb(ph, "cobrow", [1, D], BF16)
            K.dma("pool", brow[:], self.cv_b_out[j:j + 1, :], r=[self.cbuf], w=[brow])
            dw = K.sb(ph, "cdw", [128, 8, 31], F32)
            vec = K.sb(ph, "cvec", [128, 3, 8], F32)
            onesf = K.sb(ph, "conesf", [128, 128], F32)
            K.dma("sp", dw[:], self.cv_dwT, r=[self.cbuf], w=[dw])
            K.dma("sp", vec[:], self.cv_vecT, r=[self.cbuf], w=[vec])
            K.dma("sp", onesf[:], self.c_onesf, r=[self.cbuf], w=[onesf])
            e = self.epilogue_setup(ph, i, 0)
            xin = [K.sb(ph, f"cxin{k}", [128, 8, 542], BF16) for k in range(2)]
            Dg = K.sb(ph, "cDg", [128, 8, 31, 128], BF16)
            for cc in range(8):
                for k in range(31):
                    if (cc * 31 + k) % 3 == 2:
                        K.pool(lambda: nc.gpsimd.tensor_scalar(out=Dg[:, cc, k, :], in0=self.ident[:], scalar1=dw[:, cc, k:k + 1], scalar2=None,
                                                               op0=ALU.mult), r=[self.ident, dw], w=[Dg])
                    else:
                        K.dve(lambda: nc.vector.tensor_scalar(out=Dg[:, cc, k, :], in0=self.ident[:], scalar1=dw[:, cc, k:k + 1], scalar2=None,
                                                              op0=ALU.mult), r=[self.ident, dw], w=[Dg])
            cvp = [K.ps(ph, f"ccvp{k}") for k in range(2)]
            acc = K.sb(ph, "cacc", [128, 8, 512], F32)
            sq = [K.sb(ph, f"csq{k}", [128, 512], F32) for k in range(2)]
            mt = K.sb(ph, "cm", [128, 512], F32)
            msq = K.sb(ph, "cmsq", [128, 512], F32)
            var = K.sb(ph, "cvar", [128, 512], F32)
            rstd = K.sb(ph, "crstd", [128, 512], F32)
            dt_ = [K.sb(ph, f"cd{k}", [128, 512], F32) for k in range(2)]
            xh = [K.sb(ph, f"cxh{k}", [128, 512], F32) for k in range(2)]
            hT = K.sb(ph, "chT", [128, 8, 512], BF16)
            s1 = K.ps(ph, "cs1")
            s2 = K.ps(ph, "cs2")
            psY = [[K.ps(ph, f"cpsY{k}{hf}") for hf in range(2)] for k in range(2)]
            HGv = self.HG.rearrange("(cc p) t -> p cc t", p=128)

            def load_x(g):
                xt = xin[g % 2]
                if g == 0:
                    K.pool(lambda: nc.gpsimd.memset(xt[:, :, 0:30], 0.0), w=[xt])
                    K.dma("sp", xt[:, :, 30:542], HGv[:, :, 0:512], r=[self.b_hg[0]], wa=[xt])
                else:
                    K.dma("sp", xt[:], HGv[:, :, g * 512 - 30:g * 512 + 512], r=[self.b_hg[g - 1], self.b_hg[g]], w=[xt])

            load_x(0)
            for g in range(NG):
                if g + 1 < NG:
                    load_x(g + 1)
                xt = xin[g % 2]
                for cc in range(8):
                    cv = cvp[cc % 2]
                    for k in range(31):
                        K.pe(lambda: nc.tensor.matmul(cv[:], lhsT=Dg[:, cc, k, :], rhs=xt[:, cc, k:k + 512], start=(k == 0), stop=(k == 30)),
                             r=[Dg, xt], w=[cv])
                    sq_ = sq[cc % 2]
                    K.act(lambda: nc.scalar.activation(out=sq_[:], in_=cv[:], func=AF.Square, bias=vec[:, 0, cc:cc + 1]), r=[cv, vec], w=[sq_])
                    K.dve(lambda: nc.vector.tensor_scalar(out=acc[:, cc, :], in0=cv[:], scalar1=vec[:, 0, cc:cc + 1], scalar2=None, op0=ALU.add),
                          r=[cv, vec], w=[acc])
                    K.pe(lambda: nc.tensor.matmul(s1[:], lhsT=onesf[:], rhs=acc[:, cc, :], start=(cc == 0), stop=(cc == 7)),
                         r=[onesf, acc], w=[s1])
                    K.pe(lambda: nc.tensor.matmul(s2[:], lhsT=onesf[:], rhs=sq_[:], start=(cc == 0), stop=(cc == 7)),
                         r=[onesf, sq_], w=[s2])
                K.dve(lambda: nc.vector.tensor_scalar(out=mt[:], in0=s1[:], scalar1=1.0 / D, scalar2=None, op0=ALU.mult), r=[s1], w=[mt])
                K.pool(lambda: nc.gpsimd.tensor_tensor(out=msq[:], in0=mt[:], in1=mt[:], op=ALU.mult), r=[mt], w=[msq])
                K.dve(lambda: nc.vector.scalar_tensor_tensor(out=var[:], in0=s2[:], scalar=1.0 / D, in1=msq[:], op0=ALU.mult, op1=ALU.subtract),
                      r=[s2, msq], w=[var])
                K.act(lambda: nc.scalar.activation(out=var[:], in_=var[:], func=AF.Sqrt, bias=EPS), r=[var], w=[var])
                K.dve(lambda: nc.vector.reciprocal(out=rstd[:], in_=var[:]), r=[var], w=[rstd])
                for cc in range(8):
                    d_, x_ = dt_[cc % 2], xh[cc % 2]
                    K.pool(lambda: nc.gpsimd.tensor_tensor(out=d_[:], in0=acc[:, cc, :], in1=mt[:], op=ALU.subtract), r=[acc, mt], w=[d_])
                    K.dve(lambda: nc.vector.tensor_tensor(out=x_[:], in0=d_[:], in1=rstd[:], op=ALU.mult), r=[d_, rstd], w=[x_])
                    K.act(lambda: nc.scalar.activation(out=hT[:, cc, :], in_=x_[:], func=AF.Silu, scale=vec[:, 1, cc:cc + 1],
                                                       bias=vec[:, 2, cc:cc + 1]), r=[x_, vec], w=[hT])
                for tt in range(4):
                    t = g * 4 + tt
                    self.epi_load(e, t)
                    yb = psY[tt % 2]
                    for hf in range(2):
                        for cc in range(8):
                            K.pe(lambda: nc.tensor.matmul(yb[hf][:], lhsT=hT[:, cc, tt * 128:(tt + 1) * 128], rhs=w[:, cc, hf * 512:(hf + 1) * 512],
                                                          start=(cc == 0), stop=False), r=[hT, w], w=[yb[hf]])
                        K.pe(lambda: nc.tensor.matmul(yb[hf][:], lhsT=self.ones[0:1, :], rhs=brow[0:1, hf * 512:(hf + 1) * 512],
                                                      start=False, stop=True), r=[self.ones, brow], w=[yb[hf]])
                    self.epilogue(e, t, yb)
            K.barrier()


def run_pipeline(items):
    n = len(items)
    depth = max(len(it) for it in items)
    for t in range(n + depth - 1):
        for j in range(depth):
            k = t - j
            if 0 <= k < n and j < len(items[k]) and items[k][j] is not None:
                items[k][j]()


def host_consts():
    bf = ml_dtypes.bfloat16
    p = np.arange(128)[:, None]
    y = np.arange(512)[None, :]
    c = {}
    c["c_ident"] = np.eye(128, dtype=np.float32).astype(bf)
    jj = np.arange(128)[:, None]
    ss = np.arange(128)[None, :]
    c["c_tri"] = (jj >= ss).astype(np.float32).astype(bf)
    c["c_ones"] = np.ones((128, 128), np.float32).astype(bf)
    c["c_sbmask"] = (y > p).astype(np.float32).astype(bf)
    c["c_sbbias"] = np.where(y > p, 0.0, BIG).astype(np.float32).astype(bf)
    c["c_onesf"] = np.ones((128, 128), np.float32)
    perm = np.zeros((64, 64), np.float32)
    for i in range(8):
        perm[i + 8, i] = -1.0
        perm[i, i + 8] = 1.0
    c["c_perm"] = perm.astype(bf)
    invf = np.zeros((64, 1), np.float32)
    fr = (500000.0 ** (-np.arange(8, dtype=np.float32) / 8.0)).astype(np.float32)
    invf[0:8, 0] = fr
    invf[8:16, 0] = fr
    c["c_invf"] = invf
    jj = np.arange(64)[:, None, None]
    ii = np.arange(32)[None, :, None]
    sk = np.arange(128)[None, None, :]
    c["c_esel"] = (jj == 2 * ii + (sk >= 64)).astype(np.float32).reshape(64, 32 * 128).astype(bf)
    e = np.arange(8)[None, :, None]
    f = np.arange(512)[None, None, :]
    pp = np.arange(128)[:, None, None]
    dlt = f - pp + 512 - 128 * e
    c["c_win01"] = ((dlt >= 0) & (dlt < 512)).astype(np.float32).reshape(128, 8 * 512).astype(bf)
    c["c_caus01"] = (y >= p).astype(np.float32).astype(bf)
    m = np.arange(512)[None, :] - 256
    c["c_band"] = np.where(p >= 16 * m + 31, 0.0, -BIG).astype(np.float32).astype(bf)
    col = np.arange(128)[None, :]
    dl = (col - 64) - (p >= 64)
    c["c_wcm"] = (dl < -1).astype(np.float32)
    c["c_wfb"] = np.where(dl > 0, -1.0e9, np.where(dl >= -1, 1.0e4, 0.0)).astype(np.float32)
    c["c_anyok"] = (np.arange(128)[:, None] >= 31).astype(np.float32)
    return c


def make_in_map(inputs, b, prog):
    m = {}
    m["x"] = np.ascontiguousarray(inputs["x"][b])
    m["cT"] = np.ascontiguousarray(np.asarray(inputs["c"][b]).reshape(8, 128).T)
    m["pos"] = np.ascontiguousarray(np.asarray(inputs["positions"][b]).reshape(1, S).astype(np.int32))
    for n in ("ada_w", "ada_b", "mix_pre_g", "mix_post_g", "ffn_pre_g", "ffn_post_g", "ffn_w1", "ffn_w2", "sb_w_in", "sb_w_out",
              "nsa_w_in", "nsa_w_out", "nsa_w1_k", "nsa_w2_k", "nsa_w1_v", "nsa_w2_v",
              "cv_w_in", "cv_w_out", "cv_b_out"):
        m[n] = np.asarray(inputs[n])
    m["nsa_pe_kT"] = np.ascontiguousarray(np.asarray(inputs["nsa_pe_k"])[0].T)
    m["nsa_pe_vT"] = np.ascontiguousarray(np.asarray(inputs["nsa_pe_v"])[0].T)
    m["cv_b_inT"] = np.ascontiguousarray(np.asarray(inputs["cv_b_in"]).reshape(16, 128).T)
    m["cv_dwT"] = np.ascontiguousarray(np.asarray(inputs["cv_dw"]).reshape(31, 8, 128).transpose(2, 1, 0))
    m["cv_vecT"] = np.ascontiguousarray(np.stack([np.asarray(inputs[k]).reshape(8, 128).T for k in ("cv_dw_b", "cv_ln_g", "cv_ln_b")], axis=1))
    m.update(host_consts())
    return {k: v for k, v in m.items() if k in prog.in_names}


_PROG_CACHE = {}


def kernel(**inputs):
    prog = Prog()
    nc = prog.build()
    B = inputs["x"].shape[0]
    in_maps = [make_in_map(inputs, b, prog) for b in range(B)]
    res = run_bass_kernel_spmd(nc, in_maps, core_ids=list(range(B)))
    return np.stack([np.asarray(r["out"]) for r in res.results], axis=0).astype(np.float32)
```

```python
import contextlib
import os
import numpy as np
import ml_dtypes
import concourse.bass as bass
import concourse.mybir as mybir
from concourse.bass_utils import run_bass_kernel_spmd

F32 = mybir.dt.float32
BF16 = mybir.dt.bfloat16
I32 = mybir.dt.int32
AF = mybir.ActivationFunctionType
ALU = mybir.AluOpType
AX = mybir.AxisListType

D = 1024
S = 4096
DEPTH = 4
NH = 16
DH = 64
DFF = 4096
EPS = 1e-6
NT = S // 128
NG = S // 512
NSA_IN = 2608
NCMP = 255
BIG = 30000.0
NSA_STOP = os.environ.get("NSA_STOP", "")
NSA_INTERLEAVE = bool(int(os.environ.get("NSA_INTERLEAVE", "0")))
N1_PARTS = os.environ.get("N1_PARTS", "urvg")
ROPE_MODE = os.environ.get("ROPE_MODE", "")


class Src:
    __slots__ = ("sem", "count", "name")

    def __init__(self, sem, name):
        self.sem = sem
        self.count = 0
        self.name = name


class Buf:
    __slots__ = ("name", "w", "r", "const", "excl")

    def __init__(self, name, const=False):
        self.name = name
        self.excl = False
        self.w = []
        self.r = []
        self.const = const


class T:
    __slots__ = ("t", "b")

    def __init__(self, t, name):
        self.t = t
        self.b = Buf(name)

    def __getitem__(self, idx):
        return self.t[idx]


class KB:
    def __init__(self):
        self.nc = bass.Bass("TRN2", target_bir_lowering=False)
        nc = self.nc
        self.st = contextlib.ExitStack()
        self.eng = {"pe": nc.tensor, "act": nc.scalar, "dve": nc.vector, "pool": nc.gpsimd, "sp": nc.sync}
        self.src = {}
        for n in self.eng:
            self.src[n] = Src(self.st.enter_context(nc.semaphore("sem_" + n)), n)
        self.dslots = {}
        self.dnext = {}
        for q, k in (("sp", 12), ("pool", 6), ("act", 4)):
            self.dslots[q] = [Src(self.st.enter_context(nc.semaphore(f"dq_{q}{i}")), f"dq_{q}{i}") for i in range(k)]
            self.dnext[q] = 0
        self.seen = {n: {} for n in self.eng}
        self.ninstr = 0

    def _wait(self, en, tok):
        src, val = tok
        if val <= 0:
            return
        if en == "pe" and src is self.src["pe"]:
            return
        seen = self.seen[en]
        if seen.get(src, 0) >= val:
            return
        self.eng[en].wait_ge(src.sem, val)
        seen[src] = val
        self.ninstr += 1

    def _deps(self, en, r, w, wa=()):
        for x in r:
            b = x.b if isinstance(x, T) else x
            for tok in b.w:
                self._wait(en, tok)
            if b.excl:
                own = self.src.get(en)
                for tok in b.r:
                    if tok[0] is not own:
                        self._wait(en, tok)
        for x in w:
            b = x.b if isinstance(x, T) else x
            for tok in b.w:
                self._wait(en, tok)
            for tok in b.r:
                self._wait(en, tok)
        for x in wa:
            b = x.b if isinstance(x, T) else x
            for tok in b.r:
                self._wait(en, tok)

    def _mark(self, tok, r, w, wa=()):
        for x in r:
            b = x.b if isinstance(x, T) else x
            if not b.const:
                b.r.append(tok)
        for x in w:
            b = x.b if isinstance(x, T) else x
            b.w = [tok]
            b.r = []
        for x in wa:
            b = x.b if isinstance(x, T) else x
            if b.r:
                b.w = []
                b.r = []
            b.w.append(tok)

    def op(self, en, fn, r=(), w=()):
        self._deps(en, r, w)
        ins = fn()
        s = self.src[en]
        s.count += 1
        ins.then_inc(s.sem, 1)
        self._mark((s, s.count), r, w)
        self.ninstr += 1
        return ins

    def pe(self, fn, r=(), w=()):
        return self.op("pe", fn, r, w)

    def act(self, fn, r=(), w=()):
        return self.op("act", fn, r, w)

    def dve(self, fn, r=(), w=()):
        return self.op("dve", fn, r, w)

    def pool(self, fn, r=(), w=()):
        return self.op("pool", fn, r, w)

    def dma(self, q, out, in_, r=(), w=(), wa=()):
        self._deps(q, r, w, wa)
        slots = self.dslots[q]
        sl = slots[self.dnext[q] % len(slots)]
        self.dnext[q] += 1
        self._wait(q, (sl, sl.count))
        ins = self.eng[q].dma_start(out=out, in_=in_)
        sl.count += 16
        ins.then_inc(sl.sem, 16)
        self._mark((sl, sl.count), r, w, wa)
        self.ninstr += 1
        return ins

    def barrier(self):
        toks = [(s, s.count) for s in self.src.values()]
        for q in self.dslots:
            toks += [(s, s.count) for s in self.dslots[q]]
        for en in self.eng:
            for tok in toks:
                if tok[0] is self.src[en]:
                    continue
                self._wait(en, tok)

    def _uniq(self, name):
        self.nalloc = getattr(self, "nalloc", 0) + 1
        return f"{name}_{self.nalloc}"

    def sb(self, ctx, name, shape, dtype):
        name = self._uniq("s_" + name)
        return T(ctx.enter_context(self.nc.sbuf_tensor(name, list(shape), dtype)), name)

    def ps(self, ctx, name, shape=(128, 512), dtype=F32):
        name = self._uniq("p_" + name)
        t = T(ctx.enter_context(self.nc.psum_tensor(name, list(shape), dtype)), name)
        t.b.excl = True
        return t

    def dram(self, name, shape, dtype, kind="Internal"):
        return self.nc.dram_tensor(name, list(shape), dtype, kind=kind).ap()


class Prog:
    def __init__(self, layers=(0, 1, 2, 3), debug=None):
        self.K = KB()
        self.nc = self.K.nc
        self.layers = tuple(layers)
        self.debug = debug or {}
        self.in_names = []
        self._declare_io()

    def _in(self, name, shape, dtype=F32):
        self.in_names.append(name)
        return self.K.dram(name, shape, dtype, kind="ExternalInput")

    def _declare_io(self):
        K = self.K
        self.x = self._in("x", [S, D])
        self.cT = self._in("cT", [128, 8])
        self.pos = self._in("pos", [1, S], I32)
        self.ada_w = self._in("ada_w", [DEPTH, D, 6 * D])
        self.ada_b = self._in("ada_b", [DEPTH, 6 * D])
        self.gains = {n: self._in(n, [DEPTH, D]) for n in ("mix_pre_g", "mix_post_g", "ffn_pre_g", "ffn_post_g")}
        self.ffn_w1 = self._in("ffn_w1", [DEPTH, D, DFF])
        self.ffn_w2 = self._in("ffn_w2", [DEPTH, DFF, D])
        self.sb_w_in = self._in("sb_w_in", [2, D, 3 * D])
        self.sb_w_out = self._in("sb_w_out", [2, D, D])
        self.nsa_w_in = self._in("nsa_w_in", [1, D, NSA_IN])
        self.nsa_w_out = self._in("nsa_w_out", [1, D, D])
        self.nsa_peT = {"k": self._in("nsa_pe_kT", [64, 32]), "v": self._in("nsa_pe_vT", [64, 32])}
        self.nsa_w1 = {"k": self._in("nsa_w1_k", [1, 2048, 256]), "v": self._in("nsa_w1_v", [1, 2048, 256])}
        self.nsa_w2 = {"k": self._in("nsa_w2_k", [1, 256, 64]), "v": self._in("nsa_w2_v", [1, 256, 64])}
        self.cv_w_in = self._in("cv_w_in", [1, D, 2 * D])
        self.cv_b_inT = self._in("cv_b_inT", [128, 16])
        self.cv_dwT = self._in("cv_dwT", [128, 8, 31])
        self.cv_vecT = self._in("cv_vecT", [128, 3, 8])
        self.cv_w_out = self._in("cv_w_out", [1, D, D])
        self.cv_b_out = self._in("cv_b_out", [1, D])
        self.c_ident = self._in("c_ident", [128, 128], BF16)
        self.c_tri = self._in("c_tri", [128, 128], BF16)
        self.c_ones = self._in("c_ones", [128, 128], BF16)
        self.c_sbmask = self._in("c_sbmask", [128, 512], BF16)
        self.c_sbbias = self._in("c_sbbias", [128, 512], BF16)
        self.c_onesf = self._in("c_onesf", [128, 128], F32)
        self.c_perm = self._in("c_perm", [64, 64], BF16)
        self.c_invf = self._in("c_invf", [64, 1], F32)
        self.c_esel = self._in("c_esel", [64, 32 * 128], BF16)
        self.c_win01 = self._in("c_win01", [128, 8 * 512], BF16)
        self.c_caus01 = self._in("c_caus01", [128, 512], BF16)
        self.c_band = self._in("c_band", [128, 512], BF16)
        self.c_wcm = self._in("c_wcm", [128, 128], F32)
        self.c_wfb = self._in("c_wfb", [128, 128], F32)
        self.c_anyok = self._in("c_anyok", [128, 1], F32)
        self.out = K.dram("out", [S, D], F32, kind="ExternalOutput")
        self.modv = K.dram("modv", [DEPTH, 6 * D], F32)
        self.UT = K.dram("UT", [D, S], BF16)
        self.QT = K.dram("QT", [D, S], BF16)
        self.KT = K.dram("KT", [D, S], BF16)
        self.Vd = K.dram("Vd", [S, D], BF16)
        self.HG = K.dram("HG", [D, S], BF16)
        self.NQ = K.dram("NQ", [16, 64, S], BF16)
        self.NK = K.dram("NK", [8, 64, S], BF16)
        self.NC = K.dram("NC", [8, 64, S], BF16)
        self.NV = K.dram("NV", [S, 2 * 4 * 65], BF16)
        self.NGt = K.dram("NGt", [S, 48], F32)
        self.Od = self.Vd
        self.b_od = Buf("od")
        self.b_nq = Buf("nq"); self.b_nk = Buf("nk"); self.b_ncr = Buf("ncr"); self.b_nv = Buf("nv"); self.b_ngt = Buf("ngt")
        self.b_h = [Buf(f"h{t}") for t in range(NT)]
        self.b_ut = [Buf(f"ut{g}") for g in range(NG)]
        self.b_modv = [Buf(f"modv{i}") for i in range(DEPTH)]
        self.b_qt = [Buf(f"qt{g}") for g in range(NG)]
        self.b_kt = [Buf(f"kt{g}") for g in range(NG)]
        self.b_vd = [Buf(f"vd{t}") for t in range(NT)]
        self.b_hg = [Buf(f"hg{g}") for g in range(NG)]
        self.cbuf = Buf("const", const=True)
        for dbg_name, shape in self.debug.items():
            setattr(self, "dbg_" + dbg_name, K.dram("dbg_" + dbg_name, shape, F32, kind="ExternalOutput"))

    def load_consts(self):
        K, nc = self.K, self.nc
        st = K.st
        self.ident = K.sb(st, "ident", [128, 128], BF16)
        self.tri = K.sb(st, "tri", [128, 128], BF16)
        self.ones = K.sb(st, "ones", [128, 128], BF16)
        for t, src in ((self.ident, self.c_ident), (self.tri, self.c_tri), (self.ones, self.c_ones)):
            K.dma("sp", t[:], src, r=[self.cbuf], w=[t])
            t.b.const = True

    def phase_mod(self):
        K, nc = self.K, self.nc
        with contextlib.ExitStack() as ph:
            cT = K.sb(ph, "cT", [128, 8], F32)
            sig = K.sb(ph, "sig", [128, 8], F32)
            cond = K.sb(ph, "cond", [128, 8], F32)
            wsl = [K.sb(ph, f"adaw{i}", [128, 8, 512], F32) for i in range(2)]
            brow = K.sb(ph, "brow", [1, 6 * D], F32)
            mrow = K.sb(ph, "mrow", [1, 6 * D], F32)
            pss = [K.ps(ph, f"modps{i}") for i in range(2)]
            K.dma("sp", cT[:], self.cT, r=[self.cbuf], w=[cT])
            K.act(lambda: nc.scalar.activation(out=sig[:], in_=cT[:], func=AF.Sigmoid), r=[cT], w=[sig])
            K.dve(lambda: nc.vector.tensor_tensor(out=cond[:], in0=cT[:], in1=sig[:], op=ALU.mult), r=[cT, sig], w=[cond])
            it = 0
            for i in self.layers:
                K.dma("sp", brow[:], self.ada_b[i:i + 1, :], r=[self.cbuf], w=[brow])
                wv = self.ada_w[i].rearrange("(kc p) n -> p kc n", p=128)
                for n in range(12):
                    wt = wsl[it % 2]
                    ps = pss[it % 2]
                    it += 1
                    K.dma("sp", wt[:], wv[:, :, n * 512:(n + 1) * 512], r=[self.cbuf], w=[wt])
                    for kc in range(8):
                        K.pe(lambda kc=kc: nc.tensor.matmul(ps[0:1, :], lhsT=cond[:, kc:kc + 1], rhs=wt[:, kc, :],
                                                            start=(kc == 0), stop=(kc == 7)), r=[cond, wt], w=[ps])
                    K.dve(lambda n=n: nc.vector.tensor_tensor(out=mrow[0:1, n * 512:(n + 1) * 512], in0=ps[0:1, :],
                                                              in1=brow[0:1, n * 512:(n + 1) * 512], op=ALU.add),
                          r=[ps, brow], w=[mrow])
                K.dma("sp", self.modv[i:i + 1, :], mrow[:], r=[mrow], w=[self.b_modv[i]])
            K.barrier()

    def load_vec(self, ph, name, src_row):
        t = self.K.sb(ph, name, [128, D], F32)
        return t

    def mod_slice(self, i, k):
        return self.modv[i:i + 1, k * D:(k + 1) * D]

    def bcast_load(self, t, src_row, rbufs):
        self.K.dma("sp", t[:], src_row.to_broadcast([128, D]), r=rbufs, w=[t])

    def phase_norm(self, i, which, src_is_x):
        K, nc = self.K, self.nc
        gname = "mix_pre_g" if which == 0 else "ffn_pre_g"
        ksh, ksc = (0, 1) if which == 0 else (3, 4)
        src = self.x if src_is_x else self.out
        with contextlib.ExitStack() as ph:
            A = K.sb(ph, "nA", [128, D], F32)
            B = K.sb(ph, "nB", [128, D], F32)
            Gn = K.sb(ph, "nG", [128, D], F32)
            self.bcast_load(A, self.mod_slice(i, ksc), [self.b_modv[i]])
            self.bcast_load(B, self.mod_slice(i, ksh), [self.b_modv[i]])
            self.bcast_load(Gn, self.gains[gname][i:i + 1, :], [self.cbuf])
            K.dve(lambda: nc.vector.scalar_tensor_tensor(out=A[:], in0=A[:], scalar=1.0, in1=Gn[:], op0=ALU.add, op1=ALU.mult),
                  r=[A, Gn], w=[A])
            NHS = 6
            hs = [K.sb(ph, f"nh{j}", [128, D], F32) for j in range(NHS)]
            junk = K.sb(ph, "njunk", [128, D], BF16)
            tmp = [K.sb(ph, f"ntmp{j}", [128, D], F32) for j in range(2)]
            ub = [K.sb(ph, f"nub{j}", [128, D], BF16) for j in range(3)]
            st = [K.sb(ph, f"nst{j}", [128, 4], F32) for j in range(2)]
            utg = [K.sb(ph, f"nutg{j}", [128, 8, 512], BF16) for j in range(2)]
            pst = [K.ps(ph, f"npst{j}", (128, 1024), BF16) for j in range(2)]
            UTv = self.UT.rearrange("(kc p) t -> p kc t", p=128)

            def load(t):
                K.dma("sp", hs[t % NHS][:], src[t * 128:(t + 1) * 128, :], r=[self.b_h[t]], w=[hs[t % NHS]])

            for t in range(NHS - 1):
                load(t)
            items = []
            for t in range(NT):
                def s0(t=t):
                    if t + NHS - 1 < NT:
                        load(t + NHS - 1)
                    h, s_, tm, u = hs[t % NHS], st[t % 2], tmp[t % 2], ub[t % 3]
                    K.act(lambda: nc.scalar.activation(out=junk[:], in_=h[:], func=AF.Square, accum_out=s_[:, 0:1]),
                          r=[h], w=[junk, s_])
                    K.act(lambda: nc.scalar.activation(out=s_[:, 1:2], in_=s_[:, 0:1], func=AF.Sqrt, scale=1.0 / D, bias=EPS),
                          r=[s_], w=[s_])
                    K.dve(lambda: nc.vector.reciprocal(out=s_[:, 2:3], in_=s_[:, 1:2]), r=[s_], w=[s_])
                    K.dve(lambda: nc.vector.scalar_tensor_tensor(out=tm[:], in0=h[:], scalar=s_[:, 2:3], in1=A[:], op0=ALU.mult, op1=ALU.mult),
                          r=[h, s_, A], w=[tm])
                    K.dve(lambda: nc.vector.tensor_tensor(out=u[:], in0=tm[:], in1=B[:], op=ALU.add), r=[tm, B], w=[u])

                def s1(t=t):
                    u, pt = ub[t % 3], pst[t % 2]
                    g = t // 4
                    ug = utg[g % 2]
                    for kc in range(8):
                        K.pe(lambda: nc.tensor.transpose(out=pt[:, kc * 128:(kc + 1) * 128], in_=u[:, kc * 128:(kc + 1) * 128],
                                                         identity=self.ident[:]), r=[u, self.ident], w=[pt])
                    tt = t % 4
                    K.act(lambda: nc.scalar.copy(out=ug[:, :, tt * 128:(tt + 1) * 128],
                                                 in_=pt[:].rearrange("p (kc t) -> p kc t", kc=8)), r=[pt], w=[ug])
                    if tt == 3:
                        K.dma("sp", UTv[:, :, g * 512:(g + 1) * 512], ug[:], r=[ug], w=[self.b_ut[g]])
                items.append([s0, s1])
            run_pipeline(items)
            K.barrier()

    def epilogue_setup(self, ph, i, which):
        K, nc = self.K, self.nc
        gname = "mix_post_g" if which == 0 else "ffn_post_g"
        kg = 2 if which == 0 else 5
        G = K.sb(ph, "eG", [128, D], F32)
        Gn = K.sb(ph, "eGn", [128, D], F32)
        self.bcast_load(G, self.mod_slice(i, kg), [self.b_modv[i]])
        self.bcast_load(Gn, self.gains[gname][i:i + 1, :], [self.cbuf])
        K.dve(lambda: nc.vector.tensor_tensor(out=G[:], in0=G[:], in1=Gn[:], op=ALU.mult), r=[G, Gn], w=[G])
        e = {
            "G": G,
            "h": [K.sb(ph, f"eh{j}", [128, D], F32) for j in range(2)],
            "tmp": [K.sb(ph, f"etmp{j}", [128, 512], F32) for j in range(2)],
            "junk": K.sb(ph, "ejunk", [128, 512], BF16),
            "st": [K.sb(ph, f"est{j}", [128, 8], F32) for j in range(2)],
            "n": 0,
            "src_is_x": False,
        }
        return e

    def epi_load(self, e, t):
        src = self.x if self.resid_from_x else self.out
        h = e["h"][t % 2]
        self.K.dma("sp", h[:], src[t * 128:(t + 1) * 128, :], r=[self.b_h[t]], w=[h])

    def epilogue(self, e, t, ybanks):
        K, nc = self.K, self.nc
        h = e["h"][t % 2]
        s_ = e["st"][t % 2]
        junk = e["junk"]
        G = e["G"]
        for hf in range(2):
            K.act(lambda hf=hf: nc.scalar.activation(out=junk[:], in_=ybanks[hf][:], func=AF.Square, accum_out=s_[:, hf:hf + 1]),
                  r=[ybanks[hf]], w=[junk, s_])
        K.dve(lambda: nc.vector.tensor_tensor(out=s_[:, 2:3], in0=s_[:, 0:1], in1=s_[:, 1:2], op=ALU.add), r=[s_], w=[s_])
        K.act(lambda: nc.scalar.activation(out=s_[:, 3:4], in_=s_[:, 2:3], func=AF.Sqrt, scale=1.0 / D, bias=EPS),
              r=[s_], w=[s_])
        K.dve(lambda: nc.vector.reciprocal(out=s_[:, 4:5], in_=s_[:, 3:4]), r=[s_], w=[s_])
        for hf in range(2):
            tm = e["tmp"][hf]
            K.dve(lambda hf=hf, tm=tm: nc.vector.scalar_tensor_tensor(out=tm[:], in0=ybanks[hf][:], scalar=s_[:, 4:5],
                                                                      in1=G[:, hf * 512:(hf + 1) * 512], op0=ALU.mult, op1=ALU.mult),
                  r=[ybanks[hf], s_, G], w=[tm])
            K.pool(lambda hf=hf, tm=tm: nc.gpsimd.tensor_tensor(out=h[:, hf * 512:(hf + 1) * 512], in0=h[:, hf * 512:(hf + 1) * 512],
                                                                 in1=tm[:], op=ALU.add), r=[tm, h], w=[h])
        K.dma("sp", self.out[t * 128:(t + 1) * 128, :], h[:], r=[h], w=[self.b_h[t]])

    def load_w(self, wt, src, kchunks, ncols, col0=0, split=2048):
        K = self.K
        sv = src.rearrange("(kc p) n -> p kc n", p=128)
        for kc in range(kchunks):
            for c0 in range(0, ncols, split):
                c1 = min(ncols, c0 + split)
                K.dma("pool", wt[:, kc, c0:c1], sv[:, kc, col0 + c0:col0 + c1], r=[self.cbuf], wa=[wt])

    def ffn_weights(self, ctx, i):
        K = self.K
        w1 = K.sb(ctx, "fw1", [128, 8, DFF], BF16)
        w2 = K.sb(ctx, "fw2", [128, 32, D], BF16)
        self.load_w(w1, self.ffn_w1[i], 8, DFF)
        self.load_w(w2, self.ffn_w2[i], 32, D, split=1024)
        return w1, w2

    def phase_ffn(self, i, w1, w2):
        K, nc = self.K, self.nc
        with contextlib.ExitStack() as ph:
            e = self.epilogue_setup(ph, i, 1)
            utg = [K.sb(ph, f"futg{j}", [128, 8, 512], BF16) for j in range(2)]
            hid = K.sb(ph, "fhid", [128, 32, 512], BF16)
            rl = [K.sb(ph, f"frl{j}", [128, 512], F32) for j in range(2)]
            psA = [K.ps(ph, f"fpsA{j}") for j in range(2)]
            psY = [[K.ps(ph, f"fpsY{j}{hf}") for hf in range(2)] for j in range(2)]
            UTv = self.UT.rearrange("(kc p) t -> p kc t", p=128)

            def load_u(g):
                K.dma("sp", utg[g % 2][:], UTv[:, :, g * 512:(g + 1) * 512], r=[self.b_ut[g]], w=[utg[g % 2]])

            load_u(0)
            for g in range(NG):
                if g + 1 < NG:
                    load_u(g + 1)
                ug = utg[g % 2]
                for fc in range(32):
                    ps = psA[fc % 2]
                    r_ = rl[fc % 2]
                    for kc in range(8):
                        K.pe(lambda kc=kc, fc=fc, ps=ps: nc.tensor.matmul(ps[:], lhsT=w1[:, kc, fc * 128:(fc + 1) * 128], rhs=ug[:, kc, :],
                                                                          start=(kc == 0), stop=(kc == 7)), r=[w1, ug], w=[ps])
                    K.act(lambda ps=ps, r_=r_: nc.scalar.activation(out=r_[:], in_=ps[:], func=AF.Relu), r=[ps], w=[r_])
                    K.pool(lambda fc=fc, r_=r_: nc.gpsimd.tensor_tensor(out=hid[:, fc, :], in0=r_[:], in1=r_[:], op=ALU.mult),
                           r=[r_], w=[hid])
                for tt in range(4):
                    t = g * 4 + tt
                    self.epi_load(e, t)
                    yb = psY[tt % 2]
                    for hf in range(2):
                        for fc in range(32):
                            K.pe(lambda fc=fc, hf=hf, tt=tt: nc.tensor.matmul(yb[hf][:], lhsT=hid[:, fc, tt * 128:(tt + 1) * 128],
                                                                             rhs=w2[:, fc, hf * 512:(hf + 1) * 512],
                                                                             start=(fc == 0), stop=(fc == 31)), r=[hid, w2], w=[yb[hf]])
                    self.epilogue(e, t, yb)
            K.barrier()

    def phase_sb_proj(self, j):
        K, nc = self.K, self.nc
        with contextlib.ExitStack() as ph:
            w = K.sb(ph, "sw", [128, 8, 3 * D], BF16)
            self.load_w(w, self.sb_w_in[j], 8, 3 * D, split=3072)
            utg = [K.sb(ph, f"sutg{k}", [128, 8, 512], BF16) for k in range(2)]
            stg = [K.sb(ph, f"sstg{k}", [128, 512], BF16) for k in range(4)]
            vst = [K.sb(ph, f"svst{k}", [128, D], BF16) for k in range(2)]
            pss = [K.ps(ph, f"sps{k}") for k in range(4)]
            UTv = self.UT.rearrange("(kc p) t -> p kc t", p=128)

            def load_u(g):
                K.dma("sp", utg[g % 2][:], UTv[:, :, g * 512:(g + 1) * 512], r=[self.b_ut[g]], w=[utg[g % 2]])

            load_u(0)
            n = 0
            for g in range(NG):
                if g + 1 < NG:
                    load_u(g + 1)
                ug = utg[g % 2]
                for fcn in range(16):
                    ps = pss[n % 4]
                    sg = stg[n % 4]
                    for kc in range(8):
                        K.pe(lambda kc=kc, fcn=fcn, ps=ps: nc.tensor.matmul(ps[:], lhsT=w[:, kc, fcn * 128:(fcn + 1) * 128], rhs=ug[:, kc, :],
                                                                           start=(kc == 0), stop=(kc == 7)), r=[w, ug], w=[ps])
                    if n % 2 == 0:
                        K.act(lambda ps=ps, sg=sg: nc.scalar.copy(out=sg[:], in_=ps[:]), r=[ps], w=[sg])
                    else:
                        K.dve(lambda ps=ps, sg=sg: nc.vector.tensor_copy(out=sg[:], in_=ps[:]), r=[ps], w=[sg])
                    dst = self.QT if fcn < 8 else self.KT
                    fo = (fcn % 8) * 128
                    bb = self.b_qt[g] if fcn < 8 else self.b_kt[g]
                    K.dma("sp", dst[fo:fo + 128, g * 512:(g + 1) * 512], sg[:], r=[sg], wa=[bb])
                    n += 1
                for tt in range(4):
                    t = g * 4 + tt
                    vs = vst[t % 2]
                    for hf in range(2):
                        ps = pss[n % 4]
                        n += 1
                        for kc in range(8):
                            K.pe(lambda kc=kc, hf=hf, tt=tt, ps=ps: nc.tensor.matmul(ps[:], lhsT=ug[:, kc, tt * 128:(tt + 1) * 128],
                                                                                    rhs=w[:, kc, 2 * D + hf * 512:2 * D + (hf + 1) * 512],
                                                                                    start=(kc == 0), stop=(kc == 7)), r=[w, ug], w=[ps])
                        if hf == 0:
                            K.act(lambda ps=ps, vs=vs: nc.scalar.copy(out=vs[:, 0:512], in_=ps[:]), r=[ps], w=[vs])
                        else:
                            K.dve(lambda ps=ps, vs=vs: nc.vector.tensor_copy(out=vs[:, 512:1024], in_=ps[:]), r=[ps], w=[vs])
                    K.dma("sp", self.Vd[t * 128:(t + 1) * 128, :], vs[:], r=[vs], w=[self.b_vd[t]])
            K.barrier()

    def phase_sb_core(self, i, j):
        K, nc = self.K, self.nc
        with contextlib.ExitStack() as ph:
            O = K.sb(ph, "aO", [128, NT, D], BF16)
            with contextlib.ExitStack() as ph2:
                V = K.sb(ph2, "aV", [128, NT, D], BF16)
                Vv = self.Vd.rearrange("(t p) f -> p t f", p=128)
                for t0 in range(0, NT, 8):
                    K.dma("sp", V[:, t0:t0 + 8, :], Vv[:, t0:t0 + 8, :], r=self.b_vd[t0:t0 + 8], wa=[V])
                msk = K.sb(ph2, "amsk", [128, 512], BF16)
                mbias = K.sb(ph2, "ambias", [128, 512], BF16)
                K.dma("sp", msk[:], self.c_sbmask, r=[self.cbuf], w=[msk])
                K.dma("sp", mbias[:], self.c_sbbias, r=[self.cbuf], w=[mbias])
                qh = [K.sb(ph2, f"aq{k}", [64, S], BF16) for k in range(2)]
                kh = [K.sb(ph2, f"ak{k}", [64, S], BF16) for k in range(2)]
                Et = [K.sb(ph2, f"aE{k}", [128, 512], F32) for k in range(3)]
                Xt = [K.sb(ph2, f"aX{k}", [128, 512], F32) for k in range(2)]
                Lt = [K.sb(ph2, f"aL{k}", [128, 512], BF16) for k in range(3)]
                Lr = [K.sb(ph2, f"aLr{k}", [128, 512], BF16) for k in range(2)]
                At = [K.sb(ph2, f"aA{k}", [128, 512], BF16) for k in range(4)]
                Rt = [K.sb(ph2, f"aR{k}", [128, 512], BF16) for k in range(2)]
                zps = [K.ps(ph2, f"azps{k}") for k in range(2)]
                cps = [K.ps(ph2, f"acps{k}") for k in range(2)]
                avs = [K.ps(ph2, f"aav{k}") for k in range(2)]

                Rq = [K.sb(ph2, f"aRq{k}", [128, 512], BF16) for k in range(2)]

                def load_head(h):
                    s = h % 2
                    K.dma("sp", qh[s][:], self.QT[h * 64:(h + 1) * 64, :], r=self.b_qt, w=[qh[s]])
                    K.dma("sp", kh[s][:], self.KT[h * 64:(h + 1) * 64, :], r=self.b_kt, w=[kh[s]])

                units = []
                ci = 0
                for h in range(NH):
                    for qc in range(NG):
                        nkt = 4 * qc + 4
                        for k_, i_ in enumerate(range(nkt - 1, -1, -1)):
                            units.append(dict(h=h, qc=qc, i=i_, k=k_, nkt=nkt, ci=ci, idx=len(units),
                                              lasth=(qc == NG - 1 and i_ == 0)))
                        ci += 1
                Rall = [[Rt[0], Rt[1]], [Rq[0], Rq[1]]]

                def geom(U):
                    a = U["i"] - 4 * U["qc"]
                    diag = a >= 0
                    c0 = 128 * a if diag else 0
                    return a, diag, c0, 512 - c0

                def stageA(U):
                    u, h, qc, i_ = U["idx"], U["h"], U["qc"], U["i"]
                    a, diag, c0, W = geom(U)
                    q_, k_ = qh[h % 2], kh[h % 2]
                    zp, E, L = zps[u % 2], Et[u % 3], Lt[u % 3]
                    qs = q_[:, qc * 512 + c0:(qc + 1) * 512]
                    K.pe(lambda: nc.tensor.matmul(zp[:, 0:W], lhsT=k_[:, i_ * 128:(i_ + 1) * 128], rhs=qs, start=True, stop=True),
                         r=[q_, k_], w=[zp])
                    K.act(lambda: nc.scalar.activation(out=E[:, 0:W], in_=zp[:, 0:W], func=AF.Exp, scale=0.125), r=[zp], w=[E])
                    if diag:
                        Lraw = Lr[u % 2]
                        K.act(lambda: nc.scalar.activation(out=Lraw[:, 0:W], in_=E[:, 0:W], func=AF.Ln, bias=1.0), r=[E], w=[Lraw])
                        K.dve(lambda: nc.vector.tensor_tensor(out=L[:, 0:W], in0=Lraw[:, 0:W], in1=msk[:, 0:W], op=ALU.mult),
                              r=[Lraw, msk], w=[L])
                    else:
                        K.act(lambda: nc.scalar.activation(out=L[:, 0:W], in_=E[:, 0:W], func=AF.Ln, bias=1.0), r=[E], w=[L])

                def stageB(U):
                    u, h, qc, i_, k = U["idx"], U["h"], U["qc"], U["i"], U["k"]
                    a, diag, c0, W = geom(U)
                    cp, L, A, E, X = cps[u % 2], Lt[u % 3], At[u % 4], Et[u % 3], Xt[u % 2]
                    Rp = Rall[U["ci"] % 2]
                    Rc, Rn = Rp[k % 2], Rp[(k + 1) % 2]
                    if k == 0:
                        K.pool(lambda: nc.gpsimd.memset(Rp[0][:], 0.0), w=[Rp[0]])
                        K.pool(lambda: nc.gpsimd.memset(Rp[1][:], 0.0), w=[Rp[1]])
                    K.pe(lambda: nc.tensor.matmul(cp[:, 0:W], lhsT=self.tri[:], rhs=L[:, 0:W], start=True, stop=(k == 0 and not diag)),
                         r=[self.tri, L], w=[cp])
                    if k != 0:
                        K.pe(lambda: nc.tensor.matmul(cp[:, 0:W], lhsT=self.ones[:], rhs=Rc[:, c0:512], start=False, stop=(not diag)),
                             r=[self.ones, Rc], w=[cp])
                    if diag:
                        K.pe(lambda: nc.tensor.matmul(cp[:, 0:W], lhsT=self.ident[:], rhs=mbias[:, 0:W], start=False, stop=True),
                             r=[self.ident, mbias], w=[cp])
                    if i_ != 0:
                        K.dve(lambda: nc.vector.tensor_tensor(out=Rn[:, c0:512], in0=Rc[:, c0:512], in1=L[:, 0:W], op=ALU.add),
                              r=[Rc, L], w=[Rn])
                    K.act(lambda: nc.scalar.activation(out=X[:, 0:W], in_=cp[:, 0:W], func=AF.Exp, scale=-1.0), r=[cp], w=[X])
                    K.dve(lambda: nc.vector.tensor_tensor(out=A[:, 0:W], in0=E[:, 0:W], in1=X[:, 0:W], op=ALU.mult), r=[E, X], w=[A])

                def stageC(U):
                    u, h, qc, i_, k = U["idx"], U["h"], U["qc"], U["i"], U["k"]
                    a, diag, c0, W = geom(U)
                    A = At[u % 4]
                    av = avs[U["ci"] % 2]
                    for n_, c in enumerate(range(a if diag else 0, 4)):
                        K.pe(lambda: nc.tensor.matmul(av[:, c * 64:(c + 1) * 64], lhsT=A[:, c * 128 - c0:(c + 1) * 128 - c0],
                                                      rhs=V[:, i_, h * 64:(h + 1) * 64], start=(k == 0 and n_ == 0), stop=False,
                                                      skip_group_check=True), r=[A, V], w=[av])
                    if i_ == 0:
                        K.dve(lambda: nc.vector.tensor_copy(out=O[:, qc * 4:(qc + 1) * 4, h * 64:(h + 1) * 64],
                                                            in_=av[:, 0:256].rearrange("p (c d) -> p c d", c=4)), r=[av], w=[O])
                    if U["lasth"] and h + 2 < NH:
                        load_head(h + 2)

                load_head(0)
                load_head(1)
                n = len(units)
                for kk in range(n + 3):
                    if kk < n:
                        stageA(units[kk])
                    if 0 <= kk - 1 < n:
                        stageB(units[kk - 1])
                    if 0 <= kk - 3 < n:
                        stageC(units[kk - 3])
                K.barrier()
            self.out_proj(ph, i, self.sb_w_out[j], O, None)
            K.barrier()

    def out_proj(self, ph, i, w_dram, O, bias_row):
        K, nc = self.K, self.nc
        with contextlib.ExitStack() as ph3:
            w = K.sb(ph3, "ow", [128, 8, D], BF16)
            self.load_w(w, w_dram, 8, D, split=1024)
            e = self.epilogue_setup(ph3, i, 0)
            oT = [K.sb(ph3, f"ooT{k}", [128, 8, 128], BF16) for k in range(2)]
            ptr = [K.ps(ph3, f"optr{k}", (128, 1024), BF16) for k in range(2)]
            psY = [[K.ps(ph3, f"opsY{k}{hf}") for hf in range(2)] for k in range(2)]
            brow = None
            if bias_row is not None:
                brow = K.sb(ph3, "obrow", [1, D], BF16)
                K.dma("pool", brow[:], bias_row, r=[self.cbuf], w=[brow])
            oin = None
            if O is None:
                oin = [K.sb(ph3, f"ooin{k}", [128, D], BF16) for k in range(3)]
                for t in range(2):
                    K.dma("sp", oin[t % 3][:], self.Od[t * 128:(t + 1) * 128, :], r=[self.b_od], w=[oin[t % 3]])
            for t in range(NT):
                self.epi_load(e, t)
                pt = ptr[t % 2]
                ot = oT[t % 2]
                if oin is not None and t + 2 < NT:
                    K.dma("sp", oin[(t + 2) % 3][:], self.Od[(t + 2) * 128:(t + 3) * 128, :], r=[self.b_od], w=[oin[(t + 2) % 3]])
                for kc in range(8):
                    src_ = O[:, t, kc * 128:(kc + 1) * 128] if oin is None else oin[t % 3][:, kc * 128:(kc + 1) * 128]
                    srcT = O if oin is None else oin[t % 3]
                    K.pe(lambda kc=kc: nc.tensor.transpose(out=pt[:, kc * 128:(kc + 1) * 128], in_=src_,
                                                          identity=self.ident[:]), r=[srcT, self.ident], w=[pt])
                K.act(lambda: nc.scalar.copy(out=ot[:], in_=pt[:].rearrange("p (kc t) -> p kc t", kc=8)), r=[pt], w=[ot])
                yb = psY[t % 2]
                for hf in range(2):
                    for kc in range(8):
                        K.pe(lambda kc=kc, hf=hf: nc.tensor.matmul(yb[hf][:], lhsT=ot[:, kc, :], rhs=w[:, kc, hf * 512:(hf + 1) * 512],
                                                                   start=(kc == 0), stop=(kc == 7 and brow is None)), r=[ot, w], w=[yb[hf]])
                    if brow is not None:
                        K.pe(lambda hf=hf: nc.tensor.matmul(yb[hf][:], lhsT=self.ones[0:1, :], rhs=brow[0:1, hf * 512:(hf + 1) * 512],
                                                            start=False, stop=True), r=[self.ones, brow], w=[yb[hf]])
                self.epilogue(e, t, yb)

    def build(self):
        K = self.K
        self.load_consts()
        self.phase_mod()
        first = True
        for i in self.layers:
            kind, j = i % 3, i // 3
            self.phase_norm(i, 0, src_is_x=first)
            self.resid_from_x = first
            first = False
            if kind == 0:
                self.phase_sb_proj(j)
                self.phase_sb_core(i, j)
            elif kind == 1:
                self.phase_nsa(i, j)
            else:
                self.phase_conv(i, j)
            self.resid_from_x = False
            with contextlib.ExitStack() as wctx:
                w1, w2 = self.ffn_weights(wctx, i)
                self.phase_norm(i, 1, src_is_x=False)
                self.phase_ffn(i, w1, w2)
        K.barrier()
        K.st.close()
        return self.nc

    def copy_x_to_out(self):
        K = self.K
        with contextlib.ExitStack() as ph:
            bufs = [K.sb(ph, f"cx{k}", [128, 4, D], F32) for k in range(2)]
            xv = self.x.rearrange("(g c p) f -> g p c f", p=128, c=4)
            ov = self.out.rearrange("(g c p) f -> g p c f", p=128, c=4)
            for g in range(NG):
                b = bufs[g % 2]
                K.dma("sp", b[:], xv[g], r=[], w=[b])
                K.dma("sp", ov[g], b[:], r=[b], w=[self.b_h[4 * g + c] for c in range(4)])
            K.barrier()


    def phase_nsa(self, i, j):
        K, nc = self.K, self.nc
        UTv = self.UT.rearrange("(kc p) t -> p kc t", p=128)
        TWO_PI = 2.0 * np.pi
        with contextlib.ExitStack() as nsa:
            kcT = K.sb(nsa, "n_kcT", [64, 4, 256], BF16)
            vcmp = K.sb(nsa, "n_vcmp", [128, 2, 4, 64], BF16)
            cscmp = K.sb(nsa, "n_cscmp", [64, 2, 256], F32)
            perm = K.sb(nsa, "n_perm", [64, 64], BF16)
            K.dma("sp", perm[:], self.c_perm, r=[self.cbuf], w=[perm])
            with contextlib.ExitStack() as ph:
                w = K.sb(ph, "nw", [128, 8, NSA_IN], BF16)
                self.load_w(w, self.nsa_w_in[j], 8, NSA_IN, split=NSA_IN)
                cosT = K.sb(ph, "ncos", [64, S], F32)
                sinT = K.sb(ph, "nsin", [64, S], F32)
                with contextlib.ExitStack() as ph0:
                    posi = K.sb(ph0, "nposi", [64, S], I32)
                    ang = K.sb(ph0, "nang", [64, S], F32)
                    t1 = K.sb(ph0, "nt1", [64, S], F32)
                    t2 = K.sb(ph0, "nt2", [64, S], F32)
                    ki = K.sb(ph0, "nki", [64, S], I32)
                    invf = K.sb(ph0, "ninvf", [64, 1], F32)
                    K.dma("sp", posi[:], self.pos.to_broadcast([64, S]), r=[self.cbuf], w=[posi])
                    K.dma("sp", invf[:], self.c_invf, r=[self.cbuf], w=[invf])
                    K.dve(lambda: nc.vector.tensor_copy(out=ang[:], in_=posi[:]), r=[posi], w=[ang])
                    K.dve(lambda: nc.vector.tensor_scalar(out=ang[:], in0=ang[:], scalar1=invf[:, 0:1], scalar2=None, op0=ALU.mult),
                          r=[ang, invf], w=[ang])
                    for tab, shift in ((sinT, 0.0), (cosT, 0.5 * np.pi)):
                        K.dve(lambda: nc.vector.tensor_scalar(out=t1[:], in0=ang[:], scalar1=shift, scalar2=1.0 / TWO_PI,
                                                              op0=ALU.add, op1=ALU.mult), r=[ang], w=[t1])
                        K.dve(lambda: nc.vector.tensor_copy(out=ki[:], in_=t1[:]), r=[t1], w=[ki])
                        K.dve(lambda: nc.vector.tensor_copy(out=t1[:], in_=ki[:]), r=[ki], w=[t1])
                        K.dve(lambda: nc.vector.scalar_tensor_tensor(out=t2[:], in0=t1[:], scalar=-TWO_PI, in1=ang[:], op0=ALU.mult, op1=ALU.add),
                              r=[t1, ang], w=[t2])
                        K.dve(lambda: nc.vector.tensor_scalar(out=t2[:], in0=t2[:], scalar1=shift, scalar2=None, op0=ALU.add), r=[t2], w=[t2])
                        K.dve(lambda: nc.vector.tensor_scalar(out=t1[:], in0=t2[:], scalar1=np.pi, scalar2=-TWO_PI, op0=ALU.is_gt, op1=ALU.mult),
                              r=[t2], w=[t1])
                        K.dve(lambda: nc.vector.tensor_tensor(out=t2[:], in0=t2[:], in1=t1[:], op=ALU.add), r=[t2, t1], w=[t2])
                        K.dve(lambda: nc.vector.tensor_scalar(out=t1[:], in0=t2[:], scalar1=-np.pi, scalar2=TWO_PI, op0=ALU.is_lt, op1=ALU.mult),
                              r=[t2], w=[t1])
                        K.dve(lambda: nc.vector.tensor_tensor(out=t2[:], in0=t2[:], in1=t1[:], op=ALU.add), r=[t2, t1], w=[t2])
                        K.dve(lambda: nc.vector.tensor_scalar(out=t2[:], in0=t2[:], scalar1=-3.1415925, scalar2=3.1415925, op0=ALU.max, op1=ALU.min),
                              r=[t2], w=[t2])
                        K.act(lambda: nc.scalar.activation(out=tab[:], in_=t2[:], func=AF.Sin), r=[t2], w=[tab])
                    K.dve(lambda: nc.vector.tensor_copy(out=cscmp[:, 0, 0:255], in_=cosT[:, 31:S:16]), r=[cosT], w=[cscmp])
                    K.dve(lambda: nc.vector.tensor_copy(out=cscmp[:, 1, 0:255], in_=sinT[:, 31:S:16]), r=[sinT, cscmp], w=[cscmp])
                    K.barrier()
                if NSA_STOP == "n0":
                    return
                utg = [K.sb(ph, f"nutg{k}", [128, 8, 512], BF16) for k in range(2)]
                xb = [K.sb(ph, f"nxb{k}", [64, 512], BF16) for k in range(2)]
                r1 = [K.sb(ph, f"nr1{k}", [64, 512], F32) for k in range(2)]
                r2 = [K.sb(ph, f"nr2{k}", [64, 512], F32) for k in range(2)]
                ob = [K.sb(ph, f"nob{k}", [64, 512], BF16) for k in range(4)]
                va = [K.sb(ph, f"nva{k}", [128, 8, 65], BF16) for k in range(2)]
                gt = [K.sb(ph, f"ngt{k}", [128, 48], F32) for k in range(2)]
                for v_ in va:
                    K.pool(lambda: nc.gpsimd.memset(v_[:], 1.0), w=[v_])
                pp = [K.ps(ph, f"npp{k}") for k in range(3)]
                pr = [K.ps(ph, f"npr{k}") for k in range(2)]
                pv = [K.ps(ph, f"npv{k}") for k in range(2)]
                pg = K.ps(ph, "npg")

                def load_u(g):
                    K.dma("sp", utg[g % 2][:], UTv[:, :, g * 512:(g + 1) * 512], r=[self.b_ut[g]], w=[utg[g % 2]])

                units = []
                for h in range(16):
                    units.append((h * 64, self.NQ, h, True, self.b_nq))
                for g4 in range(4):
                    units.append((D + 2 * 256 + g4 * 64, self.NK, g4, True, self.b_nk))
                for g4 in range(4):
                    units.append((D + 4 * 256 + g4 * 64, self.NK, 4 + g4, True, self.b_nk))
                for g4 in range(4):
                    units.append((D + 0 * 256 + g4 * 64, self.NC, g4, False, self.b_ncr))
                for g4 in range(4):
                    units.append((D + 1 * 256 + g4 * 64, self.NC, 4 + g4, False, self.b_ncr))
                load_u(0)
                n = 0
                nr = 0
                for g in range(NG):
                    if g + 1 < NG:
                        load_u(g + 1)
                    ug = utg[g % 2]
                    tsl = slice(g * 512, (g + 1) * 512)
                    for (col, dst, ui, rope, bb) in units:
                        if (rope and "r" not in N1_PARTS) or ((not rope) and "u" not in N1_PARTS):
                            continue
                        ps = pp[n % 3]
                        o_ = ob[n % 4]
                        n += 1
                        for kc in range(8):
                            K.pe(lambda: nc.tensor.matmul(ps[0:64, :], lhsT=w[:, kc, col:col + 64], rhs=ug[:, kc, :],
                                                          start=(kc == 0), stop=(kc == 7)), r=[w, ug], w=[ps])
                        if rope and "asu" not in ROPE_MODE:
                            x_, a_, b_, p2 = xb[nr % 2], r1[nr % 2], r2[nr % 2], pr[nr % 2]
                            nr += 1
                            K.act(lambda: nc.scalar.copy(out=x_[:], in_=ps[0:64, :]), r=[ps], w=[x_])
                            if "noperm" not in ROPE_MODE:
                                K.pe(lambda: nc.tensor.matmul(p2[0:64, :], lhsT=perm[:], rhs=x_[:], start=True, stop=True), r=[perm, x_], w=[p2])
                            K.dve(lambda: nc.vector.tensor_tensor(out=a_[:], in0=ps[0:64, :], in1=cosT[:, tsl], op=ALU.mult), r=[ps, cosT], w=[a_])
                            if "noperm" not in ROPE_MODE:
                                K.dve(lambda: nc.vector.tensor_tensor(out=b_[:], in0=p2[0:64, :], in1=sinT[:, tsl], op=ALU.mult), r=[p2, sinT], w=[b_])
                            else:
                                K.dve(lambda: nc.vector.tensor_tensor(out=b_[:], in0=ps[0:64, :], in1=sinT[:, tsl], op=ALU.mult), r=[ps, sinT], w=[b_])
                            if "dveadd" in ROPE_MODE:
                                K.dve(lambda: nc.vector.tensor_tensor(out=o_[:], in0=a_[:], in1=b_[:], op=ALU.add), r=[a_, b_], w=[o_])
                            else:
                                K.pool(lambda: nc.gpsimd.tensor_tensor(out=o_[:], in0=a_[:], in1=b_[:], op=ALU.add), r=[a_, b_], w=[o_])
                        else:
                            K.act(lambda: nc.scalar.copy(out=o_[:], in_=ps[0:64, :]), r=[ps], w=[o_])
                        K.dma("sp", dst[ui, :, tsl], o_[:], r=[o_], wa=[bb])
                    for tt in range(4):
                        t = g * 4 + tt
                        v_ = va[t % 2]
                        pv_ = pv[t % 2]
                        for m, c0 in ((0, D + 3 * 256), (1, D + 5 * 256)) if "v" in N1_PARTS else ():
                            for kc in range(8):
                                K.pe(lambda: nc.tensor.matmul(pv_[:, m * 256:(m + 1) * 256], lhsT=ug[:, kc, tt * 128:(tt + 1) * 128],
                                                              rhs=w[:, kc, c0:c0 + 256], start=(kc == 0), stop=(kc == 7)), r=[w, ug], w=[pv_])
                        if "v" in N1_PARTS:
                            K.dve(lambda: nc.vector.tensor_copy(out=v_[:, :, 0:64], in_=pv_[:].rearrange("p (u d) -> p u d", d=64)), r=[pv_], w=[v_])
                            K.dma("sp", self.NV[t * 128:(t + 1) * 128, :], v_[:].rearrange("p u d -> p (u d)"), r=[v_], wa=[self.b_nv])
                        g_ = gt[t % 2]
                        if "g" not in N1_PARTS:
                            continue
                        for kc in range(8):
                            K.pe(lambda: nc.tensor.matmul(pg[:, 0:48], lhsT=ug[:, kc, tt * 128:(tt + 1) * 128], rhs=w[:, kc, 2560:2608],
                                                          start=(kc == 0), stop=(kc == 7)), r=[w, ug], w=[pg])
                        K.act(lambda: nc.scalar.activation(out=g_[:], in_=pg[:, 0:48], func=AF.Sigmoid), r=[pg], w=[g_])
                        K.dma("sp", self.NGt[t * 128:(t + 1) * 128, :], g_[:], r=[g_], wa=[self.b_ngt])
                K.barrier()
            if NSA_STOP == "n1":
                return
            with contextlib.ExitStack() as ph:
                raw = K.sb(ph, "craw", [64, 8, S], BF16)
                for u_ in range(8):
                    K.dma("sp", raw[:, u_, :], self.NC[u_], r=[self.b_ncr], wa=[raw])
                K.pool(lambda: nc.gpsimd.memset(vcmp[:], 0.0), w=[vcmp])
                K.pool(lambda: nc.gpsimd.memset(kcT[:], 0.0), w=[kcT])
                hps = [K.ps(ph, f"chps{k}") for k in range(2)]
                bps = K.ps(ph, "cbps")
                ops_ = [K.ps(ph, f"cops{k}") for k in range(2)]
                p2 = K.ps(ph, "cp2")
                for kv in ("k", "v"):
                    w1 = K.sb(ph, "cw1" + kv, [64, 32, 256], BF16)
                    w1v = self.nsa_w1[kv][j].rearrange("(l d) h -> d l h", d=64)
                    for l0 in range(0, 32, 8):
                        K.dma("pool", w1[:, l0:l0 + 8, :], w1v[:, l0:l0 + 8, :], r=[self.cbuf], wa=[w1])
                    w2 = K.sb(ph, "cw2" + kv, [128, 2, 64], BF16)
                    K.dma("pool", w2[:], self.nsa_w2[kv][j].rearrange("(hc p) d -> p hc d", p=128), r=[self.cbuf], w=[w2])
                    peT = K.sb(ph, "cpeT" + kv, [64, 32], F32)
                    peTb = K.sb(ph, "cpeTb" + kv, [64, 32], BF16)
                    K.dma("sp", peT[:], self.nsa_peT[kv], r=[self.cbuf], w=[peT])
                    K.dve(lambda: nc.vector.tensor_copy(out=peTb[:], in_=peT[:]), r=[peT], w=[peTb])
                    bias = K.sb(ph, "cbias" + kv, [128, 2], F32)
                    for hc in range(2):
                        for l in range(32):
                            K.pe(lambda: nc.tensor.matmul(bps[:, hc:hc + 1], lhsT=w1[:, l, hc * 128:(hc + 1) * 128], rhs=peTb[:, l:l + 1],
                                                          start=(l == 0), stop=(l == 31)), r=[w1, peTb], w=[bps])
                        K.dve(lambda: nc.vector.tensor_copy(out=bias[:, hc:hc + 1], in_=bps[:, hc:hc + 1]), r=[bps], w=[bias])
                    xb_ = K.sb(ph, "cxb" + kv, [128, 256], F32)
                    x2_ = K.sb(ph, "cx2" + kv, [128, 256], F32)
                    x3_ = K.sb(ph, "cx3" + kv, [128, 256], F32)
                    hidT = K.sb(ph, "chid" + kv, [128, 2, 256], BF16)
                    kx = K.sb(ph, "ckx" + kv, [64, 256], BF16)
                    ka = K.sb(ph, "cka" + kv, [64, 256], F32)
                    kb_ = K.sb(ph, "ckb" + kv, [64, 256], F32)
                    for g4 in range(4):
                        ui = g4 if kv == "k" else 4 + g4
                        for hc in range(2):
                            hp = hps[hc]
                            for l in range(32):
                                K.pe(lambda: nc.tensor.matmul(hp[:, 0:255], lhsT=w1[:, l, hc * 128:(hc + 1) * 128],
                                                              rhs=raw[:, ui, l:l + 16 * 254 + 1:16], start=(l == 0), stop=(l == 31)),
                                     r=[w1, raw], w=[hp])
                            K.dve(lambda: nc.vector.tensor_scalar(out=xb_[:, 0:255], in0=hp[:, 0:255], scalar1=bias[:, hc:hc + 1], scalar2=None,
                                                                  op0=ALU.add), r=[hp, bias], w=[xb_])
                            K.pool(lambda: nc.gpsimd.tensor_tensor(out=x2_[:, 0:255], in0=xb_[:, 0:255], in1=xb_[:, 0:255], op=ALU.mult), r=[xb_], w=[x2_])
                            K.dve(lambda: nc.vector.tensor_scalar(out=x2_[:, 0:255], in0=x2_[:, 0:255], scalar1=0.044715, scalar2=1.0,
                                                                  op0=ALU.mult, op1=ALU.add), r=[x2_], w=[x2_])
                            K.dve(lambda: nc.vector.tensor_tensor(out=x3_[:, 0:255], in0=x2_[:, 0:255], in1=xb_[:, 0:255], op=ALU.mult), r=[x2_, xb_], w=[x3_])
                            K.act(lambda: nc.scalar.activation(out=x3_[:, 0:255], in_=x3_[:, 0:255], func=AF.Tanh, scale=0.7978845608028654),
                                  r=[x3_], w=[x3_])
                            K.dve(lambda: nc.vector.scalar_tensor_tensor(out=x2_[:, 0:255], in0=x3_[:, 0:255], scalar=1.0, in1=xb_[:, 0:255],
                                                                         op0=ALU.add, op1=ALU.mult), r=[x3_, xb_], w=[x2_])
                            K.pool(lambda: nc.gpsimd.tensor_scalar(out=hidT[:, hc, 0:255], in0=x2_[:, 0:255], scalar1=0.5, scalar2=None, op0=ALU.mult),
                                   r=[x2_], w=[hidT])
                        if kv == "k":
                            op_ = ops_[0]
                            for hc in range(2):
                                K.pe(lambda: nc.tensor.matmul(op_[0:64, 0:255], lhsT=w2[:, hc, :], rhs=hidT[:, hc, 0:255],
                                                              start=(hc == 0), stop=(hc == 1)), r=[w2, hidT], w=[op_])
                            K.act(lambda: nc.scalar.copy(out=kx[:, 0:255], in_=op_[0:64, 0:255]), r=[op_], w=[kx])
                            K.pe(lambda: nc.tensor.matmul(p2[0:64, 0:255], lhsT=perm[:], rhs=kx[:, 0:255], start=True, stop=True), r=[perm, kx], w=[p2])
                            K.dve(lambda: nc.vector.tensor_tensor(out=ka[:, 0:255], in0=op_[0:64, 0:255], in1=cscmp[:, 0, 0:255], op=ALU.mult),
                                  r=[op_, cscmp], w=[ka])
                            K.dve(lambda: nc.vector.tensor_tensor(out=kb_[:, 0:255], in0=p2[0:64, 0:255], in1=cscmp[:, 1, 0:255], op=ALU.mult),
                                  r=[p2, cscmp], w=[kb_])
                            K.pool(lambda: nc.gpsimd.tensor_tensor(out=kcT[:, g4, 0:255], in0=ka[:, 0:255], in1=kb_[:, 0:255], op=ALU.add),
                                   r=[ka, kb_], w=[kcT])
                        else:
                            for nch, m in ((0, 128), (1, 127)):
                                op_ = ops_[nch]
                                for hc in range(2):
                                    K.pe(lambda: nc.tensor.matmul(op_[0:m, 0:64], lhsT=hidT[:, hc, nch * 128:nch * 128 + m], rhs=w2[:, hc, :],
                                                                  start=(hc == 0), stop=(hc == 1)), r=[w2, hidT], w=[op_])
                                K.act(lambda: nc.scalar.copy(out=vcmp[0:m, nch, g4, :], in_=op_[0:m, 0:64]), r=[op_], w=[vcmp])
                K.barrier()
            if NSA_STOP == "n2":
                return
            with contextlib.ExitStack() as ph:
                with contextlib.ExitStack() as ph2:
                    self._nsa_attn(ph2, None, kcT, vcmp)
                    K.barrier()
                self.out_proj(ph, i, self.nsa_w_out[j], None, None)
                K.barrier()

    def _nsa_attn(self, ph, O, kcT, vcmp):
        K, nc = self.K, self.nc
        V = K.sb(ph, "tV", [128, NT, 8 * 65], BF16)
        NVv = self.NV.rearrange("(t p) f -> p t f", p=128)
        for t0 in range(0, NT, 8):
            K.dma("sp", V[:, t0:t0 + 8, :], NVv[:, t0:t0 + 8, :], r=[self.b_nv], wa=[V])
        GT = K.sb(ph, "tGT", [128, NT, 48], F32)
        NGv = self.NGt.rearrange("(t p) f -> p t f", p=128)
        for t0 in range(0, NT, 8):
            K.dma("sp", GT[:, t0:t0 + 8, :], NGv[:, t0:t0 + 8, :], r=[self.b_ngt], wa=[GT])
        esel = K.sb(ph, "tesel", [64, 32 * 128], BF16)
        winb = K.sb(ph, "twin01", [128, 8 * 512], BF16)
        causb = K.sb(ph, "tcaus01", [128, 512], BF16)
        Mks = [K.sb(ph, f"tMk{k}", [128, 28 + 4 * k, 512], BF16) for k in range(2)]
        Ob = [K.sb(ph, f"tOb{k}", [128, 4, 256], BF16) for k in range(2)]
        Odv = self.Od.rearrange("(t p) f -> p t f", p=128)
        band = K.sb(ph, "tband", [128, 512], BF16)
        wcm = K.sb(ph, "twcm", [128, 128], F32)
        wfb = K.sb(ph, "twfb", [128, 128], F32)
        anyok = K.sb(ph, "tanyok", [128, 1], F32)
        for t_, src in ((esel, self.c_esel), (winb, self.c_win01), (causb, self.c_caus01), (band, self.c_band),
                        (wcm, self.c_wcm), (wfb, self.c_wfb), (anyok, self.c_anyok)):
            K.dma("sp", t_[:], src, r=[self.cbuf], w=[t_])
        ks = K.sb(ph, "tks", [64, S], BF16)
        kw = K.sb(ph, "tkw", [64, S], BF16)
        qh = [K.sb(ph, f"tq{k}", [64, S], BF16) for k in range(4)]
        psg = K.sb(ph, "tpsg", [128, 4, 256], F32)
        NS = 3
        pun = [K.sb(ph, f"tpun{k}", [128, 256], F32) for k in range(NS)]
        pb = [K.sb(ph, f"tpb{k}", [128, 256], BF16) for k in range(NS)]
        pTs = [K.sb(ph, f"tpT{k}", [128, 2, 128], BF16) for k in range(NS)]
        st = [K.sb(ph, f"tst{k}", [128, 8], F32) for k in range(NS)]
        s4 = K.sb(ph, "ts4", [128, 64], F32)
        imp = K.sb(ph, "timp", [128, 64], F32)
        sc = K.sb(ph, "tsc", [128, 64], F32)
        wk = K.sb(ph, "twk", [128, 64], F32)
        m8a = K.sb(ph, "tm8a", [128, 8], F32)
        m8b = K.sb(ph, "tm8b", [128, 8], F32)
        selt = K.sb(ph, "tsel", [128, 64], F32)
        negm = [K.sb(ph, f"tnegm{k}", [128, 64], BF16) for k in range(2)]
        nmT = [K.sb(ph, f"tnmT{k}", [64, 512], BF16) for k in range(2)]
        Pt = [K.sb(ph, f"tP{k}", [128, 512], BF16) for k in range(5)]
        Oq = [K.sb(ph, f"tOq{k}", [128, 4, 256], F32) for k in range(2)]
        cf = [K.sb(ph, f"tcf{k}", [128, 8], F32) for k in range(2)]
        sps = [K.ps(ph, f"tsps{k}") for k in range(2)]
        accs = K.ps(ph, "taccs")
        accw = K.ps(ph, "taccw")
        cpsb = [K.ps(ph, f"tcps{k}") for k in range(2)]
        misc = K.ps(ph, "tmisc", (128, 1024), BF16)
        ocp = K.ps(ph, "tocp")
        items = []
        cnt = {"cmp": 0, "u": 0, "gq": 0}

        def add_loads(g):
            def f():
                K.dma("sp", ks[:], self.NK[g], r=[self.b_nk], w=[ks])
                K.dma("sp", kw[:], self.NK[4 + g], r=[self.b_nk], w=[kw])
                for r in range(4):
                    K.dma("sp", qh[r][:], self.NQ[4 * g + r], r=[self.b_nq], w=[qh[r]])
            items.append([f])

        def add_cmp(g, qc, c, r, gq):
            T_ = 4 * qc + c
            ncols = min(8 * T_ + 7, NCMP)
            b0 = 256 - 8 * T_
            h = 4 * g + r
            q_ = qh[r]
            Oq_ = Oq[gq % 2]
            chunks = [(0, min(128, ncols))] + ([(1, ncols - 128)] if ncols > 128 else [])
            stt = {}

            def s0():
                n_ = cnt["cmp"]
                cnt["cmp"] += 1
                stt["n"] = n_
                s_, pu, pb_, cps = st[n_ % NS], pun[n_ % NS], pb[n_ % NS], cpsb[n_ % 2]
                K.pe(lambda: nc.tensor.matmul(cps[:, 0:ncols], lhsT=q_[:, T_ * 128:(T_ + 1) * 128], rhs=kcT[:, g, 0:ncols],
                                              start=True, stop=False), r=[q_, kcT], w=[cps])
                K.pe(lambda: nc.tensor.matmul(cps[:, 0:ncols], lhsT=self.ident[:], rhs=band[:, b0:b0 + ncols],
                                              start=False, stop=True), r=[self.ident, band], w=[cps])
                K.dve(lambda: nc.vector.reduce_max(out=s_[:, 0:1], in_=cps[:, 0:ncols], axis=AX.X), r=[cps], w=[s_])
                K.dve(lambda: nc.vector.tensor_scalar(out=s_[:, 1:2], in0=s_[:, 0:1], scalar1=-0.125, scalar2=None, op0=ALU.mult),
                      r=[s_], w=[s_])
                K.act(lambda: nc.scalar.activation(out=pu[:, 0:ncols], in_=cps[:, 0:ncols], func=AF.Exp, scale=0.125,
                                                   bias=s_[:, 1:2], accum_out=s_[:, 2:3]), r=[cps, s_], w=[pu, s_])
                K.dve(lambda: nc.vector.reciprocal(out=s_[:, 3:4], in_=s_[:, 2:3]), r=[s_], w=[s_])
                if T_ == 0:
                    K.dve(lambda: nc.vector.tensor_tensor(out=s_[:, 3:4], in0=s_[:, 3:4], in1=anyok[:], op=ALU.mult), r=[s_, anyok], w=[s_])
                if r == 0:
                    K.dve(lambda: nc.vector.tensor_scalar(out=psg[:, c, 0:ncols], in0=pu[:, 0:ncols], scalar1=s_[:, 3:4], scalar2=None,
                                                          op0=ALU.mult), r=[pu, s_], w=[psg])
                else:
                    K.dve(lambda: nc.vector.scalar_tensor_tensor(out=psg[:, c, 0:ncols], in0=pu[:, 0:ncols], scalar=s_[:, 3:4],
                                                                 in1=psg[:, c, 0:ncols], op0=ALU.mult, op1=ALU.add), r=[pu, s_, psg], w=[psg])
                K.pool(lambda: nc.gpsimd.tensor_scalar(out=pb_[:, 0:ncols], in0=pu[:, 0:ncols], scalar1=s_[:, 3:4], scalar2=None,
                                                       op0=ALU.mult), r=[pu, s_], w=[pb_])

            def s1():
                n_ = stt["n"]
                pb_, pT = pb[n_ % NS], pTs[n_ % NS]
                for ch, wd in chunks:
                    K.pe(lambda: nc.tensor.transpose(out=misc[0:wd, ch * 128:(ch + 1) * 128], in_=pb_[:, ch * 128:ch * 128 + wd],
                                                     identity=self.ident[:]), r=[pb_, self.ident], w=[misc])
                for ch, wd in chunks:
                    K.act(lambda: nc.scalar.copy(out=pT[0:wd, ch, :], in_=misc[0:wd, ch * 128:(ch + 1) * 128]), r=[misc], w=[pT])

            def s2():
                pT = pTs[stt["n"] % NS]
                for k_, (ch, wd) in enumerate(chunks):
                    K.pe(lambda: nc.tensor.matmul(ocp[:, 0:64], lhsT=pT[0:wd, ch, :], rhs=vcmp[0:wd, ch, g, :],
                                                  start=(k_ == 0), stop=(k_ == len(chunks) - 1)), r=[pT, vcmp], w=[ocp])
                K.dve(lambda: nc.vector.tensor_scalar(out=Oq_[:, c, r * 64:(r + 1) * 64], in0=ocp[:, 0:64],
                                                      scalar1=GT[:, T_, 3 * h:3 * h + 1], scalar2=None, op0=ALU.mult),
                      r=[ocp, GT], w=[Oq_])
            items.append([s0, s1, s2])

        def add_select(g, qc, c, gq):
            T_ = 4 * qc + c
            w0 = 64 - 2 * T_
            nm_ = negm[c % 2]

            def f():
                pv4 = psg[:, c, :].rearrange("p (j f) -> p j f", f=4)
                K.dve(lambda: nc.vector.tensor_reduce(out=s4[:], in_=pv4, axis=AX.X, op=ALU.add), r=[psg], w=[s4])
                K.dve(lambda: nc.vector.scalar_tensor_tensor(out=imp[:], in0=pv4[:, :, 3], scalar=-0.5, in1=s4[:], op0=ALU.mult, op1=ALU.add),
                      r=[psg, s4], w=[imp])
                K.dve(lambda: nc.vector.scalar_tensor_tensor(out=imp[:, 1:64], in0=pv4[:, 0:63, 3], scalar=0.5, in1=imp[:, 1:64],
                                                             op0=ALU.mult, op1=ALU.add), r=[psg, imp], w=[imp])
                K.dve(lambda: nc.vector.tensor_tensor(out=sc[:], in0=imp[:], in1=wcm[:, w0:w0 + 64], op=ALU.mult), r=[imp, wcm], w=[sc])
                K.dve(lambda: nc.vector.tensor_tensor(out=sc[:], in0=sc[:], in1=wfb[:, w0:w0 + 64], op=ALU.add), r=[sc, wfb], w=[sc])
                K.dve(lambda: nc.vector.memset(sc[:, 0:1], 1.0e4), r=[sc], w=[sc])
                K.dve(lambda: nc.vector.max(out=m8a[:], in_=sc[:]), r=[sc], w=[m8a])
                K.dve(lambda: nc.vector.match_replace(out=wk[:], in_to_replace=m8a[:], in_values=sc[:], imm_value=-3.0e38), r=[sc, m8a], w=[wk])
                K.dve(lambda: nc.vector.max(out=m8b[:], in_=wk[:]), r=[wk], w=[m8b])
                K.dve(lambda: nc.vector.tensor_scalar(out=nm_[:], in0=sc[:], scalar1=m8b[:, 7:8], scalar2=None, op0=ALU.is_ge),
                      r=[sc, m8b], w=[nm_])
                K.pe(lambda: nc.tensor.transpose(out=misc[0:64, 512 + c * 128:512 + (c + 1) * 128], in_=nm_[:], identity=self.ident[:]),
                     r=[nm_, self.ident], w=[misc])
                if c == 3:
                    K.act(lambda: nc.scalar.copy(out=nmT[gq % 2][:], in_=misc[0:64, 512:1024]), r=[misc], w=[nmT[gq % 2]])
            items.append([f])

        def add_unit(g, qc, r, i_, kind, first, gq):
            a = i_ - 4 * qc
            if kind == "s":
                diag = a >= 0
                c0, c1 = (128 * a if diag else 0), 512
                kt, acc, voff = ks, accs, g * 65
            else:
                e_ = i_ - (4 * qc - 4)
                c0, c1 = (0, 128 * (e_ + 1)) if e_ < 4 else (128 * (e_ - 4), 512)
                kt, acc, voff = kw, accw, (4 + g) * 65
            W = c1 - c0
            q_ = qh[r]
            Mk = Mks[gq % 2]
            stt = {}

            def s0():
                u = cnt["u"]
                cnt["u"] += 1
                stt["u"] = u
                sp_, P = sps[u % 2], Pt[u % 5]
                qs = q_[:, qc * 512 + c0:qc * 512 + c1]
                K.pe(lambda: nc.tensor.matmul(sp_[:, 0:W], lhsT=kt[:, i_ * 128:(i_ + 1) * 128], rhs=qs, start=True, stop=True),
                     r=[kt, q_], w=[sp_])
                K.act(lambda: nc.scalar.activation(out=P[:, 0:W], in_=sp_[:, 0:W], func=AF.Exp, scale=0.125), r=[sp_], w=[P])
                if kind == "s":
                    K.dve(lambda: nc.vector.tensor_tensor(out=P[:, 0:W], in0=P[:, 0:W], in1=Mk[:, i_, 0:W], op=ALU.mult), r=[P, Mk], w=[P])
                else:
                    K.dve(lambda: nc.vector.tensor_tensor(out=P[:, 0:W], in0=P[:, 0:W], in1=winb[:, e_ * 512 + c0:e_ * 512 + c1], op=ALU.mult),
                          r=[P, winb], w=[P])

            def s1():
                P = Pt[stt["u"] % 5]
                for n_, c in enumerate(range(c0 // 128, c1 // 128)):
                    K.pe(lambda: nc.tensor.matmul(acc[:, c * 65:(c + 1) * 65], lhsT=P[:, c * 128 - c0:(c + 1) * 128 - c0],
                                                  rhs=V[:, i_, voff:voff + 65], start=(first and n_ == 0), stop=False, skip_group_check=True),
                         r=[P, V], w=[acc])
            items.append([s0, None, None, s1])

        def add_mask(g, qc, i_, gq):
            a = i_ - 4 * qc
            c0 = 128 * a if a >= 0 else 0
            W = 512 - c0
            nm = nmT[gq % 2]
            Mk = Mks[gq % 2]
            stt = {}

            def s0():
                u = cnt["u"]
                cnt["u"] += 1
                stt["u"] = u
                sp_ = sps[u % 2]
                K.pe(lambda: nc.tensor.matmul(sp_[:, 0:W], lhsT=esel[:, i_ * 128:(i_ + 1) * 128], rhs=nm[:, c0:512], start=True, stop=True),
                     r=[esel, nm], w=[sp_])

            def s1():
                sp_ = sps[stt["u"] % 2]
                if a >= 0:
                    K.dve(lambda: nc.vector.tensor_tensor(out=Mk[:, i_, 0:W], in0=sp_[:, 0:W], in1=causb[:, 0:W], op=ALU.mult),
                          r=[sp_, causb], w=[Mk])
                else:
                    K.act(lambda: nc.scalar.copy(out=Mk[:, i_, 0:W], in_=sp_[:, 0:W]), r=[sp_], w=[Mk])
            items.append([s0, s1])

        def add_combine(g, qc, r, gq):
            h = 4 * g + r
            Oq_ = Oq[gq % 2]

            def f():
                for bi, acc in ((1, accs), (2, accw)):
                    cf_ = cf[bi - 1]
                    av = acc[:, 0:260].rearrange("p (c d) -> p c d", d=65)
                    K.dve(lambda: nc.vector.reciprocal(out=cf_[:, 0:4], in_=av[:, :, 64]), r=[acc], w=[cf_])
                    K.dve(lambda: nc.vector.tensor_tensor(out=cf_[:, 4:8], in0=cf_[:, 0:4], in1=GT[:, 4 * qc:4 * qc + 4, 3 * h + bi], op=ALU.mult),
                          r=[cf_, GT], w=[cf_])
                    for c in range(4):
                        K.dve(lambda: nc.vector.scalar_tensor_tensor(out=Oq_[:, c, r * 64:(r + 1) * 64], in0=av[:, c, 0:64], scalar=cf_[:, 4 + c:5 + c],
                                                                     in1=Oq_[:, c, r * 64:(r + 1) * 64], op0=ALU.mult, op1=ALU.add),
                              r=[acc, cf_, Oq_], w=[Oq_])
                if r == 3:
                    ob_ = Ob[gq % 2]
                    K.pool(lambda: nc.gpsimd.tensor_copy(out=ob_[:], in_=Oq_[:]), r=[Oq_], w=[ob_])
                    K.dma("sp", Odv[:, 4 * qc:4 * qc + 4, g * 256:(g + 1) * 256], ob_[:], r=[ob_], wa=[self.b_od])
            items.append([None, None, None, f])

        pre, un = [], []
        for g in range(4):
            for qc in range(NG):
                gq = cnt["gq"]
                cnt["gq"] += 1
                del items[:]
                if qc == 0:
                    add_loads(g)
                items.append([lambda: K.pool(lambda: nc.gpsimd.memset(psg[:], 0.0), w=[psg])])
                for c in range(4):
                    for r in range(4):
                        add_cmp(g, qc, c, r, gq)
                    add_select(g, qc, c, gq)
                for i_ in range(0, 4 * qc + 4):
                    add_mask(g, qc, i_, gq)
                items.append([lambda: None])
                pre.append(list(items))
                del items[:]
                for r in range(4):
                    first = True
                    for i_ in range(0, 4 * qc + 4):
                        add_unit(g, qc, r, i_, "s", first, gq)
                        first = False
                    first = True
                    for i_ in range(max(0, 4 * qc - 4), 4 * qc + 4):
                        add_unit(g, qc, r, i_, "w", first, gq)
                        first = False
                    add_combine(g, qc, r, gq)
                un.append(list(items))
        final = list(pre[0])
        ngq = len(un)
        for gq in range(ngq):
            nxt = pre[gq + 1] if gq + 1 < ngq else []
            if not nxt or (gq + 1) % NG == 0 or not NSA_INTERLEAVE:
                final += un[gq]
                final += [[lambda: None]] * 4
                final += nxt
            else:
                a_, b_ = un[gq], nxt
                bi = 0
                for ai, it in enumerate(a_):
                    final.append(it)
                    tgt = ((ai + 1) * len(b_)) // len(a_)
                    while bi < tgt:
                        final.append(b_[bi])
                        bi += 1
                final += b_[bi:]
        del items[:]
        items.extend(final)
        run_pipeline(items)


    def phase_conv(self, i, j):
        K, nc = self.K, self.nc
        UTv = self.UT.rearrange("(kc p) t -> p kc t", p=128)
        with contextlib.ExitStack() as ph:
            w = K.sb(ph, "cw", [128, 8, 2 * D], BF16)
            self.load_w(w, self.cv_w_in[j], 8, 2 * D)
            bcol = K.sb(ph, "cbcol", [128, 16], F32)
            K.dma("sp", bcol[:], self.cv_b_inT, r=[self.cbuf], w=[bcol])
            utg = [K.sb(ph, f"cutg{k}", [128, 8, 512], BF16) for k in range(2)]
            sgt = [K.sb(ph, f"csg{k}", [128, 512], F32) for k in range(2)]
            hgt = [K.sb(ph, f"chg{k}", [128, 512], BF16) for k in range(3)]
            psa = [K.ps(ph, f"cpsa{k}") for k in range(2)]
            psg = [K.ps(ph, f"cpsg{k}") for k in range(2)]

            def load_u(g):
                K.dma("sp", utg[g % 2][:], UTv[:, :, g * 512:(g + 1) * 512], r=[self.b_ut[g]], w=[utg[g % 2]])

            load_u(0)
            n = 0
            for g in range(NG):
                if g + 1 < NG:
                    load_u(g + 1)
                ug = utg[g % 2]
                for cc in range(8):
                    pa, pg = psa[n % 2], psg[n % 2]
                    sg, hg = sgt[n % 2], hgt[n % 3]
                    n += 1
                    for kc in range(8):
                        K.pe(lambda: nc.tensor.matmul(pa[:], lhsT=w[:, kc, cc * 128:(cc + 1) * 128], rhs=ug[:, kc, :],
                                                      start=(kc == 0), stop=(kc == 7)), r=[w, ug], w=[pa])
                    for kc in range(8):
                        K.pe(lambda: nc.tensor.matmul(pg[:], lhsT=w[:, kc, D + cc * 128:D + (cc + 1) * 128], rhs=ug[:, kc, :],
                                                      start=(kc == 0), stop=(kc == 7)), r=[w, ug], w=[pg])
                    K.act(lambda: nc.scalar.activation(out=sg[:], in_=pg[:], func=AF.Sigmoid, bias=bcol[:, 8 + cc:9 + cc]),
                          r=[pg, bcol], w=[sg])
                    K.dve(lambda: nc.vector.scalar_tensor_tensor(out=hg[:], in0=pa[:], scalar=bcol[:, cc:cc + 1], in1=sg[:],
                                                                 op0=ALU.add, op1=ALU.mult), r=[pa, bcol, sg], w=[hg])
                    K.dma("sp", self.HG[cc * 128:(cc + 1) * 128, g * 512:(g + 1) * 512], hg[:], r=[hg], wa=[self.b_hg[g]])
            K.barrier()
        with contextlib.ExitStack() as ph:
            w = K.sb(ph, "cow", [128, 8, D], BF16)
            self.load_w(w, self.cv_w_out[j], 8, D, split=1024)
            brow = K.sb(ph, "cobrow", [1, D], BF16)
            K.dma("pool", brow[:], self.cv_b_out[j:j + 1, :], r=[self.cbuf], w=[brow])
            dw = K.sb(ph, "cdw", [128, 8, 31], F32)
            vec = K.sb(ph, "cvec", [128, 3, 8], F32)
            onesf = K.sb(ph, "conesf", [128, 128], F32)
            K.dma("sp", dw[:], self.cv_dwT, r=[self.cbuf], w=[dw])
            K.dma("sp", vec[:], self.cv_vecT, r=[self.cbuf], w=[vec])
            K.dma("sp", onesf[:], self.c_onesf, r=[self.cbuf], w=[onesf])
            e = self.epilogue_setup(ph, i, 0)
            xin = [K.sb(ph, f"cxin{k}", [128, 8, 542], BF16) for k in range(2)]
            Dg = K.sb(ph, "cDg", [128, 8, 31, 128], BF16)
            for cc in range(8):
                for k in range(31):
                    if (cc * 31 + k) % 3 == 2:
                        K.pool(lambda: nc.gpsimd.tensor_scalar(out=Dg[:, cc, k, :], in0=self.ident[:], scalar1=dw[:, cc, k:k + 1], scalar2=None,
                                                               op0=ALU.mult), r=[self.ident, dw], w=[Dg])
                    else:
                        K.dve(lambda: nc.vector.tensor_scalar(out=Dg[:, cc, k, :], in0=self.ident[:], scalar1=dw[:, cc, k:k + 1], scalar2=None,
                                                              op0=ALU.mult), r=[self.ident, dw], w=[Dg])
            cvp = [K.ps(ph, f"ccvp{k}") for k in range(2)]
            acc = K.sb(ph, "cacc", [128, 8, 512], F32)
            sq = [K.sb(ph, f"csq{k}", [128, 512], F32) for k in range(2)]
            mt = K.sb(ph, "cm", [128, 512], F32)
            msq = K.sb(ph, "cmsq", [128, 512], F32)
            var = K.sb(ph, "cvar", [128, 512], F32)
            rstd = K.sb(ph, "crstd", [128, 512], F32)
            dt_ = [K.sb(ph, f"cd{k}", [128, 512], F32) for k in range(2)]
            xh = [K.sb(ph, f"cxh{k}", [128, 512], F32) for k in range(2)]
            hT = K.sb(ph, "chT", [128, 8, 512], BF16)
            s1 = K.ps(ph, "cs1")
            s2 = K.ps(ph, "cs2")
            psY = [[K.ps(ph, f"cpsY{k}{hf}") for hf in range(2)] for k in range(2)]
            HGv = self.HG.rearrange("(cc p) t -> p cc t", p=128)

            def load_x(g):
                xt = xin[g % 2]
                if g == 0:
                    K.pool(lambda: nc.gpsimd.memset(xt[:, :, 0:30], 0.0), w=[xt])
                    K.dma("sp", xt[:, :, 30:542], HGv[:, :, 0:512], r=[self.b_hg[0]], wa=[xt])
                else:
                    K.dma("sp", xt[:], HGv[:, :, g * 512 - 30:g * 512 + 512], r=[self.b_hg[g - 1], self.b_hg[g]], w=[xt])

            load_x(0)
            for g in range(NG):
                if g + 1 < NG:
                    load_x(g + 1)
                xt = xin[g % 2]
                for cc in range(8):
                    cv = cvp[cc % 2]
                    for k in range(31):
                        K.pe(lambda: nc.tensor.matmul(cv[:], lhsT=Dg[:, cc, k, :], rhs=xt[:, cc, k:k + 512], start=(k == 0), stop=(k == 30)),
                             r=[Dg, xt], w=[cv])
                    sq_ = sq[cc % 2]
                    K.act(lambda: nc.scalar.activation(out=sq_[:], in_=cv[:], func=AF.Square, bias=vec[:, 0, cc:cc + 1]), r=[cv, vec], w=[sq_])
                    K.dve(lambda: nc.vector.tensor_scalar(out=acc[:, cc, :], in0=cv[:], scalar1=vec[:, 0, cc:cc + 1], scalar2=None, op0=ALU.add),
                          r=[cv, vec], w=[acc])
                    K.pe(lambda: nc.tensor.matmul(s1[:], lhsT=onesf[:], rhs=acc[:, cc, :], start=(cc == 0), stop=(cc == 7)),
                         r=[onesf, acc], w=[s1])
                    K.pe(lambda: nc.tensor.matmul(s2[:], lhsT=onesf[:], rhs=sq_[:], start=(cc == 0), stop=(cc == 7)),
                         r=[onesf, sq_], w=[s2])
                K.dve(lambda: nc.vector.tensor_scalar(out=mt[:], in0=s1[:], scalar1=1.0 / D, scalar2=None, op0=ALU.mult), r=[s1], w=[mt])
                K.pool(lambda: nc.gpsimd.tensor_tensor(out=msq[:], in0=mt[:], in1=mt[:], op=ALU.mult), r=[mt], w=[msq])
                K.dve(lambda: nc.vector.scalar_tensor_tensor(out=var[:], in0=s2[:], scalar=1.0 / D, in1=msq[:], op0=ALU.mult, op1=ALU.subtract),
                      r=[s2, msq], w=[var])
                K.act(lambda: nc.scalar.activation(out=var[:], in_=var[:], func=AF.Sqrt, bias=EPS), r=[var], w=[var])
                K.dve(lambda: nc.vector.reciprocal(out=rstd[:], in_=var[:]), r=[var], w=[rstd])
                for cc in range(8):
                    d_, x_ = dt_[cc % 2], xh[cc % 2]
                    K.pool(lambda: nc.gpsimd.tensor_tensor(out=d_[:], in0=acc[:, cc, :], in1=mt[:], op=ALU.subtract), r=[acc, mt], w=[d_])
                    K.dve(lambda: nc.vector.tensor_tensor(out=x_[:], in0=d_[:], in1=rstd[:], op=ALU.mult), r=[d_, rstd], w=[x_])
                    K.act(lambda: nc.scalar.activation(out=hT[:, cc, :], in_=x_[:], func=AF.Silu, scale=vec[:, 1, cc:cc + 1],
                                                       bias=vec[:, 2, cc:cc + 1]), r=[x_, vec], w=[hT])
                for tt in range(4):
                    t = g * 4 + tt
                    self.epi_load(e, t)
                    yb = psY[tt % 2]
                    for hf in range(2):
                        for cc in range(8):
                            K.pe(lambda: nc.tensor.matmul(yb[hf][:], lhsT=hT[:, cc, tt * 128:(tt + 1) * 128], rhs=w[:, cc, hf * 512:(hf + 1) * 512],
                                                          start=(cc == 0), stop=False), r=[hT, w], w=[yb[hf]])
                        K.pe(lambda: nc.tensor.matmul(yb[hf][:], lhsT=self.ones[0:1, :], rhs=brow[0:1, hf * 512:(hf + 1) * 512],
                                                      start=False, stop=True), r=[self.ones, brow], w=[yb[hf]])
                    self.epilogue(e, t, yb)
            K.barrier()


def run_pipeline(items):
    n = len(items)
    depth = max(len(it) for it in items)
    for t in range(n + depth - 1):
        for j in range(depth):
            k = t - j
            if 0 <= k < n and j < len(items[k]) and items[k][j] is not None:
                items[k][j]()


def host_consts():
    bf = ml_dtypes.bfloat16
    p = np.arange(128)[:, None]
    y = np.arange(512)[None, :]
    c = {}
    c["c_ident"] = np.eye(128, dtype=np.float32).astype(bf)
    jj = np.arange(128)[:, None]
    ss = np.arange(128)[None, :]
    c["c_tri"] = (jj >= ss).astype(np.float32).astype(bf)
    c["c_ones"] = np.ones((128, 128), np.float32).astype(bf)
    c["c_sbmask"] = (y > p).astype(np.float32).astype(bf)
    c["c_sbbias"] = np.where(y > p, 0.0, BIG).astype(np.float32).astype(bf)
    c["c_onesf"] = np.ones((128, 128), np.float32)
    perm = np.zeros((64, 64), np.float32)
    for i in range(8):
        perm[i + 8, i] = -1.0
        perm[i, i + 8] = 1.0
    c["c_perm"] = perm.astype(bf)
    invf = np.zeros((64, 1), np.float32)
    fr = (500000.0 ** (-np.arange(8, dtype=np.float32) / 8.0)).astype(np.float32)
    invf[0:8, 0] = fr
    invf[8:16, 0] = fr
    c["c_invf"] = invf
    jj = np.arange(64)[:, None, None]
    ii = np.arange(32)[None, :, None]
    sk = np.arange(128)[None, None, :]
    c["c_esel"] = (jj == 2 * ii + (sk >= 64)).astype(np.float32).reshape(64, 32 * 128).astype(bf)
    e = np.arange(8)[None, :, None]
    f = np.arange(512)[None, None, :]
    pp = np.arange(128)[:, None, None]
    dlt = f - pp + 512 - 128 * e
    c["c_win01"] = ((dlt >= 0) & (dlt < 512)).astype(np.float32).reshape(128, 8 * 512).astype(bf)
    c["c_caus01"] = (y >= p).astype(np.float32).astype(bf)
    m = np.arange(512)[None, :] - 256
    c["c_band"] = np.where(p >= 16 * m + 31, 0.0, -BIG).astype(np.float32).astype(bf)
    col = np.arange(128)[None, :]
    dl = (col - 64) - (p >= 64)
    c["c_wcm"] = (dl < -1).astype(np.float32)
    c["c_wfb"] = np.where(dl > 0, -1.0e9, np.where(dl >= -1, 1.0e4, 0.0)).astype(np.float32)
    c["c_anyok"] = (np.arange(128)[:, None] >= 31).astype(np.float32)
    return c


def make_in_map(inputs, b, prog):
    m = {}
    m["x"] = np.ascontiguousarray(inputs["x"][b])
    m["cT"] = np.ascontiguousarray(np.asarray(inputs["c"][b]).reshape(8, 128).T)
    m["pos"] = np.ascontiguousarray(np.asarray(inputs["positions"][b]).reshape(1, S).astype(np.int32))
    for n in ("ada_w", "ada_b", "mix_pre_g", "mix_post_g", "ffn_pre_g", "ffn_post_g", "ffn_w1", "ffn_w2", "sb_w_in", "sb_w_out",
              "nsa_w_in", "nsa_w_out", "nsa_w1_k", "nsa_w2_k", "nsa_w1_v", "nsa_w2_v",
              "cv_w_in", "cv_w_out", "cv_b_out"):
        m[n] = np.asarray(inputs[n])
    m["nsa_pe_kT"] = np.ascontiguousarray(np.asarray(inputs["nsa_pe_k"])[0].T)
    m["nsa_pe_vT"] = np.ascontiguousarray(np.asarray(inputs["nsa_pe_v"])[0].T)
    m["cv_b_inT"] = np.ascontiguousarray(np.asarray(inputs["cv_b_in"]).reshape(16, 128).T)
    m["cv_dwT"] = np.ascontiguousarray(np.asarray(inputs["cv_dw"]).reshape(31, 8, 128).transpose(2, 1, 0))
    m["cv_vecT"] = np.ascontiguousarray(np.stack([np.asarray(inputs[k]).reshape(8, 128).T for k in ("cv_dw_b", "cv_ln_g", "cv_ln_b")], axis=1))
    m.update(host_consts())
    return {k: v for k, v in m.items() if k in prog.in_names}


_PROG_CACHE = {}


def kernel(**inputs):
    prog = Prog()
    nc = prog.build()
    B = inputs["x"].shape[0]
    in_maps = [make_in_map(inputs, b, prog) for b in range(B)]
    res = run_bass_kernel_spmd(nc, in_maps, core_ids=list(range(B)))
    return np.stack([np.asarray(r["out"]) for r in res.results], axis=0).astype(np.float32)
```

```python
import contextlib
import os
import numpy as np
import ml_dtypes
import concourse.bass as bass
import concourse.mybir as mybir
from concourse.bass_utils import run_bass_kernel_spmd

F32 = mybir.dt.float32
BF16 = mybir.dt.bfloat16
I32 = mybir.dt.int32
AF = mybir.ActivationFunctionType
ALU = mybir.AluOpType
AX = mybir.AxisListType

D = 1024
S = 4096
DEPTH = 4
NH = 16
DH = 64
DFF = 4096
EPS = 1e-6
NT = S // 128
NG = S // 512
NSA_IN = 2608
NCMP = 255
BIG = 30000.0
NSA_STOP = os.environ.get("NSA_STOP", "")
NSA_INTERLEAVE = bool(int(os.environ.get("NSA_INTERLEAVE", "0")))
N1_PARTS = os.environ.get("N1_PARTS", "urvg")
ROPE_MODE = os.environ.get("ROPE_MODE", "")


class Src:
    __slots__ = ("sem", "count", "name")

    def __init__(self, sem, name):
        self.sem = sem
        self.count = 0
        self.name = name


class Buf:
    __slots__ = ("name", "w", "r", "const", "excl")

    def __init__(self, name, const=False):
        self.name = name
        self.excl = False
        self.w = []
        self.r = []
        self.const = const


class T:
    __slots__ = ("t", "b")

    def __init__(self, t, name):
        self.t = t
        self.b = Buf(name)

    def __getitem__(self, idx):
        return self.t[idx]


class KB:
    def __init__(self):
        self.nc = bass.Bass("TRN2", target_bir_lowering=False)
        nc = self.nc
        self.st = contextlib.ExitStack()
        self.eng = {"pe": nc.tensor, "act": nc.scalar, "dve": nc.vector, "pool": nc.gpsimd, "sp": nc.sync}
        self.src = {}
        for n in self.eng:
            self.src[n] = Src(self.st.enter_context(nc.semaphore("sem_" + n)), n)
        self.dslots = {}
        self.dnext = {}
        for q, k in (("sp", 12), ("pool", 6), ("act", 4)):
            self.dslots[q] = [Src(self.st.enter_context(nc.semaphore(f"dq_{q}{i}")), f"dq_{q}{i}") for i in range(k)]
            self.dnext[q] = 0
        self.seen = {n: {} for n in self.eng}
        self.ninstr = 0

    def _wait(self, en, tok):
        src, val = tok
        if val <= 0:
            return
        if en == "pe" and src is self.src["pe"]:
            return
        seen = self.seen[en]
        if seen.get(src, 0) >= val:
            return
        self.eng[en].wait_ge(src.sem, val)
        seen[src] = val
        self.ninstr += 1

    def _deps(self, en, r, w, wa=()):
        for x in r:
            b = x.b if isinstance(x, T) else x
            for tok in b.w:
                self._wait(en, tok)
            if b.excl:
                own = self.src.get(en)
                for tok in b.r:
                    if tok[0] is not own:
                        self._wait(en, tok)
        for x in w:
            b = x.b if isinstance(x, T) else x
            for tok in b.w:
                self._wait(en, tok)
            for tok in b.r:
                self._wait(en, tok)
        for x in wa:
            b = x.b if isinstance(x, T) else x
            for tok in b.r:
                self._wait(en, tok)

    def _mark(self, tok, r, w, wa=()):
        for x in r:
            b = x.b if isinstance(x, T) else x
            if not b.const:
                b.r.append(tok)
        for x in w:
            b = x.b if isinstance(x, T) else x
            b.w = [tok]
            b.r = []
        for x in wa:
            b = x.b if isinstance(x, T) else x
            if b.r:
                b.w = []
                b.r = []
            b.w.append(tok)

    def op(self, en, fn, r=(), w=()):
        self._deps(en, r, w)
        ins = fn()
        s = self.src[en]
        s.count += 1
        ins.then_inc(s.sem, 1)
        self._mark((s, s.count), r, w)
        self.ninstr += 1
        return ins

    def pe(self, fn, r=(), w=()):
        return self.op("pe", fn, r, w)

    def act(self, fn, r=(), w=()):
        return self.op("act", fn, r, w)

    def dve(self, fn, r=(), w=()):
        return self.op("dve", fn, r, w)

    def pool(self, fn, r=(), w=()):
        return self.op("pool", fn, r, w)

    def dma(self, q, out, in_, r=(), w=(), wa=()):
        self._deps(q, r, w, wa)
        slots = self.dslots[q]
        sl = slots[self.dnext[q] % len(slots)]
        self.dnext[q] += 1
        self._wait(q, (sl, sl.count))
        ins = self.eng[q].dma_start(out=out, in_=in_)
        sl.count += 16
        ins.then_inc(sl.sem, 16)
        self._mark((sl, sl.count), r, w, wa)
        self.ninstr += 1
        return ins

    def barrier(self):
        toks = [(s, s.count) for s in self.src.values()]
        for q in self.dslots:
            toks += [(s, s.count) for s in self.dslots[q]]
        for en in self.eng:
            for tok in toks:
                if tok[0] is self.src[en]:
                    continue
                self._wait(en, tok)

    def _uniq(self, name):
        self.nalloc = getattr(self, "nalloc", 0) + 1
        return f"{name}_{self.nalloc}"

    def sb(self, ctx, name, shape, dtype):
        name = self._uniq("s_" + name)
        return T(ctx.enter_context(self.nc.sbuf_tensor(name, list(shape), dtype)), name)

    def ps(self, ctx, name, shape=(128, 512), dtype=F32):
        name = self._uniq("p_" + name)
        t = T(ctx.enter_context(self.nc.psum_tensor(name, list(shape), dtype)), name)
        t.b.excl = True
        return t

    def dram(self, name, shape, dtype, kind="Internal"):
        return self.nc.dram_tensor(name, list(shape), dtype, kind=kind).ap()


class Prog:
    def __init__(self, layers=(0, 1, 2, 3), debug=None):
        self.K = KB()
        self.nc = self.K.nc
        self.layers = tuple(layers)
        self.debug = debug or {}
        self.in_names = []
        self._declare_io()

    def _in(self, name, shape, dtype=F32):
        self.in_names.append(name)
        return self.K.dram(name, shape, dtype, kind="ExternalInput")

    def _declare_io(self):
        K = self.K
        self.x = self._in("x", [S, D])
        self.cT = self._in("cT", [128, 8])
        self.pos = self._in("pos", [1, S], I32)
        self.ada_w = self._in("ada_w", [DEPTH, D, 6 * D])
        self.ada_b = self._in("ada_b", [DEPTH, 6 * D])
        self.gains = {n: self._in(n, [DEPTH, D]) for n in ("mix_pre_g", "mix_post_g", "ffn_pre_g", "ffn_post_g")}
        self.ffn_w1 = self._in("ffn_w1", [DEPTH, D, DFF])
        self.ffn_w2 = self._in("ffn_w2", [DEPTH, DFF, D])
        self.sb_w_in = self._in("sb_w_in", [2, D, 3 * D])
        self.sb_w_out = self._in("sb_w_out", [2, D, D])
        self.nsa_w_in = self._in("nsa_w_in", [1, D, NSA_IN])
        self.nsa_w_out = self._in("nsa_w_out", [1, D, D])
        self.nsa_peT = {"k": self._in("nsa_pe_kT", [64, 32]), "v": self._in("nsa_pe_vT", [64, 32])}
        self.nsa_w1 = {"k": self._in("nsa_w1_k", [1, 2048, 256]), "v": self._in("nsa_w1_v", [1, 2048, 256])}
        self.nsa_w2 = {"k": self._in("nsa_w2_k", [1, 256, 64]), "v": self._in("nsa_w2_v", [1, 256, 64])}
        self.cv_w_in = self._in("cv_w_in", [1, D, 2 * D])
        self.cv_b_inT = self._in("cv_b_inT", [128, 16])
        self.cv_dwT = self._in("cv_dwT", [128, 8, 31])
        self.cv_vecT = self._in("cv_vecT", [128, 3, 8])
        self.cv_w_out = self._in("cv_w_out", [1, D, D])
        self.cv_b_out = self._in("cv_b_out", [1, D])
        self.c_ident = self._in("c_ident", [128, 128], BF16)
        self.c_tri = self._in("c_tri", [128, 128], BF16)
        self.c_ones = self._in("c_ones", [128, 128], BF16)
        self.c_sbmask = self._in("c_sbmask", [128, 512], BF16)
        self.c_sbbias = self._in("c_sbbias", [128, 512], BF16)
        self.c_onesf = self._in("c_onesf", [128, 128], F32)
        self.c_perm = self._in("c_perm", [64, 64], BF16)
        self.c_invf = self._in("c_invf", [64, 1], F32)
        self.c_esel = self._in("c_esel", [64, 32 * 128], BF16)
        self.c_win01 = self._in("c_win01", [128, 8 * 512], BF16)
        self.c_caus01 = self._in("c_caus01", [128, 512], BF16)
        self.c_band = self._in("c_band", [128, 512], BF16)
        self.c_wcm = self._in("c_wcm", [128, 128], F32)
        self.c_wfb = self._in("c_wfb", [128, 128], F32)
        self.c_anyok = self._in("c_anyok", [128, 1], F32)
        self.out = K.dram("out", [S, D], F32, kind="ExternalOutput")
        self.modv = K.dram("modv", [DEPTH, 6 * D], F32)
        self.UT = K.dram("UT", [D, S], BF16)
        self.QT = K.dram("QT", [D, S], BF16)
        self.KT = K.dram("KT", [D, S], BF16)
        self.Vd = K.dram("Vd", [S, D], BF16)
        self.HG = K.dram("HG", [D, S], BF16)
        self.NQ = K.dram("NQ", [16, 64, S], BF16)
        self.NK = K.dram("NK", [8, 64, S], BF16)
        self.NC = K.dram("NC", [8, 64, S], BF16)
        self.NV = K.dram("NV", [S, 2 * 4 * 65], BF16)
        self.NGt = K.dram("NGt", [S, 48], F32)
        self.Od = self.Vd
        self.b_od = Buf("od")
        self.b_nq = Buf("nq"); self.b_nk = Buf("nk"); self.b_ncr = Buf("ncr"); self.b_nv = Buf("nv"); self.b_ngt = Buf("ngt")
        self.b_h = [Buf(f"h{t}") for t in range(NT)]
        self.b_ut = [Buf(f"ut{g}") for g in range(NG)]
        self.b_modv = [Buf(f"modv{i}") for i in range(DEPTH)]
        self.b_qt = [Buf(f"qt{g}") for g in range(NG)]
        self.b_kt = [Buf(f"kt{g}") for g in range(NG)]
        self.b_vd = [Buf(f"vd{t}") for t in range(NT)]
        self.b_hg = [Buf(f"hg{g}") for g in range(NG)]
        self.cbuf = Buf("const", const=True)
        for dbg_name, shape in self.debug.items():
            setattr(self, "dbg_" + dbg_name, K.dram("dbg_" + dbg_name, shape, F32, kind="ExternalOutput"))

    def load_consts(self):
        K, nc = self.K, self.nc
        st = K.st
        self.ident = K.sb(st, "ident", [128, 128], BF16)
        self.tri = K.sb(st, "tri", [128, 128], BF16)
        self.ones = K.sb(st, "ones", [128, 128], BF16)
        for t, src in ((self.ident, self.c_ident), (self.tri, self.c_tri), (self.ones, self.c_ones)):
            K.dma("sp", t[:], src, r=[self.cbuf], w=[t])
            t.b.const = True

    def phase_mod(self):
        K, nc = self.K, self.nc
        with contextlib.ExitStack() as ph:
            cT = K.sb(ph, "cT", [128, 8], F32)
            sig = K.sb(ph, "sig", [128, 8], F32)
            cond = K.sb(ph, "cond", [128, 8], F32)
            wsl = [K.sb(ph, f"adaw{i}", [128, 8, 512], F32) for i in range(5)]
            brow = K.sb(ph, "brow", [1, 6 * D], F32)
            mrow = K.sb(ph, "mrow", [1, 6 * D], F32)
            pss = [K.ps(ph, f"modps{i}") for i in range(2)]
            K.dma("sp", cT[:], self.cT, r=[self.cbuf], w=[cT])
            K.act(lambda: nc.scalar.activation(out=sig[:], in_=cT[:], func=AF.Sigmoid), r=[cT], w=[sig])
            K.dve(lambda: nc.vector.tensor_tensor(out=cond[:], in0=cT[:], in1=sig[:], op=ALU.mult), r=[cT, sig], w=[cond])
            it = 0
            for i in self.layers:
                K.dma("sp", brow[:], self.ada_b[i:i + 1, :], r=[self.cbuf], w=[brow])
                wv = self.ada_w[i].rearrange("(kc p) n -> p kc n", p=128)
                for n in range(12):
                    if it == 0:
                        for pre_ in range(4):
                            K.dma("sp", wsl[pre_ % 5][:], wv[:, :, pre_ * 512:(pre_ + 1) * 512], r=[self.cbuf], w=[wsl[pre_ % 5]])
                    wt = wsl[it % 5]
                    ps = pss[it % 2]
                    nxt_ = it + 4
                    it += 1
                    li_, ln_ = nxt_ // 12, nxt_ % 12
                    if li_ < len(self.layers):
                        wv2 = self.ada_w[self.layers[li_]].rearrange("(kc p) n -> p kc n", p=128)
                        K.dma("sp", wsl[nxt_ % 5][:], wv2[:, :, ln_ * 512:(ln_ + 1) * 512], r=[self.cbuf], w=[wsl[nxt_ % 5]])
                    for kc in range(8):
                        K.pe(lambda kc=kc: nc.tensor.matmul(ps[0:1, :], lhsT=cond[:, kc:kc + 1], rhs=wt[:, kc, :],
                                                            start=(kc == 0), stop=(kc == 7)), r=[cond, wt], w=[ps])
                    K.dve(lambda n=n: nc.vector.tensor_tensor(out=mrow[0:1, n * 512:(n + 1) * 512], in0=ps[0:1, :],
                                                              in1=brow[0:1, n * 512:(n + 1) * 512], op=ALU.add),
                          r=[ps, brow], w=[mrow])
                K.dma("sp", self.modv[i:i + 1, :], mrow[:], r=[mrow], w=[self.b_modv[i]])
            K.barrier()

    def load_vec(self, ph, name, src_row):
        t = self.K.sb(ph, name, [128, D], F32)
        return t

    def mod_slice(self, i, k):
        return self.modv[i:i + 1, k * D:(k + 1) * D]

    def bcast_load(self, t, src_row, rbufs):
        self.K.dma("sp", t[:], src_row.to_broadcast([128, D]), r=rbufs, w=[t])

    def phase_norm(self, i, which, src_is_x):
        K, nc = self.K, self.nc
        gname = "mix_pre_g" if which == 0 else "ffn_pre_g"
        ksh, ksc = (0, 1) if which == 0 else (3, 4)
        src = self.x if src_is_x else self.out
        with contextlib.ExitStack() as ph:
            A = K.sb(ph, "nA", [128, D], F32)
            B = K.sb(ph, "nB", [128, D], F32)
            Gn = K.sb(ph, "nG", [128, D], F32)
            self.bcast_load(A, self.mod_slice(i, ksc), [self.b_modv[i]])
            self.bcast_load(B, self.mod_slice(i, ksh), [self.b_modv[i]])
            self.bcast_load(Gn, self.gains[gname][i:i + 1, :], [self.cbuf])
            K.dve(lambda: nc.vector.scalar_tensor_tensor(out=A[:], in0=A[:], scalar=1.0, in1=Gn[:], op0=ALU.add, op1=ALU.mult),
                  r=[A, Gn], w=[A])
            NHS = 6
            hs = [K.sb(ph, f"nh{j}", [128, D], F32) for j in range(NHS)]
            junk = K.sb(ph, "njunk", [128, D], BF16)
            tmp = [K.sb(ph, f"ntmp{j}", [128, D], F32) for j in range(2)]
            ub = [K.sb(ph, f"nub{j}", [128, D], BF16) for j in range(3)]
            st = [K.sb(ph, f"nst{j}", [128, 4], F32) for j in range(2)]
            utg = [K.sb(ph, f"nutg{j}", [128, 8, 512], BF16) for j in range(2)]
            pst = [K.ps(ph, f"npst{j}", (128, 1024), BF16) for j in range(2)]
            UTv = self.UT.rearrange("(kc p) t -> p kc t", p=128)

            def load(t):
                K.dma("sp", hs[t % NHS][:], src[t * 128:(t + 1) * 128, :], r=[self.b_h[t]], w=[hs[t % NHS]])

            for t in range(NHS - 1):
                load(t)
            items = []
            for t in range(NT):
                def s0(t=t):
                    if t + NHS - 1 < NT:
                        load(t + NHS - 1)
                    h, s_, tm, u = hs[t % NHS], st[t % 2], tmp[t % 2], ub[t % 3]
                    K.act(lambda: nc.scalar.activation(out=junk[:], in_=h[:], func=AF.Square, accum_out=s_[:, 0:1]),
                          r=[h], w=[junk, s_])
                    K.act(lambda: nc.scalar.activation(out=s_[:, 1:2], in_=s_[:, 0:1], func=AF.Sqrt, scale=1.0 / D, bias=EPS),
                          r=[s_], w=[s_])
                    K.dve(lambda: nc.vector.reciprocal(out=s_[:, 2:3], in_=s_[:, 1:2]), r=[s_], w=[s_])
                    K.dve(lambda: nc.vector.scalar_tensor_tensor(out=tm[:], in0=h[:], scalar=s_[:, 2:3], in1=A[:], op0=ALU.mult, op1=ALU.mult),
                          r=[h, s_, A], w=[tm])
                    K.dve(lambda: nc.vector.tensor_tensor(out=u[:], in0=tm[:], in1=B[:], op=ALU.add), r=[tm, B], w=[u])

                def s1(t=t):
                    u, pt = ub[t % 3], pst[t % 2]
                    g = t // 4
                    ug = utg[g % 2]
                    for kc in range(8):
                        K.pe(lambda: nc.tensor.transpose(out=pt[:, kc * 128:(kc + 1) * 128], in_=u[:, kc * 128:(kc + 1) * 128],
                                                         identity=self.ident[:]), r=[u, self.ident], w=[pt])
                    tt = t % 4
                    K.act(lambda: nc.scalar.copy(out=ug[:, :, tt * 128:(tt + 1) * 128],
                                                 in_=pt[:].rearrange("p (kc t) -> p kc t", kc=8)), r=[pt], w=[ug])
                    if tt == 3:
                        K.dma("sp", UTv[:, :, g * 512:(g + 1) * 512], ug[:], r=[ug], w=[self.b_ut[g]])
                items.append([s0, s1])
            run_pipeline(items)
            K.barrier()

    def epilogue_setup(self, ph, i, which):
        K, nc = self.K, self.nc
        gname = "mix_post_g" if which == 0 else "ffn_post_g"
        kg = 2 if which == 0 else 5
        G = K.sb(ph, "eG", [128, D], F32)
        Gn = K.sb(ph, "eGn", [128, D], F32)
        self.bcast_load(G, self.mod_slice(i, kg), [self.b_modv[i]])
        self.bcast_load(Gn, self.gains[gname][i:i + 1, :], [self.cbuf])
        K.dve(lambda: nc.vector.tensor_tensor(out=G[:], in0=G[:], in1=Gn[:], op=ALU.mult), r=[G, Gn], w=[G])
        e = {
            "G": G,
            "h": [K.sb(ph, f"eh{j}", [128, D], F32) for j in range(2)],
            "tmp": [K.sb(ph, f"etmp{j}", [128, 512], F32) for j in range(2)],
            "junk": K.sb(ph, "ejunk", [128, 512], BF16),
            "st": [K.sb(ph, f"est{j}", [128, 8], F32) for j in range(2)],
            "n": 0,
            "src_is_x": False,
        }
        return e

    def epi_load(self, e, t):
        src = self.x if self.resid_from_x else self.out
        h = e["h"][t % 2]
        self.K.dma("sp", h[:], src[t * 128:(t + 1) * 128, :], r=[self.b_h[t]], w=[h])

    def epilogue(self, e, t, ybanks):
        K, nc = self.K, self.nc
        h = e["h"][t % 2]
        s_ = e["st"][t % 2]
        junk = e["junk"]
        G = e["G"]
        for hf in range(2):
            K.act(lambda hf=hf: nc.scalar.activation(out=junk[:], in_=ybanks[hf][:], func=AF.Square, accum_out=s_[:, hf:hf + 1]),
                  r=[ybanks[hf]], w=[junk, s_])
        K.dve(lambda: nc.vector.tensor_tensor(out=s_[:, 2:3], in0=s_[:, 0:1], in1=s_[:, 1:2], op=ALU.add), r=[s_], w=[s_])
        K.act(lambda: nc.scalar.activation(out=s_[:, 3:4], in_=s_[:, 2:3], func=AF.Sqrt, scale=1.0 / D, bias=EPS),
              r=[s_], w=[s_])
        K.dve(lambda: nc.vector.reciprocal(out=s_[:, 4:5], in_=s_[:, 3:4]), r=[s_], w=[s_])
        for hf in range(2):
            tm = e["tmp"][hf]
            K.dve(lambda hf=hf, tm=tm: nc.vector.scalar_tensor_tensor(out=tm[:], in0=ybanks[hf][:], scalar=s_[:, 4:5],
                                                                      in1=G[:, hf * 512:(hf + 1) * 512], op0=ALU.mult, op1=ALU.mult),
                  r=[ybanks[hf], s_, G], w=[tm])
            K.pool(lambda hf=hf, tm=tm: nc.gpsimd.tensor_tensor(out=h[:, hf * 512:(hf + 1) * 512], in0=h[:, hf * 512:(hf + 1) * 512],
                                                                 in1=tm[:], op=ALU.add), r=[tm, h], w=[h])
        K.dma("sp", self.out[t * 128:(t + 1) * 128, :], h[:], r=[h], w=[self.b_h[t]])

    def load_w(self, wt, src, kchunks, ncols, col0=0, split=2048):
        K = self.K
        sv = src.rearrange("(kc p) n -> p kc n", p=128)
        for kc in range(kchunks):
            for c0 in range(0, ncols, split):
                c1 = min(ncols, c0 + split)
                K.dma("pool", wt[:, kc, c0:c1], sv[:, kc, col0 + c0:col0 + c1], r=[self.cbuf], wa=[wt])

    def ffn_weights(self, ctx, i):
        K = self.K
        w1 = K.sb(ctx, "fw1", [128, 8, DFF], BF16)
        w2 = K.sb(ctx, "fw2", [128, 32, D], BF16)
        self.load_w(w1, self.ffn_w1[i], 8, DFF)
        self.load_w(w2, self.ffn_w2[i], 32, D, split=1024)
        return w1, w2

    def phase_ffn(self, i, w1, w2):
        K, nc = self.K, self.nc
        with contextlib.ExitStack() as ph:
            e = self.epilogue_setup(ph, i, 1)
            utg = [K.sb(ph, f"futg{j}", [128, 8, 512], BF16) for j in range(2)]
            hid = K.sb(ph, "fhid", [128, 32, 512], BF16)
            rl = [K.sb(ph, f"frl{j}", [128, 512], F32) for j in range(2)]
            psA = [K.ps(ph, f"fpsA{j}") for j in range(2)]
            psY = [[K.ps(ph, f"fpsY{j}{hf}") for hf in range(2)] for j in range(2)]
            UTv = self.UT.rearrange("(kc p) t -> p kc t", p=128)

            def load_u(g):
                K.dma("sp", utg[g % 2][:], UTv[:, :, g * 512:(g + 1) * 512], r=[self.b_ut[g]], w=[utg[g % 2]])

            load_u(0)
            for g in range(NG):
                if g + 1 < NG:
                    load_u(g + 1)
                ug = utg[g % 2]
                for fc in range(32):
                    ps = psA[fc % 2]
                    r_ = rl[fc % 2]
                    for kc in range(8):
                        K.pe(lambda kc=kc, fc=fc, ps=ps: nc.tensor.matmul(ps[:], lhsT=w1[:, kc, fc * 128:(fc + 1) * 128], rhs=ug[:, kc, :],
                                                                          start=(kc == 0), stop=(kc == 7)), r=[w1, ug], w=[ps])
                    K.act(lambda ps=ps, r_=r_: nc.scalar.activation(out=r_[:], in_=ps[:], func=AF.Relu), r=[ps], w=[r_])
                    K.pool(lambda fc=fc, r_=r_: nc.gpsimd.tensor_tensor(out=hid[:, fc, :], in0=r_[:], in1=r_[:], op=ALU.mult),
                           r=[r_], w=[hid])
                for tt in range(4):
                    t = g * 4 + tt
                    self.epi_load(e, t)
                    yb = psY[tt % 2]
                    for hf in range(2):
                        for fc in range(32):
                            K.pe(lambda fc=fc, hf=hf, tt=tt: nc.tensor.matmul(yb[hf][:], lhsT=hid[:, fc, tt * 128:(tt + 1) * 128],
                                                                             rhs=w2[:, fc, hf * 512:(hf + 1) * 512],
                                                                             start=(fc == 0), stop=(fc == 31)), r=[hid, w2], w=[yb[hf]])
                    self.epilogue(e, t, yb)
            K.barrier()

    def phase_sb_proj(self, j):
        K, nc = self.K, self.nc
        with contextlib.ExitStack() as ph:
            w = K.sb(ph, "sw", [128, 8, 3 * D], BF16)
            self.load_w(w, self.sb_w_in[j], 8, 3 * D, split=3072)
            utg = [K.sb(ph, f"sutg{k}", [128, 8, 512], BF16) for k in range(2)]
            stg = [K.sb(ph, f"sstg{k}", [128, 512], BF16) for k in range(4)]
            vst = [K.sb(ph, f"svst{k}", [128, D], BF16) for k in range(2)]
            pss = [K.ps(ph, f"sps{k}") for k in range(4)]
            UTv = self.UT.rearrange("(kc p) t -> p kc t", p=128)

            def load_u(g):
                K.dma("sp", utg[g % 2][:], UTv[:, :, g * 512:(g + 1) * 512], r=[self.b_ut[g]], w=[utg[g % 2]])

            load_u(0)
            n = 0
            for g in range(NG):
                if g + 1 < NG:
                    load_u(g + 1)
                ug = utg[g % 2]
                for fcn in range(16):
                    ps = pss[n % 4]
                    sg = stg[n % 4]
                    for kc in range(8):
                        K.pe(lambda kc=kc, fcn=fcn, ps=ps: nc.tensor.matmul(ps[:], lhsT=w[:, kc, fcn * 128:(fcn + 1) * 128], rhs=ug[:, kc, :],
                                                                           start=(kc == 0), stop=(kc == 7)), r=[w, ug], w=[ps])
                    if n % 2 == 0:
                        K.act(lambda ps=ps, sg=sg: nc.scalar.copy(out=sg[:], in_=ps[:]), r=[ps], w=[sg])
                    else:
                        K.dve(lambda ps=ps, sg=sg: nc.vector.tensor_copy(out=sg[:], in_=ps[:]), r=[ps], w=[sg])
                    dst = self.QT if fcn < 8 else self.KT
                    fo = (fcn % 8) * 128
                    bb = self.b_qt[g] if fcn < 8 else self.b_kt[g]
                    K.dma("sp", dst[fo:fo + 128, g * 512:(g + 1) * 512], sg[:], r=[sg], wa=[bb])
                    n += 1
                for tt in range(4):
                    t = g * 4 + tt
                    vs = vst[t % 2]
                    for hf in range(2):
                        ps = pss[n % 4]
                        n += 1
                        for kc in range(8):
                            K.pe(lambda kc=kc, hf=hf, tt=tt, ps=ps: nc.tensor.matmul(ps[:], lhsT=ug[:, kc, tt * 128:(tt + 1) * 128],
                                                                                    rhs=w[:, kc, 2 * D + hf * 512:2 * D + (hf + 1) * 512],
                                                                                    start=(kc == 0), stop=(kc == 7)), r=[w, ug], w=[ps])
                        if hf == 0:
                            K.act(lambda ps=ps, vs=vs: nc.scalar.copy(out=vs[:, 0:512], in_=ps[:]), r=[ps], w=[vs])
                        else:
                            K.dve(lambda ps=ps, vs=vs: nc.vector.tensor_copy(out=vs[:, 512:1024], in_=ps[:]), r=[ps], w=[vs])
                    K.dma("sp", self.Vd[t * 128:(t + 1) * 128, :], vs[:], r=[vs], w=[self.b_vd[t]])
            K.barrier()

    def phase_sb_core(self, i, j):
        K, nc = self.K, self.nc
        with contextlib.ExitStack() as ph:
            O = K.sb(ph, "aO", [128, NT, D], BF16)
            with contextlib.ExitStack() as ph2:
                V = K.sb(ph2, "aV", [128, NT, D], BF16)
                Vv = self.Vd.rearrange("(t p) f -> p t f", p=128)
                for t0 in range(0, NT, 8):
                    K.dma("sp", V[:, t0:t0 + 8, :], Vv[:, t0:t0 + 8, :], r=self.b_vd[t0:t0 + 8], wa=[V])
                msk = K.sb(ph2, "amsk", [128, 512], BF16)
                mbias = K.sb(ph2, "ambias", [128, 512], BF16)
                K.dma("sp", msk[:], self.c_sbmask, r=[self.cbuf], w=[msk])
                K.dma("sp", mbias[:], self.c_sbbias, r=[self.cbuf], w=[mbias])
                qh = [K.sb(ph2, f"aq{k}", [64, S], BF16) for k in range(2)]
                kh = [K.sb(ph2, f"ak{k}", [64, S], BF16) for k in range(2)]
                Et = [K.sb(ph2, f"aE{k}", [128, 512], F32) for k in range(3)]
                Xt = [K.sb(ph2, f"aX{k}", [128, 512], F32) for k in range(2)]
                Lt = [K.sb(ph2, f"aL{k}", [128, 512], BF16) for k in range(3)]
                Lr = [K.sb(ph2, f"aLr{k}", [128, 512], BF16) for k in range(2)]
                At = [K.sb(ph2, f"aA{k}", [128, 512], BF16) for k in range(4)]
                Rt = [K.sb(ph2, f"aR{k}", [128, 512], BF16) for k in range(2)]
                zps = [K.ps(ph2, f"azps{k}") for k in range(2)]
                cps = [K.ps(ph2, f"acps{k}") for k in range(2)]
                avs = [K.ps(ph2, f"aav{k}") for k in range(2)]

                Rq = [K.sb(ph2, f"aRq{k}", [128, 512], BF16) for k in range(2)]

                def load_head(h):
                    s = h % 2
                    K.dma("sp", qh[s][:], self.QT[h * 64:(h + 1) * 64, :], r=self.b_qt, w=[qh[s]])
                    K.dma("sp", kh[s][:], self.KT[h * 64:(h + 1) * 64, :], r=self.b_kt, w=[kh[s]])

                units = []
                ci = 0
                for h in range(NH):
                    for qc in range(NG):
                        nkt = 4 * qc + 4
                        for k_, i_ in enumerate(range(nkt - 1, -1, -1)):
                            units.append(dict(h=h, qc=qc, i=i_, k=k_, nkt=nkt, ci=ci, idx=len(units),
                                              lasth=(qc == NG - 1 and i_ == 0)))
                        ci += 1
                Rall = [[Rt[0], Rt[1]], [Rq[0], Rq[1]]]

                def geom(U):
                    a = U["i"] - 4 * U["qc"]
                    diag = a >= 0
                    c0 = 128 * a if diag else 0
                    return a, diag, c0, 512 - c0

                def stageA(U):
                    u, h, qc, i_ = U["idx"], U["h"], U["qc"], U["i"]
                    a, diag, c0, W = geom(U)
                    q_, k_ = qh[h % 2], kh[h % 2]
                    zp, E, L = zps[u % 2], Et[u % 3], Lt[u % 3]
                    qs = q_[:, qc * 512 + c0:(qc + 1) * 512]
                    K.pe(lambda: nc.tensor.matmul(zp[:, 0:W], lhsT=k_[:, i_ * 128:(i_ + 1) * 128], rhs=qs, start=True, stop=True),
                         r=[q_, k_], w=[zp])
                    K.act(lambda: nc.scalar.activation(out=E[:, 0:W], in_=zp[:, 0:W], func=AF.Exp, scale=0.125), r=[zp], w=[E])
                    if diag:
                        Lraw = Lr[u % 2]
                        K.act(lambda: nc.scalar.activation(out=Lraw[:, 0:W], in_=E[:, 0:W], func=AF.Ln, bias=1.0), r=[E], w=[Lraw])
                        K.dve(lambda: nc.vector.tensor_tensor(out=L[:, 0:W], in0=Lraw[:, 0:W], in1=msk[:, 0:W], op=ALU.mult),
                              r=[Lraw, msk], w=[L])
                    else:
                        K.act(lambda: nc.scalar.activation(out=L[:, 0:W], in_=E[:, 0:W], func=AF.Ln, bias=1.0), r=[E], w=[L])

                def stageB(U):
                    u, h, qc, i_, k = U["idx"], U["h"], U["qc"], U["i"], U["k"]
                    a, diag, c0, W = geom(U)
                    cp, L, A, E, X = cps[u % 2], Lt[u % 3], At[u % 4], Et[u % 3], Xt[u % 2]
                    Rp = Rall[U["ci"] % 2]
                    Rc, Rn = Rp[k % 2], Rp[(k + 1) % 2]
                    if k == 0:
                        K.pool(lambda: nc.gpsimd.memset(Rp[0][:], 0.0), w=[Rp[0]])
                        K.pool(lambda: nc.gpsimd.memset(Rp[1][:], 0.0), w=[Rp[1]])
                    K.pe(lambda: nc.tensor.matmul(cp[:, 0:W], lhsT=self.tri[:], rhs=L[:, 0:W], start=True, stop=(k == 0 and not diag)),
                         r=[self.tri, L], w=[cp])
                    if k != 0:
                        K.pe(lambda: nc.tensor.matmul(cp[:, 0:W], lhsT=self.ones[:], rhs=Rc[:, c0:512], start=False, stop=(not diag)),
                             r=[self.ones, Rc], w=[cp])
                    if diag:
                        K.pe(lambda: nc.tensor.matmul(cp[:, 0:W], lhsT=self.ident[:], rhs=mbias[:, 0:W], start=False, stop=True),
                             r=[self.ident, mbias], w=[cp])
                    if i_ != 0:
                        K.dve(lambda: nc.vector.tensor_tensor(out=Rn[:, c0:512], in0=Rc[:, c0:512], in1=L[:, 0:W], op=ALU.add),
                              r=[Rc, L], w=[Rn])
                    K.act(lambda: nc.scalar.activation(out=X[:, 0:W], in_=cp[:, 0:W], func=AF.Exp, scale=-1.0), r=[cp], w=[X])
                    K.dve(lambda: nc.vector.tensor_tensor(out=A[:, 0:W], in0=E[:, 0:W], in1=X[:, 0:W], op=ALU.mult), r=[E, X], w=[A])

                def stageC(U):
                    u, h, qc, i_, k = U["idx"], U["h"], U["qc"], U["i"], U["k"]
                    a, diag, c0, W = geom(U)
                    A = At[u % 4]
                    av = avs[U["ci"] % 2]
                    for n_, c in enumerate(range(a if diag else 0, 4)):
                        K.pe(lambda: nc.tensor.matmul(av[:, c * 64:(c + 1) * 64], lhsT=A[:, c * 128 - c0:(c + 1) * 128 - c0],
                                                      rhs=V[:, i_, h * 64:(h + 1) * 64], start=(k == 0 and n_ == 0), stop=False,
                                                      skip_group_check=True), r=[A, V], w=[av])
                    if i_ == 0:
                        K.dve(lambda: nc.vector.tensor_copy(out=O[:, qc * 4:(qc + 1) * 4, h * 64:(h + 1) * 64],
                                                            in_=av[:, 0:256].rearrange("p (c d) -> p c d", c=4)), r=[av], w=[O])
                    if U["lasth"] and h + 2 < NH:
                        load_head(h + 2)

                load_head(0)
                load_head(1)
                n = len(units)
                for kk in range(n + 3):
                    if kk < n:
                        stageA(units[kk])
                    if 0 <= kk - 1 < n:
                        stageB(units[kk - 1])
                    if 0 <= kk - 3 < n:
                        stageC(units[kk - 3])
                K.barrier()
            self.out_proj(ph, i, self.sb_w_out[j], O, None)
            K.barrier()

    def out_proj(self, ph, i, w_dram, O, bias_row):
        K, nc = self.K, self.nc
        with contextlib.ExitStack() as ph3:
            w = K.sb(ph3, "ow", [128, 8, D], BF16)
            self.load_w(w, w_dram, 8, D, split=1024)
            e = self.epilogue_setup(ph3, i, 0)
            oT = [K.sb(ph3, f"ooT{k}", [128, 8, 128], BF16) for k in range(2)]
            ptr = [K.ps(ph3, f"optr{k}", (128, 1024), BF16) for k in range(2)]
            psY = [[K.ps(ph3, f"opsY{k}{hf}") for hf in range(2)] for k in range(2)]
            brow = None
            if bias_row is not None:
                brow = K.sb(ph3, "obrow", [1, D], BF16)
                K.dma("pool", brow[:], bias_row, r=[self.cbuf], w=[brow])
            oin = None
            if O is None:
                oin = [K.sb(ph3, f"ooin{k}", [128, D], BF16) for k in range(3)]
                for t in range(2):
                    K.dma("sp", oin[t % 3][:], self.Od[t * 128:(t + 1) * 128, :], r=[self.b_od], w=[oin[t % 3]])
            for t in range(NT):
                self.epi_load(e, t)
                pt = ptr[t % 2]
                ot = oT[t % 2]
                if oin is not None and t + 2 < NT:
                    K.dma("sp", oin[(t + 2) % 3][:], self.Od[(t + 2) * 128:(t + 3) * 128, :], r=[self.b_od], w=[oin[(t + 2) % 3]])
                for kc in range(8):
                    src_ = O[:, t, kc * 128:(kc + 1) * 128] if oin is None else oin[t % 3][:, kc * 128:(kc + 1) * 128]
                    srcT = O if oin is None else oin[t % 3]
                    K.pe(lambda kc=kc: nc.tensor.transpose(out=pt[:, kc * 128:(kc + 1) * 128], in_=src_,
                                                          identity=self.ident[:]), r=[srcT, self.ident], w=[pt])
                K.act(lambda: nc.scalar.copy(out=ot[:], in_=pt[:].rearrange("p (kc t) -> p kc t", kc=8)), r=[pt], w=[ot])
                yb = psY[t % 2]
                for hf in range(2):
                    for kc in range(8):
                        K.pe(lambda kc=kc, hf=hf: nc.tensor.matmul(yb[hf][:], lhsT=ot[:, kc, :], rhs=w[:, kc, hf * 512:(hf + 1) * 512],
                                                                   start=(kc == 0), stop=(kc == 7 and brow is None)), r=[ot, w], w=[yb[hf]])
                    if brow is not None:
                        K.pe(lambda hf=hf: nc.tensor.matmul(yb[hf][:], lhsT=self.ones[0:1, :], rhs=brow[0:1, hf * 512:(hf + 1) * 512],
                                                            start=False, stop=True), r=[self.ones, brow], w=[yb[hf]])
                self.epilogue(e, t, yb)

    def build(self):
        K = self.K
        self.load_consts()
        self.phase_mod()
        first = True
        for i in self.layers:
            kind, j = i % 3, i // 3
            self.phase_norm(i, 0, src_is_x=first)
            self.resid_from_x = first
            first = False
            if kind == 0:
                self.phase_sb_proj(j)
                self.phase_sb_core(i, j)
            elif kind == 1:
                self.phase_nsa(i, j)
            else:
                self.phase_conv(i, j)
            self.resid_from_x = False
            with contextlib.ExitStack() as wctx:
                w1, w2 = self.ffn_weights(wctx, i)
                self.phase_norm(i, 1, src_is_x=False)
                self.phase_ffn(i, w1, w2)
        K.barrier()
        K.st.close()
        return self.nc

    def copy_x_to_out(self):
        K = self.K
        with contextlib.ExitStack() as ph:
            bufs = [K.sb(ph, f"cx{k}", [128, 4, D], F32) for k in range(2)]
            xv = self.x.rearrange("(g c p) f -> g p c f", p=128, c=4)
            ov = self.out.rearrange("(g c p) f -> g p c f", p=128, c=4)
            for g in range(NG):
                b = bufs[g % 2]
                K.dma("sp", b[:], xv[g], r=[], w=[b])
                K.dma("sp", ov[g], b[:], r=[b], w=[self.b_h[4 * g + c] for c in range(4)])
            K.barrier()


    def phase_nsa(self, i, j):
        K, nc = self.K, self.nc
        UTv = self.UT.rearrange("(kc p) t -> p kc t", p=128)
        TWO_PI = 2.0 * np.pi
        with contextlib.ExitStack() as nsa:
            kcT = K.sb(nsa, "n_kcT", [64, 4, 256], BF16)
            vcmp = K.sb(nsa, "n_vcmp", [128, 2, 4, 64], BF16)
            cscmp = K.sb(nsa, "n_cscmp", [64, 2, 256], F32)
            perm = K.sb(nsa, "n_perm", [64, 64], BF16)
            K.dma("sp", perm[:], self.c_perm, r=[self.cbuf], w=[perm])
            with contextlib.ExitStack() as ph:
                w = K.sb(ph, "nw", [128, 8, NSA_IN], BF16)
                self.load_w(w, self.nsa_w_in[j], 8, NSA_IN, split=NSA_IN)
                cosT = K.sb(ph, "ncos", [64, S], F32)
                sinT = K.sb(ph, "nsin", [64, S], F32)
                with contextlib.ExitStack() as ph0:
                    posi = K.sb(ph0, "nposi", [64, S], I32)
                    ang = K.sb(ph0, "nang", [64, S], F32)
                    t1 = K.sb(ph0, "nt1", [64, S], F32)
                    t2 = K.sb(ph0, "nt2", [64, S], F32)
                    ki = K.sb(ph0, "nki", [64, S], I32)
                    invf = K.sb(ph0, "ninvf", [64, 1], F32)
                    K.dma("sp", posi[:], self.pos.to_broadcast([64, S]), r=[self.cbuf], w=[posi])
                    K.dma("sp", invf[:], self.c_invf, r=[self.cbuf], w=[invf])
                    K.dve(lambda: nc.vector.tensor_copy(out=ang[:], in_=posi[:]), r=[posi], w=[ang])
                    K.dve(lambda: nc.vector.tensor_scalar(out=ang[:], in0=ang[:], scalar1=invf[:, 0:1], scalar2=None, op0=ALU.mult),
                          r=[ang, invf], w=[ang])
                    for tab, shift in ((sinT, 0.0), (cosT, 0.5 * np.pi)):
                        K.dve(lambda: nc.vector.tensor_scalar(out=t1[:], in0=ang[:], scalar1=shift, scalar2=1.0 / TWO_PI,
                                                              op0=ALU.add, op1=ALU.mult), r=[ang], w=[t1])
                        K.dve(lambda: nc.vector.tensor_copy(out=ki[:], in_=t1[:]), r=[t1], w=[ki])
                        K.dve(lambda: nc.vector.tensor_copy(out=t1[:], in_=ki[:]), r=[ki], w=[t1])
                        K.dve(lambda: nc.vector.scalar_tensor_tensor(out=t2[:], in0=t1[:], scalar=-TWO_PI, in1=ang[:], op0=ALU.mult, op1=ALU.add),
                              r=[t1, ang], w=[t2])
                        K.dve(lambda: nc.vector.tensor_scalar(out=t2[:], in0=t2[:], scalar1=shift, scalar2=None, op0=ALU.add), r=[t2], w=[t2])
                        K.dve(lambda: nc.vector.tensor_scalar(out=t1[:], in0=t2[:], scalar1=np.pi, scalar2=-TWO_PI, op0=ALU.is_gt, op1=ALU.mult),
                              r=[t2], w=[t1])
                        K.dve(lambda: nc.vector.tensor_tensor(out=t2[:], in0=t2[:], in1=t1[:], op=ALU.add), r=[t2, t1], w=[t2])
                        K.dve(lambda: nc.vector.tensor_scalar(out=t1[:], in0=t2[:], scalar1=-np.pi, scalar2=TWO_PI, op0=ALU.is_lt, op1=ALU.mult),
                              r=[t2], w=[t1])
                        K.dve(lambda: nc.vector.tensor_tensor(out=t2[:], in0=t2[:], in1=t1[:], op=ALU.add), r=[t2, t1], w=[t2])
                        K.dve(lambda: nc.vector.tensor_scalar(out=t2[:], in0=t2[:], scalar1=-3.1415925, scalar2=3.1415925, op0=ALU.max, op1=ALU.min),
                              r=[t2], w=[t2])
                        K.act(lambda: nc.scalar.activation(out=tab[:], in_=t2[:], func=AF.Sin), r=[t2], w=[tab])
                    K.dve(lambda: nc.vector.tensor_copy(out=cscmp[:, 0, 0:255], in_=cosT[:, 31:S:16]), r=[cosT], w=[cscmp])
                    K.dve(lambda: nc.vector.tensor_copy(out=cscmp[:, 1, 0:255], in_=sinT[:, 31:S:16]), r=[sinT, cscmp], w=[cscmp])
                    K.barrier()
                if NSA_STOP == "n0":
                    return
                utg = [K.sb(ph, f"nutg{k}", [128, 8, 512], BF16) for k in range(2)]
                xb = [K.sb(ph, f"nxb{k}", [64, 512], BF16) for k in range(2)]
                r1 = [K.sb(ph, f"nr1{k}", [64, 512], F32) for k in range(2)]
                r2 = [K.sb(ph, f"nr2{k}", [64, 512], F32) for k in range(2)]
                ob = [K.sb(ph, f"nob{k}", [64, 512], BF16) for k in range(4)]
                va = [K.sb(ph, f"nva{k}", [128, 8, 65], BF16) for k in range(2)]
                gt = [K.sb(ph, f"ngt{k}", [128, 48], F32) for k in range(2)]
                for v_ in va:
                    K.pool(lambda: nc.gpsimd.memset(v_[:], 1.0), w=[v_])
                pp = [K.ps(ph, f"npp{k}") for k in range(3)]
                pr = [K.ps(ph, f"npr{k}") for k in range(2)]
                pv = [K.ps(ph, f"npv{k}") for k in range(2)]
                pg = K.ps(ph, "npg")

                def load_u(g):
                    K.dma("sp", utg[g % 2][:], UTv[:, :, g * 512:(g + 1) * 512], r=[self.b_ut[g]], w=[utg[g % 2]])

                units = []
                for h in range(16):
                    units.append((h * 64, self.NQ, h, True, self.b_nq))
                for g4 in range(4):
                    units.append((D + 2 * 256 + g4 * 64, self.NK, g4, True, self.b_nk))
                for g4 in range(4):
                    units.append((D + 4 * 256 + g4 * 64, self.NK, 4 + g4, True, self.b_nk))
                for g4 in range(4):
                    units.append((D + 0 * 256 + g4 * 64, self.NC, g4, False, self.b_ncr))
                for g4 in range(4):
                    units.append((D + 1 * 256 + g4 * 64, self.NC, 4 + g4, False, self.b_ncr))
                load_u(0)
                n = 0
                nr = 0
                for g in range(NG):
                    if g + 1 < NG:
                        load_u(g + 1)
                    ug = utg[g % 2]
                    tsl = slice(g * 512, (g + 1) * 512)
                    for (col, dst, ui, rope, bb) in units:
                        if (rope and "r" not in N1_PARTS) or ((not rope) and "u" not in N1_PARTS):
                            continue
                        ps = pp[n % 3]
                        o_ = ob[n % 4]
                        n += 1
                        for kc in range(8):
                            K.pe(lambda: nc.tensor.matmul(ps[0:64, :], lhsT=w[:, kc, col:col + 64], rhs=ug[:, kc, :],
                                                          start=(kc == 0), stop=(kc == 7)), r=[w, ug], w=[ps])
                        if rope and "asu" not in ROPE_MODE:
                            x_, a_, b_, p2 = xb[nr % 2], r1[nr % 2], r2[nr % 2], pr[nr % 2]
                            nr += 1
                            K.act(lambda: nc.scalar.copy(out=x_[:], in_=ps[0:64, :]), r=[ps], w=[x_])
                            if "noperm" not in ROPE_MODE:
                                K.pe(lambda: nc.tensor.matmul(p2[0:64, :], lhsT=perm[:], rhs=x_[:], start=True, stop=True), r=[perm, x_], w=[p2])
                            K.dve(lambda: nc.vector.tensor_tensor(out=a_[:], in0=ps[0:64, :], in1=cosT[:, tsl], op=ALU.mult), r=[ps, cosT], w=[a_])
                            if "noperm" not in ROPE_MODE:
                                K.dve(lambda: nc.vector.tensor_tensor(out=b_[:], in0=p2[0:64, :], in1=sinT[:, tsl], op=ALU.mult), r=[p2, sinT], w=[b_])
                            else:
                                K.dve(lambda: nc.vector.tensor_tensor(out=b_[:], in0=ps[0:64, :], in1=sinT[:, tsl], op=ALU.mult), r=[ps, sinT], w=[b_])
                            if "dveadd" in ROPE_MODE:
                                K.dve(lambda: nc.vector.tensor_tensor(out=o_[:], in0=a_[:], in1=b_[:], op=ALU.add), r=[a_, b_], w=[o_])
                            else:
                                K.pool(lambda: nc.gpsimd.tensor_tensor(out=o_[:], in0=a_[:], in1=b_[:], op=ALU.add), r=[a_, b_], w=[o_])
                        else:
                            K.act(lambda: nc.scalar.copy(out=o_[:], in_=ps[0:64, :]), r=[ps], w=[o_])
                        K.dma("sp", dst[ui, :, tsl], o_[:], r=[o_], wa=[bb])
                    for tt in range(4):
                        t = g * 4 + tt
                        v_ = va[t % 2]
                        pv_ = pv[t % 2]
                        for m, c0 in ((0, D + 3 * 256), (1, D + 5 * 256)) if "v" in N1_PARTS else ():
                            for kc in range(8):
                                K.pe(lambda: nc.tensor.matmul(pv_[:, m * 256:(m + 1) * 256], lhsT=ug[:, kc, tt * 128:(tt + 1) * 128],
                                                              rhs=w[:, kc, c0:c0 + 256], start=(kc == 0), stop=(kc == 7)), r=[w, ug], w=[pv_])
                        if "v" in N1_PARTS:
                            K.dve(lambda: nc.vector.tensor_copy(out=v_[:, :, 0:64], in_=pv_[:].rearrange("p (u d) -> p u d", d=64)), r=[pv_], w=[v_])
                            K.dma("sp", self.NV[t * 128:(t + 1) * 128, :], v_[:].rearrange("p u d -> p (u d)"), r=[v_], wa=[self.b_nv])
                        g_ = gt[t % 2]
                        if "g" not in N1_PARTS:
                            continue
                        for kc in range(8):
                            K.pe(lambda: nc.tensor.matmul(pg[:, 0:48], lhsT=ug[:, kc, tt * 128:(tt + 1) * 128], rhs=w[:, kc, 2560:2608],
                                                          start=(kc == 0), stop=(kc == 7)), r=[w, ug], w=[pg])
                        K.act(lambda: nc.scalar.activation(out=g_[:], in_=pg[:, 0:48], func=AF.Sigmoid), r=[pg], w=[g_])
                        K.dma("sp", self.NGt[t * 128:(t + 1) * 128, :], g_[:], r=[g_], wa=[self.b_ngt])
                K.barrier()
            if NSA_STOP == "n1":
                return
            with contextlib.ExitStack() as ph:
                raw = K.sb(ph, "craw", [64, 8, S], BF16)
                for u_ in range(8):
                    K.dma("sp", raw[:, u_, :], self.NC[u_], r=[self.b_ncr], wa=[raw])
                K.pool(lambda: nc.gpsimd.memset(vcmp[:], 0.0), w=[vcmp])
                K.pool(lambda: nc.gpsimd.memset(kcT[:], 0.0), w=[kcT])
                hps = [K.ps(ph, f"chps{k}") for k in range(2)]
                bps = K.ps(ph, "cbps")
                ops_ = [K.ps(ph, f"cops{k}") for k in range(2)]
                p2 = K.ps(ph, "cp2")
                for kv in ("k", "v"):
                    w1 = K.sb(ph, "cw1" + kv, [64, 32, 256], BF16)
                    w1v = self.nsa_w1[kv][j].rearrange("(l d) h -> d l h", d=64)
                    for l0 in range(0, 32, 8):
                        K.dma("pool", w1[:, l0:l0 + 8, :], w1v[:, l0:l0 + 8, :], r=[self.cbuf], wa=[w1])
                    w2 = K.sb(ph, "cw2" + kv, [128, 2, 64], BF16)
                    K.dma("pool", w2[:], self.nsa_w2[kv][j].rearrange("(hc p) d -> p hc d", p=128), r=[self.cbuf], w=[w2])
                    peT = K.sb(ph, "cpeT" + kv, [64, 32], F32)
                    peTb = K.sb(ph, "cpeTb" + kv, [64, 32], BF16)
                    K.dma("sp", peT[:], self.nsa_peT[kv], r=[self.cbuf], w=[peT])
                    K.dve(lambda: nc.vector.tensor_copy(out=peTb[:], in_=peT[:]), r=[peT], w=[peTb])
                    bias = K.sb(ph, "cbias" + kv, [128, 2], F32)
                    for hc in range(2):
                        for l in range(32):
                            K.pe(lambda: nc.tensor.matmul(bps[:, hc:hc + 1], lhsT=w1[:, l, hc * 128:(hc + 1) * 128], rhs=peTb[:, l:l + 1],
                                                          start=(l == 0), stop=(l == 31)), r=[w1, peTb], w=[bps])
                        K.dve(lambda: nc.vector.tensor_copy(out=bias[:, hc:hc + 1], in_=bps[:, hc:hc + 1]), r=[bps], w=[bias])
                    xb_ = K.sb(ph, "cxb" + kv, [128, 256], F32)
                    x2_ = K.sb(ph, "cx2" + kv, [128, 256], F32)
                    x3_ = K.sb(ph, "cx3" + kv, [128, 256], F32)
                    hidT = K.sb(ph, "chid" + kv, [128, 2, 256], BF16)
                    kx = K.sb(ph, "ckx" + kv, [64, 256], BF16)
                    ka = K.sb(ph, "cka" + kv, [64, 256], F32)
                    kb_ = K.sb(ph, "ckb" + kv, [64, 256], F32)
                    for g4 in range(4):
                        ui = g4 if kv == "k" else 4 + g4
                        for hc in range(2):
                            hp = hps[hc]
                            for l in range(32):
                                K.pe(lambda: nc.tensor.matmul(hp[:, 0:255], lhsT=w1[:, l, hc * 128:(hc + 1) * 128],
                                                              rhs=raw[:, ui, l:l + 16 * 254 + 1:16], start=(l == 0), stop=(l == 31)),
                                     r=[w1, raw], w=[hp])
                            K.dve(lambda: nc.vector.tensor_scalar(out=xb_[:, 0:255], in0=hp[:, 0:255], scalar1=bias[:, hc:hc + 1], scalar2=None,
                                                                  op0=ALU.add), r=[hp, bias], w=[xb_])
                            K.pool(lambda: nc.gpsimd.tensor_tensor(out=x2_[:, 0:255], in0=xb_[:, 0:255], in1=xb_[:, 0:255], op=ALU.mult), r=[xb_], w=[x2_])
                            K.dve(lambda: nc.vector.tensor_scalar(out=x2_[:, 0:255], in0=x2_[:, 0:255], scalar1=0.044715, scalar2=1.0,
                                                                  op0=ALU.mult, op1=ALU.add), r=[x2_], w=[x2_])
                            K.dve(lambda: nc.vector.tensor_tensor(out=x3_[:, 0:255], in0=x2_[:, 0:255], in1=xb_[:, 0:255], op=ALU.mult), r=[x2_, xb_], w=[x3_])
                            K.act(lambda: nc.scalar.activation(out=x3_[:, 0:255], in_=x3_[:, 0:255], func=AF.Tanh, scale=0.7978845608028654),
                                  r=[x3_], w=[x3_])
                            K.dve(lambda: nc.vector.scalar_tensor_tensor(out=x2_[:, 0:255], in0=x3_[:, 0:255], scalar=1.0, in1=xb_[:, 0:255],
                                                                         op0=ALU.add, op1=ALU.mult), r=[x3_, xb_], w=[x2_])
                            K.pool(lambda: nc.gpsimd.tensor_scalar(out=hidT[:, hc, 0:255], in0=x2_[:, 0:255], scalar1=0.5, scalar2=None, op0=ALU.mult),
                                   r=[x2_], w=[hidT])
                        if kv == "k":
                            op_ = ops_[0]
                            for hc in range(2):
                                K.pe(lambda: nc.tensor.matmul(op_[0:64, 0:255], lhsT=w2[:, hc, :], rhs=hidT[:, hc, 0:255],
                                                              start=(hc == 0), stop=(hc == 1)), r=[w2, hidT], w=[op_])
                            K.act(lambda: nc.scalar.copy(out=kx[:, 0:255], in_=op_[0:64, 0:255]), r=[op_], w=[kx])
                            K.pe(lambda: nc.tensor.matmul(p2[0:64, 0:255], lhsT=perm[:], rhs=kx[:, 0:255], start=True, stop=True), r=[perm, kx], w=[p2])
                            K.dve(lambda: nc.vector.tensor_tensor(out=ka[:, 0:255], in0=op_[0:64, 0:255], in1=cscmp[:, 0, 0:255], op=ALU.mult),
                                  r=[op_, cscmp], w=[ka])
                            K.dve(lambda: nc.vector.tensor_tensor(out=kb_[:, 0:255], in0=p2[0:64, 0:255], in1=cscmp[:, 1, 0:255], op=ALU.mult),
                                  r=[p2, cscmp], w=[kb_])
                            K.pool(lambda: nc.gpsimd.tensor_tensor(out=kcT[:, g4, 0:255], in0=ka[:, 0:255], in1=kb_[:, 0:255], op=ALU.add),
                                   r=[ka, kb_], w=[kcT])
                        else:
                            for nch, m in ((0, 128), (1, 127)):
                                op_ = ops_[nch]
                                for hc in range(2):
                                    K.pe(lambda: nc.tensor.matmul(op_[0:m, 0:64], lhsT=hidT[:, hc, nch * 128:nch * 128 + m], rhs=w2[:, hc, :],
                                                                  start=(hc == 0), stop=(hc == 1)), r=[w2, hidT], w=[op_])
                                K.act(lambda: nc.scalar.copy(out=vcmp[0:m, nch, g4, :], in_=op_[0:m, 0:64]), r=[op_], w=[vcmp])
                K.barrier()
            if NSA_STOP == "n2":
                return
            with contextlib.ExitStack() as ph:
                with contextlib.ExitStack() as ph2:
                    self._nsa_attn(ph2, None, kcT, vcmp)
                    K.barrier()
                self.out_proj(ph, i, self.nsa_w_out[j], None, None)
                K.barrier()

    def _nsa_attn(self, ph, O, kcT, vcmp):
        K, nc = self.K, self.nc
        V = K.sb(ph, "tV", [128, NT, 8 * 65], BF16)
        NVv = self.NV.rearrange("(t p) f -> p t f", p=128)
        for t0 in range(0, NT, 8):
            K.dma("sp", V[:, t0:t0 + 8, :], NVv[:, t0:t0 + 8, :], r=[self.b_nv], wa=[V])
        GT = K.sb(ph, "tGT", [128, NT, 48], F32)
        NGv = self.NGt.rearrange("(t p) f -> p t f", p=128)
        for t0 in range(0, NT, 8):
            K.dma("sp", GT[:, t0:t0 + 8, :], NGv[:, t0:t0 + 8, :], r=[self.b_ngt], wa=[GT])
        esel = K.sb(ph, "tesel", [64, 32 * 128], BF16)
        winb = K.sb(ph, "twin01", [128, 8 * 512], BF16)
        causb = K.sb(ph, "tcaus01", [128, 512], BF16)
        Mks = [K.sb(ph, f"tMk{k}", [128, 28 + 4 * k, 512], BF16) for k in range(2)]
        Ob = [K.sb(ph, f"tOb{k}", [128, 4, 256], BF16) for k in range(2)]
        Odv = self.Od.rearrange("(t p) f -> p t f", p=128)
        band = K.sb(ph, "tband", [128, 512], BF16)
        wcm = K.sb(ph, "twcm", [128, 128], F32)
        wfb = K.sb(ph, "twfb", [128, 128], F32)
        anyok = K.sb(ph, "tanyok", [128, 1], F32)
        for t_, src in ((esel, self.c_esel), (winb, self.c_win01), (causb, self.c_caus01), (band, self.c_band),
                        (wcm, self.c_wcm), (wfb, self.c_wfb), (anyok, self.c_anyok)):
            K.dma("sp", t_[:], src, r=[self.cbuf], w=[t_])
        ks = K.sb(ph, "tks", [64, S], BF16)
        kw = K.sb(ph, "tkw", [64, S], BF16)
        qh = [K.sb(ph, f"tq{k}", [64, S], BF16) for k in range(4)]
        psg = K.sb(ph, "tpsg", [128, 4, 256], F32)
        NS = 3
        pun = [K.sb(ph, f"tpun{k}", [128, 256], F32) for k in range(NS)]
        pb = [K.sb(ph, f"tpb{k}", [128, 256], BF16) for k in range(NS)]
        pTs = [K.sb(ph, f"tpT{k}", [128, 2, 128], BF16) for k in range(NS)]
        st = [K.sb(ph, f"tst{k}", [128, 8], F32) for k in range(NS)]
        s4 = K.sb(ph, "ts4", [128, 64], F32)
        imp = K.sb(ph, "timp", [128, 64], F32)
        sc = K.sb(ph, "tsc", [128, 64], F32)
        wk = K.sb(ph, "twk", [128, 64], F32)
        m8a = K.sb(ph, "tm8a", [128, 8], F32)
        m8b = K.sb(ph, "tm8b", [128, 8], F32)
        selt = K.sb(ph, "tsel", [128, 64], F32)
        negm = [K.sb(ph, f"tnegm{k}", [128, 64], BF16) for k in range(2)]
        nmT = [K.sb(ph, f"tnmT{k}", [64, 512], BF16) for k in range(2)]
        Pt = [K.sb(ph, f"tP{k}", [128, 512], BF16) for k in range(5)]
        Oq = [K.sb(ph, f"tOq{k}", [128, 4, 256], F32) for k in range(2)]
        cf = [K.sb(ph, f"tcf{k}", [128, 8], F32) for k in range(2)]
        sps = [K.ps(ph, f"tsps{k}") for k in range(2)]
        accs = K.ps(ph, "taccs")
        accw = K.ps(ph, "taccw")
        cpsb = [K.ps(ph, f"tcps{k}") for k in range(2)]
        misc = K.ps(ph, "tmisc", (128, 1024), BF16)
        ocp = K.ps(ph, "tocp")
        items = []
        cnt = {"cmp": 0, "u": 0, "gq": 0}

        def add_loads(g):
            def f():
                K.dma("sp", ks[:], self.NK[g], r=[self.b_nk], w=[ks])
                K.dma("sp", kw[:], self.NK[4 + g], r=[self.b_nk], w=[kw])
                for r in range(4):
                    K.dma("sp", qh[r][:], self.NQ[4 * g + r], r=[self.b_nq], w=[qh[r]])
            items.append([f])

        def add_cmp(g, qc, c, r, gq):
            T_ = 4 * qc + c
            ncols = min(8 * T_ + 7, NCMP)
            b0 = 256 - 8 * T_
            h = 4 * g + r
            q_ = qh[r]
            Oq_ = Oq[gq % 2]
            chunks = [(0, min(128, ncols))] + ([(1, ncols - 128)] if ncols > 128 else [])
            stt = {}

            def s0():
                n_ = cnt["cmp"]
                cnt["cmp"] += 1
                stt["n"] = n_
                s_, pu, pb_, cps = st[n_ % NS], pun[n_ % NS], pb[n_ % NS], cpsb[n_ % 2]
                K.pe(lambda: nc.tensor.matmul(cps[:, 0:ncols], lhsT=q_[:, T_ * 128:(T_ + 1) * 128], rhs=kcT[:, g, 0:ncols],
                                              start=True, stop=False), r=[q_, kcT], w=[cps])
                K.pe(lambda: nc.tensor.matmul(cps[:, 0:ncols], lhsT=self.ident[:], rhs=band[:, b0:b0 + ncols],
                                              start=False, stop=True), r=[self.ident, band], w=[cps])
                K.dve(lambda: nc.vector.reduce_max(out=s_[:, 0:1], in_=cps[:, 0:ncols], axis=AX.X), r=[cps], w=[s_])
                K.dve(lambda: nc.vector.tensor_scalar(out=s_[:, 1:2], in0=s_[:, 0:1], scalar1=-0.125, scalar2=None, op0=ALU.mult),
                      r=[s_], w=[s_])
                K.act(lambda: nc.scalar.activation(out=pu[:, 0:ncols], in_=cps[:, 0:ncols], func=AF.Exp, scale=0.125,
                                                   bias=s_[:, 1:2], accum_out=s_[:, 2:3]), r=[cps, s_], w=[pu, s_])

            def s0b():
                n_ = stt["n"]
                s_, pu, pb_ = st[n_ % NS], pun[n_ % NS], pb[n_ % NS]
                K.dve(lambda: nc.vector.reciprocal(out=s_[:, 3:4], in_=s_[:, 2:3]), r=[s_], w=[s_])
                if T_ == 0:
                    K.dve(lambda: nc.vector.tensor_tensor(out=s_[:, 3:4], in0=s_[:, 3:4], in1=anyok[:], op=ALU.mult), r=[s_, anyok], w=[s_])
                if r == 0:
                    K.dve(lambda: nc.vector.tensor_scalar(out=psg[:, c, 0:ncols], in0=pu[:, 0:ncols], scalar1=s_[:, 3:4], scalar2=None,
                                                          op0=ALU.mult), r=[pu, s_], w=[psg])
                else:
                    K.dve(lambda: nc.vector.scalar_tensor_tensor(out=psg[:, c, 0:ncols], in0=pu[:, 0:ncols], scalar=s_[:, 3:4],
                                                                 in1=psg[:, c, 0:ncols], op0=ALU.mult, op1=ALU.add), r=[pu, s_, psg], w=[psg])
                K.pool(lambda: nc.gpsimd.tensor_scalar(out=pb_[:, 0:ncols], in0=pu[:, 0:ncols], scalar1=s_[:, 3:4], scalar2=None,
                                                       op0=ALU.mult), r=[pu, s_], w=[pb_])

            def s1():
                n_ = stt["n"]
                pb_, pT = pb[n_ % NS], pTs[n_ % NS]
                for ch, wd in chunks:
                    K.pe(lambda: nc.tensor.transpose(out=misc[0:wd, ch * 128:(ch + 1) * 128], in_=pb_[:, ch * 128:ch * 128 + wd],
                                                     identity=self.ident[:]), r=[pb_, self.ident], w=[misc])
                for ch, wd in chunks:
                    K.act(lambda: nc.scalar.copy(out=pT[0:wd, ch, :], in_=misc[0:wd, ch * 128:(ch + 1) * 128]), r=[misc], w=[pT])

            def s2():
                pT = pTs[stt["n"] % NS]
                for k_, (ch, wd) in enumerate(chunks):
                    K.pe(lambda: nc.tensor.matmul(ocp[:, 0:64], lhsT=pT[0:wd, ch, :], rhs=vcmp[0:wd, ch, g, :],
                                                  start=(k_ == 0), stop=(k_ == len(chunks) - 1)), r=[pT, vcmp], w=[ocp])
                K.dve(lambda: nc.vector.tensor_scalar(out=Oq_[:, c, r * 64:(r + 1) * 64], in0=ocp[:, 0:64],
                                                      scalar1=GT[:, T_, 3 * h:3 * h + 1], scalar2=None, op0=ALU.mult),
                      r=[ocp, GT], w=[Oq_])
            items.append([s0, s0b, s1, s2])

        def add_select(g, qc, c, gq):
            T_ = 4 * qc + c
            w0 = 64 - 2 * T_
            nm_ = negm[c % 2]

            def f():
                pv4 = psg[:, c, :].rearrange("p (j f) -> p j f", f=4)
                K.dve(lambda: nc.vector.tensor_reduce(out=s4[:], in_=pv4, axis=AX.X, op=ALU.add), r=[psg], w=[s4])
                K.dve(lambda: nc.vector.scalar_tensor_tensor(out=imp[:], in0=pv4[:, :, 3], scalar=-0.5, in1=s4[:], op0=ALU.mult, op1=ALU.add),
                      r=[psg, s4], w=[imp])
                K.dve(lambda: nc.vector.scalar_tensor_tensor(out=imp[:, 1:64], in0=pv4[:, 0:63, 3], scalar=0.5, in1=imp[:, 1:64],
                                                             op0=ALU.mult, op1=ALU.add), r=[psg, imp], w=[imp])
                K.dve(lambda: nc.vector.tensor_tensor(out=sc[:], in0=imp[:], in1=wcm[:, w0:w0 + 64], op=ALU.mult), r=[imp, wcm], w=[sc])
                K.dve(lambda: nc.vector.tensor_tensor(out=sc[:], in0=sc[:], in1=wfb[:, w0:w0 + 64], op=ALU.add), r=[sc, wfb], w=[sc])
                K.dve(lambda: nc.vector.memset(sc[:, 0:1], 1.0e4), r=[sc], w=[sc])

            def f2():
                K.dve(lambda: nc.vector.max(out=m8a[:], in_=sc[:]), r=[sc], w=[m8a])
                K.dve(lambda: nc.vector.match_replace(out=wk[:], in_to_replace=m8a[:], in_values=sc[:], imm_value=-3.0e38), r=[sc, m8a], w=[wk])
                K.dve(lambda: nc.vector.max(out=m8b[:], in_=wk[:]), r=[wk], w=[m8b])

            def f3():
                K.dve(lambda: nc.vector.tensor_scalar(out=nm_[:], in0=sc[:], scalar1=m8b[:, 7:8], scalar2=None, op0=ALU.is_ge),
                      r=[sc, m8b], w=[nm_])
                K.pe(lambda: nc.tensor.transpose(out=misc[0:64, 512 + c * 128:512 + (c + 1) * 128], in_=nm_[:], identity=self.ident[:]),
                     r=[nm_, self.ident], w=[misc])
                if c == 3:
                    K.act(lambda: nc.scalar.copy(out=nmT[gq % 2][:], in_=misc[0:64, 512:1024]), r=[misc], w=[nmT[gq % 2]])
            items.append([None, f, f2, f3])

        def add_unit(g, qc, r, i_, kind, first, gq):
            a = i_ - 4 * qc
            if kind == "s":
                diag = a >= 0
                c0, c1 = (128 * a if diag else 0), 512
                kt, acc, voff = ks, accs, g * 65
            else:
                e_ = i_ - (4 * qc - 4)
                c0, c1 = (0, 128 * (e_ + 1)) if e_ < 4 else (128 * (e_ - 4), 512)
                kt, acc, voff = kw, accw, (4 + g) * 65
            W = c1 - c0
            q_ = qh[r]
            Mk = Mks[gq % 2]
            stt = {}

            def s0():
                u = cnt["u"]
                cnt["u"] += 1
                stt["u"] = u
                sp_, P = sps[u % 2], Pt[u % 5]
                qs = q_[:, qc * 512 + c0:qc * 512 + c1]
                K.pe(lambda: nc.tensor.matmul(sp_[:, 0:W], lhsT=kt[:, i_ * 128:(i_ + 1) * 128], rhs=qs, start=True, stop=True),
                     r=[kt, q_], w=[sp_])
                K.act(lambda: nc.scalar.activation(out=P[:, 0:W], in_=sp_[:, 0:W], func=AF.Exp, scale=0.125), r=[sp_], w=[P])
                if kind == "s":
                    K.dve(lambda: nc.vector.tensor_tensor(out=P[:, 0:W], in0=P[:, 0:W], in1=Mk[:, i_, 0:W], op=ALU.mult), r=[P, Mk], w=[P])
                else:
                    K.dve(lambda: nc.vector.tensor_tensor(out=P[:, 0:W], in0=P[:, 0:W], in1=winb[:, e_ * 512 + c0:e_ * 512 + c1], op=ALU.mult),
                          r=[P, winb], w=[P])

            def s1():
                P = Pt[stt["u"] % 5]
                for n_, c in enumerate(range(c0 // 128, c1 // 128)):
                    K.pe(lambda: nc.tensor.matmul(acc[:, c * 65:(c + 1) * 65], lhsT=P[:, c * 128 - c0:(c + 1) * 128 - c0],
                                                  rhs=V[:, i_, voff:voff + 65], start=(first and n_ == 0), stop=False, skip_group_check=True),
                         r=[P, V], w=[acc])
            items.append([s0, None, None, s1])

        def add_mask(g, qc, i_, gq):
            a = i_ - 4 * qc
            c0 = 128 * a if a >= 0 else 0
            W = 512 - c0
            nm = nmT[gq % 2]
            Mk = Mks[gq % 2]
            stt = {}

            def s0():
                u = cnt["u"]
                cnt["u"] += 1
                stt["u"] = u
                sp_ = sps[u % 2]
                K.pe(lambda: nc.tensor.matmul(sp_[:, 0:W], lhsT=esel[:, i_ * 128:(i_ + 1) * 128], rhs=nm[:, c0:512], start=True, stop=True),
                     r=[esel, nm], w=[sp_])

            def s1():
                sp_ = sps[stt["u"] % 2]
                if a >= 0:
                    K.dve(lambda: nc.vector.tensor_tensor(out=Mk[:, i_, 0:W], in0=sp_[:, 0:W], in1=causb[:, 0:W], op=ALU.mult),
                          r=[sp_, causb], w=[Mk])
                else:
                    K.act(lambda: nc.scalar.copy(out=Mk[:, i_, 0:W], in_=sp_[:, 0:W]), r=[sp_], w=[Mk])
            items.append([s0, s1])

        def add_combine(g, qc, r, gq):
            h = 4 * g + r
            Oq_ = Oq[gq % 2]

            def f():
                for bi, acc in ((1, accs), (2, accw)):
                    cf_ = cf[bi - 1]
                    av = acc[:, 0:260].rearrange("p (c d) -> p c d", d=65)
                    K.dve(lambda: nc.vector.reciprocal(out=cf_[:, 0:4], in_=av[:, :, 64]), r=[acc], w=[cf_])
                    K.dve(lambda: nc.vector.tensor_tensor(out=cf_[:, 4:8], in0=cf_[:, 0:4], in1=GT[:, 4 * qc:4 * qc + 4, 3 * h + bi], op=ALU.mult),
                          r=[cf_, GT], w=[cf_])
                    for c in range(4):
                        K.dve(lambda: nc.vector.scalar_tensor_tensor(out=Oq_[:, c, r * 64:(r + 1) * 64], in0=av[:, c, 0:64], scalar=cf_[:, 4 + c:5 + c],
                                                                     in1=Oq_[:, c, r * 64:(r + 1) * 64], op0=ALU.mult, op1=ALU.add),
                              r=[acc, cf_, Oq_], w=[Oq_])
                if r == 3:
                    ob_ = Ob[gq % 2]
                    K.pool(lambda: nc.gpsimd.tensor_copy(out=ob_[:], in_=Oq_[:]), r=[Oq_], w=[ob_])
                    K.dma("sp", Odv[:, 4 * qc:4 * qc + 4, g * 256:(g + 1) * 256], ob_[:], r=[ob_], wa=[self.b_od])
            items.append([None, None, None, f])

        pre, un = [], []
        for g in range(4):
            for qc in range(NG):
                gq = cnt["gq"]
                cnt["gq"] += 1
                del items[:]
                if qc == 0:
                    add_loads(g)
                items.append([lambda: K.pool(lambda: nc.gpsimd.memset(psg[:], 0.0), w=[psg])])
                for c in range(4):
                    for r in range(4):
                        add_cmp(g, qc, c, r, gq)
                    add_select(g, qc, c, gq)
                for _ in range(3):
                    items.append([lambda: None])
                for i_ in range(0, 4 * qc + 4):
                    add_mask(g, qc, i_, gq)
                items.append([lambda: None])
                pre.append(list(items))
                del items[:]
                for r in range(4):
                    first = True
                    for i_ in range(0, 4 * qc + 4):
                        add_unit(g, qc, r, i_, "s", first, gq)
                        first = False
                    first = True
                    for i_ in range(max(0, 4 * qc - 4), 4 * qc + 4):
                        add_unit(g, qc, r, i_, "w", first, gq)
                        first = False
                    add_combine(g, qc, r, gq)
                un.append(list(items))
        final = list(pre[0])
        ngq = len(un)
        for gq in range(ngq):
            nxt = pre[gq + 1] if gq + 1 < ngq else []
            if not nxt or (gq + 1) % NG == 0 or not NSA_INTERLEAVE:
                final += un[gq]
                final += [[lambda: None]] * 4
                final += nxt
            else:
                a_, b_ = un[gq], nxt
                bi = 0
                for ai, it in enumerate(a_):
                    final.append(it)
                    tgt = ((ai + 1) * len(b_)) // len(a_)
                    while bi < tgt:
                        final.append(b_[bi])
                        bi += 1
                final += b_[bi:]
        del items[:]
        items.extend(final)
        run_pipeline(items)


    def phase_conv(self, i, j):
        K, nc = self.K, self.nc
        UTv = self.UT.rearrange("(kc p) t -> p kc t", p=128)
        with contextlib.ExitStack() as ph:
            w = K.sb(ph, "cw", [128, 8, 2 * D], BF16)
            self.load_w(w, self.cv_w_in[j], 8, 2 * D)
            bcol = K.sb(ph, "cbcol", [128, 16], F32)
            K.dma("sp", bcol[:], self.cv_b_inT, r=[self.cbuf], w=[bcol])
            utg = [K.sb(ph, f"cutg{k}", [128, 8, 512], BF16) for k in range(2)]
            sgt = [K.sb(ph, f"csg{k}", [128, 512], F32) for k in range(2)]
            hgt = [K.sb(ph, f"chg{k}", [128, 512], BF16) for k in range(3)]
            psa = [K.ps(ph, f"cpsa{k}") for k in range(2)]
            psg = [K.ps(ph, f"cpsg{k}") for k in range(2)]

            def load_u(g):
                K.dma("sp", utg[g % 2][:], UTv[:, :, g * 512:(g + 1) * 512], r=[self.b_ut[g]], w=[utg[g % 2]])

            load_u(0)
            n = 0
            for g in range(NG):
                if g + 1 < NG:
                    load_u(g + 1)
                ug = utg[g % 2]
                for cc in range(8):
                    pa, pg = psa[n % 2], psg[n % 2]
                    sg, hg = sgt[n % 2], hgt[n % 3]
                    n += 1
                    for kc in range(8):
                        K.pe(lambda: nc.tensor.matmul(pa[:], lhsT=w[:, kc, cc * 128:(cc + 1) * 128], rhs=ug[:, kc, :],
                                                      start=(kc == 0), stop=(kc == 7)), r=[w, ug], w=[pa])
                    for kc in range(8):
                        K.pe(lambda: nc.tensor.matmul(pg[:], lhsT=w[:, kc, D + cc * 128:D + (cc + 1) * 128], rhs=ug[:, kc, :],
                                                      start=(kc == 0), stop=(kc == 7)), r=[w, ug], w=[pg])
                    K.act(lambda: nc.scalar.activation(out=sg[:], in_=pg[:], func=AF.Sigmoid, bias=bcol[:, 8 + cc:9 + cc]),
                          r=[pg, bcol], w=[sg])
                    K.dve(lambda: nc.vector.scalar_tensor_tensor(out=hg[:], in0=pa[:], scalar=bcol[:, cc:cc + 1], in1=sg[:],
                                                                 op0=ALU.add, op1=ALU.mult), r=[pa, bcol, sg], w=[hg])
                    K.dma("sp", self.HG[cc * 128:(cc + 1) * 128, g * 512:(g + 1) * 512], hg[:], r=[hg], wa=[self.b_hg[g]])
            K.barrier()
        with contextlib.ExitStack() as ph:
            w = K.sb(ph, "cow", [128, 8, D], BF16)
            self.load_w(w, self.cv_w_out[j], 8, D, split=1024)
            brow = K.sb(ph, "cobrow", [1, D], BF16)
            K.dma("pool", brow[:], self.cv_b_out[j:j + 1, :], r=[self.cbuf], w=[brow])
            dw = K.sb(ph, "cdw", [128, 8, 31], F32)
            vec = K.sb(ph, "cvec", [128, 3, 8], F32)
            onesf = K.sb(ph, "conesf", [128, 128], F32)
            K.dma("sp", dw[:], self.cv_dwT, r=[self.cbuf], w=[dw])
            K.dma("sp", vec[:], self.cv_vecT, r=[self.cbuf], w=[vec])
            K.dma("sp", onesf[:], self.c_onesf, r=[self.cbuf], w=[onesf])
            e = self.epilogue_setup(ph, i, 0)
            xin = [K.sb(ph, f"cxin{k}", [128, 8, 542], BF16) for k in range(2)]
            Dg = K.sb(ph, "cDg", [128, 8, 31, 128], BF16)
            for cc in range(8):
                for k in range(31):
                    if (cc * 31 + k) % 3 == 2:
                        K.pool(lambda: nc.gpsimd.tensor_scalar(out=Dg[:, cc, k, :], in0=self.ident[:], scalar1=dw[:, cc, k:k + 1], scalar2=None,
                                                               op0=ALU.mult), r=[self.ident, dw], w=[Dg])
                    else:
                        K.dve(lambda: nc.vector.tensor_scalar(out=Dg[:, cc, k, :], in0=self.ident[:], scalar1=dw[:, cc, k:k + 1], scalar2=None,
                                                              op0=ALU.mult), r=[self.ident, dw], w=[Dg])
            cvp = [K.ps(ph, f"ccvp{k}") for k in range(2)]
            acc = K.sb(ph, "cacc", [128, 8, 512], F32)
            sq = [K.sb(ph, f"csq{k}", [128, 512], F32) for k in range(2)]
            mt = K.sb(ph, "cm", [128, 512], F32)
            msq = K.sb(ph, "cmsq", [128, 512], F32)
            var = K.sb(ph, "cvar", [128, 512], F32)
            rstd = K.sb(ph, "crstd", [128, 512], F32)
            dt_ = [K.sb(ph, f"cd{k}", [128, 512], F32) for k in range(2)]
            xh = [K.sb(ph, f"cxh{k}", [128, 512], F32) for k in range(2)]
            hT = K.sb(ph, "chT", [128, 8, 512], BF16)
            s1 = K.ps(ph, "cs1")
            s2 = K.ps(ph, "cs2")
            psY = [[K.ps(ph, f"cpsY{k}{hf}") for hf in range(2)] for k in range(2)]
            HGv = self.HG.rearrange("(cc p) t -> p cc t", p=128)

            def load_x(g):
                xt = xin[g % 2]
                if g == 0:
                    K.pool(lambda: nc.gpsimd.memset(xt[:, :, 0:30], 0.0), w=[xt])
                    K.dma("sp", xt[:, :, 30:542], HGv[:, :, 0:512], r=[self.b_hg[0]], wa=[xt])
                else:
                    K.dma("sp", xt[:], HGv[:, :, g * 512 - 30:g * 512 + 512], r=[self.b_hg[g - 1], self.b_hg[g]], w=[xt])

            load_x(0)
            for g in range(NG):
                if g + 1 < NG:
                    load_x(g + 1)
                xt = xin[g % 2]
                for cc in range(8):
                    cv = cvp[cc % 2]
                    for k in range(31):
                        K.pe(lambda: nc.tensor.matmul(cv[:], lhsT=Dg[:, cc, k, :], rhs=xt[:, cc, k:k + 512], start=(k == 0), stop=(k == 30)),
                             r=[Dg, xt], w=[cv])
                    sq_ = sq[cc % 2]
                    K.act(lambda: nc.scalar.activation(out=sq_[:], in_=cv[:], func=AF.Square, bias=vec[:, 0, cc:cc + 1]), r=[cv, vec], w=[sq_])
                    K.dve(lambda: nc.vector.tensor_scalar(out=acc[:, cc, :], in0=cv[:], scalar1=vec[:, 0, cc:cc + 1], scalar2=None, op0=ALU.add),
                          r=[cv, vec], w=[acc])
                    K.pe(lambda: nc.tensor.matmul(s1[:], lhsT=onesf[:], rhs=acc[:, cc, :], start=(cc == 0), stop=(cc == 7)),
                         r=[onesf, acc], w=[s1])
                    K.pe(lambda: nc.tensor.matmul(s2[:], lhsT=onesf[:], rhs=sq_[:], start=(cc == 0), stop=(cc == 7)),
                         r=[onesf, sq_], w=[s2])
                K.dve(lambda: nc.vector.tensor_scalar(out=mt[:], in0=s1[:], scalar1=1.0 / D, scalar2=None, op0=ALU.mult), r=[s1], w=[mt])
                K.pool(lambda: nc.gpsimd.tensor_tensor(out=msq[:], in0=mt[:], in1=mt[:], op=ALU.mult), r=[mt], w=[msq])
                K.dve(lambda: nc.vector.scalar_tensor_tensor(out=var[:], in0=s2[:], scalar=1.0 / D, in1=msq[:], op0=ALU.mult, op1=ALU.subtract),
                      r=[s2, msq], w=[var])
                K.act(lambda: nc.scalar.activation(out=var[:], in_=var[:], func=AF.Sqrt, bias=EPS), r=[var], w=[var])
                K.dve(lambda: nc.vector.reciprocal(out=rstd[:], in_=var[:]), r=[var], w=[rstd])
                for cc in range(8):
                    d_, x_ = dt_[cc % 2], xh[cc % 2]
                    K.pool(lambda: nc.gpsimd.tensor_tensor(out=d_[:], in0=acc[:, cc, :], in1=mt[:], op=ALU.subtract), r=[acc, mt], w=[d_])
                    K.dve(lambda: nc.vector.tensor_tensor(out=x_[:], in0=d_[:], in1=rstd[:], op=ALU.mult), r=[d_, rstd], w=[x_])
                    K.act(lambda: nc.scalar.activation(out=hT[:, cc, :], in_=x_[:], func=AF.Silu, scale=vec[:, 1, cc:cc + 1],
                                                       bias=vec[:, 2, cc:cc + 1]), r=[x_, vec], w=[hT])
                for tt in range(4):
                    t = g * 4 + tt
                    self.epi_load(e, t)
                    yb = psY[tt % 2]
                    for hf in range(2):
                        for cc in range(8):
                            K.pe(lambda: nc.tensor.matmul(yb[hf][:], lhsT=hT[:, cc, tt * 128:(tt + 1) * 128], rhs=w[:, cc, hf * 512:(hf + 1) * 512],
                                                          start=(cc == 0), stop=False), r=[hT, w], w=[yb[hf]])
                        K.pe(lambda: nc.tensor.matmul(yb[hf][:], lhsT=self.ones[0:1, :], rhs=brow[0:1, hf * 512:(hf + 1) * 512],
                                                      start=False, stop=True), r=[self.ones, brow], w=[yb[hf]])
                    self.epilogue(e, t, yb)
            K.barrier()


def run_pipeline(items):
    n = len(items)
    depth = max(len(it) for it in items)
    for t in range(n + depth - 1):
        for j in range(depth):
            k = t - j
            if 0 <= k < n and j < len(items[k]) and items[k][j] is not None:
                items[k][j]()


def host_consts():
    bf = ml_dtypes.bfloat16
    p = np.arange(128)[:, None]
    y = np.arange(512)[None, :]
    c = {}
    c["c_ident"] = np.eye(128, dtype=np.float32).astype(bf)
    jj = np.arange(128)[:, None]
    ss = np.arange(128)[None, :]
    c["c_tri"] = (jj >= ss).astype(np.float32).astype(bf)
    c["c_ones"] = np.ones((128, 128), np.float32).astype(bf)
    c["c_sbmask"] = (y > p).astype(np.float32).astype(bf)
    c["c_sbbias"] = np.where(y > p, 0.0, BIG).astype(np.float32).astype(bf)
    c["c_onesf"] = np.ones((128, 128), np.float32)
    perm = np.zeros((64, 64), np.float32)
    for i in range(8):
        perm[i + 8, i] = -1.0
        perm[i, i + 8] = 1.0
    c["c_perm"] = perm.astype(bf)
    invf = np.zeros((64, 1), np.float32)
    fr = (500000.0 ** (-np.arange(8, dtype=np.float32) / 8.0)).astype(np.float32)
    invf[0:8, 0] = fr
    invf[8:16, 0] = fr
    c["c_invf"] = invf
    jj = np.arange(64)[:, None, None]
    ii = np.arange(32)[None, :, None]
    sk = np.arange(128)[None, None, :]
    c["c_esel"] = (jj == 2 * ii + (sk >= 64)).astype(np.float32).reshape(64, 32 * 128).astype(bf)
    e = np.arange(8)[None, :, None]
    f = np.arange(512)[None, None, :]
    pp = np.arange(128)[:, None, None]
    dlt = f - pp + 512 - 128 * e
    c["c_win01"] = ((dlt >= 0) & (dlt < 512)).astype(np.float32).reshape(128, 8 * 512).astype(bf)
    c["c_caus01"] = (y >= p).astype(np.float32).astype(bf)
    m = np.arange(512)[None, :] - 256
    c["c_band"] = np.where(p >= 16 * m + 31, 0.0, -BIG).astype(np.float32).astype(bf)
    col = np.arange(128)[None, :]
    dl = (col - 64) - (p >= 64)
    c["c_wcm"] = (dl < -1).astype(np.float32)
    c["c_wfb"] = np.where(dl > 0, -1.0e9, np.where(dl >= -1, 1.0e4, 0.0)).astype(np.float32)
    c["c_anyok"] = (np.arange(128)[:, None] >= 31).astype(np.float32)
    return c


def make_in_map(inputs, b, prog):
    m = {}
    m["x"] = np.ascontiguousarray(inputs["x"][b])
    m["cT"] = np.ascontiguousarray(np.asarray(inputs["c"][b]).reshape(8, 128).T)
    m["pos"] = np.ascontiguousarray(np.asarray(inputs["positions"][b]).reshape(1, S).astype(np.int32))
    for n in ("ada_w", "ada_b", "mix_pre_g", "mix_post_g", "ffn_pre_g", "ffn_post_g", "ffn_w1", "ffn_w2", "sb_w_in", "sb_w_out",
              "nsa_w_in", "nsa_w_out", "nsa_w1_k", "nsa_w2_k", "nsa_w1_v", "nsa_w2_v",
              "cv_w_in", "cv_w_out", "cv_b_out"):
        m[n] = np.asarray(inputs[n])
    m["nsa_pe_kT"] = np.ascontiguousarray(np.asarray(inputs["nsa_pe_k"])[0].T)
    m["nsa_pe_vT"] = np.ascontiguousarray(np.asarray(inputs["nsa_pe_v"])[0].T)
    m["cv_b_inT"] = np.ascontiguousarray(np.asarray(inputs["cv_b_in"]).reshape(16, 128).T)
    m["cv_dwT"] = np.ascontiguousarray(np.asarray(inputs["cv_dw"]).reshape(31, 8, 128).transpose(2, 1, 0))
    m["cv_vecT"] = np.ascontiguousarray(np.stack([np.asarray(inputs[k]).reshape(8, 128).T for k in ("cv_dw_b", "cv_ln_g", "cv_ln_b")], axis=1))
    m.update(host_consts())
    return {k: v for k, v in m.items() if k in prog.in_names}


_PROG_CACHE = {}


def kernel(**inputs):
    prog = Prog()
    nc = prog.build()
    B = inputs["x"].shape[0]
    in_maps = [make_in_map(inputs, b, prog) for b in range(B)]
    res = run_bass_kernel_spmd(nc, in_maps, core_ids=list(range(B)))
    return np.stack([np.asarray(r["out"]) for r in res.results], axis=0).astype(np.float32)
```

```python
import contextlib
import os
import numpy as np
import ml_dtypes
import concourse.bass as bass
import concourse.mybir as mybir
from concourse.bass_utils import run_bass_kernel_spmd

F32 = mybir.dt.float32
BF16 = mybir.dt.bfloat16
I32 = mybir.dt.int32
AF = mybir.ActivationFunctionType
ALU = mybir.AluOpType
AX = mybir.AxisListType

D = 1024
S = 4096
DEPTH = 4
NH = 16
DH = 64
DFF = 4096
EPS = 1e-6
NT = S // 128
NG = S // 512
NSA_IN = 2608
NCMP = 255
BIG = 30000.0
NSA_STOP = os.environ.get("NSA_STOP", "")
NSA_INTERLEAVE = bool(int(os.environ.get("NSA_INTERLEAVE", "0")))
N1_PARTS = os.environ.get("N1_PARTS", "urvg")
ROPE_MODE = os.environ.get("ROPE_MODE", "")


class Src:
    __slots__ = ("sem", "count", "name")

    def __init__(self, sem, name):
        self.sem = sem
        self.count = 0
        self.name = name


class Buf:
    __slots__ = ("name", "w", "r", "const", "excl")

    def __init__(self, name, const=False):
        self.name = name
        self.excl = False
        self.w = []
        self.r = []
        self.const = const


class T:
    __slots__ = ("t", "b")

    def __init__(self, t, name):
        self.t = t
        self.b = Buf(name)

    def __getitem__(self, idx):
        return self.t[idx]


class KB:
    def __init__(self):
        self.nc = bass.Bass("TRN2", target_bir_lowering=False)
        nc = self.nc
        self.st = contextlib.ExitStack()
        self.eng = {"pe": nc.tensor, "act": nc.scalar, "dve": nc.vector, "pool": nc.gpsimd, "sp": nc.sync}
        self.src = {}
        for n in self.eng:
            self.src[n] = Src(self.st.enter_context(nc.semaphore("sem_" + n)), n)
        self.dslots = {}
        self.dnext = {}
        for q, k in (("sp", 12), ("pool", 6), ("act", 4)):
            self.dslots[q] = [Src(self.st.enter_context(nc.semaphore(f"dq_{q}{i}")), f"dq_{q}{i}") for i in range(k)]
            self.dnext[q] = 0
        self.seen = {n: {} for n in self.eng}
        self.ninstr = 0

    def _wait(self, en, tok):
        src, val = tok
        if val <= 0:
            return
        if en == "pe" and src is self.src["pe"]:
            return
        seen = self.seen[en]
        if seen.get(src, 0) >= val:
            return
        self.eng[en].wait_ge(src.sem, val)
        seen[src] = val
        self.ninstr += 1

    def _deps(self, en, r, w, wa=()):
        for x in r:
            b = x.b if isinstance(x, T) else x
            for tok in b.w:
                self._wait(en, tok)
            if b.excl:
                own = self.src.get(en)
                for tok in b.r:
                    if tok[0] is not own:
                        self._wait(en, tok)
        for x in w:
            b = x.b if isinstance(x, T) else x
            for tok in b.w:
                self._wait(en, tok)
            for tok in b.r:
                self._wait(en, tok)
        for x in wa:
            b = x.b if isinstance(x, T) else x
            for tok in b.r:
                self._wait(en, tok)

    def _mark(self, tok, r, w, wa=()):
        for x in r:
            b = x.b if isinstance(x, T) else x
            if not b.const:
                b.r.append(tok)
        for x in w:
            b = x.b if isinstance(x, T) else x
            b.w = [tok]
            b.r = []
        for x in wa:
            b = x.b if isinstance(x, T) else x
            if b.r:
                b.w = []
                b.r = []
            b.w.append(tok)

    def op(self, en, fn, r=(), w=()):
        self._deps(en, r, w)
        ins = fn()
        s = self.src[en]
        s.count += 1
        ins.then_inc(s.sem, 1)
        self._mark((s, s.count), r, w)
        self.ninstr += 1
        return ins

    def pe(self, fn, r=(), w=()):
        return self.op("pe", fn, r, w)

    def act(self, fn, r=(), w=()):
        return self.op("act", fn, r, w)

    def dve(self, fn, r=(), w=()):
        return self.op("dve", fn, r, w)

    def pool(self, fn, r=(), w=()):
        return self.op("pool", fn, r, w)

    def dma(self, q, out, in_, r=(), w=(), wa=()):
        self._deps(q, r, w, wa)
        slots = self.dslots[q]
        sl = slots[self.dnext[q] % len(slots)]
        self.dnext[q] += 1
        self._wait(q, (sl, sl.count))
        ins = self.eng[q].dma_start(out=out, in_=in_)
        sl.count += 16
        ins.then_inc(sl.sem, 16)
        self._mark((sl, sl.count), r, w, wa)
        self.ninstr += 1
        return ins

    def barrier(self):
        toks = [(s, s.count) for s in self.src.values()]
        for q in self.dslots:
            toks += [(s, s.count) for s in self.dslots[q]]
        for en in self.eng:
            for tok in toks:
                if tok[0] is self.src[en]:
                    continue
                self._wait(en, tok)

    def _uniq(self, name):
        self.nalloc = getattr(self, "nalloc", 0) + 1
        return f"{name}_{self.nalloc}"

    def sb(self, ctx, name, shape, dtype):
        name = self._uniq("s_" + name)
        return T(ctx.enter_context(self.nc.sbuf_tensor(name, list(shape), dtype)), name)

    def ps(self, ctx, name, shape=(128, 512), dtype=F32):
        name = self._uniq("p_" + name)
        t = T(ctx.enter_context(self.nc.psum_tensor(name, list(shape), dtype)), name)
        t.b.excl = True
        return t

    def dram(self, name, shape, dtype, kind="Internal"):
        return self.nc.dram_tensor(name, list(shape), dtype, kind=kind).ap()


class Prog:
    def __init__(self, layers=(0, 1, 2, 3), debug=None):
        self.K = KB()
        self.nc = self.K.nc
        self.layers = tuple(layers)
        self.debug = debug or {}
        self.in_names = []
        self._declare_io()

    def _in(self, name, shape, dtype=F32):
        self.in_names.append(name)
        return self.K.dram(name, shape, dtype, kind="ExternalInput")

    def _declare_io(self):
        K = self.K
        self.x = self._in("x", [S, D])
        self.cT = self._in("cT", [128, 8])
        self.pos = self._in("pos", [1, S], I32)
        self.ada_w = self._in("ada_w", [DEPTH, D, 6 * D])
        self.ada_b = self._in("ada_b", [DEPTH, 6 * D])
        self.gains = {n: self._in(n, [DEPTH, D]) for n in ("mix_pre_g", "mix_post_g", "ffn_pre_g", "ffn_post_g")}
        self.ffn_w1 = self._in("ffn_w1", [DEPTH, D, DFF])
        self.ffn_w2 = self._in("ffn_w2", [DEPTH, DFF, D])
        self.sb_w_in = self._in("sb_w_in", [2, D, 3 * D])
        self.sb_w_out = self._in("sb_w_out", [2, D, D])
        self.nsa_w_in = self._in("nsa_w_in", [1, D, NSA_IN])
        self.nsa_w_out = self._in("nsa_w_out", [1, D, D])
        self.nsa_peT = {"k": self._in("nsa_pe_kT", [64, 32]), "v": self._in("nsa_pe_vT", [64, 32])}
        self.nsa_w1 = {"k": self._in("nsa_w1_k", [1, 2048, 256]), "v": self._in("nsa_w1_v", [1, 2048, 256])}
        self.nsa_w2 = {"k": self._in("nsa_w2_k", [1, 256, 64]), "v": self._in("nsa_w2_v", [1, 256, 64])}
        self.cv_w_in = self._in("cv_w_in", [1, D, 2 * D])
        self.cv_b_inT = self._in("cv_b_inT", [128, 16])
        self.cv_dwT = self._in("cv_dwT", [128, 8, 31])
        self.cv_vecT = self._in("cv_vecT", [128, 3, 8])
        self.cv_w_out = self._in("cv_w_out", [1, D, D])
        self.cv_b_out = self._in("cv_b_out", [1, D])
        self.c_ident = self._in("c_ident", [128, 128], BF16)
        self.c_tri = self._in("c_tri", [128, 128], BF16)
        self.c_ones = self._in("c_ones", [128, 128], BF16)
        self.c_sbmask = self._in("c_sbmask", [128, 512], BF16)
        self.c_sbbias = self._in("c_sbbias", [128, 512], BF16)
        self.c_onesf = self._in("c_onesf", [128, 128], F32)
        self.c_perm = self._in("c_perm", [64, 64], BF16)
        self.c_invf = self._in("c_invf", [64, 1], F32)
        self.c_esel = self._in("c_esel", [64, 32 * 128], BF16)
        self.c_win01 = self._in("c_win01", [128, 8 * 512], BF16)
        self.c_caus01 = self._in("c_caus01", [128, 512], BF16)
        self.c_band = self._in("c_band", [128, 512], BF16)
        self.c_wcm = self._in("c_wcm", [128, 128], F32)
        self.c_wfb = self._in("c_wfb", [128, 128], F32)
        self.c_anyok = self._in("c_anyok", [128, 1], F32)
        self.out = K.dram("out", [S, D], F32, kind="ExternalOutput")
        self.modv = K.dram("modv", [DEPTH, 6 * D], F32)
        self.UT = K.dram("UT", [D, S], BF16)
        self.QT = K.dram("QT", [D, S], BF16)
        self.KT = K.dram("KT", [D, S], BF16)
        self.Vd = K.dram("Vd", [S, D], BF16)
        self.HG = K.dram("HG", [D, S], BF16)
        self.NQ = K.dram("NQ", [16, 64, S], BF16)
        self.NK = K.dram("NK", [8, 64, S], BF16)
        self.NC = K.dram("NC", [8, 64, S], BF16)
        self.NV = K.dram("NV", [S, 2 * 4 * 65], BF16)
        self.NGt = K.dram("NGt", [S, 48], F32)
        self.Od = self.Vd
        self.b_od = Buf("od")
        self.b_nq = Buf("nq"); self.b_nk = Buf("nk"); self.b_ncr = Buf("ncr"); self.b_nv = Buf("nv"); self.b_ngt = Buf("ngt")
        self.b_h = [Buf(f"h{t}") for t in range(NT)]
        self.b_ut = [Buf(f"ut{g}") for g in range(NG)]
        self.b_modv = [Buf(f"modv{i}") for i in range(DEPTH)]
        self.b_qt = [Buf(f"qt{g}") for g in range(NG)]
        self.b_kt = [Buf(f"kt{g}") for g in range(NG)]
        self.b_vd = [Buf(f"vd{t}") for t in range(NT)]
        self.b_hg = [Buf(f"hg{g}") for g in range(NG)]
        self.cbuf = Buf("const", const=True)
        for dbg_name, shape in self.debug.items():
            setattr(self, "dbg_" + dbg_name, K.dram("dbg_" + dbg_name, shape, F32, kind="ExternalOutput"))

    def load_consts(self):
        K, nc = self.K, self.nc
        st = K.st
        self.ident = K.sb(st, "ident", [128, 128], BF16)
        self.tri = K.sb(st, "tri", [128, 128], BF16)
        self.ones = K.sb(st, "ones", [128, 128], BF16)
        for t, src in ((self.ident, self.c_ident), (self.tri, self.c_tri), (self.ones, self.c_ones)):
            K.dma("sp", t[:], src, r=[self.cbuf], w=[t])
            t.b.const = True

    def phase_mod(self):
        K, nc = self.K, self.nc
        with contextlib.ExitStack() as ph:
            cT = K.sb(ph, "cT", [128, 8], F32)
            sig = K.sb(ph, "sig", [128, 8], F32)
            cond = K.sb(ph, "cond", [128, 8], F32)
            wsl = [K.sb(ph, f"adaw{i}", [128, 8, 512], F32) for i in range(5)]
            brow = K.sb(ph, "brow", [1, 6 * D], F32)
            mrow = K.sb(ph, "mrow", [1, 6 * D], F32)
            pss = [K.ps(ph, f"modps{i}") for i in range(2)]
            K.dma("sp", cT[:], self.cT, r=[self.cbuf], w=[cT])
            K.act(lambda: nc.scalar.activation(out=sig[:], in_=cT[:], func=AF.Sigmoid), r=[cT], w=[sig])
            K.dve(lambda: nc.vector.tensor_tensor(out=cond[:], in0=cT[:], in1=sig[:], op=ALU.mult), r=[cT, sig], w=[cond])
            it = 0
            for i in self.layers:
                K.dma("sp", brow[:], self.ada_b[i:i + 1, :], r=[self.cbuf], w=[brow])
                wv = self.ada_w[i].rearrange("(kc p) n -> p kc n", p=128)
                for n in range(12):
                    if it == 0:
                        for pre_ in range(4):
                            K.dma("sp", wsl[pre_ % 5][:], wv[:, :, pre_ * 512:(pre_ + 1) * 512], r=[self.cbuf], w=[wsl[pre_ % 5]])
                    wt = wsl[it % 5]
                    ps = pss[it % 2]
                    nxt_ = it + 4
                    it += 1
                    li_, ln_ = nxt_ // 12, nxt_ % 12
                    if li_ < len(self.layers):
                        wv2 = self.ada_w[self.layers[li_]].rearrange("(kc p) n -> p kc n", p=128)
                        K.dma("sp", wsl[nxt_ % 5][:], wv2[:, :, ln_ * 512:(ln_ + 1) * 512], r=[self.cbuf], w=[wsl[nxt_ % 5]])
                    for kc in range(8):
                        K.pe(lambda kc=kc: nc.tensor.matmul(ps[0:1, :], lhsT=cond[:, kc:kc + 1], rhs=wt[:, kc, :],
                                                            start=(kc == 0), stop=(kc == 7)), r=[cond, wt], w=[ps])
                    K.dve(lambda n=n: nc.vector.tensor_tensor(out=mrow[0:1, n * 512:(n + 1) * 512], in0=ps[0:1, :],
                                                              in1=brow[0:1, n * 512:(n + 1) * 512], op=ALU.add),
                          r=[ps, brow], w=[mrow])
                K.dma("sp", self.modv[i:i + 1, :], mrow[:], r=[mrow], w=[self.b_modv[i]])
            K.barrier()

    def load_vec(self, ph, name, src_row):
        t = self.K.sb(ph, name, [128, D], F32)
        return t

    def mod_slice(self, i, k):
        return self.modv[i:i + 1, k * D:(k + 1) * D]

    def bcast_load(self, t, src_row, rbufs):
        self.K.dma("sp", t[:], src_row.to_broadcast([128, D]), r=rbufs, w=[t])

    def phase_norm(self, i, which, src_is_x):
        K, nc = self.K, self.nc
        gname = "mix_pre_g" if which == 0 else "ffn_pre_g"
        ksh, ksc = (0, 1) if which == 0 else (3, 4)
        src = self.x if src_is_x else self.out
        with contextlib.ExitStack() as ph:
            A = K.sb(ph, "nA", [128, D], F32)
            B = K.sb(ph, "nB", [128, D], F32)
            Gn = K.sb(ph, "nG", [128, D], F32)
            self.bcast_load(A, self.mod_slice(i, ksc), [self.b_modv[i]])
            self.bcast_load(B, self.mod_slice(i, ksh), [self.b_modv[i]])
            self.bcast_load(Gn, self.gains[gname][i:i + 1, :], [self.cbuf])
            K.dve(lambda: nc.vector.scalar_tensor_tensor(out=A[:], in0=A[:], scalar=1.0, in1=Gn[:], op0=ALU.add, op1=ALU.mult),
                  r=[A, Gn], w=[A])
            NHS = 6
            hs = [K.sb(ph, f"nh{j}", [128, D], F32) for j in range(NHS)]
            junk = K.sb(ph, "njunk", [128, D], BF16)
            tmp = [K.sb(ph, f"ntmp{j}", [128, D], F32) for j in range(2)]
            ub = [K.sb(ph, f"nub{j}", [128, D], BF16) for j in range(3)]
            st = [K.sb(ph, f"nst{j}", [128, 4], F32) for j in range(2)]
            utg = [K.sb(ph, f"nutg{j}", [128, 8, 512], BF16) for j in range(2)]
            pst = [K.ps(ph, f"npst{j}", (128, 1024), BF16) for j in range(2)]
            UTv = self.UT.rearrange("(kc p) t -> p kc t", p=128)

            def load(t):
                K.dma("sp", hs[t % NHS][:], src[t * 128:(t + 1) * 128, :], r=[self.b_h[t]], w=[hs[t % NHS]])

            for t in range(NHS - 1):
                load(t)
            items = []
            for t in range(NT):
                def s0(t=t):
                    if t + NHS - 1 < NT:
                        load(t + NHS - 1)
                    h, s_, tm, u = hs[t % NHS], st[t % 2], tmp[t % 2], ub[t % 3]
                    K.act(lambda: nc.scalar.activation(out=junk[:], in_=h[:], func=AF.Square, accum_out=s_[:, 0:1]),
                          r=[h], w=[junk, s_])
                    K.act(lambda: nc.scalar.activation(out=s_[:, 1:2], in_=s_[:, 0:1], func=AF.Sqrt, scale=1.0 / D, bias=EPS),
                          r=[s_], w=[s_])
                    K.dve(lambda: nc.vector.reciprocal(out=s_[:, 2:3], in_=s_[:, 1:2]), r=[s_], w=[s_])
                    K.dve(lambda: nc.vector.scalar_tensor_tensor(out=tm[:], in0=h[:], scalar=s_[:, 2:3], in1=A[:], op0=ALU.mult, op1=ALU.mult),
                          r=[h, s_, A], w=[tm])
                    K.dve(lambda: nc.vector.tensor_tensor(out=u[:], in0=tm[:], in1=B[:], op=ALU.add), r=[tm, B], w=[u])

                def s1(t=t):
                    u, pt = ub[t % 3], pst[t % 2]
                    g = t // 4
                    ug = utg[g % 2]
                    for kc in range(8):
                        K.pe(lambda: nc.tensor.transpose(out=pt[:, kc * 128:(kc + 1) * 128], in_=u[:, kc * 128:(kc + 1) * 128],
                                                         identity=self.ident[:]), r=[u, self.ident], w=[pt])
                    tt = t % 4
                    K.act(lambda: nc.scalar.copy(out=ug[:, :, tt * 128:(tt + 1) * 128],
                                                 in_=pt[:].rearrange("p (kc t) -> p kc t", kc=8)), r=[pt], w=[ug])
                    if tt == 3:
                        K.dma("sp", UTv[:, :, g * 512:(g + 1) * 512], ug[:], r=[ug], w=[self.b_ut[g]])
                items.append([s0, s1])
            run_pipeline(items)
            K.barrier()

    def epilogue_setup(self, ph, i, which):
        K, nc = self.K, self.nc
        gname = "mix_post_g" if which == 0 else "ffn_post_g"
        kg = 2 if which == 0 else 5
        G = K.sb(ph, "eG", [128, D], F32)
        Gn = K.sb(ph, "eGn", [128, D], F32)
        self.bcast_load(G, self.mod_slice(i, kg), [self.b_modv[i]])
        self.bcast_load(Gn, self.gains[gname][i:i + 1, :], [self.cbuf])
        K.dve(lambda: nc.vector.tensor_tensor(out=G[:], in0=G[:], in1=Gn[:], op=ALU.mult), r=[G, Gn], w=[G])
        e = {
            "G": G,
            "h": [K.sb(ph, f"eh{j}", [128, D], F32) for j in range(2)],
            "tmp": [K.sb(ph, f"etmp{j}", [128, 512], F32) for j in range(2)],
            "junk": K.sb(ph, "ejunk", [128, 512], BF16),
            "st": [K.sb(ph, f"est{j}", [128, 8], F32) for j in range(2)],
            "n": 0,
            "src_is_x": False,
        }
        return e

    def epi_load(self, e, t):
        src = self.x if self.resid_from_x else self.out
        h = e["h"][t % 2]
        self.K.dma("sp", h[:], src[t * 128:(t + 1) * 128, :], r=[self.b_h[t]], w=[h])

    def epilogue(self, e, t, ybanks):
        K, nc = self.K, self.nc
        h = e["h"][t % 2]
        s_ = e["st"][t % 2]
        junk = e["junk"]
        G = e["G"]
        for hf in range(2):
            K.act(lambda hf=hf: nc.scalar.activation(out=junk[:], in_=ybanks[hf][:], func=AF.Square, accum_out=s_[:, hf:hf + 1]),
                  r=[ybanks[hf]], w=[junk, s_])
        K.dve(lambda: nc.vector.tensor_tensor(out=s_[:, 2:3], in0=s_[:, 0:1], in1=s_[:, 1:2], op=ALU.add), r=[s_], w=[s_])
        K.act(lambda: nc.scalar.activation(out=s_[:, 3:4], in_=s_[:, 2:3], func=AF.Sqrt, scale=1.0 / D, bias=EPS),
              r=[s_], w=[s_])
        K.dve(lambda: nc.vector.reciprocal(out=s_[:, 4:5], in_=s_[:, 3:4]), r=[s_], w=[s_])
        for hf in range(2):
            tm = e["tmp"][hf]
            K.dve(lambda hf=hf, tm=tm: nc.vector.scalar_tensor_tensor(out=tm[:], in0=ybanks[hf][:], scalar=s_[:, 4:5],
                                                                      in1=G[:, hf * 512:(hf + 1) * 512], op0=ALU.mult, op1=ALU.mult),
                  r=[ybanks[hf], s_, G], w=[tm])
            K.pool(lambda hf=hf, tm=tm: nc.gpsimd.tensor_tensor(out=h[:, hf * 512:(hf + 1) * 512], in0=h[:, hf * 512:(hf + 1) * 512],
                                                                 in1=tm[:], op=ALU.add), r=[tm, h], w=[h])
        K.dma("sp", self.out[t * 128:(t + 1) * 128, :], h[:], r=[h], w=[self.b_h[t]])

    def load_w(self, wt, src, kchunks, ncols, col0=0, split=2048):
        K = self.K
        sv = src.rearrange("(kc p) n -> p kc n", p=128)
        for kc in range(kchunks):
            for c0 in range(0, ncols, split):
                c1 = min(ncols, c0 + split)
                K.dma("pool", wt[:, kc, c0:c1], sv[:, kc, col0 + c0:col0 + c1], r=[self.cbuf], wa=[wt])

    def ffn_weights(self, ctx, i):
        K = self.K
        w1 = K.sb(ctx, "fw1", [128, 8, DFF], BF16)
        w2 = K.sb(ctx, "fw2", [128, 32, D], BF16)
        self.load_w(w1, self.ffn_w1[i], 8, DFF)
        self.load_w(w2, self.ffn_w2[i], 32, D, split=1024)
        return w1, w2

    def phase_ffn(self, i, w1, w2):
        K, nc = self.K, self.nc
        with contextlib.ExitStack() as ph:
            e = self.epilogue_setup(ph, i, 1)
            utg = [K.sb(ph, f"futg{j}", [128, 8, 512], BF16) for j in range(2)]
            hid = K.sb(ph, "fhid", [128, 32, 512], BF16)
            rl = [K.sb(ph, f"frl{j}", [128, 512], F32) for j in range(2)]
            psA = [K.ps(ph, f"fpsA{j}") for j in range(2)]
            psY = [[K.ps(ph, f"fpsY{j}{hf}") for hf in range(2)] for j in range(2)]
            UTv = self.UT.rearrange("(kc p) t -> p kc t", p=128)

            def load_u(g):
                K.dma("sp", utg[g % 2][:], UTv[:, :, g * 512:(g + 1) * 512], r=[self.b_ut[g]], w=[utg[g % 2]])

            load_u(0)
            for g in range(NG):
                if g + 1 < NG:
                    load_u(g + 1)
                ug = utg[g % 2]
                for fc in range(32):
                    ps = psA[fc % 2]
                    r_ = rl[fc % 2]
                    for kc in range(8):
                        K.pe(lambda kc=kc, fc=fc, ps=ps: nc.tensor.matmul(ps[:], lhsT=w1[:, kc, fc * 128:(fc + 1) * 128], rhs=ug[:, kc, :],
                                                                          start=(kc == 0), stop=(kc == 7)), r=[w1, ug], w=[ps])
                    K.act(lambda ps=ps, r_=r_: nc.scalar.activation(out=r_[:], in_=ps[:], func=AF.Relu), r=[ps], w=[r_])
                    K.pool(lambda fc=fc, r_=r_: nc.gpsimd.tensor_tensor(out=hid[:, fc, :], in0=r_[:], in1=r_[:], op=ALU.mult),
                           r=[r_], w=[hid])
                for tt in range(4):
                    t = g * 4 + tt
                    self.epi_load(e, t)
                    yb = psY[tt % 2]
                    for hf in range(2):
                        for fc in range(32):
                            K.pe(lambda fc=fc, hf=hf, tt=tt: nc.tensor.matmul(yb[hf][:], lhsT=hid[:, fc, tt * 128:(tt + 1) * 128],
                                                                             rhs=w2[:, fc, hf * 512:(hf + 1) * 512],
                                                                             start=(fc == 0), stop=(fc == 31)), r=[hid, w2], w=[yb[hf]])
                    self.epilogue(e, t, yb)
            K.barrier()

    def phase_sb_proj(self, j):
        K, nc = self.K, self.nc
        with contextlib.ExitStack() as ph:
            w = K.sb(ph, "sw", [128, 8, 3 * D], BF16)
            self.load_w(w, self.sb_w_in[j], 8, 3 * D, split=3072)
            utg = [K.sb(ph, f"sutg{k}", [128, 8, 512], BF16) for k in range(2)]
            stg = [K.sb(ph, f"sstg{k}", [128, 512], BF16) for k in range(4)]
            vst = [K.sb(ph, f"svst{k}", [128, D], BF16) for k in range(2)]
            pss = [K.ps(ph, f"sps{k}") for k in range(4)]
            UTv = self.UT.rearrange("(kc p) t -> p kc t", p=128)

            def load_u(g):
                K.dma("sp", utg[g % 2][:], UTv[:, :, g * 512:(g + 1) * 512], r=[self.b_ut[g]], w=[utg[g % 2]])

            load_u(0)
            n = 0
            for g in range(NG):
                if g + 1 < NG:
                    load_u(g + 1)
                ug = utg[g % 2]
                for fcn in range(16):
                    ps = pss[n % 4]
                    sg = stg[n % 4]
                    for kc in range(8):
                        K.pe(lambda kc=kc, fcn=fcn, ps=ps: nc.tensor.matmul(ps[:], lhsT=w[:, kc, fcn * 128:(fcn + 1) * 128], rhs=ug[:, kc, :],
                                                                           start=(kc == 0), stop=(kc == 7)), r=[w, ug], w=[ps])
                    if n % 2 == 0:
                        K.act(lambda ps=ps, sg=sg: nc.scalar.copy(out=sg[:], in_=ps[:]), r=[ps], w=[sg])
                    else:
                        K.dve(lambda ps=ps, sg=sg: nc.vector.tensor_copy(out=sg[:], in_=ps[:]), r=[ps], w=[sg])
                    dst = self.QT if fcn < 8 else self.KT
                    fo = (fcn % 8) * 128
                    bb = self.b_qt[g] if fcn < 8 else self.b_kt[g]
                    K.dma("sp", dst[fo:fo + 128, g * 512:(g + 1) * 512], sg[:], r=[sg], wa=[bb])
                    n += 1
                for tt in range(4):
                    t = g * 4 + tt
                    vs = vst[t % 2]
                    for hf in range(2):
                        ps = pss[n % 4]
                        n += 1
                        for kc in range(8):
                            K.pe(lambda kc=kc, hf=hf, tt=tt, ps=ps: nc.tensor.matmul(ps[:], lhsT=ug[:, kc, tt * 128:(tt + 1) * 128],
                                                                                    rhs=w[:, kc, 2 * D + hf * 512:2 * D + (hf + 1) * 512],
                                                                                    start=(kc == 0), stop=(kc == 7)), r=[w, ug], w=[ps])
                        if hf == 0:
                            K.act(lambda ps=ps, vs=vs: nc.scalar.copy(out=vs[:, 0:512], in_=ps[:]), r=[ps], w=[vs])
                        else:
                            K.dve(lambda ps=ps, vs=vs: nc.vector.tensor_copy(out=vs[:, 512:1024], in_=ps[:]), r=[ps], w=[vs])
                    K.dma("sp", self.Vd[t * 128:(t + 1) * 128, :], vs[:], r=[vs], w=[self.b_vd[t]])
            K.barrier()

    def phase_sb_core(self, i, j):
        K, nc = self.K, self.nc
        with contextlib.ExitStack() as ph:
            O = K.sb(ph, "aO", [128, NT, D], BF16)
            with contextlib.ExitStack() as ph2:
                V = K.sb(ph2, "aV", [128, NT, D], BF16)
                Vv = self.Vd.rearrange("(t p) f -> p t f", p=128)
                for t0 in range(0, NT, 8):
                    K.dma("sp", V[:, t0:t0 + 8, :], Vv[:, t0:t0 + 8, :], r=self.b_vd[t0:t0 + 8], wa=[V])
                msk = K.sb(ph2, "amsk", [128, 512], BF16)
                mbias = K.sb(ph2, "ambias", [128, 512], BF16)
                K.dma("sp", msk[:], self.c_sbmask, r=[self.cbuf], w=[msk])
                K.dma("sp", mbias[:], self.c_sbbias, r=[self.cbuf], w=[mbias])
                qh = [K.sb(ph2, f"aq{k}", [64, S], BF16) for k in range(2)]
                kh = [K.sb(ph2, f"ak{k}", [64, S], BF16) for k in range(2)]
                Et = [K.sb(ph2, f"aE{k}", [128, 512], F32) for k in range(4)]
                Xt = [K.sb(ph2, f"aX{k}", [128, 512], F32) for k in range(2)]
                Lt = [K.sb(ph2, f"aL{k}", [128, 512], BF16) for k in range(4)]
                Lr = [K.sb(ph2, f"aLr{k}", [128, 512], BF16) for k in range(2)]
                At = [K.sb(ph2, f"aA{k}", [128, 512], BF16) for k in range(5)]
                Rt = [K.sb(ph2, f"aR{k}", [128, 512], BF16) for k in range(2)]
                zps = [K.ps(ph2, f"azps{k}") for k in range(2)]
                cps = [K.ps(ph2, f"acps{k}") for k in range(2)]
                avs = [K.ps(ph2, f"aav{k}") for k in range(2)]

                Rq = [K.sb(ph2, f"aRq{k}", [128, 512], BF16) for k in range(2)]

                def load_head(h):
                    s = h % 2
                    K.dma("sp", qh[s][:], self.QT[h * 64:(h + 1) * 64, :], r=self.b_qt, w=[qh[s]])
                    K.dma("sp", kh[s][:], self.KT[h * 64:(h + 1) * 64, :], r=self.b_kt, w=[kh[s]])

                units = []
                ci = 0
                for h in range(NH):
                    for qc in range(NG):
                        nkt = 4 * qc + 4
                        for k_, i_ in enumerate(range(nkt - 1, -1, -1)):
                            units.append(dict(h=h, qc=qc, i=i_, k=k_, nkt=nkt, ci=ci, idx=len(units),
                                              lasth=(qc == NG - 1 and i_ == 0)))
                        ci += 1
                Rall = [[Rt[0], Rt[1]], [Rq[0], Rq[1]]]

                def geom(U):
                    a = U["i"] - 4 * U["qc"]
                    diag = a >= 0
                    c0 = 128 * a if diag else 0
                    return a, diag, c0, 512 - c0

                def stageA(U):
                    u, h, qc, i_ = U["idx"], U["h"], U["qc"], U["i"]
                    a, diag, c0, W = geom(U)
                    q_, k_ = qh[h % 2], kh[h % 2]
                    zp, E, L = zps[u % 2], Et[u % 4], Lt[u % 4]
                    qs = q_[:, qc * 512 + c0:(qc + 1) * 512]
                    K.pe(lambda: nc.tensor.matmul(zp[:, 0:W], lhsT=k_[:, i_ * 128:(i_ + 1) * 128], rhs=qs, start=True, stop=True),
                         r=[q_, k_], w=[zp])
                    K.act(lambda: nc.scalar.activation(out=E[:, 0:W], in_=zp[:, 0:W], func=AF.Exp, scale=0.125), r=[zp], w=[E])
                    if diag:
                        Lraw = Lr[u % 2]
                        K.act(lambda: nc.scalar.activation(out=Lraw[:, 0:W], in_=E[:, 0:W], func=AF.Ln, bias=1.0), r=[E], w=[Lraw])
                        K.dve(lambda: nc.vector.tensor_tensor(out=L[:, 0:W], in0=Lraw[:, 0:W], in1=msk[:, 0:W], op=ALU.mult),
                              r=[Lraw, msk], w=[L])
                    else:
                        K.act(lambda: nc.scalar.activation(out=L[:, 0:W], in_=E[:, 0:W], func=AF.Ln, bias=1.0), r=[E], w=[L])

                def stageB(U):
                    u, h, qc, i_, k = U["idx"], U["h"], U["qc"], U["i"], U["k"]
                    a, diag, c0, W = geom(U)
                    cp, L, A, E, X = cps[u % 2], Lt[u % 4], At[u % 5], Et[u % 4], Xt[u % 2]
                    Rp = Rall[U["ci"] % 2]
                    Rc, Rn = Rp[k % 2], Rp[(k + 1) % 2]
                    if k == 0:
                        K.pool(lambda: nc.gpsimd.memset(Rp[0][:], 0.0), w=[Rp[0]])
                        K.pool(lambda: nc.gpsimd.memset(Rp[1][:], 0.0), w=[Rp[1]])
                    K.pe(lambda: nc.tensor.matmul(cp[:, 0:W], lhsT=self.tri[:], rhs=L[:, 0:W], start=True, stop=(k == 0 and not diag)),
                         r=[self.tri, L], w=[cp])
                    if k != 0:
                        K.pe(lambda: nc.tensor.matmul(cp[:, 0:W], lhsT=self.ones[:], rhs=Rc[:, c0:512], start=False, stop=(not diag)),
                             r=[self.ones, Rc], w=[cp])
                    if diag:
                        K.pe(lambda: nc.tensor.matmul(cp[:, 0:W], lhsT=self.ident[:], rhs=mbias[:, 0:W], start=False, stop=True),
                             r=[self.ident, mbias], w=[cp])
                    if i_ != 0:
                        K.dve(lambda: nc.vector.tensor_tensor(out=Rn[:, c0:512], in0=Rc[:, c0:512], in1=L[:, 0:W], op=ALU.add),
                              r=[Rc, L], w=[Rn])
                    K.act(lambda: nc.scalar.activation(out=X[:, 0:W], in_=cp[:, 0:W], func=AF.Exp, scale=-1.0), r=[cp], w=[X])
                    K.dve(lambda: nc.vector.tensor_tensor(out=A[:, 0:W], in0=E[:, 0:W], in1=X[:, 0:W], op=ALU.mult), r=[E, X], w=[A])

                def stageC(U):
                    u, h, qc, i_, k = U["idx"], U["h"], U["qc"], U["i"], U["k"]
                    a, diag, c0, W = geom(U)
                    A = At[u % 5]
                    av = avs[U["ci"] % 2]
                    for n_, c in enumerate(range(a if diag else 0, 4)):
                        K.pe(lambda: nc.tensor.matmul(av[:, c * 64:(c + 1) * 64], lhsT=A[:, c * 128 - c0:(c + 1) * 128 - c0],
                                                      rhs=V[:, i_, h * 64:(h + 1) * 64], start=(k == 0 and n_ == 0), stop=False,
                                                      skip_group_check=True), r=[A, V], w=[av])
                    if i_ == 0:
                        K.dve(lambda: nc.vector.tensor_copy(out=O[:, qc * 4:(qc + 1) * 4, h * 64:(h + 1) * 64],
                                                            in_=av[:, 0:256].rearrange("p (c d) -> p c d", c=4)), r=[av], w=[O])
                    if U["lasth"] and h + 2 < NH:
                        load_head(h + 2)

                load_head(0)
                load_head(1)
                n = len(units)
                for kk in range(n + 4):
                    if kk < n:
                        stageA(units[kk])
                    if 0 <= kk - 2 < n:
                        stageB(units[kk - 2])
                    if 0 <= kk - 4 < n:
                        stageC(units[kk - 4])
                K.barrier()
            self.out_proj(ph, i, self.sb_w_out[j], O, None)
            K.barrier()

    def out_proj(self, ph, i, w_dram, O, bias_row):
        K, nc = self.K, self.nc
        with contextlib.ExitStack() as ph3:
            w = K.sb(ph3, "ow", [128, 8, D], BF16)
            self.load_w(w, w_dram, 8, D, split=1024)
            e = self.epilogue_setup(ph3, i, 0)
            oT = [K.sb(ph3, f"ooT{k}", [128, 8, 128], BF16) for k in range(2)]
            ptr = [K.ps(ph3, f"optr{k}", (128, 1024), BF16) for k in range(2)]
            psY = [[K.ps(ph3, f"opsY{k}{hf}") for hf in range(2)] for k in range(2)]
            brow = None
            if bias_row is not None:
                brow = K.sb(ph3, "obrow", [1, D], BF16)
                K.dma("pool", brow[:], bias_row, r=[self.cbuf], w=[brow])
            oin = None
            if O is None:
                oin = [K.sb(ph3, f"ooin{k}", [128, D], BF16) for k in range(3)]
                for t in range(2):
                    K.dma("sp", oin[t % 3][:], self.Od[t * 128:(t + 1) * 128, :], r=[self.b_od], w=[oin[t % 3]])
            for t in range(NT):
                self.epi_load(e, t)
                pt = ptr[t % 2]
                ot = oT[t % 2]
                if oin is not None and t + 2 < NT:
                    K.dma("sp", oin[(t + 2) % 3][:], self.Od[(t + 2) * 128:(t + 3) * 128, :], r=[self.b_od], w=[oin[(t + 2) % 3]])
                for kc in range(8):
                    src_ = O[:, t, kc * 128:(kc + 1) * 128] if oin is None else oin[t % 3][:, kc * 128:(kc + 1) * 128]
                    srcT = O if oin is None else oin[t % 3]
                    K.pe(lambda kc=kc: nc.tensor.transpose(out=pt[:, kc * 128:(kc + 1) * 128], in_=src_,
                                                          identity=self.ident[:]), r=[srcT, self.ident], w=[pt])
                K.act(lambda: nc.scalar.copy(out=ot[:], in_=pt[:].rearrange("p (kc t) -> p kc t", kc=8)), r=[pt], w=[ot])
                yb = psY[t % 2]
                for hf in range(2):
                    for kc in range(8):
                        K.pe(lambda kc=kc, hf=hf: nc.tensor.matmul(yb[hf][:], lhsT=ot[:, kc, :], rhs=w[:, kc, hf * 512:(hf + 1) * 512],
                                                                   start=(kc == 0), stop=(kc == 7 and brow is None)), r=[ot, w], w=[yb[hf]])
                    if brow is not None:
                        K.pe(lambda hf=hf: nc.tensor.matmul(yb[hf][:], lhsT=self.ones[0:1, :], rhs=brow[0:1, hf * 512:(hf + 1) * 512],
                                                            start=False, stop=True), r=[self.ones, brow], w=[yb[hf]])
                self.epilogue(e, t, yb)

    def build(self):
        K = self.K
        self.load_consts()
        self.phase_mod()
        first = True
        for i in self.layers:
            kind, j = i % 3, i // 3
            self.phase_norm(i, 0, src_is_x=first)
            self.resid_from_x = first
            first = False
            if kind == 0:
                self.phase_sb_proj(j)
                self.phase_sb_core(i, j)
            elif kind == 1:
                self.phase_nsa(i, j)
            else:
                self.phase_conv(i, j)
            self.resid_from_x = False
            with contextlib.ExitStack() as wctx:
                w1, w2 = self.ffn_weights(wctx, i)
                self.phase_norm(i, 1, src_is_x=False)
                self.phase_ffn(i, w1, w2)
        K.barrier()
        K.st.close()
        return self.nc

    def copy_x_to_out(self):
        K = self.K
        with contextlib.ExitStack() as ph:
            bufs = [K.sb(ph, f"cx{k}", [128, 4, D], F32) for k in range(2)]
            xv = self.x.rearrange("(g c p) f -> g p c f", p=128, c=4)
            ov = self.out.rearrange("(g c p) f -> g p c f", p=128, c=4)
            for g in range(NG):
                b = bufs[g % 2]
                K.dma("sp", b[:], xv[g], r=[], w=[b])
                K.dma("sp", ov[g], b[:], r=[b], w=[self.b_h[4 * g + c] for c in range(4)])
            K.barrier()


    def phase_nsa(self, i, j):
        K, nc = self.K, self.nc
        UTv = self.UT.rearrange("(kc p) t -> p kc t", p=128)
        TWO_PI = 2.0 * np.pi
        with contextlib.ExitStack() as nsa:
            kcT = K.sb(nsa, "n_kcT", [64, 4, 256], BF16)
            vcmp = K.sb(nsa, "n_vcmp", [128, 2, 4, 64], BF16)
            cscmp = K.sb(nsa, "n_cscmp", [64, 2, 256], F32)
            perm = K.sb(nsa, "n_perm", [64, 64], BF16)
            K.dma("sp", perm[:], self.c_perm, r=[self.cbuf], w=[perm])
            with contextlib.ExitStack() as ph:
                w = K.sb(ph, "nw", [128, 8, NSA_IN], BF16)
                self.load_w(w, self.nsa_w_in[j], 8, NSA_IN, split=NSA_IN)
                cosT = K.sb(ph, "ncos", [64, S], F32)
                sinT = K.sb(ph, "nsin", [64, S], F32)
                with contextlib.ExitStack() as ph0:
                    posi = K.sb(ph0, "nposi", [64, S], I32)
                    ang = K.sb(ph0, "nang", [64, S], F32)
                    t1 = K.sb(ph0, "nt1", [64, S], F32)
                    t2 = K.sb(ph0, "nt2", [64, S], F32)
                    ki = K.sb(ph0, "nki", [64, S], I32)
                    invf = K.sb(ph0, "ninvf", [64, 1], F32)
                    K.dma("sp", posi[:], self.pos.to_broadcast([64, S]), r=[self.cbuf], w=[posi])
                    K.dma("sp", invf[:], self.c_invf, r=[self.cbuf], w=[invf])
                    K.dve(lambda: nc.vector.tensor_copy(out=ang[:], in_=posi[:]), r=[posi], w=[ang])
                    K.dve(lambda: nc.vector.tensor_scalar(out=ang[:], in0=ang[:], scalar1=invf[:, 0:1], scalar2=None, op0=ALU.mult),
                          r=[ang, invf], w=[ang])
                    for tab, shift in ((sinT, 0.0), (cosT, 0.5 * np.pi)):
                        K.dve(lambda: nc.vector.tensor_scalar(out=t1[:], in0=ang[:], scalar1=shift, scalar2=1.0 / TWO_PI,
                                                              op0=ALU.add, op1=ALU.mult), r=[ang], w=[t1])
                        K.dve(lambda: nc.vector.tensor_copy(out=ki[:], in_=t1[:]), r=[t1], w=[ki])
                        K.dve(lambda: nc.vector.tensor_copy(out=t1[:], in_=ki[:]), r=[ki], w=[t1])
                        K.dve(lambda: nc.vector.scalar_tensor_tensor(out=t2[:], in0=t1[:], scalar=-TWO_PI, in1=ang[:], op0=ALU.mult, op1=ALU.add),
                              r=[t1, ang], w=[t2])
                        K.dve(lambda: nc.vector.tensor_scalar(out=t2[:], in0=t2[:], scalar1=shift, scalar2=None, op0=ALU.add), r=[t2], w=[t2])
                        K.dve(lambda: nc.vector.tensor_scalar(out=t1[:], in0=t2[:], scalar1=np.pi, scalar2=-TWO_PI, op0=ALU.is_gt, op1=ALU.mult),
                              r=[t2], w=[t1])
                        K.dve(lambda: nc.vector.tensor_tensor(out=t2[:], in0=t2[:], in1=t1[:], op=ALU.add), r=[t2, t1], w=[t2])
                        K.dve(lambda: nc.vector.tensor_scalar(out=t1[:], in0=t2[:], scalar1=-np.pi, scalar2=TWO_PI, op0=ALU.is_lt, op1=ALU.mult),
                              r=[t2], w=[t1])
                        K.dve(lambda: nc.vector.tensor_tensor(out=t2[:], in0=t2[:], in1=t1[:], op=ALU.add), r=[t2, t1], w=[t2])
                        K.dve(lambda: nc.vector.tensor_scalar(out=t2[:], in0=t2[:], scalar1=-3.1415925, scalar2=3.1415925, op0=ALU.max, op1=ALU.min),
                              r=[t2], w=[t2])
                        K.act(lambda: nc.scalar.activation(out=tab[:], in_=t2[:], func=AF.Sin), r=[t2], w=[tab])
                    K.dve(lambda: nc.vector.tensor_copy(out=cscmp[:, 0, 0:255], in_=cosT[:, 31:S:16]), r=[cosT], w=[cscmp])
                    K.dve(lambda: nc.vector.tensor_copy(out=cscmp[:, 1, 0:255], in_=sinT[:, 31:S:16]), r=[sinT, cscmp], w=[cscmp])
                    K.barrier()
                if NSA_STOP == "n0":
                    return
                utg = [K.sb(ph, f"nutg{k}", [128, 8, 512], BF16) for k in range(2)]
                xb = [K.sb(ph, f"nxb{k}", [64, 512], BF16) for k in range(2)]
                r1 = [K.sb(ph, f"nr1{k}", [64, 512], F32) for k in range(2)]
                r2 = [K.sb(ph, f"nr2{k}", [64, 512], F32) for k in range(2)]
                ob = [K.sb(ph, f"nob{k}", [64, 512], BF16) for k in range(4)]
                va = [K.sb(ph, f"nva{k}", [128, 8, 65], BF16) for k in range(2)]
                gt = [K.sb(ph, f"ngt{k}", [128, 48], F32) for k in range(2)]
                for v_ in va:
                    K.pool(lambda: nc.gpsimd.memset(v_[:], 1.0), w=[v_])
                pp = [K.ps(ph, f"npp{k}") for k in range(3)]
                pr = [K.ps(ph, f"npr{k}") for k in range(2)]
                pv = [K.ps(ph, f"npv{k}") for k in range(2)]
                pg = K.ps(ph, "npg")

                def load_u(g):
                    K.dma("sp", utg[g % 2][:], UTv[:, :, g * 512:(g + 1) * 512], r=[self.b_ut[g]], w=[utg[g % 2]])

                units = []
                for h in range(16):
                    units.append((h * 64, self.NQ, h, True, self.b_nq))
                for g4 in range(4):
                    units.append((D + 2 * 256 + g4 * 64, self.NK, g4, True, self.b_nk))
                for g4 in range(4):
                    units.append((D + 4 * 256 + g4 * 64, self.NK, 4 + g4, True, self.b_nk))
                for g4 in range(4):
                    units.append((D + 0 * 256 + g4 * 64, self.NC, g4, False, self.b_ncr))
                for g4 in range(4):
                    units.append((D + 1 * 256 + g4 * 64, self.NC, 4 + g4, False, self.b_ncr))
                load_u(0)
                n = 0
                nr = 0
                for g in range(NG):
                    if g + 1 < NG:
                        load_u(g + 1)
                    ug = utg[g % 2]
                    tsl = slice(g * 512, (g + 1) * 512)
                    for (col, dst, ui, rope, bb) in units:
                        if (rope and "r" not in N1_PARTS) or ((not rope) and "u" not in N1_PARTS):
                            continue
                        ps = pp[n % 3]
                        o_ = ob[n % 4]
                        n += 1
                        for kc in range(8):
                            K.pe(lambda: nc.tensor.matmul(ps[0:64, :], lhsT=w[:, kc, col:col + 64], rhs=ug[:, kc, :],
                                                          start=(kc == 0), stop=(kc == 7)), r=[w, ug], w=[ps])
                        if rope and "asu" not in ROPE_MODE:
                            x_, a_, b_, p2 = xb[nr % 2], r1[nr % 2], r2[nr % 2], pr[nr % 2]
                            nr += 1
                            K.act(lambda: nc.scalar.copy(out=x_[:], in_=ps[0:64, :]), r=[ps], w=[x_])
                            if "noperm" not in ROPE_MODE:
                                K.pe(lambda: nc.tensor.matmul(p2[0:64, :], lhsT=perm[:], rhs=x_[:], start=True, stop=True), r=[perm, x_], w=[p2])
                            K.dve(lambda: nc.vector.tensor_tensor(out=a_[:], in0=ps[0:64, :], in1=cosT[:, tsl], op=ALU.mult), r=[ps, cosT], w=[a_])
                            if "noperm" not in ROPE_MODE:
                                K.dve(lambda: nc.vector.tensor_tensor(out=b_[:], in0=p2[0:64, :], in1=sinT[:, tsl], op=ALU.mult), r=[p2, sinT], w=[b_])
                            else:
                                K.dve(lambda: nc.vector.tensor_tensor(out=b_[:], in0=ps[0:64, :], in1=sinT[:, tsl], op=ALU.mult), r=[ps, sinT], w=[b_])
                            if "dveadd" in ROPE_MODE:
                                K.dve(lambda: nc.vector.tensor_tensor(out=o_[:], in0=a_[:], in1=b_[:], op=ALU.add), r=[a_, b_], w=[o_])
                            else:
                                K.pool(lambda: nc.gpsimd.tensor_tensor(out=o_[:], in0=a_[:], in1=b_[:], op=ALU.add), r=[a_, b_], w=[o_])
                        else:
                            K.act(lambda: nc.scalar.copy(out=o_[:], in_=ps[0:64, :]), r=[ps], w=[o_])
                        K.dma("sp", dst[ui, :, tsl], o_[:], r=[o_], wa=[bb])
                    for tt in range(4):
                        t = g * 4 + tt
                        v_ = va[t % 2]
                        pv_ = pv[t % 2]
                        for m, c0 in ((0, D + 3 * 256), (1, D + 5 * 256)) if "v" in N1_PARTS else ():
                            for kc in range(8):
                                K.pe(lambda: nc.tensor.matmul(pv_[:, m * 256:(m + 1) * 256], lhsT=ug[:, kc, tt * 128:(tt + 1) * 128],
                                                              rhs=w[:, kc, c0:c0 + 256], start=(kc == 0), stop=(kc == 7)), r=[w, ug], w=[pv_])
                        if "v" in N1_PARTS:
                            K.dve(lambda: nc.vector.tensor_copy(out=v_[:, :, 0:64], in_=pv_[:].rearrange("p (u d) -> p u d", d=64)), r=[pv_], w=[v_])
                            K.dma("sp", self.NV[t * 128:(t + 1) * 128, :], v_[:].rearrange("p u d -> p (u d)"), r=[v_], wa=[self.b_nv])
                        g_ = gt[t % 2]
                        if "g" not in N1_PARTS:
                            continue
                        for kc in range(8):
                            K.pe(lambda: nc.tensor.matmul(pg[:, 0:48], lhsT=ug[:, kc, tt * 128:(tt + 1) * 128], rhs=w[:, kc, 2560:2608],
                                                          start=(kc == 0), stop=(kc == 7)), r=[w, ug], w=[pg])
                        K.act(lambda: nc.scalar.activation(out=g_[:], in_=pg[:, 0:48], func=AF.Sigmoid), r=[pg], w=[g_])
                        K.dma("sp", self.NGt[t * 128:(t + 1) * 128, :], g_[:], r=[g_], wa=[self.b_ngt])
                K.barrier()
            if NSA_STOP == "n1":
                return
            with contextlib.ExitStack() as ph:
                raw = K.sb(ph, "craw", [64, 8, S], BF16)
                for u_ in range(8):
                    K.dma("sp", raw[:, u_, :], self.NC[u_], r=[self.b_ncr], wa=[raw])
                K.pool(lambda: nc.gpsimd.memset(vcmp[:], 0.0), w=[vcmp])
                K.pool(lambda: nc.gpsimd.memset(kcT[:], 0.0), w=[kcT])
                hps = [K.ps(ph, f"chps{k}") for k in range(2)]
                bps = K.ps(ph, "cbps")
                ops_ = [K.ps(ph, f"cops{k}") for k in range(2)]
                p2 = K.ps(ph, "cp2")
                for kv in ("k", "v"):
                    w1 = K.sb(ph, "cw1" + kv, [64, 32, 256], BF16)
                    w1v = self.nsa_w1[kv][j].rearrange("(l d) h -> d l h", d=64)
                    for l0 in range(0, 32, 8):
                        K.dma("pool", w1[:, l0:l0 + 8, :], w1v[:, l0:l0 + 8, :], r=[self.cbuf], wa=[w1])
                    w2 = K.sb(ph, "cw2" + kv, [128, 2, 64], BF16)
                    K.dma("pool", w2[:], self.nsa_w2[kv][j].rearrange("(hc p) d -> p hc d", p=128), r=[self.cbuf], w=[w2])
                    peT = K.sb(ph, "cpeT" + kv, [64, 32], F32)
                    peTb = K.sb(ph, "cpeTb" + kv, [64, 32], BF16)
                    K.dma("sp", peT[:], self.nsa_peT[kv], r=[self.cbuf], w=[peT])
                    K.dve(lambda: nc.vector.tensor_copy(out=peTb[:], in_=peT[:]), r=[peT], w=[peTb])
                    bias = K.sb(ph, "cbias" + kv, [128, 2], F32)
                    for hc in range(2):
                        for l in range(32):
                            K.pe(lambda: nc.tensor.matmul(bps[:, hc:hc + 1], lhsT=w1[:, l, hc * 128:(hc + 1) * 128], rhs=peTb[:, l:l + 1],
                                                          start=(l == 0), stop=(l == 31)), r=[w1, peTb], w=[bps])
                        K.dve(lambda: nc.vector.tensor_copy(out=bias[:, hc:hc + 1], in_=bps[:, hc:hc + 1]), r=[bps], w=[bias])
                    xb_ = K.sb(ph, "cxb" + kv, [128, 256], F32)
                    x2_ = K.sb(ph, "cx2" + kv, [128, 256], F32)
                    x3_ = K.sb(ph, "cx3" + kv, [128, 256], F32)
                    hidT = K.sb(ph, "chid" + kv, [128, 2, 256], BF16)
                    kx = K.sb(ph, "ckx" + kv, [64, 256], BF16)
                    ka = K.sb(ph, "cka" + kv, [64, 256], F32)
                    kb_ = K.sb(ph, "ckb" + kv, [64, 256], F32)
                    for g4 in range(4):
                        ui = g4 if kv == "k" else 4 + g4
                        for hc in range(2):
                            hp = hps[hc]
                            for l in range(32):
                                K.pe(lambda: nc.tensor.matmul(hp[:, 0:255], lhsT=w1[:, l, hc * 128:(hc + 1) * 128],
                                                              rhs=raw[:, ui, l:l + 16 * 254 + 1:16], start=(l == 0), stop=(l == 31)),
                                     r=[w1, raw], w=[hp])
                            K.dve(lambda: nc.vector.tensor_scalar(out=xb_[:, 0:255], in0=hp[:, 0:255], scalar1=bias[:, hc:hc + 1], scalar2=None,
                                                                  op0=ALU.add), r=[hp, bias], w=[xb_])
                            K.pool(lambda: nc.gpsimd.tensor_tensor(out=x2_[:, 0:255], in0=xb_[:, 0:255], in1=xb_[:, 0:255], op=ALU.mult), r=[xb_], w=[x2_])
                            K.dve(lambda: nc.vector.tensor_scalar(out=x2_[:, 0:255], in0=x2_[:, 0:255], scalar1=0.044715, scalar2=1.0,
                                                                  op0=ALU.mult, op1=ALU.add), r=[x2_], w=[x2_])
                            K.dve(lambda: nc.vector.tensor_tensor(out=x3_[:, 0:255], in0=x2_[:, 0:255], in1=xb_[:, 0:255], op=ALU.mult), r=[x2_, xb_], w=[x3_])
                            K.act(lambda: nc.scalar.activation(out=x3_[:, 0:255], in_=x3_[:, 0:255], func=AF.Tanh, scale=0.7978845608028654),
                                  r=[x3_], w=[x3_])
                            K.dve(lambda: nc.vector.scalar_tensor_tensor(out=x2_[:, 0:255], in0=x3_[:, 0:255], scalar=1.0, in1=xb_[:, 0:255],
                                                                         op0=ALU.add, op1=ALU.mult), r=[x3_, xb_], w=[x2_])
                            K.pool(lambda: nc.gpsimd.tensor_scalar(out=hidT[:, hc, 0:255], in0=x2_[:, 0:255], scalar1=0.5, scalar2=None, op0=ALU.mult),
                                   r=[x2_], w=[hidT])
                        if kv == "k":
                            op_ = ops_[0]
                            for hc in range(2):
                                K.pe(lambda: nc.tensor.matmul(op_[0:64, 0:255], lhsT=w2[:, hc, :], rhs=hidT[:, hc, 0:255],
                                                              start=(hc == 0), stop=(hc == 1)), r=[w2, hidT], w=[op_])
                            K.act(lambda: nc.scalar.copy(out=kx[:, 0:255], in_=op_[0:64, 0:255]), r=[op_], w=[kx])
                            K.pe(lambda: nc.tensor.matmul(p2[0:64, 0:255], lhsT=perm[:], rhs=kx[:, 0:255], start=True, stop=True), r=[perm, kx], w=[p2])
                            K.dve(lambda: nc.vector.tensor_tensor(out=ka[:, 0:255], in0=op_[0:64, 0:255], in1=cscmp[:, 0, 0:255], op=ALU.mult),
                                  r=[op_, cscmp], w=[ka])
                            K.dve(lambda: nc.vector.tensor_tensor(out=kb_[:, 0:255], in0=p2[0:64, 0:255], in1=cscmp[:, 1, 0:255], op=ALU.mult),
                                  r=[p2, cscmp], w=[kb_])
                            K.pool(lambda: nc.gpsimd.tensor_tensor(out=kcT[:, g4, 0:255], in0=ka[:, 0:255], in1=kb_[:, 0:255], op=ALU.add),
                                   r=[ka, kb_], w=[kcT])
                        else:
                            for nch, m in ((0, 128), (1, 127)):
                                op_ = ops_[nch]
                                for hc in range(2):
                                    K.pe(lambda: nc.tensor.matmul(op_[0:m, 0:64], lhsT=hidT[:, hc, nch * 128:nch * 128 + m], rhs=w2[:, hc, :],
                                                                  start=(hc == 0), stop=(hc == 1)), r=[w2, hidT], w=[op_])
                                K.act(lambda: nc.scalar.copy(out=vcmp[0:m, nch, g4, :], in_=op_[0:m, 0:64]), r=[op_], w=[vcmp])
                K.barrier()
            if NSA_STOP == "n2":
                return
            with contextlib.ExitStack() as ph:
                with contextlib.ExitStack() as ph2:
                    self._nsa_attn(ph2, None, kcT, vcmp)
                    K.barrier()
                self.out_proj(ph, i, self.nsa_w_out[j], None, None)
                K.barrier()

    def _nsa_attn(self, ph, O, kcT, vcmp):
        K, nc = self.K, self.nc
        V = K.sb(ph, "tV", [128, NT, 8 * 65], BF16)
        NVv = self.NV.rearrange("(t p) f -> p t f", p=128)
        for t0 in range(0, NT, 8):
            K.dma("sp", V[:, t0:t0 + 8, :], NVv[:, t0:t0 + 8, :], r=[self.b_nv], wa=[V])
        GT = K.sb(ph, "tGT", [128, NT, 48], F32)
        NGv = self.NGt.rearrange("(t p) f -> p t f", p=128)
        for t0 in range(0, NT, 8):
            K.dma("sp", GT[:, t0:t0 + 8, :], NGv[:, t0:t0 + 8, :], r=[self.b_ngt], wa=[GT])
        esel = K.sb(ph, "tesel", [64, 32 * 128], BF16)
        winb = K.sb(ph, "twin01", [128, 8 * 512], BF16)
        causb = K.sb(ph, "tcaus01", [128, 512], BF16)
        Mks = [K.sb(ph, f"tMk{k}", [128, 28 + 4 * k, 512], BF16) for k in range(2)]
        Ob = [K.sb(ph, f"tOb{k}", [128, 4, 256], BF16) for k in range(2)]
        Odv = self.Od.rearrange("(t p) f -> p t f", p=128)
        band = K.sb(ph, "tband", [128, 512], BF16)
        wcm = K.sb(ph, "twcm", [128, 128], F32)
        wfb = K.sb(ph, "twfb", [128, 128], F32)
        anyok = K.sb(ph, "tanyok", [128, 1], F32)
        for t_, src in ((esel, self.c_esel), (winb, self.c_win01), (causb, self.c_caus01), (band, self.c_band),
                        (wcm, self.c_wcm), (wfb, self.c_wfb), (anyok, self.c_anyok)):
            K.dma("sp", t_[:], src, r=[self.cbuf], w=[t_])
        ks = K.sb(ph, "tks", [64, S], BF16)
        kw = K.sb(ph, "tkw", [64, S], BF16)
        qh = [K.sb(ph, f"tq{k}", [64, S], BF16) for k in range(4)]
        psg = K.sb(ph, "tpsg", [128, 4, 256], F32)
        NS = 3
        pun = [K.sb(ph, f"tpun{k}", [128, 256], F32) for k in range(NS)]
        pb = [K.sb(ph, f"tpb{k}", [128, 256], BF16) for k in range(NS)]
        pTs = [K.sb(ph, f"tpT{k}", [128, 2, 128], BF16) for k in range(NS)]
        st = [K.sb(ph, f"tst{k}", [128, 8], F32) for k in range(NS)]
        s4 = K.sb(ph, "ts4", [128, 64], F32)
        imp = K.sb(ph, "timp", [128, 64], F32)
        sc = K.sb(ph, "tsc", [128, 64], F32)
        wk = K.sb(ph, "twk", [128, 64], F32)
        m8a = K.sb(ph, "tm8a", [128, 8], F32)
        m8b = K.sb(ph, "tm8b", [128, 8], F32)
        selt = K.sb(ph, "tsel", [128, 64], F32)
        negm = [K.sb(ph, f"tnegm{k}", [128, 64], BF16) for k in range(2)]
        nmT = [K.sb(ph, f"tnmT{k}", [64, 512], BF16) for k in range(2)]
        Pt = [K.sb(ph, f"tP{k}", [128, 512], BF16) for k in range(5)]
        Oq = [K.sb(ph, f"tOq{k}", [128, 4, 256], F32) for k in range(2)]
        cf = [K.sb(ph, f"tcf{k}", [128, 8], F32) for k in range(2)]
        sps = [K.ps(ph, f"tsps{k}") for k in range(2)]
        accs = K.ps(ph, "taccs")
        accw = K.ps(ph, "taccw")
        cpsb = [K.ps(ph, f"tcps{k}") for k in range(2)]
        misc = K.ps(ph, "tmisc", (128, 1024), BF16)
        ocp = K.ps(ph, "tocp")
        items = []
        cnt = {"cmp": 0, "u": 0, "gq": 0}

        def add_loads(g):
            def f():
                K.dma("sp", ks[:], self.NK[g], r=[self.b_nk], w=[ks])
                K.dma("sp", kw[:], self.NK[4 + g], r=[self.b_nk], w=[kw])
                for r in range(4):
                    K.dma("sp", qh[r][:], self.NQ[4 * g + r], r=[self.b_nq], w=[qh[r]])
            items.append([f])

        def add_cmp(g, qc, c, r, gq):
            T_ = 4 * qc + c
            ncols = min(8 * T_ + 7, NCMP)
            b0 = 256 - 8 * T_
            h = 4 * g + r
            q_ = qh[r]
            Oq_ = Oq[gq % 2]
            chunks = [(0, min(128, ncols))] + ([(1, ncols - 128)] if ncols > 128 else [])
            stt = {}

            def s0():
                n_ = cnt["cmp"]
                cnt["cmp"] += 1
                stt["n"] = n_
                s_, pu, pb_, cps = st[n_ % NS], pun[n_ % NS], pb[n_ % NS], cpsb[n_ % 2]
                K.pe(lambda: nc.tensor.matmul(cps[:, 0:ncols], lhsT=q_[:, T_ * 128:(T_ + 1) * 128], rhs=kcT[:, g, 0:ncols],
                                              start=True, stop=False), r=[q_, kcT], w=[cps])
                K.pe(lambda: nc.tensor.matmul(cps[:, 0:ncols], lhsT=self.ident[:], rhs=band[:, b0:b0 + ncols],
                                              start=False, stop=True), r=[self.ident, band], w=[cps])
                K.dve(lambda: nc.vector.reduce_max(out=s_[:, 0:1], in_=cps[:, 0:ncols], axis=AX.X), r=[cps], w=[s_])
                K.dve(lambda: nc.vector.tensor_scalar(out=s_[:, 1:2], in0=s_[:, 0:1], scalar1=-0.125, scalar2=None, op0=ALU.mult),
                      r=[s_], w=[s_])
                K.act(lambda: nc.scalar.activation(out=pu[:, 0:ncols], in_=cps[:, 0:ncols], func=AF.Exp, scale=0.125,
                                                   bias=s_[:, 1:2], accum_out=s_[:, 2:3]), r=[cps, s_], w=[pu, s_])

            def s0b():
                n_ = stt["n"]
                s_, pu, pb_ = st[n_ % NS], pun[n_ % NS], pb[n_ % NS]
                K.dve(lambda: nc.vector.reciprocal(out=s_[:, 3:4], in_=s_[:, 2:3]), r=[s_], w=[s_])
                if T_ == 0:
                    K.dve(lambda: nc.vector.tensor_tensor(out=s_[:, 3:4], in0=s_[:, 3:4], in1=anyok[:], op=ALU.mult), r=[s_, anyok], w=[s_])
                if r == 0:
                    K.dve(lambda: nc.vector.tensor_scalar(out=psg[:, c, 0:ncols], in0=pu[:, 0:ncols], scalar1=s_[:, 3:4], scalar2=None,
                                                          op0=ALU.mult), r=[pu, s_], w=[psg])
                else:
                    K.dve(lambda: nc.vector.scalar_tensor_tensor(out=psg[:, c, 0:ncols], in0=pu[:, 0:ncols], scalar=s_[:, 3:4],
                                                                 in1=psg[:, c, 0:ncols], op0=ALU.mult, op1=ALU.add), r=[pu, s_, psg], w=[psg])
                K.pool(lambda: nc.gpsimd.tensor_scalar(out=pb_[:, 0:ncols], in0=pu[:, 0:ncols], scalar1=s_[:, 3:4], scalar2=None,
                                                       op0=ALU.mult), r=[pu, s_], w=[pb_])

            def s1():
                n_ = stt["n"]
                pb_, pT = pb[n_ % NS], pTs[n_ % NS]
                for ch, wd in chunks:
                    K.pe(lambda: nc.tensor.transpose(out=misc[0:wd, ch * 128:(ch + 1) * 128], in_=pb_[:, ch * 128:ch * 128 + wd],
                                                     identity=self.ident[:]), r=[pb_, self.ident], w=[misc])
                for ch, wd in chunks:
                    K.act(lambda: nc.scalar.copy(out=pT[0:wd, ch, :], in_=misc[0:wd, ch * 128:(ch + 1) * 128]), r=[misc], w=[pT])

            def s2():
                pT = pTs[stt["n"] % NS]
                for k_, (ch, wd) in enumerate(chunks):
                    K.pe(lambda: nc.tensor.matmul(ocp[:, 0:64], lhsT=pT[0:wd, ch, :], rhs=vcmp[0:wd, ch, g, :],
                                                  start=(k_ == 0), stop=(k_ == len(chunks) - 1)), r=[pT, vcmp], w=[ocp])
                K.dve(lambda: nc.vector.tensor_scalar(out=Oq_[:, c, r * 64:(r + 1) * 64], in0=ocp[:, 0:64],
                                                      scalar1=GT[:, T_, 3 * h:3 * h + 1], scalar2=None, op0=ALU.mult),
                      r=[ocp, GT], w=[Oq_])
            items.append([s0, s0b, s1, s2])

        def add_select(g, qc, c, gq):
            T_ = 4 * qc + c
            w0 = 64 - 2 * T_
            nm_ = negm[c % 2]

            def f():
                pv4 = psg[:, c, :].rearrange("p (j f) -> p j f", f=4)
                K.dve(lambda: nc.vector.tensor_reduce(out=s4[:], in_=pv4, axis=AX.X, op=ALU.add), r=[psg], w=[s4])
                K.dve(lambda: nc.vector.scalar_tensor_tensor(out=imp[:], in0=pv4[:, :, 3], scalar=-0.5, in1=s4[:], op0=ALU.mult, op1=ALU.add),
                      r=[psg, s4], w=[imp])
                K.dve(lambda: nc.vector.scalar_tensor_tensor(out=imp[:, 1:64], in0=pv4[:, 0:63, 3], scalar=0.5, in1=imp[:, 1:64],
                                                             op0=ALU.mult, op1=ALU.add), r=[psg, imp], w=[imp])
                K.dve(lambda: nc.vector.tensor_tensor(out=sc[:], in0=imp[:], in1=wcm[:, w0:w0 + 64], op=ALU.mult), r=[imp, wcm], w=[sc])
                K.dve(lambda: nc.vector.tensor_tensor(out=sc[:], in0=sc[:], in1=wfb[:, w0:w0 + 64], op=ALU.add), r=[sc, wfb], w=[sc])
                K.dve(lambda: nc.vector.memset(sc[:, 0:1], 1.0e4), r=[sc], w=[sc])

            def f2():
                K.dve(lambda: nc.vector.max(out=m8a[:], in_=sc[:]), r=[sc], w=[m8a])
                K.dve(lambda: nc.vector.match_replace(out=wk[:], in_to_replace=m8a[:], in_values=sc[:], imm_value=-3.0e38), r=[sc, m8a], w=[wk])
                K.dve(lambda: nc.vector.max(out=m8b[:], in_=wk[:]), r=[wk], w=[m8b])

            def f3():
                K.dve(lambda: nc.vector.tensor_scalar(out=nm_[:], in0=sc[:], scalar1=m8b[:, 7:8], scalar2=None, op0=ALU.is_ge),
                      r=[sc, m8b], w=[nm_])
                K.pe(lambda: nc.tensor.transpose(out=misc[0:64, 512 + c * 128:512 + (c + 1) * 128], in_=nm_[:], identity=self.ident[:]),
                     r=[nm_, self.ident], w=[misc])
                if c == 3:
                    K.act(lambda: nc.scalar.copy(out=nmT[gq % 2][:], in_=misc[0:64, 512:1024]), r=[misc], w=[nmT[gq % 2]])
            items.append([None, f, f2, f3])

        def add_unit(g, qc, r, i_, kind, first, gq):
            a = i_ - 4 * qc
            if kind == "s":
                diag = a >= 0
                c0, c1 = (128 * a if diag else 0), 512
                kt, acc, voff = ks, accs, g * 65
            else:
                e_ = i_ - (4 * qc - 4)
                c0, c1 = (0, 128 * (e_ + 1)) if e_ < 4 else (128 * (e_ - 4), 512)
                kt, acc, voff = kw, accw, (4 + g) * 65
            W = c1 - c0
            q_ = qh[r]
            Mk = Mks[gq % 2]
            stt = {}

            def s0():
                u = cnt["u"]
                cnt["u"] += 1
                stt["u"] = u
                sp_, P = sps[u % 2], Pt[u % 5]
                qs = q_[:, qc * 512 + c0:qc * 512 + c1]
                K.pe(lambda: nc.tensor.matmul(sp_[:, 0:W], lhsT=kt[:, i_ * 128:(i_ + 1) * 128], rhs=qs, start=True, stop=True),
                     r=[kt, q_], w=[sp_])
                K.act(lambda: nc.scalar.activation(out=P[:, 0:W], in_=sp_[:, 0:W], func=AF.Exp, scale=0.125), r=[sp_], w=[P])
                if kind == "s":
                    K.dve(lambda: nc.vector.tensor_tensor(out=P[:, 0:W], in0=P[:, 0:W], in1=Mk[:, i_, 0:W], op=ALU.mult), r=[P, Mk], w=[P])
                else:
                    K.dve(lambda: nc.vector.tensor_tensor(out=P[:, 0:W], in0=P[:, 0:W], in1=winb[:, e_ * 512 + c0:e_ * 512 + c1], op=ALU.mult),
                          r=[P, winb], w=[P])

            def s1():
                P = Pt[stt["u"] % 5]
                for n_, c in enumerate(range(c0 // 128, c1 // 128)):
                    K.pe(lambda: nc.tensor.matmul(acc[:, c * 65:(c + 1) * 65], lhsT=P[:, c * 128 - c0:(c + 1) * 128 - c0],
                                                  rhs=V[:, i_, voff:voff + 65], start=(first and n_ == 0), stop=False, skip_group_check=True),
                         r=[P, V], w=[acc])
            items.append([s0, None, None, s1])

        def add_mask(g, qc, i_, gq):
            a = i_ - 4 * qc
            c0 = 128 * a if a >= 0 else 0
            W = 512 - c0
            nm = nmT[gq % 2]
            Mk = Mks[gq % 2]
            stt = {}

            def s0():
                u = cnt["u"]
                cnt["u"] += 1
                stt["u"] = u
                sp_ = sps[u % 2]
                K.pe(lambda: nc.tensor.matmul(sp_[:, 0:W], lhsT=esel[:, i_ * 128:(i_ + 1) * 128], rhs=nm[:, c0:512], start=True, stop=True),
                     r=[esel, nm], w=[sp_])

            def s1():
                sp_ = sps[stt["u"] % 2]
                if a >= 0:
                    K.dve(lambda: nc.vector.tensor_tensor(out=Mk[:, i_, 0:W], in0=sp_[:, 0:W], in1=causb[:, 0:W], op=ALU.mult),
                          r=[sp_, causb], w=[Mk])
                else:
                    K.act(lambda: nc.scalar.copy(out=Mk[:, i_, 0:W], in_=sp_[:, 0:W]), r=[sp_], w=[Mk])
            items.append([s0, s1])

        def add_combine(g, qc, r, gq):
            h = 4 * g + r
            Oq_ = Oq[gq % 2]

            def f():
                for bi, acc in ((1, accs), (2, accw)):
                    cf_ = cf[bi - 1]
                    av = acc[:, 0:260].rearrange("p (c d) -> p c d", d=65)
                    K.dve(lambda: nc.vector.reciprocal(out=cf_[:, 0:4], in_=av[:, :, 64]), r=[acc], w=[cf_])
                    K.dve(lambda: nc.vector.tensor_tensor(out=cf_[:, 4:8], in0=cf_[:, 0:4], in1=GT[:, 4 * qc:4 * qc + 4, 3 * h + bi], op=ALU.mult),
                          r=[cf_, GT], w=[cf_])
                    for c in range(4):
                        K.dve(lambda: nc.vector.scalar_tensor_tensor(out=Oq_[:, c, r * 64:(r + 1) * 64], in0=av[:, c, 0:64], scalar=cf_[:, 4 + c:5 + c],
                                                                     in1=Oq_[:, c, r * 64:(r + 1) * 64], op0=ALU.mult, op1=ALU.add),
                              r=[acc, cf_, Oq_], w=[Oq_])
                if r == 3:
                    ob_ = Ob[gq % 2]
                    K.pool(lambda: nc.gpsimd.tensor_copy(out=ob_[:], in_=Oq_[:]), r=[Oq_], w=[ob_])
                    K.dma("sp", Odv[:, 4 * qc:4 * qc + 4, g * 256:(g + 1) * 256], ob_[:], r=[ob_], wa=[self.b_od])
            items.append([None, None, None, f])

        pre, un = [], []
        for g in range(4):
            for qc in range(NG):
                gq = cnt["gq"]
                cnt["gq"] += 1
                del items[:]
                if qc == 0:
                    add_loads(g)
                items.append([lambda: K.pool(lambda: nc.gpsimd.memset(psg[:], 0.0), w=[psg])])
                for c in range(4):
                    for r in range(4):
                        add_cmp(g, qc, c, r, gq)
                    add_select(g, qc, c, gq)
                for _ in range(3):
                    items.append([lambda: None])
                for i_ in range(0, 4 * qc + 4):
                    add_mask(g, qc, i_, gq)
                items.append([lambda: None])
                pre.append(list(items))
                del items[:]
                for r in range(4):
                    first = True
                    for i_ in range(0, 4 * qc + 4):
                        add_unit(g, qc, r, i_, "s", first, gq)
                        first = False
                    first = True
                    for i_ in range(max(0, 4 * qc - 4), 4 * qc + 4):
                        add_unit(g, qc, r, i_, "w", first, gq)
                        first = False
                    add_combine(g, qc, r, gq)
                un.append(list(items))
        final = list(pre[0])
        ngq = len(un)
        for gq in range(ngq):
            nxt = pre[gq + 1] if gq + 1 < ngq else []
            if not nxt or (gq + 1) % NG == 0 or not NSA_INTERLEAVE:
                final += un[gq]
                final += [[lambda: None]] * 4
                final += nxt
            else:
                a_, b_ = un[gq], nxt
                bi = 0
                for ai, it in enumerate(a_):
                    final.append(it)
                    tgt = ((ai + 1) * len(b_)) // len(a_)
                    while bi < tgt:
                        final.append(b_[bi])
                        bi += 1
                final += b_[bi:]
        del items[:]
        items.extend(final)
        run_pipeline(items)


    def phase_conv(self, i, j):
        K, nc = self.K, self.nc
        UTv = self.UT.rearrange("(kc p) t -> p kc t", p=128)
        with contextlib.ExitStack() as ph:
            w = K.sb(ph, "cw", [128, 8, 2 * D], BF16)
            self.load_w(w, self.cv_w_in[j], 8, 2 * D)
            bcol = K.sb(ph, "cbcol", [128, 16], F32)
            K.dma("sp", bcol[:], self.cv_b_inT, r=[self.cbuf], w=[bcol])
            utg = [K.sb(ph, f"cutg{k}", [128, 8, 512], BF16) for k in range(2)]
            sgt = [K.sb(ph, f"csg{k}", [128, 512], F32) for k in range(2)]
            hgt = [K.sb(ph, f"chg{k}", [128, 512], BF16) for k in range(3)]
            psa = [K.ps(ph, f"cpsa{k}") for k in range(2)]
            psg = [K.ps(ph, f"cpsg{k}") for k in range(2)]

            def load_u(g):
                K.dma("sp", utg[g % 2][:], UTv[:, :, g * 512:(g + 1) * 512], r=[self.b_ut[g]], w=[utg[g % 2]])

            load_u(0)
            n = 0
            for g in range(NG):
                if g + 1 < NG:
                    load_u(g + 1)
                ug = utg[g % 2]
                for cc in range(8):
                    pa, pg = psa[n % 2], psg[n % 2]
                    sg, hg = sgt[n % 2], hgt[n % 3]
                    n += 1
                    for kc in range(8):
                        K.pe(lambda: nc.tensor.matmul(pa[:], lhsT=w[:, kc, cc * 128:(cc + 1) * 128], rhs=ug[:, kc, :],
                                                      start=(kc == 0), stop=(kc == 7)), r=[w, ug], w=[pa])
                    for kc in range(8):
                        K.pe(lambda: nc.tensor.matmul(pg[:], lhsT=w[:, kc, D + cc * 128:D + (cc + 1) * 128], rhs=ug[:, kc, :],
                                                      start=(kc == 0), stop=(kc == 7)), r=[w, ug], w=[pg])
                    K.act(lambda: nc.scalar.activation(out=sg[:], in_=pg[:], func=AF.Sigmoid, bias=bcol[:, 8 + cc:9 + cc]),
                          r=[pg, bcol], w=[sg])
                    K.dve(lambda: nc.vector.scalar_tensor_tensor(out=hg[:], in0=pa[:], scalar=bcol[:, cc:cc + 1], in1=sg[:],
                                                                 op0=ALU.add, op1=ALU.mult), r=[pa, bcol, sg], w=[hg])
                    K.dma("sp", self.HG[cc * 128:(cc + 1) * 128, g * 512:(g + 1) * 512], hg[:], r=[hg], wa=[self.b_hg[g]])
            K.barrier()
        with contextlib.ExitStack() as ph:
            w = K.sb(ph, "cow", [128, 8, D], BF16)
            self.load_w(w, self.cv_w_out[j], 8, D, split=1024)
            brow = K.sb(ph, "cobrow", [1, D], BF16)
            K.dma("pool", brow[:], self.cv_b_out[j:j + 1, :], r=[self.cbuf], w=[brow])
            dw = K.sb(ph, "cdw", [128, 8, 31], F32)
            vec = K.sb(ph, "cvec", [128, 3, 8], F32)
            onesf = K.sb(ph, "conesf", [128, 128], F32)
            K.dma("sp", dw[:], self.cv_dwT, r=[self.cbuf], w=[dw])
            K.dma("sp", vec[:], self.cv_vecT, r=[self.cbuf], w=[vec])
            K.dma("sp", onesf[:], self.c_onesf, r=[self.cbuf], w=[onesf])
            e = self.epilogue_setup(ph, i, 0)
            xin = [K.sb(ph, f"cxin{k}", [128, 8, 542], BF16) for k in range(2)]
            Dg = K.sb(ph, "cDg", [128, 8, 31, 128], BF16)
            for cc in range(8):
                for k in range(31):
                    if (cc * 31 + k) % 3 == 2:
                        K.pool(lambda: nc.gpsimd.tensor_scalar(out=Dg[:, cc, k, :], in0=self.ident[:], scalar1=dw[:, cc, k:k + 1], scalar2=None,
                                                               op0=ALU.mult), r=[self.ident, dw], w=[Dg])
                    else:
                        K.dve(lambda: nc.vector.tensor_scalar(out=Dg[:, cc, k, :], in0=self.ident[:], scalar1=dw[:, cc, k:k + 1], scalar2=None,
                                                              op0=ALU.mult), r=[self.ident, dw], w=[Dg])
            cvp = [K.ps(ph, f"ccvp{k}") for k in range(2)]
            acc = K.sb(ph, "cacc", [128, 8, 512], F32)
            sq = [K.sb(ph, f"csq{k}", [128, 512], F32) for k in range(2)]
            mt = K.sb(ph, "cm", [128, 512], F32)
            msq = K.sb(ph, "cmsq", [128, 512], F32)
            var = K.sb(ph, "cvar", [128, 512], F32)
            rstd = K.sb(ph, "crstd", [128, 512], F32)
            dt_ = [K.sb(ph, f"cd{k}", [128, 512], F32) for k in range(2)]
            xh = [K.sb(ph, f"cxh{k}", [128, 512], F32) for k in range(2)]
            hT = K.sb(ph, "chT", [128, 8, 512], BF16)
            s1 = K.ps(ph, "cs1")
            s2 = K.ps(ph, "cs2")
            psY = [[K.ps(ph, f"cpsY{k}{hf}") for hf in range(2)] for k in range(2)]
            HGv = self.HG.rearrange("(cc p) t -> p cc t", p=128)

            def load_x(g):
                xt = xin[g % 2]
                if g == 0:
                    K.pool(lambda: nc.gpsimd.memset(xt[:, :, 0:30], 0.0), w=[xt])
                    K.dma("sp", xt[:, :, 30:542], HGv[:, :, 0:512], r=[self.b_hg[0]], wa=[xt])
                else:
                    K.dma("sp", xt[:], HGv[:, :, g * 512 - 30:g * 512 + 512], r=[self.b_hg[g - 1], self.b_hg[g]], w=[xt])

            load_x(0)
            for g in range(NG):
                if g + 1 < NG:
                    load_x(g + 1)
                xt = xin[g % 2]
                for cc in range(8):
                    cv = cvp[cc % 2]
                    for k in range(31):
                        K.pe(lambda: nc.tensor.matmul(cv[:], lhsT=Dg[:, cc, k, :], rhs=xt[:, cc, k:k + 512], start=(k == 0), stop=(k == 30)),
                             r=[Dg, xt], w=[cv])
                    sq_ = sq[cc % 2]
                    K.act(lambda: nc.scalar.activation(out=sq_[:], in_=cv[:], func=AF.Square, bias=vec[:, 0, cc:cc + 1]), r=[cv, vec], w=[sq_])
                    K.dve(lambda: nc.vector.tensor_scalar(out=acc[:, cc, :], in0=cv[:], scalar1=vec[:, 0, cc:cc + 1], scalar2=None, op0=ALU.add),
                          r=[cv, vec], w=[acc])
                    K.pe(lambda: nc.tensor.matmul(s1[:], lhsT=onesf[:], rhs=acc[:, cc, :], start=(cc == 0), stop=(cc == 7)),
                         r=[onesf, acc], w=[s1])
                    K.pe(lambda: nc.tensor.matmul(s2[:], lhsT=onesf[:], rhs=sq_[:], start=(cc == 0), stop=(cc == 7)),
                         r=[onesf, sq_], w=[s2])
                K.dve(lambda: nc.vector.tensor_scalar(out=mt[:], in0=s1[:], scalar1=1.0 / D, scalar2=None, op0=ALU.mult), r=[s1], w=[mt])
                K.pool(lambda: nc.gpsimd.tensor_tensor(out=msq[:], in0=mt[:], in1=mt[:], op=ALU.mult), r=[mt], w=[msq])
                K.dve(lambda: nc.vector.scalar_tensor_tensor(out=var[:], in0=s2[:], scalar=1.0 / D, in1=msq[:], op0=ALU.mult, op1=ALU.subtract),
                      r=[s2, msq], w=[var])
                K.act(lambda: nc.scalar.activation(out=var[:], in_=var[:], func=AF.Sqrt, bias=EPS), r=[var], w=[var])
                K.dve(lambda: nc.vector.reciprocal(out=rstd[:], in_=var[:]), r=[var], w=[rstd])
                for cc in range(8):
                    d_, x_ = dt_[cc % 2], xh[cc % 2]
                    K.pool(lambda: nc.gpsimd.tensor_tensor(out=d_[:], in0=acc[:, cc, :], in1=mt[:], op=ALU.subtract), r=[acc, mt], w=[d_])
                    K.dve(lambda: nc.vector.tensor_tensor(out=x_[:], in0=d_[:], in1=rstd[:], op=ALU.mult), r=[d_, rstd], w=[x_])
                    K.act(lambda: nc.scalar.activation(out=hT[:, cc, :], in_=x_[:], func=AF.Silu, scale=vec[:, 1, cc:cc + 1],
                                                       bias=vec[:, 2, cc:cc + 1]), r=[x_, vec], w=[hT])
                for tt in range(4):
                    t = g * 4 + tt
                    self.epi_load(e, t)
                    yb = psY[tt % 2]
                    for hf in range(2):
                        for cc in range(8):
                            K.pe(lambda: nc.tensor.matmul(yb[hf][:], lhsT=hT[:, cc, tt * 128:(tt + 1) * 128], rhs=w[:, cc, hf * 512:(hf + 1) * 512],
                                                          start=(cc == 0), stop=False), r=[hT, w], w=[yb[hf]])
                        K.pe(lambda: nc.tensor.matmul(yb[hf][:], lhsT=self.ones[0:1, :], rhs=brow[0:1, hf * 512:(hf + 1) * 512],
                                                      start=False, stop=True), r=[self.ones, brow], w=[yb[hf]])
                    self.epilogue(e, t, yb)
            K.barrier()


def run_pipeline(items):
    n = len(items)
    depth = max(len(it) for it in items)
    for t in range(n + depth - 1):
        for j in range(depth):
            k = t - j
            if 0 <= k < n and j < len(items[k]) and items[k][j] is not None:
                items[k][j]()


def host_consts():
    bf = ml_dtypes.bfloat16
    p = np.arange(128)[:, None]
    y = np.arange(512)[None, :]
    c = {}
    c["c_ident"] = np.eye(128, dtype=np.float32).astype(bf)
    jj = np.arange(128)[:, None]
    ss = np.arange(128)[None, :]
    c["c_tri"] = (jj >= ss).astype(np.float32).astype(bf)
    c["c_ones"] = np.ones((128, 128), np.float32).astype(bf)
    c["c_sbmask"] = (y > p).astype(np.float32).astype(bf)
    c["c_sbbias"] = np.where(y > p, 0.0, BIG).astype(np.float32).astype(bf)
    c["c_onesf"] = np.ones((128, 128), np.float32)
    perm = np.zeros((64, 64), np.float32)
    for i in range(8):
        perm[i + 8, i] = -1.0
        perm[i, i + 8] = 1.0
    c["c_perm"] = perm.astype(bf)
    invf = np.zeros((64, 1), np.float32)
    fr = (500000.0 ** (-np.arange(8, dtype=np.float32) / 8.0)).astype(np.float32)
    invf[0:8, 0] = fr
    invf[8:16, 0] = fr
    c["c_invf"] = invf
    jj = np.arange(64)[:, None, None]
    ii = np.arange(32)[None, :, None]
    sk = np.arange(128)[None, None, :]
    c["c_esel"] = (jj == 2 * ii + (sk >= 64)).astype(np.float32).reshape(64, 32 * 128).astype(bf)
    e = np.arange(8)[None, :, None]
    f = np.arange(512)[None, None, :]
    pp = np.arange(128)[:, None, None]
    dlt = f - pp + 512 - 128 * e
    c["c_win01"] = ((dlt >= 0) & (dlt < 512)).astype(np.float32).reshape(128, 8 * 512).astype(bf)
    c["c_caus01"] = (y >= p).astype(np.float32).astype(bf)
    m = np.arange(512)[None, :] - 256
    c["c_band"] = np.where(p >= 16 * m + 31, 0.0, -BIG).astype(np.float32).astype(bf)
    col = np.arange(128)[None, :]
    dl = (col - 64) - (p >= 64)
    c["c_wcm"] = (dl < -1).astype(np.float32)
    c["c_wfb"] = np.where(dl > 0, -1.0e9, np.where(dl >= -1, 1.0e4, 0.0)).astype(np.float32)
    c["c_anyok"] = (np.arange(128)[:, None] >= 31).astype(np.float32)
    return c


def make_in_map(inputs, b, prog):
    m = {}
    m["x"] = np.ascontiguousarray(inputs["x"][b])
    m["cT"] = np.ascontiguousarray(np.asarray(inputs["c"][b]).reshape(8, 128).T)
    m["pos"] = np.ascontiguousarray(np.asarray(inputs["positions"][b]).reshape(1, S).astype(np.int32))
    for n in ("ada_w", "ada_b", "mix_pre_g", "mix_post_g", "ffn_pre_g", "ffn_post_g", "ffn_w1", "ffn_w2", "sb_w_in", "sb_w_out",
              "nsa_w_in", "nsa_w_out", "nsa_w1_k", "nsa_w2_k", "nsa_w1_v", "nsa_w2_v",
              "cv_w_in", "cv_w_out", "cv_b_out"):
        m[n] = np.asarray(inputs[n])
    m["nsa_pe_kT"] = np.ascontiguousarray(np.asarray(inputs["nsa_pe_k"])[0].T)
    m["nsa_pe_vT"] = np.ascontiguousarray(np.asarray(inputs["nsa_pe_v"])[0].T)
    m["cv_b_inT"] = np.ascontiguousarray(np.asarray(inputs["cv_b_in"]).reshape(16, 128).T)
    m["cv_dwT"] = np.ascontiguousarray(np.asarray(inputs["cv_dw"]).reshape(31, 8, 128).transpose(2, 1, 0))
    m["cv_vecT"] = np.ascontiguousarray(np.stack([np.asarray(inputs[k]).reshape(8, 128).T for k in ("cv_dw_b", "cv_ln_g", "cv_ln_b")], axis=1))
    m.update(host_consts())
    return {k: v for k, v in m.items() if k in prog.in_names}


_PROG_CACHE = {}


def kernel(**inputs):
    prog = Prog()
    nc = prog.build()
    B = inputs["x"].shape[0]
    in_maps = [make_in_map(inputs, b, prog) for b in range(B)]
    res = run_bass_kernel_spmd(nc, in_maps, core_ids=list(range(B)))
    return np.stack([np.asarray(r["out"]) for r in res.results], axis=0).astype(np.float32)
```
